# Optimizing a Trainium2 kernel written in Bass

```python
import math
import jax, jax.numpy as jnp
from jax import lax
import numpy as np

D_MODEL = 2048
BATCH = 16
SEQ = 2048
DEPTH = 4

GRID_W = 64
CTX_LEN = 256
EPS = 1e-6

D_MIX = D_MODEL
N_GROUPS = 4
GROUP_W = D_MIX // N_GROUPS
S5_WIDTH = GROUP_W
S5_CH_PER_GROUP = 16
S5_GROUPS = S5_WIDTH // S5_CH_PER_GROUP
S5_STATE = 64
S5_MIN_NEG = -1e-4
ML_HEADS = 4
ML_HEAD_DIM = GROUP_W // ML_HEADS
MLSTM_CHUNK = 64
MLA_HEADS = 4
MLA_NOPE = 128
MLA_ROPE = 64
MLA_QK = MLA_NOPE + MLA_ROPE
MLA_V = GROUP_W // MLA_HEADS
MLA_Q_LORA = 384
MLA_KV_LORA = 128
ATTN_SCALE = MLA_QK ** -0.5
ROPE_BASE = 10000.0
Q_BLOCK = 128
LRU_WIDTH = GROUP_W
LRU_BLOCKS = 4
LRU_BLOCK_W = LRU_WIDTH // LRU_BLOCKS
LRU_CONV = 4
LRU_C = 8.0
D_FF = 4 * D_MODEL
IN_SPLITS = (S5_WIDTH, GROUP_W, GROUP_W, GROUP_W, GROUP_W, 4 * ML_HEADS,
             MLA_Q_LORA, MLA_KV_LORA, MLA_ROPE, LRU_WIDTH, LRU_WIDTH)
IN_COLS = sum(IN_SPLITS)

kernel_name = 'hybrid_parallel_group_flow_trunk'


def rmsnorm(x, w):
    xf = x.astype(jnp.float32)
    y = xf * lax.rsqrt(jnp.mean(xf * xf, axis=-1, keepdims=True) + EPS)
    return (y * w.astype(jnp.float32)).astype(x.dtype)


def modulate(h, shift, scale):
    return h * (1.0 + scale) + shift


def split_cols(z):
    idx = np.cumsum(np.array(IN_SPLITS))[:-1].tolist()
    return jnp.split(z, idx, axis=-1)


def last_state(h, reverse):
    return h[:, 0] if reverse else h[:, -1]


def _real_combine(e1, e2):
    a1, b1 = e1
    a2, b2 = e2
    return a1 * a2, a2 * b1 + b2


def linear_scan(a, b, h0, reverse):
    if reverse:
        a, b = jnp.flip(a, 1), jnp.flip(b, 1)
    if h0 is not None:
        b = b.at[:, 0].add(a[:, 0] * h0)
    _, h = lax.associative_scan(_real_combine, (a, b), axis=1)
    return jnp.flip(h, 1) if reverse else h


def _complex_combine(e1, e2):
    a1r, a1i, b1r, b1i = e1
    a2r, a2i, b2r, b2i = e2
    return (a1r * a2r - a1i * a2i, a1r * a2i + a1i * a2r,
            a2r * b1r - a2i * b1i + b2r, a2r * b1i + a2i * b1r + b2i)


def complex_scan(ar, ai, br, bi, h0, reverse):
    ar = jnp.broadcast_to(ar, br.shape)
    ai = jnp.broadcast_to(ai, br.shape)
    if reverse:
        br, bi = jnp.flip(br, 1), jnp.flip(bi, 1)
    if h0 is not None:
        h0r, h0i = h0
        br = br.at[:, 0].add(ar[:, 0] * h0r - ai[:, 0] * h0i)
        bi = bi.at[:, 0].add(ar[:, 0] * h0i + ai[:, 0] * h0r)
    _, _, hr, hi = lax.associative_scan(_complex_combine, (ar, ai, br, bi), axis=1)
    if reverse:
        hr, hi = jnp.flip(hr, 1), jnp.flip(hi, 1)
    return hr, hi


def s5_mixer(u_l, u_c, lam_re, lam_im, log_dt, b_re, b_im, c_re, c_im, d_skip, glu_w, glu_b, need_ctx):
    def groups(u):
        return u.astype(jnp.float32).reshape(u.shape[0], u.shape[1], S5_GROUPS, S5_CH_PER_GROUP)
    gl, gc = groups(u_l), groups(u_c)
    y_l, y_c = 0.0, 0.0
    for d in range(2):
        rev = d == 1
        lr = jnp.minimum(lam_re[d].astype(jnp.float32), S5_MIN_NEG)
        li = lam_im[d].astype(jnp.float32)
        dt = jnp.exp(log_dt[d].astype(jnp.float32))[:, None]
        mag = jnp.exp(lr * dt)
        ar, ai = mag * jnp.cos(li * dt), mag * jnp.sin(li * dt)
        den = lr * lr + li * li
        fr = ((ar - 1.0) * lr + ai * li) / den
        fi = (ai * lr - (ar - 1.0) * li) / den
        bbr = fr[..., None] * b_re[d] - fi[..., None] * b_im[d]
        bbi = fr[..., None] * b_im[d] + fi[..., None] * b_re[d]

        def drive(u):
            return (jnp.einsum('blgc,gpc->blgp', u, bbr), jnp.einsum('blgc,gpc->blgp', u, bbi))

        def readout(sr, si):
            return jnp.einsum('blgp,gcp->blgc', sr, c_re[d]) - jnp.einsum('blgp,gcp->blgc', si, c_im[d])

        sc_r, sc_i = complex_scan(ar, ai, *drive(gc), None, rev)
        sl_r, sl_i = complex_scan(ar, ai, *drive(gl), (last_state(sc_r, rev), last_state(sc_i, rev)), rev)
        y_l = y_l + readout(sl_r, sl_i)
        if need_ctx:
            y_c = y_c + readout(sc_r, sc_i)

    def finish(y, u):
        y = y.reshape(u.shape[0], u.shape[1], S5_WIDTH) + d_skip * u.astype(jnp.float32)
        y = jax.nn.gelu(y)
        return y * jax.nn.sigmoid(y @ glu_w + glu_b)

    return finish(y_l, u_l), (finish(y_c, u_c) if need_ctx else None)


def mlstm_scan(q, k, v, logi, logf, state0, with_output):
    B_, H_, L_, dh = q.shape
    nc = L_ // MLSTM_CHUNK
    lower = jnp.tril(jnp.ones((MLSTM_CHUNK, MLSTM_CHUNK), dtype=bool))

    def chunks(t):
        return jnp.moveaxis(t.reshape((B_, H_, nc, MLSTM_CHUNK) + t.shape[3:]), 2, 0)

    def step(carry, xs):
        C, n, m = carry
        qc, kc, vc, li, lf = xs
        b = jnp.cumsum(lf, axis=-1)
        b_last = b[..., -1]
        w_log = b_last[..., None] - b + li
        m_new = jnp.maximum(b_last + m, jnp.max(w_log, axis=-1))
        decay = jnp.exp(b_last + m - m_new)
        w = jnp.exp(w_log - m_new[..., None])
        C_new = decay[..., None, None] * C + jnp.einsum('bhs,bhsd,bhse->bhde', w, vc, kc)
        n_new = decay[..., None] * n + jnp.einsum('bhs,bhse->bhe', w, kc)
        if not with_output:
            return (C_new, n_new, m_new), None
        g = b + m[..., None]
        d_log = jnp.where(lower, b[..., :, None] - b[..., None, :] + li[..., None, :], -jnp.inf)
        m_t = jnp.maximum(g, jnp.max(d_log, axis=-1))
        inter = jnp.exp(g - m_t)
        s = jnp.einsum('bhtd,bhsd->bhts', qc, kc) * jnp.exp(d_log - m_t[..., None])
        num = inter[..., None] * jnp.einsum('bhde,bhte->bhtd', C, qc) + jnp.einsum('bhts,bhsd->bhtd', s, vc)
        den = inter * jnp.einsum('bhe,bhte->bht', n, qc) + jnp.sum(s, axis=-1)
        h = num / jnp.maximum(jnp.abs(den), jnp.exp(-m_t))[..., None]
        return (C_new, n_new, m_new), h

    state, h = lax.scan(step, state0, tuple(chunks(t) for t in (q, k, v, logi, logf)))
    if with_output:
        h = jnp.moveaxis(h, 0, 2).reshape(B_, H_, L_, dh)
    return h, state


def mlstm_mixer(q_l, k_l, v_l, o_l, g_l, q_c, k_c, v_c, o_c, g_c, ig_bias, fg_bias, out_norm, need_ctx):
    def heads(t):
        return jnp.transpose(t.astype(jnp.float32).reshape(t.shape[0], t.shape[1], ML_HEADS, ML_HEAD_DIM), (0, 2, 1, 3))

    def gates(g, d):
        g = g.astype(jnp.float32).reshape(g.shape[0], g.shape[1], 2, 2, ML_HEADS)
        logi = g[:, :, d, 0] + ig_bias[d]
        logf = jax.nn.log_sigmoid(g[:, :, d, 1] + fg_bias[d])
        return (jnp.transpose(logi, (0, 2, 1)), jnp.transpose(logf, (0, 2, 1)))

    k_scale = ML_HEAD_DIM ** -0.5
    lat = (heads(q_l), heads(k_l) * k_scale, heads(v_l))
    ctx = (heads(q_c), heads(k_c) * k_scale, heads(v_c))
    B_ = q_l.shape[0]
    zero = (jnp.zeros((B_, ML_HEADS, ML_HEAD_DIM, ML_HEAD_DIM), jnp.float32),
            jnp.zeros((B_, ML_HEADS, ML_HEAD_DIM), jnp.float32),
            jnp.zeros((B_, ML_HEADS), jnp.float32))
    h_l, h_c = 0.0, 0.0
    for d in range(2):
        flip = (lambda t: jnp.flip(t, axis=2)) if d == 1 else (lambda t: t)
        seq_c = [flip(t) for t in ctx + gates(g_c, d)]
        seq_l = [flip(t) for t in lat + gates(g_l, d)]
        out_c, state = mlstm_scan(*seq_c, zero, need_ctx)
        out_l, _ = mlstm_scan(*seq_l, state, True)
        h_l = h_l + flip(out_l)
        if need_ctx:
            h_c = h_c + flip(out_c)

    def finish(h, o):
        hn = rmsnorm(h, out_norm[:, None, :])
        hn = jnp.transpose(hn, (0, 2, 1, 3)).reshape(o.shape[0], o.shape[1], GROUP_W)
        return hn * jax.nn.sigmoid(o.astype(jnp.float32))

    return finish(h_l, o_l), (finish(h_c, o_c) if need_ctx else None)


def axial_angles(n_tokens):
    n_rows = n_tokens // GRID_W
    rows = jnp.repeat(jnp.arange(n_rows, dtype=jnp.float32), GRID_W)
    cols = jnp.tile(jnp.arange(GRID_W, dtype=jnp.float32), n_rows)
    n_freq = MLA_ROPE // 4
    inv_freq = ROPE_BASE ** (-jnp.arange(n_freq, dtype=jnp.float32) / n_freq)
    return rows[:, None] * inv_freq, cols[:, None] * inv_freq


def rotate_pairs(x, ang):
    f = ang.shape[-1]
    cos, sin = jnp.cos(ang)[None, :, None, :], jnp.sin(ang)[None, :, None, :]
    x1, x2 = x[..., :f].astype(jnp.float32), x[..., f:].astype(jnp.float32)
    return jnp.concatenate([x1 * cos - x2 * sin, x2 * cos + x1 * sin], axis=-1)


def rope_2d(x, ang_row, ang_col):
    half = MLA_ROPE // 2
    rope = x[..., MLA_NOPE:]
    rot = jnp.concatenate([rotate_pairs(rope[..., :half], ang_row), rotate_pairs(rope[..., half:], ang_col)], axis=-1)
    return jnp.concatenate([x[..., :MLA_NOPE], rot.astype(x.dtype)], axis=-1)


def mla_queries(cq, q_a_norm, w_q_up, q_norm, angles):
    q = (rmsnorm(cq, q_a_norm) @ w_q_up).reshape(cq.shape[0], cq.shape[1], MLA_HEADS, MLA_QK)
    q = rmsnorm(q, q_norm)
    return q if angles is None else rope_2d(q, *angles)


def mla_keys_values(ckv, kr, kv_a_norm, w_kv_up, k_norm, angles):
    B_, L_ = ckv.shape[:2]
    kv = (rmsnorm(ckv, kv_a_norm) @ w_kv_up).reshape(B_, L_, MLA_HEADS, MLA_NOPE + MLA_V)
    k_rope = jnp.broadcast_to(kr[:, :, None, :], (B_, L_, MLA_HEADS, MLA_ROPE))
    k = rmsnorm(jnp.concatenate([kv[..., :MLA_NOPE], k_rope], axis=-1), k_norm)
    k = k if angles is None else rope_2d(k, *angles)
    return k, kv[..., MLA_NOPE:]


def attend(q, k, v):
    s = jnp.einsum('bqhd,bkhd->bhqk', q, k).astype(jnp.float32) * ATTN_SCALE
    p = jax.nn.softmax(s, axis=-1)
    return jnp.einsum('bhqk,bkhd->bqhd', p.astype(v.dtype), v)


def mla_mixer(cq_l, ckv_l, kr_l, cq_c, ckv_c, kr_c, q_a_norm, w_q_up, kv_a_norm, w_kv_up, q_norm, k_norm, need_ctx):
    B_, L_ = cq_l.shape[:2]
    ang = axial_angles(L_)
    q_l = mla_queries(cq_l, q_a_norm, w_q_up, q_norm, ang)
    k_l, v_l = mla_keys_values(ckv_l, kr_l, kv_a_norm, w_kv_up, k_norm, ang)
    k_c, v_c = mla_keys_values(ckv_c, kr_c, kv_a_norm, w_kv_up, k_norm, None)
    k_all = jnp.concatenate([k_c, k_l], axis=1)
    v_all = jnp.concatenate([v_c, v_l], axis=1)
    n_blocks = L_ // Q_BLOCK
    q_blocks = jnp.moveaxis(q_l.reshape(B_, n_blocks, Q_BLOCK, MLA_HEADS, MLA_QK), 1, 0)
    o_blocks = lax.map(lambda qb: attend(qb, k_all, v_all), q_blocks)
    y_l = jnp.moveaxis(o_blocks, 0, 1).reshape(B_, L_, MLA_HEADS * MLA_V)
    if not need_ctx:
        return y_l, None
    q_c = mla_queries(cq_c, q_a_norm, w_q_up, q_norm, None)
    y_c = attend(q_c, k_c, v_c).reshape(B_, cq_c.shape[1], MLA_HEADS * MLA_V)
    return y_l, y_c


def conv_centred(x, w, b):
    L_ = x.shape[1]
    left = LRU_CONV // 2
    xp = jnp.pad(x, ((0, 0), (left, LRU_CONV - 1 - left), (0, 0)))
    y = b
    for j in range(LRU_CONV):
        y = y + xp[:, j:j + L_] * w[j]
    return y


def rglru_mixer(x_l, gate_l, x_c, gate_c, conv_w, conv_b, wa, ba, wx, bx, lam, need_ctx):
    xs_l = conv_centred(x_l, conv_w, conv_b).astype(jnp.float32)
    xs_c = conv_centred(x_c, conv_w, conv_b).astype(jnp.float32)

    def drive(xs, d):
        B_, L_ = xs.shape[:2]
        xb = xs.reshape(B_, L_, LRU_BLOCKS, LRU_BLOCK_W)
        r = jax.nn.sigmoid(jnp.einsum('blnc,ncd->blnd', xb, wa[d]).reshape(B_, L_, LRU_WIDTH) + ba[d])
        i = jax.nn.sigmoid(jnp.einsum('blnc,ncd->blnd', xb, wx[d]).reshape(B_, L_, LRU_WIDTH) + bx[d])
        log_a = -LRU_C * r * jax.nn.softplus(-lam[d].astype(jnp.float32))
        return jnp.exp(log_a), jnp.sqrt(-jnp.expm1(2.0 * log_a)) * (i * xs)

    y_l, y_c = 0.0, 0.0
    for d in range(2):
        rev = d == 1
        h_c = linear_scan(*drive(xs_c, d), None, rev)
        h_l = linear_scan(*drive(xs_l, d), last_state(h_c, rev), rev)
        y_l = y_l + h_l
        if need_ctx:
            y_c = y_c + h_c
    out_l = y_l * jax.nn.gelu(gate_l.astype(jnp.float32))
    out_c = y_c * jax.nn.gelu(gate_c.astype(jnp.float32)) if need_ctx else None
    return out_l, out_c


def sq_relu_mlp(h, w1, w2):
    a = jax.nn.relu(h @ w1)
    return (a * a) @ w2


def setup_inputs(seed: int = 0) -> dict:
    key = jax.random.key(seed)
    ks = iter(jax.random.split(key, 48))

    def nrm(shape, scale):
        return jax.random.normal(next(ks), shape, jnp.float32) * scale

    def gain(shape):
        return 1.0 + nrm(shape, 0.02)

    L2 = (DEPTH, 2)
    lru_u = jax.random.uniform(next(ks), L2 + (LRU_WIDTH,), jnp.float32, 0.9, 0.999)
    lru_a = lru_u ** (1.0 / LRU_C)
    return {
        'x': nrm((BATCH, SEQ, D_MODEL), 1.0),
        'c': nrm((BATCH, D_MODEL), 1.0),
        'ctx': nrm((BATCH, CTX_LEN, D_MODEL), 1.0),
        'c_ctx': nrm((D_MODEL,), 1.0),
        'ada_w': nrm((DEPTH, D_MODEL, 6 * D_MODEL), 0.5 * D_MODEL ** -0.5),
        'ada_b': nrm((DEPTH, 6 * D_MODEL), 0.02),
        'norm1_w': gain((DEPTH, D_MODEL)),
        'norm2_w': gain((DEPTH, D_MODEL)),
        'w_in': nrm((DEPTH, D_MODEL, IN_COLS), D_MODEL ** -0.5),
        'w_out': nrm((DEPTH, D_MIX, D_MODEL), D_MIX ** -0.5),
        's5_lam_re': -0.5 + nrm(L2 + (S5_GROUPS, S5_STATE), 0.01),
        's5_lam_im': jnp.pi * jnp.arange(S5_STATE, dtype=jnp.float32) + nrm(L2 + (S5_GROUPS, S5_STATE), 0.01),
        's5_log_dt': jax.random.uniform(next(ks), L2 + (S5_GROUPS,), jnp.float32, math.log(1e-3), math.log(1e-1)),
        's5_b_re': nrm(L2 + (S5_GROUPS, S5_STATE, S5_CH_PER_GROUP), (2 * S5_CH_PER_GROUP) ** -0.5),
        's5_b_im': nrm(L2 + (S5_GROUPS, S5_STATE, S5_CH_PER_GROUP), (2 * S5_CH_PER_GROUP) ** -0.5),
        's5_c_re': nrm(L2 + (S5_GROUPS, S5_CH_PER_GROUP, S5_STATE), (2 * S5_STATE) ** -0.5),
        's5_c_im': nrm(L2 + (S5_GROUPS, S5_CH_PER_GROUP, S5_STATE), (2 * S5_STATE) ** -0.5),
        's5_d': nrm((DEPTH, S5_WIDTH), 1.0),
        's5_glu_w': nrm((DEPTH, S5_WIDTH, S5_WIDTH), S5_WIDTH ** -0.5),
        's5_glu_b': nrm((DEPTH, S5_WIDTH), 0.02),
        'ml_ig_bias': nrm(L2 + (ML_HEADS,), 0.1),
        'ml_fg_bias': jnp.linspace(3.0, 6.0, ML_HEADS) + nrm(L2 + (ML_HEADS,), 0.1),
        'ml_out_norm': gain((DEPTH, ML_HEADS, ML_HEAD_DIM)),
        'mla_q_a_norm': gain((DEPTH, MLA_Q_LORA)),
        'mla_w_q_up': nrm((DEPTH, MLA_Q_LORA, MLA_HEADS * MLA_QK), MLA_Q_LORA ** -0.5),
        'mla_kv_a_norm': gain((DEPTH, MLA_KV_LORA)),
        'mla_w_kv_up': nrm((DEPTH, MLA_KV_LORA, MLA_HEADS * (MLA_NOPE + MLA_V)), MLA_KV_LORA ** -0.5),
        'mla_q_norm': gain((DEPTH, MLA_QK)),
        'mla_k_norm': gain((DEPTH, MLA_QK)),
        'lru_conv_w': nrm((DEPTH, LRU_CONV, LRU_WIDTH), 0.5),
        'lru_conv_b': nrm((DEPTH, LRU_WIDTH), 0.02),
        'lru_wa': nrm(L2 + (LRU_BLOCKS, LRU_BLOCK_W, LRU_BLOCK_W), LRU_BLOCK_W ** -0.5),
        'lru_ba': nrm(L2 + (LRU_WIDTH,), 0.02),
        'lru_wx': nrm(L2 + (LRU_BLOCKS, LRU_BLOCK_W, LRU_BLOCK_W), LRU_BLOCK_W ** -0.5),
        'lru_bx': nrm(L2 + (LRU_WIDTH,), 0.02),
        'lru_lam': jnp.log(lru_a) - jnp.log1p(-lru_a),
        'mlp_w1': nrm((DEPTH, D_MODEL, D_FF), D_MODEL ** -0.5),
        'mlp_w2': nrm((DEPTH, D_FF, D_MODEL), D_FF ** -0.5),
    }


def reference(x, c, ctx, c_ctx, ada_w, ada_b, norm1_w, norm2_w, w_in, w_out,
              s5_lam_re, s5_lam_im, s5_log_dt, s5_b_re, s5_b_im, s5_c_re, s5_c_im, s5_d, s5_glu_w, s5_glu_b,
              ml_ig_bias, ml_fg_bias, ml_out_norm,
              mla_q_a_norm, mla_w_q_up, mla_kv_a_norm, mla_w_kv_up, mla_q_norm, mla_k_norm,
              lru_conv_w, lru_conv_b, lru_wa, lru_ba, lru_wx, lru_bx, lru_lam,
              mlp_w1, mlp_w2):
    x_lat, x_ctx = x, ctx
    c_act = jax.nn.silu(c.astype(jnp.float32))
    cctx_act = jax.nn.silu(c_ctx.astype(jnp.float32))
    for l in range(DEPTH):
        need_ctx = l < DEPTH - 1
        mod_l = jnp.split((c_act @ ada_w[l] + ada_b[l])[:, None, :], 6, axis=-1)
        mod_c = jnp.split(cctx_act @ ada_w[l] + ada_b[l], 6, axis=-1)

        h_l = modulate(rmsnorm(x_lat, norm1_w[l]), mod_l[0], mod_l[1])
        h_c = modulate(rmsnorm(x_ctx, norm1_w[l]), mod_c[0], mod_c[1])
        (u_l, mq_l, mk_l, mv_l, mo_l, mg_l, cq_l, ckv_l, kr_l, lx_l, lg_l) = split_cols(h_l @ w_in[l])
        (u_c, mq_c, mk_c, mv_c, mo_c, mg_c, cq_c, ckv_c, kr_c, lx_c, lg_c) = split_cols(h_c @ w_in[l])

        a_l, a_c = s5_mixer(u_l, u_c, s5_lam_re[l], s5_lam_im[l], s5_log_dt[l], s5_b_re[l], s5_b_im[l],
                            s5_c_re[l], s5_c_im[l], s5_d[l], s5_glu_w[l], s5_glu_b[l], need_ctx)
        b_l, b_c = mlstm_mixer(mq_l, mk_l, mv_l, mo_l, mg_l, mq_c, mk_c, mv_c, mo_c, mg_c,
                               ml_ig_bias[l], ml_fg_bias[l], ml_out_norm[l], need_ctx)
        m_l, m_c = mla_mixer(cq_l, ckv_l, kr_l, cq_c, ckv_c, kr_c, mla_q_a_norm[l], mla_w_q_up[l],
                             mla_kv_a_norm[l], mla_w_kv_up[l], mla_q_norm[l], mla_k_norm[l], need_ctx)
        r_l, r_c = rglru_mixer(lx_l, lg_l, lx_c, lg_c, lru_conv_w[l], lru_conv_b[l], lru_wa[l], lru_ba[l],
                               lru_wx[l], lru_bx[l], lru_lam[l], need_ctx)
        x_lat = x_lat + mod_l[2] * (jnp.concatenate([a_l, b_l, m_l, r_l], axis=-1) @ w_out[l])
        if need_ctx:
            x_ctx = x_ctx + mod_c[2] * (jnp.concatenate([a_c, b_c, m_c, r_c], axis=-1) @ w_out[l])

        x_lat = x_lat + mod_l[5] * sq_relu_mlp(modulate(rmsnorm(x_lat, norm2_w[l]), mod_l[3], mod_l[4]), mlp_w1[l], mlp_w2[l])
        if need_ctx:
            x_ctx = x_ctx + mod_c[5] * sq_relu_mlp(modulate(rmsnorm(x_ctx, norm2_w[l]), mod_c[3], mod_c[4]), mlp_w1[l], mlp_w2[l])
    return x_lat
```

```python
import numpy as np
import ml_dtypes
from contextlib import ExitStack, contextmanager
import concourse.bass as bass
import concourse.mybir as mybir
from concourse.bass_utils import run_bass_kernel_spmd

F32 = mybir.dt.float32
BF16 = mybir.dt.bfloat16
I32 = mybir.dt.int32
AF = mybir.ActivationFunctionType
ALU = mybir.AluOpType
AX = mybir.AxisListType

D = 2048
T = 2304
CTX = 256
LAT = 2048
NT = 18
DEPTH = 4
DFF = 8192
INC = 4176
EPS = 1e-6
TB = [(0, 256), (256, 512), (768, 512), (1280, 512), (1792, 512)]
TWO_PI = float(2 * np.pi)
SAME_SYNC = True
MERGE_WAIT = True
CAP = 30000


class Res:
    __slots__ = ("w", "r")

    def __init__(self):
        self.w = None
        self.r = {}


class Buf:
    def __init__(self, t, psum=False):
        self.t = t
        self.res = Res()
        self.psum = psum

    def __getitem__(self, key):
        return self.t[key]


class Ring:
    def __init__(self, bufs):
        self.bufs = bufs
        self.i = 0

    def next(self):
        b = self.bufs[self.i]
        self.i = (self.i + 1) % len(self.bufs)
        return b


class KB:
    def __init__(self, nc):
        self.nc = nc
        self.E = {"pe": nc.tensor, "dve": nc.vector, "act": nc.scalar, "pool": nc.gpsimd, "sp": nc.sync}
        self.sem = {}
        self.cnt = {}
        self.owner = {}
        self.nsem = 0
        for e in self.E:
            self._fresh(e)
        self.waited = {e: {} for e in self.E}
        self.dq = {}
        self.dqi = {}
        for q, n in (("sp", 12), ("pool", 4), ("act", 4)):
            self.dq[q] = [[self._newsem("d"), 0] for _ in range(n)]
            self.dqi[q] = 0
        self.stack = []
        self.uid = 0

    def _newsem(self, pfx):
        self.nsem += 1
        return self.nc.alloc_semaphore(f"{pfx}{self.nsem}")

    def _fresh(self, e):
        s = self._newsem("e")
        self.sem[e] = s
        self.cnt[e] = 0
        self.owner[s] = e

    @contextmanager
    def scope(self):
        st = ExitStack()
        self.stack.append(st)
        try:
            yield
        finally:
            self.barrier()
            self.stack.pop()
            st.close()

    def _name(self, n):
        self.uid += 1
        return f"{n}_{self.uid}"

    def sb(self, name, shape, dtype):
        t = self.stack[-1].enter_context(self.nc.sbuf_tensor(self._name(name), list(shape), dtype))
        return Buf(t)

    def ps(self, name, shape=(128, 512), dtype=F32):
        t = self.stack[-1].enter_context(self.nc.psum_tensor(self._name(name), list(shape), dtype))
        return Buf(t, psum=True)

    def ring(self, name, shape, dtype, n):
        return Ring([self.sb(name, shape, dtype) for _ in range(n)])

    def psring(self, name, n, shape=(128, 512), dtype=F32):
        return Ring([self.ps(name, shape, dtype) for _ in range(n)])

    def _need(self, e, tok, out):
        if tok is None:
            return
        sem, val = tok
        own = self.owner.get(sem)
        if own == e and (e == "pe" or not SAME_SYNC):
            return
        w = self.waited[e]
        if w.get(sem, 0) >= val:
            return
        w[sem] = val
        for i, (s_, v_) in enumerate(out):
            if s_ is sem or s_ == sem:
                out[i] = (sem, max(v_, val))
                return
        out.append((sem, val))

    def _wait(self, e, tok):
        out = []
        self._need(e, tok, out)
        for (s_, v_) in out:
            self.E[e].wait_ge(s_, v_)

    def _deps(self, e, reads, writes):
        out = []
        for r in reads:
            r = r.res if isinstance(r, Buf) else r
            self._need(e, r.w, out)
        for wr in writes:
            wr = wr.res if isinstance(wr, Buf) else wr
            self._need(e, wr.w, out)
            for s_, v_ in wr.r.items():
                self._need(e, (s_, v_), out)
        return out

    def _commit(self, tok, reads, writes):
        sem, val = tok
        for r in reads:
            r = r.res if isinstance(r, Buf) else r
            if r.r.get(sem, 0) < val:
                r.r[sem] = val
        for wr in writes:
            wr = wr.res if isinstance(wr, Buf) else wr
            wr.w = tok
            wr.r = {}

    def op(self, e, fn, reads=(), writes=(), merge=True):
        pr = [r for r in reads if isinstance(r, Buf) and r.psum]
        if pr:
            reads = [r for r in reads if not (isinstance(r, Buf) and r.psum)]
            writes = list(writes) + pr
        need = self._deps(e, reads, writes)
        last = None
        if merge and MERGE_WAIT and need:
            last = need.pop()
        for (s_, v_) in need:
            self.E[e].wait_ge(s_, v_)
        ins = fn(self.E[e])
        if last is not None:
            ins._wait_ge(last[0], last[1])
        self.cnt[e] += 1
        ins.then_inc(self.sem[e], 1)
        tok = (self.sem[e], self.cnt[e])
        self._commit(tok, reads, writes)
        if self.cnt[e] >= CAP:
            self._fresh(e)

    def dma(self, q, out, in_, reads=(), writes=()):
        slots = self.dq[q]
        slot = slots[self.dqi[q]]
        self.dqi[q] = (self.dqi[q] + 1) % len(slots)
        if slot[1] > 0:
            self._wait(q, (slot[0], slot[1]))
        if slot[1] >= CAP:
            slot[0] = self._newsem("d")
            slot[1] = 0
        for (s_, v_) in self._deps(q, reads, writes):
            self.E[q].wait_ge(s_, v_)
        ins = self.E[q].dma_start(out=out, in_=in_)
        slot[1] += 16
        ins.then_inc(slot[0], 16)
        self._commit((slot[0], slot[1]), reads, writes)

    def barrier(self):
        toks = [(self.sem[e], self.cnt[e]) for e in self.E if self.cnt[e] > 0]
        for q in self.dq:
            for slot in self.dq[q]:
                if slot[1] > 0:
                    toks.append((slot[0], slot[1]))
        for e in self.E:
            for tok in toks:
                if self.owner.get(tok[0]) == e:
                    continue
                self._wait(e, tok)

    def mm(self, out, lhsT, rhs, start, stop, reads, writes):
        self.op("pe", lambda e: e.matmul(out, lhsT, rhs, start=start, stop=stop), reads, writes)

    def tr(self, out, in_, ident, reads, writes):
        self.op("pe", lambda e: e.transpose(out, in_, ident), reads, writes)

    def act(self, out, in_, func, reads, writes, bias=0.0, scale=1.0, **kw):
        self.op("act", lambda e: e.activation(out=out, in_=in_, func=func, bias=bias, scale=scale, **kw), reads, writes,
                merge=("accum_out" not in kw))

    def ts(self, out, in0, s1, s2, op0, op1, reads, writes, eng="dve"):
        if s2 is None:
            self.op(eng, lambda e: e.tensor_scalar(out=out, in0=in0, scalar1=s1, scalar2=None, op0=op0), reads, writes)
        else:
            self.op(eng, lambda e: e.tensor_scalar(out=out, in0=in0, scalar1=s1, scalar2=s2, op0=op0, op1=op1), reads, writes)

    def tt(self, out, in0, in1, op, reads, writes, eng="dve"):
        self.op(eng, lambda e: e.tensor_tensor(out=out, in0=in0, in1=in1, op=op), reads, writes)

    def stt(self, out, in0, scalar, in1, op0, op1, reads, writes):
        self.op("dve", lambda e: e.scalar_tensor_tensor(out=out, in0=in0, scalar=scalar, in1=in1, op0=op0, op1=op1), reads, writes)

    def cp(self, out, in_, reads, writes, eng="dve"):
        if eng == "act":
            self.op("act", lambda e: e.copy(out=out, in_=in_), reads, writes)
        else:
            self.op(eng, lambda e: e.tensor_copy(out=out, in_=in_), reads, writes)

    def scan(self, out, d0, d1, init, reads, writes):
        self.op("dve", lambda e: e.tensor_tensor_scan(out=out, data0=d0, data1=d1, initial=init, op0=ALU.mult, op1=ALU.add), reads, writes)


class Prog:
    def __init__(self, n_layers=DEPTH, dbg=None, layers=None):
        self.dbg = dbg or {}
        self.layers = list(range(n_layers)) if layers is None else layers
        nc = bass.Bass("TRN2", target_bir_lowering=False)
        self.nc = nc
        self.k = KB(nc)
        self.inp = {}
        self.build()

    def din(self, name, shape, dtype=F32):
        t = self.nc.dram_tensor(name, list(shape), dtype, kind="ExternalInput")
        self.inp[name] = (tuple(shape), dtype)
        return t.ap()

    def dscr(self, name, shape, dtype=F32):
        kind = "ExternalOutput" if name in self.dbg else "Internal"
        return self.nc.dram_tensor(name, list(shape), dtype, kind=kind).ap()

    def build(self):
        nc, k = self.nc, self.k
        L = DEPTH
        self.xin = self.din("xin", [2, D, T])
        self.cT = self.din("cT", [128, 16, 3])
        self.ada_w = self.din("ada_w", [L, D, 6 * D])
        self.ada_bT = self.din("ada_bT", [L, 128, 96])
        self.n1w = self.din("n1w", [L, 128, 16])
        self.n2w = self.din("n2w", [L, 128, 16])
        self.w_in = self.din("w_in", [L, D, INC])
        self.w_out = self.din("w_out", [L, D, D])
        self.mlp_w1 = self.din("mlp_w1", [L, D, DFF])
        self.mlp_w2 = self.din("mlp_w2", [L, DFF, D])
        self.consts = self.din("consts", [128, 2048])
        self.declare_mixer_inputs()
        self.yout = nc.dram_tensor("yout", [2, D, LAT], F32, kind="ExternalOutput").ap()
        self.xs = self.dscr("xs", [2, D, T])
        self.zf = self.dscr("zf", [2, 2560, T])
        self.zt = self.dscr("zt", [2, T, 2128])
        self.W1t = self.dscr("W1t", [64, 128, 16, 128], BF16)
        self.W2t = self.dscr("W2t", [4, 16, 128, 16, 128], BF16)
        if self.dbg.get("cc_in"):
            self.cc = self.din("cc", [2, D, T], BF16)
        else:
            self.cc = self.dscr("cc", [2, D, T], BF16)
        if "modv_o" in self.dbg:
            self.modv_o = self.dscr("modv_o", [128, 288])

        with k.scope():
            self.setup_consts()
            for b in range(2):
                for c in range(16):
                    k.dma("sp", self.xs[b, c * 128:(c + 1) * 128, :], self.xin[b, c * 128:(c + 1) * 128, :])
            k.barrier()
            for l in self.layers:
                self.layer(l)
            for b in range(2):
                for c in range(16):
                    k.dma("sp", self.yout[b, c * 128:(c + 1) * 128, :], self.xs[b, c * 128:(c + 1) * 128, CTX:T])

    def setup_consts(self):
        k = self.k
        self.C = k.sb("consts", [128, 2048], F32)
        k.dma("sp", self.C.t[:], self.consts[:, :], [], [self.C])
        self.identF = self.C.t[:, 0:128]
        self.onesF = self.C.t[:, 128:256]
        self.identB = k.sb("identB", [128, 128], BF16)
        k.cp(self.identB.t[:], self.identF, [self.C], [self.identB])
        self.cs = k.sb("cs", [128, 16, 3], F32)
        k.dma("sp", self.cs.t[:], self.cT[:, :, :], [], [self.cs])
        k.act(self.cs.t[:], self.cs.t[:], AF.Silu, [self.cs], [self.cs])
        self.modv = k.sb("modv", [128, 96, 3], F32)
        self.g1 = k.sb("g1", [128, 16, 3], F32)
        self.g2 = k.sb("g2", [128, 16, 3], F32)
        self.epsT = k.sb("epsT", [128, 1], F32)
        k.op("dve", lambda e: e.memset(self.epsT.t[:], EPS), [], [self.epsT])
        self.setup_mixer_consts()

    def layer(self, l):
        k = self.k
        self.stage_mod(l)
        for b in range(2):
            self.stage_A(l, b)
        self.mixers(l)
        for b in range(2):
            self.stage_proj_res(l, b, which="out")
        self.stage_wcast(l)
        for b in range(2):
            self.stage_mlp(l, b)

    def stage_mod(self, l):
        k = self.k
        with k.scope():
            wr = k.ring("adaw", [128, 16, 128], F32, 3)
            pm = k.ps("pmod")
            adab = k.sb("adab", [128, 96], F32)
            nw1 = k.sb("nw1", [128, 16], F32)
            nw2 = k.sb("nw2", [128, 16], F32)
            k.dma("sp", adab.t[:], self.ada_bT[l], [], [adab])
            k.dma("sp", nw1.t[:], self.n1w[l], [], [nw1])
            k.dma("sp", nw2.t[:], self.n2w[l], [], [nw2])
            wv = self.ada_w[l].rearrange("(kc p) f -> p kc f", p=128)
            for j in range(96):
                w = wr.next()
                k.dma("sp", w.t[:], wv[:, :, j * 128:(j + 1) * 128], [], [w])
                for kc in range(16):
                    k.mm(pm.t[:, 3 * j:3 * j + 3], w.t[:, kc, :], self.cs.t[:, kc, :], kc == 0, kc == 15,
                         [w, self.cs], [pm])
            pv = pm.t[:, 0:288].rearrange("p (j r) -> p j r", r=3)
            for r in range(3):
                k.tt(self.modv.t[:, :, r], pv[:, :, r], adab.t[:], ALU.add, [pm, adab], [self.modv])
            if "modv_o" in self.dbg:
                k.dma("sp", self.modv_o[:, :], self.modv.t[:].rearrange("p j r -> p (j r)"), [self.modv], [])
            for r in range(3):
                k.stt(self.g1.t[:, :, r], self.modv.t[:, 16:32, r], 1.0, nw1.t[:], ALU.add, ALU.mult,
                      [self.modv, nw1], [self.g1])
                k.stt(self.g2.t[:, :, r], self.modv.t[:, 64:80, r], 1.0, nw2.t[:], ALU.add, ALU.mult,
                      [self.modv, nw2], [self.g2])

    def make_hT(self, hT, b, g, shift_base, blocks=TB, rel=False, nx=2, base=None, nmax=512):
        k = self.k
        with k.scope():
            xr = k.ring("xblk", [128, 16, nmax], F32, nx)
            sqr = k.ring("sq", [128, nmax], F32, 3)
            rsr = k.ring("rstd", [128, nmax], F32, 2)
            tmr = k.ring("tmp", [128, nmax], F32, 3)
            pss = k.psring("ss", 2)
            xv = self.xs[b].rearrange("(c p) t -> p c t", p=128)
            for (t0, n) in blocks:
                r = 2 if t0 < CTX else b
                o0 = (t0 - base) if base is not None else (0 if rel else t0)
                xb = xr.next()
                for c in range(16):
                    k.dma("sp", xb.t[:, c, 0:n], xv[:, c, t0:t0 + n], [], [xb])
                ss = pss.next()
                for c in range(16):
                    sq = sqr.next()
                    k.act(sq.t[:, 0:n], xb.t[:, c, 0:n], AF.Square, [xb], [sq])
                    k.mm(ss.t[:, 0:n], self.onesF, sq.t[:, 0:n], c == 0, c == 15, [sq, self.C], [ss])
                rs = rsr.next()
                k.act(rs.t[:, 0:n], ss.t[:, 0:n], AF.Sqrt, [ss, self.epsT], [rs], bias=self.epsT.t[:, 0:1], scale=1.0 / D)
                k.op("dve", lambda e: e.reciprocal(out=rs.t[:, 0:n], in_=rs.t[:, 0:n]), [rs], [rs])
                for c in range(16):
                    tm = tmr.next()
                    k.stt(tm.t[:, 0:n], xb.t[:, c, 0:n], g.t[:, c, r:r + 1], rs.t[:, 0:n], ALU.mult, ALU.mult,
                          [xb, g, rs], [tm])
                    k.act(hT.t[:, c, o0:o0 + n], tm.t[:, 0:n], AF.Identity, [tm, self.modv], [hT],
                          bias=self.modv.t[:, shift_base + c, r:r + 1], scale=1.0)

    FM_CHUNKS = [0, 128, 256, 384, 512, 640, 768, 896, 1024, 1152, 1280, 1408,
                 3152, 3280, 3408, 3536, 3664, 3792, 3920, 4048]
    TM_BLOCKS = [(1024, 512), (1536, 512), (2048, 512), (2560, 512), (3072, 80)]

    def stage_A(self, l, b):
        k = self.k
        with k.scope():
            hT = k.sb("hT", [128, 16, T], BF16)
            self.make_hT(hT, b, self.g1, 0)
            wv = self.w_in[l].rearrange("(kc p) c -> p kc c", p=128)
            with k.scope():
                wf = k.ring("wf", [128, 16, 128], F32, 2)
                wb = k.ring("wb", [128, 16, 128], BF16, 2)
                ob = k.ring("ob", [128, 512], F32, 3)
                pp = k.psring("pp", 3)
                ei = 0
                for ci, c0 in enumerate(self.FM_CHUNKS):
                    w32 = wf.next()
                    k.dma("sp", w32.t[:], wv[:, :, c0:c0 + 128], [], [w32])
                    w16 = wb.next()
                    k.cp(w16.t[:], w32.t[:], [w32], [w16], eng="pool")
                    for (t0, n) in TB:
                        p = pp.next()
                        for kc in range(16):
                            k.mm(p.t[:, 0:n], w16.t[:, kc, :], hT.t[:, kc, t0:t0 + n], kc == 0, kc == 15, [w16, hT], [p])
                        o = ob.next()
                        k.cp(o.t[:, 0:n], p.t[:, 0:n], [p], [o], eng=("act" if ei % 2 else "dve"))
                        ei += 1
                        k.dma("sp", self.zf[b, ci * 128:(ci + 1) * 128, t0:t0 + n], o.t[:, 0:n], [o], [])
            with k.scope():
                wf = k.ring("wf2", [128, 8, 512], F32, 2)
                wb = k.ring("wb2", [128, 16, 512], BF16, 2)
                ob = k.ring("ob2", [128, 512], F32, 3)
                pp = k.psring("pp2", 3)
                ei = 0
                for (c0, w) in self.TM_BLOCKS:
                    w16 = wb.next()
                    for hf in range(2):
                        w32 = wf.next()
                        k.dma("sp", w32.t[:, :, 0:w], wv[:, hf * 8:(hf + 1) * 8, c0:c0 + w], [], [w32])
                        k.cp(w16.t[:, hf * 8:(hf + 1) * 8, 0:w], w32.t[:, :, 0:w], [w32], [w16], eng="pool")
                    for tt in range(NT):
                        p = pp.next()
                        for kc in range(16):
                            k.mm(p.t[:, 0:w], hT.t[:, kc, tt * 128:(tt + 1) * 128], w16.t[:, kc, 0:w], kc == 0, kc == 15,
                                 [w16, hT], [p])
                        o = ob.next()
                        k.cp(o.t[:, 0:w], p.t[:, 0:w], [p], [o], eng=("act" if ei % 2 else "dve"))
                        ei += 1
                        k.dma("sp", self.zt[b, tt * 128:(tt + 1) * 128, c0 - 1024:c0 - 1024 + w], o.t[:, 0:w], [o], [])

    def stage_proj_res(self, l, b, which):
        k = self.k
        with k.scope():
            cT = k.sb("ccT", [128, 16, T], BF16)
            cv = self.cc[b].rearrange("(c p) t -> p c t", p=128)
            for c in range(16):
                k.dma("sp", cT.t[:, c, :], cv[:, c, :], [], [cT])
            wv = self.w_out[l].rearrange("(kc p) c -> p kc c", p=128)
            xv = self.xs[b].rearrange("(c p) t -> p c t", p=128)
            wf = k.ring("wf", [128, 16, 128], F32, 2)
            wb = k.ring("wb", [128, 16, 128], BF16, 2)
            xr = k.ring("xo", [128, 512], F32, 3)
            pp = k.psring("pp", 3)
            for fc in range(16):
                w32 = wf.next()
                k.dma("sp", w32.t[:], wv[:, :, fc * 128:(fc + 1) * 128], [], [w32])
                w16 = wb.next()
                k.cp(w16.t[:], w32.t[:], [w32], [w16], eng="pool")
                for (t0, n) in TB:
                    r = 2 if t0 < CTX else b
                    xo = xr.next()
                    k.dma("sp", xo.t[:, 0:n], xv[:, fc, t0:t0 + n], [], [xo])
                    p = pp.next()
                    for kc in range(16):
                        k.mm(p.t[:, 0:n], w16.t[:, kc, :], cT.t[:, kc, t0:t0 + n], kc == 0, kc == 15, [w16, cT], [p])
                    k.stt(xo.t[:, 0:n], p.t[:, 0:n], self.modv.t[:, 32 + fc, r:r + 1], xo.t[:, 0:n], ALU.mult, ALU.add,
                          [p, self.modv, xo], [xo])
                    k.dma("sp", xv[:, fc, t0:t0 + n], xo.t[:, 0:n], [xo], [])

    MLP_BLOCKS = [[(0, 256), (256, 256), (512, 256)], [(768, 256), (1024, 256), (1280, 256)],
                  [(1536, 256), (1792, 256), (2048, 256)]]

    def stage_wcast(self, l):
        k = self.k
        with k.scope():
            f32r = k.ring("wc32", [128, 8192], F32, 2)
            b16r = k.ring("wc16", [128, 8192], BF16, 2)
            engs = ["dve", "act", "pool"]
            ei = 0
            w1tv = self.W1t.rearrange("fc p kc j -> p fc kc j")
            for kc in range(16):
                a = f32r.next()
                k.dma("sp", a.t[:], self.mlp_w1[l, kc * 128:(kc + 1) * 128, :], [], [a])
                bb = b16r.next()
                for q in range(4):
                    k.cp(bb.t[:, q * 2048:(q + 1) * 2048], a.t[:, q * 2048:(q + 1) * 2048], [a], [bb], eng=engs[ei % 3])
                    ei += 1
                k.dma("sp", w1tv[:, :, kc, :], bb.t[:].rearrange("p (fc j) -> p fc j", j=128), [bb], [])
            w2v = self.mlp_w2[l].rearrange("(fg fc p) d -> fg fc p d", fc=16, p=128)
            w2tv = self.W2t.rearrange("fg dc p fc j -> fg fc p dc j")
            for fg in range(4):
                for f4 in range(4):
                    a = f32r.next()
                    for f in range(4):
                        k.dma("sp", a.t[:, f * 2048:(f + 1) * 2048], w2v[fg, f4 * 4 + f], [], [a])
                    bb = b16r.next()
                    for q in range(4):
                        k.cp(bb.t[:, q * 2048:(q + 1) * 2048], a.t[:, q * 2048:(q + 1) * 2048], [a], [bb], eng=engs[ei % 3])
                        ei += 1
                    for f in range(4):
                        k.dma("sp", w2tv[fg, f4 * 4 + f], bb.t[:, f * 2048:(f + 1) * 2048].rearrange("p (dc j) -> p dc j", j=128), [bb], [])

    def stage_mlp(self, l, b):
        k = self.k
        with k.scope():
            hT = k.sb("hT2", [128, 16, 768], BF16)
            oacc = k.sb("oacc", [128, 16, 768], F32)
            aTr = k.ring("aT", [128, 16, 768], BF16, 2)
            w1r = k.ring("w1s", [128, 16, 128], BF16, 3)
            w2r = k.ring("w2s", [128, 16, 128], BF16, 3)
            rl = k.ring("rl", [128, 512], F32, 3)
            xr = k.ring("xo", [128, 768], F32, 2)
            pp = k.psring("pp", 3)
            pq = k.psring("pq", 2)
            xv = self.xs[b].rearrange("(c p) t -> p c t", p=128)
            MM = ((0, 512), (512, 256))
            for subs in self.MLP_BLOCKS:
                base = subs[0][0]
                self.make_hT(hT, b, self.g2, 48, subs, nx=1, base=base, nmax=256)
                for fg in range(4):
                    aT = aTr.next()
                    for fc in range(16):
                        w = w1r.next()
                        k.dma("sp", w.t[:], self.W1t[fg * 16 + fc], [], [w])
                        for (o, n) in MM:
                            p = pp.next()
                            for kc in range(16):
                                k.mm(p.t[:, 0:n], w.t[:, kc, :], hT.t[:, kc, o:o + n], kc == 0, kc == 15, [w, hT], [p])
                            rr = rl.next()
                            k.act(rr.t[:, 0:n], p.t[:, 0:n], AF.Relu, [p], [rr])
                            k.tt(aT.t[:, fc, o:o + n], rr.t[:, 0:n], rr.t[:, 0:n], ALU.mult, [rr], [aT])
                    for dc in range(16):
                        w = w2r.next()
                        k.dma("sp", w.t[:], self.W2t[fg, dc], [], [w])
                        for (o, n) in MM:
                            p = pq.next()
                            for fc in range(16):
                                k.mm(p.t[:, 0:n], w.t[:, fc, :], aT.t[:, fc, o:o + n], fc == 0, fc == 15, [w, aT], [p])
                            if fg == 0:
                                k.cp(oacc.t[:, dc, o:o + n], p.t[:, 0:n], [p], [oacc], eng="act")
                            else:
                                k.tt(oacc.t[:, dc, o:o + n], p.t[:, 0:n], oacc.t[:, dc, o:o + n], ALU.add, [p, oacc], [oacc])
                rng = []
                for (t0, n) in subs:
                    r = 2 if t0 < CTX else b
                    if rng and rng[-1][2] == r:
                        rng[-1][1] += n
                    else:
                        rng.append([t0 - base, n, r])
                for dc in range(16):
                    xo = xr.next()
                    k.dma("sp", xo.t[:], xv[:, dc, base:base + 768], [], [xo])
                    for (o, n, r) in rng:
                        k.stt(xo.t[:, o:o + n], oacc.t[:, dc, o:o + n], self.modv.t[:, 80 + dc, r:r + 1], xo.t[:, o:o + n],
                              ALU.mult, ALU.add, [oacc, self.modv, xo], [xo])
                    k.dma("sp", xv[:, dc, base:base + 768], xo.t[:], [xo], [])

    def declare_mixer_inputs(self):
        L = DEPTH
        self.s5v = self.din("s5v", [L, 128, 192])
        self.s5A = self.din("s5A", [L, 128, 64, 16])
        self.s5B = self.din("s5B", [L, 128, 64, 16])
        self.s5CA = self.din("s5CA", [L, 128, 64, 16])
        self.s5CB = self.din("s5CB", [L, 128, 64, 16])
        self.s5w = self.din("s5w", [L, 128, 8])
        self.glu_w = self.din("glu_w", [L, 512, 512])
        self.nidx = self.din("nidx", [2, 128, T])
        self.lruv = self.din("lruv", [L, 128, 44])
        self.lru_wa = self.din("lru_wa", [L, 2, 4, 128, 128])
        self.lru_wx = self.din("lru_wx", [L, 2, 4, 128, 128])
        self.ygd = self.dscr("ygd", [2, 512, T])
        self.mlb = self.din("mlb", [L, 128, 16])
        self.onw = self.din("onw", [L, 128, 512])
        self.selc = self.din("selc", [16, 2048])
        self.mlav = self.din("mlav", [L, 128, 896])
        self.w_q_up = self.din("w_q_up", [L, 384, 768])
        self.w_kv_up = self.din("w_kv_up", [L, 128, 1024])
        self.ropec = self.din("ropec", [128, NT, 32])
        self.ropes = self.din("ropes", [128, NT, 32])

    def setup_mixer_consts(self):
        k = self.k
        self.sgn = self.C.t[:, 896:897]
        self.oneT = k.sb("oneT", [128, 1], F32)
        k.op("dve", lambda e: e.memset(self.oneT.t[:], 1.0), [], [self.oneT])
        self.hpiT = k.sb("hpiT", [128, 1], F32)
        k.op("dve", lambda e: e.memset(self.hpiT.t[:], float(np.pi / 2)), [], [self.hpiT])

    def mixers(self, l):
        which = self.dbg.get("mixers", ("s5", "lru", "mlstm", "mla"))
        if "s5" in which:
            self.mixer_s5(l)
        if "lru" in which:
            self.mixer_lru(l)
        if "mlstm" in which:
            self.mixer_mlstm(l)
        if "mla" in which:
            self.mixer_mla(l)

    def frac_centered(self, out, u, ki, tmp, n, bufs):
        k = self.k
        k.cp(ki, u, bufs, bufs)
        k.tt(tmp, u, ki, ALU.subtract, bufs, bufs)
        k.stt(out, tmp, 0.5, tmp, ALU.is_gt, ALU.subtract, bufs, bufs)
        k.stt(out, out, 0.5, out, ALU.is_gt, ALU.subtract, bufs, bufs)

    def sincos(self, sin_out, cos_out, r, tmp, bufs):
        k = self.k
        k.act(sin_out, r, AF.Sin, bufs, bufs, scale=TWO_PI)
        k.stt(tmp, r, 0.25, r, ALU.is_gt, ALU.subtract, bufs, bufs)
        k.act(cos_out, tmp, AF.Sin, bufs + [self.hpiT], bufs, scale=-TWO_PI, bias=self.hpiT.t[:, 0:1])

    def gelu_tanh(self, out, y, t1, s1, bufs):
        k = self.k
        k.tt(t1, y, y, ALU.mult, bufs, bufs)
        k.ts(t1, t1, 0.044715, 1.0, ALU.mult, ALU.add, bufs, bufs)
        k.tt(t1, t1, y, ALU.mult, bufs, bufs)
        k.act(s1, t1, AF.Sigmoid, bufs, bufs, scale=1.5957691216057308)
        k.tt(out, y, s1, ALU.mult, bufs, bufs)

    def mixer_s5(self, l):
        k = self.k
        with k.scope():
            pv = k.sb("s5pv", [128, 192], F32)
            k.dma("sp", pv.t[:], self.s5v[l], [], [pv])
            A = k.sb("s5A", [128, 64, 16], F32)
            Bm = k.sb("s5B", [128, 64, 16], F32)
            CA = k.sb("s5CA", [128, 64, 16], F32)
            CB = k.sb("s5CB", [128, 64, 16], F32)
            k.dma("sp", A.t[:], self.s5A[l], [], [A])
            k.dma("sp", Bm.t[:], self.s5B[l], [], [Bm])
            k.dma("sp", CA.t[:], self.s5CA[l], [], [CA])
            k.dma("sp", CB.t[:], self.s5CB[l], [], [CB])
            sw = k.sb("s5w", [128, 8], F32)
            k.dma("sp", sw.t[:], self.s5w[l], [], [sw])
            nid = k.sb("nidx", [128, 2, T], F32)
            for d in range(2):
                k.dma("sp", nid.t[:, d, :], self.nidx[d], [], [nid])
            W = k.sb("s5work", [128, 16, 64], F32)
            WI = k.sb("s5worki", [128, 64], I32)
            Wb = [W]
            lr, li, dt, mag, ang, fT, sn, cs_, t0_, t1_, fr, fi, den, fis, frs, ar1 = [W.t[:, i, :] for i in range(16)]
            k.ts(lr, pv.t[:, 0:64], -1e-4, None, ALU.min, None, [pv], Wb)
            k.cp(li, pv.t[:, 64:128], [pv], Wb)
            k.act(dt, pv.t[:, 128:192], AF.Exp, [pv], Wb)
            k.tt(t0_, lr, dt, ALU.mult, Wb, Wb)
            k.act(mag, t0_, AF.Exp, Wb, Wb)
            k.tt(ang, li, dt, ALU.mult, Wb, Wb)
            k.ts(t0_, ang, 1.0 / TWO_PI, None, ALU.mult, None, Wb, Wb)
            self.frac_centered(fT, t0_, WI.t[:], t1_, 64, Wb + [WI])
            self.sincos(sn, cs_, fT, t1_, Wb)
            k.tt(t0_, mag, cs_, ALU.mult, Wb, Wb)
            k.ts(ar1, t0_, -1.0, None, ALU.add, None, Wb, Wb)
            k.tt(t1_, mag, sn, ALU.mult, Wb, Wb)
            k.tt(den, lr, lr, ALU.mult, Wb, Wb)
            k.tt(t0_, li, li, ALU.mult, Wb, Wb)
            k.tt(den, den, t0_, ALU.add, Wb, Wb)
            k.op("dve", lambda e: e.reciprocal(out=den, in_=den), Wb, Wb)
            k.tt(fr, ar1, lr, ALU.mult, Wb, Wb)
            k.tt(t0_, t1_, li, ALU.mult, Wb, Wb)
            k.tt(fr, fr, t0_, ALU.add, Wb, Wb)
            k.tt(fr, fr, den, ALU.mult, Wb, Wb)
            k.tt(fi, t1_, lr, ALU.mult, Wb, Wb)
            k.tt(t0_, ar1, li, ALU.mult, Wb, Wb)
            k.tt(fi, fi, t0_, ALU.subtract, Wb, Wb)
            k.tt(fi, fi, den, ALU.mult, Wb, Wb)
            k.ts(fis, fi, self.sgn, None, ALU.mult, None, Wb + [self.C], Wb)
            k.ts(frs, fr, self.sgn, -1.0, ALU.mult, ALU.mult, Wb + [self.C], Wb)
            nsgn = k.sb("nsgn", [128, 1], F32)
            k.ts(nsgn.t[:], self.sgn, -1.0, None, ALU.mult, None, [self.C], [nsgn])

            X1 = k.sb("X1", [128, 128], F32)
            X2 = k.sb("X2", [128, 128], F32)
            xt = k.sb("xtmp", [128, 16], F32)
            BP1 = k.ring("BP1", [128, 128], BF16, 2)
            BP2 = k.ring("BP2", [128, 128], BF16, 2)
            W1p = k.ring("W1p", [128, 128], BF16, 2)
            W2p = k.ring("W2p", [128, 128], BF16, 2)
            SIN = k.sb("SIN", [128, T], F32)
            COS = k.sb("COS", [128, T], F32)
            U_ = k.sb("U", [128, T], F32)
            KI = k.sb("KI", [128, T], I32)
            R_ = k.sb("R", [128, T], F32)
            ub = [k.sb("ub", [128, T], BF16) for _ in range(2)]
            u32 = k.sb("u32", [128, T], F32)
            bt = k.sb("bt", [128, T], F32)
            G = k.sb("G", [128, T], F32)
            V1 = k.sb("V1", [128, T], BF16)
            V2 = k.sb("V2", [128, T], BF16)
            yacc = [k.sb("yacc", [128, T], F32) for _ in range(2)]
            t1r = k.ring("t1", [128, 512], F32, 2)
            pd = k.psring("pd", 4)
            py = k.psring("py", 2)
            ptr = k.ps("ptr")
            for c in range(4):
                for b in range(2):
                    k.dma("sp", u32.t[:], self.zf[b, c * 128:(c + 1) * 128, :], [], [u32])
                    k.cp(ub[b].t[:], u32.t[:], [u32], [ub[b]], eng="act")
                first = True
                for j in range(8):
                    g = 8 * c + j
                    for d in range(2):
                        dg = d * 32 + g
                        cols = slice(16 * j, 16 * j + 16)
                        k.op("dve", lambda e: e.memset(X1.t[:], 0.0), [], [X1])
                        k.op("dve", lambda e: e.memset(X2.t[:], 0.0), [], [X2])
                        k.ts(xt.t[:], A.t[:, dg, :], fr[:, dg:dg + 1], None, ALU.mult, None, [A] + Wb, [xt])
                        k.stt(X1.t[:, cols], Bm.t[:, dg, :], fis[:, dg:dg + 1], xt.t[:], ALU.mult, ALU.add, [Bm, xt] + Wb, [X1])
                        k.ts(xt.t[:], A.t[:, dg, :], fi[:, dg:dg + 1], None, ALU.mult, None, [A] + Wb, [xt])
                        k.stt(X2.t[:, cols], Bm.t[:, dg, :], frs[:, dg:dg + 1], xt.t[:], ALU.mult, ALU.add, [Bm, xt] + Wb, [X2])
                        bp1 = BP1.next()
                        bp2 = BP2.next()
                        k.tr(ptr.t[:, 0:128], X1.t[:], self.identF, [X1, self.C], [ptr])
                        k.cp(bp1.t[:], ptr.t[:, 0:128], [ptr], [bp1], eng="act")
                        k.tr(ptr.t[:, 128:256], X2.t[:], self.identF, [X2, self.C], [ptr])
                        k.cp(bp2.t[:], ptr.t[:, 128:256], [ptr], [bp2], eng="act")
                        w1 = W1p.next()
                        w2 = W2p.next()
                        k.op("pool", lambda e: e.memset(w1.t[:], 0.0), [], [w1])
                        k.op("pool", lambda e: e.memset(w2.t[:], 0.0), [], [w2])
                        k.ts(w1.t[:, cols], CA.t[:, dg, :], nsgn.t[:, 0:1], None, ALU.mult, None, [CA, nsgn], [w1])
                        k.ts(w2.t[:, cols], CB.t[:, dg, :], -1.0, None, ALU.mult, None, [CB], [w2])
                        k.ts(U_.t[:], nid.t[:, d, :], fT[:, dg:dg + 1], None, ALU.mult, None, [nid] + Wb, [U_])
                        self.frac_centered(R_.t[:], U_.t[:], KI.t[:], U_.t[:], T, [U_, KI, R_])
                        self.sincos(SIN.t[:], COS.t[:], R_.t[:], U_.t[:], [R_, U_, SIN, COS])
                        for b in range(2):
                            for (t0, n) in TB:
                                p1 = pd.next()
                                p2 = pd.next()
                                k.mm(p1.t[:, 0:n], bp1.t[:], ub[b].t[:, t0:t0 + n], True, True, [bp1, ub[b]], [p1])
                                k.mm(p2.t[:, 0:n], bp2.t[:], ub[b].t[:, t0:t0 + n], True, True, [bp2, ub[b]], [p2])
                                t1 = t1r.next()
                                k.tt(t1.t[:, 0:n], p1.t[:, 0:n], COS.t[:, t0:t0 + n], ALU.mult, [p1, COS], [t1])
                                k.tt(bt.t[:, t0:t0 + n], p2.t[:, 0:n], SIN.t[:, t0:t0 + n], ALU.mult, [p2, SIN], [bt])
                                k.tt(bt.t[:, t0:t0 + n], bt.t[:, t0:t0 + n], t1.t[:, 0:n], ALU.add, [bt, t1], [bt])
                            rm = mag[:, dg:dg + 1]
                            if d == 0:
                                k.scan(G.t[:, 0:CTX], rm.to_broadcast([128, CTX]), bt.t[:, 0:CTX], 0.0, [bt] + Wb, [G])
                                k.scan(G.t[:, CTX:T], rm.to_broadcast([128, LAT]), bt.t[:, CTX:T], G.t[:, CTX - 1:CTX], [bt, G] + Wb, [G])
                            else:
                                k.scan(G.t[:, 0:CTX][:, ::-1], rm.to_broadcast([128, CTX]), bt.t[:, 0:CTX][:, ::-1], 0.0, [bt] + Wb, [G])
                                k.scan(G.t[:, CTX:T][:, ::-1], rm.to_broadcast([128, LAT]), bt.t[:, CTX:T][:, ::-1], G.t[:, 0:1], [bt, G] + Wb, [G])
                            k.tt(V1.t[:], G.t[:], COS.t[:], ALU.mult, [G, COS], [V1], eng="pool")
                            k.tt(V2.t[:], G.t[:], SIN.t[:], ALU.mult, [G, SIN], [V2], eng="pool")
                            for (t0, n) in TB:
                                p = py.next()
                                k.mm(p.t[:, 0:n], w1.t[:], V1.t[:, t0:t0 + n], True, False, [w1, V1], [p])
                                k.mm(p.t[:, 0:n], w2.t[:], V2.t[:, t0:t0 + n], False, True, [w2, V2], [p])
                                if first:
                                    k.cp(yacc[b].t[:, t0:t0 + n], p.t[:, 0:n], [p], [yacc[b]])
                                else:
                                    k.tt(yacc[b].t[:, t0:t0 + n], p.t[:, 0:n], yacc[b].t[:, t0:t0 + n], ALU.add, [p, yacc[b]], [yacc[b]])
                        first = False
                for b in range(2):
                    k.dma("sp", u32.t[:], self.zf[b, c * 128:(c + 1) * 128, :], [], [u32])
                    k.stt(yacc[b].t[:], u32.t[:], sw.t[:, c:c + 1], yacc[b].t[:], ALU.mult, ALU.add, [u32, sw, yacc[b]], [yacc[b]])
                    self.gelu_tanh(G.t[:], yacc[b].t[:], bt.t[:], U_.t[:], [G, yacc[b], bt, U_])
                    k.dma("sp", self.ygd[b, c * 128:(c + 1) * 128, :], G.t[:], [G], [])
        with k.scope():
            sw = k.sb("s5w", [128, 8], F32)
            k.dma("sp", sw.t[:], self.s5w[l], [], [sw])
            gw32 = k.sb("gw32", [128, 4, 512], F32)
            gw = k.sb("gw", [128, 4, 512], BF16)
            k.dma("sp", gw32.t[:], self.glu_w[l].rearrange("(kc p) o -> p kc o", p=128), [], [gw32])
            k.cp(gw.t[:], gw32.t[:], [gw32], [gw], eng="pool")
            yg = k.sb("yg", [128, 4, T], F32)
            ygb = k.sb("ygb", [128, 4, T], BF16)
            sg = k.ring("sg", [128, 512], F32, 2)
            ob = k.ring("ob", [128, 512], BF16, 3)
            pp = k.psring("pg", 3)
            for b in range(2):
                for c in range(4):
                    k.dma("sp", yg.t[:, c, :], self.ygd[b, c * 128:(c + 1) * 128, :], [], [yg])
                    k.cp(ygb.t[:, c, :], yg.t[:, c, :], [yg], [ygb], eng="act")
                for co in range(4):
                    for (t0, n) in TB:
                        p = pp.next()
                        for kc in range(4):
                            k.mm(p.t[:, 0:n], gw.t[:, kc, co * 128:(co + 1) * 128], ygb.t[:, kc, t0:t0 + n], kc == 0, kc == 3, [gw, ygb], [p])
                        s_ = sg.next()
                        k.act(s_.t[:, 0:n], p.t[:, 0:n], AF.Sigmoid, [p, sw], [s_], bias=sw.t[:, 4 + co:5 + co])
                        o = ob.next()
                        k.tt(o.t[:, 0:n], yg.t[:, co, t0:t0 + n], s_.t[:, 0:n], ALU.mult, [yg, s_], [o])
                        k.dma("sp", self.cc[b, co * 128:(co + 1) * 128, t0:t0 + n], o.t[:, 0:n], [o], [])

    def mixer_lru(self, l):
        k = self.k
        with k.scope():
            lv = k.sb("lruv", [128, 44], F32)
            k.dma("sp", lv.t[:], self.lruv[l], [], [lv])
            cw = lv.t[:, 0:16].rearrange("p (c j) -> p c j", j=4)
            cb = lv.t[:, 16:20]
            ba = lv.t[:, 20:28].rearrange("p (d c) -> p d c", c=4)
            bx = lv.t[:, 28:36].rearrange("p (d c) -> p d c", c=4)
            lam = lv.t[:, 36:44]
            sp = k.sb("lrusp", [128, 16], F32)
            k.act(sp.t[:, 0:8], lam, AF.Exp, [lv], [sp], scale=-1.0)
            k.act(sp.t[:, 0:8], sp.t[:, 0:8], AF.Ln, [sp, self.oneT], [sp], bias=self.oneT.t[:, 0:1])
            k.ts(sp.t[:, 8:16], sp.t[:, 0:8], -16.0, None, ALU.mult, None, [sp], [sp])
            k.ts(sp.t[:, 0:8], sp.t[:, 0:8], -8.0, None, ALU.mult, None, [sp], [sp])
            wa32 = k.sb("wa32", [128, 8, 128], F32)
            wx32 = k.sb("wx32", [128, 8, 128], F32)
            wa = k.sb("wa", [128, 8, 128], BF16)
            wx = k.sb("wx", [128, 8, 128], BF16)
            k.dma("sp", wa32.t[:], self.lru_wa[l].rearrange("d n c o -> c (d n) o"), [], [wa32])
            k.dma("sp", wx32.t[:], self.lru_wx[l].rearrange("d n c o -> c (d n) o"), [], [wx32])
            k.cp(wa.t[:], wa32.t[:], [wa32], [wa])
            k.cp(wx.t[:], wx32.t[:], [wx32], [wx])
            x = k.sb("lx", [128, T], F32)
            gt = k.sb("lg", [128, T], F32)
            xs = k.sb("lxs", [128, T], F32)
            xsb = k.sb("lxsb", [128, T], BF16)
            r_ = k.sb("lr", [128, T], F32)
            i_ = k.sb("li", [128, T], F32)
            a_ = k.sb("la", [128, T], F32)
            q_ = k.sb("lq", [128, T], F32)
            h_ = k.sb("lh", [128, T], F32)
            ys = k.sb("lys", [128, T], F32)
            ob = k.sb("lob", [128, T], BF16)
            pp = k.psring("pl", 4)
            for b in range(2):
                for c in range(4):
                    k.dma("sp", x.t[:], self.zf[b, 1536 + c * 128:1536 + (c + 1) * 128, :], [], [x])
                    k.dma("sp", gt.t[:], self.zf[b, 2048 + c * 128:2048 + (c + 1) * 128, :], [], [gt])
                    k.ts(xs.t[:], x.t[:], cw[:, c, 2:3], cb[:, c:c + 1], ALU.mult, ALU.add, [x, lv], [xs])
                    for jtap in (0, 1, 3):
                        o = jtap - 2
                        for (r0, r1) in ((0, CTX), (CTX, T)):
                            a0 = r0 + max(0, -o)
                            a1 = r1 - max(0, o)
                            k.stt(xs.t[:, a0:a1], x.t[:, a0 + o:a1 + o], cw[:, c, jtap:jtap + 1], xs.t[:, a0:a1],
                                  ALU.mult, ALU.add, [x, lv, xs], [xs])
                    k.cp(xsb.t[:], xs.t[:], [xs], [xsb], eng="act")
                    for d in range(2):
                        for (t0, n) in TB:
                            p = pp.next()
                            k.mm(p.t[:, 0:n], wa.t[:, d * 4 + c, :], xsb.t[:, t0:t0 + n], True, True, [wa, xsb], [p])
                            k.act(r_.t[:, t0:t0 + n], p.t[:, 0:n], AF.Sigmoid, [p, lv], [r_], bias=ba[:, d, c:c + 1])
                            p = pp.next()
                            k.mm(p.t[:, 0:n], wx.t[:, d * 4 + c, :], xsb.t[:, t0:t0 + n], True, True, [wx, xsb], [p])
                            k.act(i_.t[:, t0:t0 + n], p.t[:, 0:n], AF.Sigmoid, [p, lv], [i_], bias=bx[:, d, c:c + 1])
                        dc = d * 4 + c
                        k.act(a_.t[:], r_.t[:], AF.Exp, [r_, sp], [a_], scale=sp.t[:, dc:dc + 1])
                        k.act(q_.t[:], r_.t[:], AF.Exp, [r_, sp], [q_], scale=sp.t[:, 8 + dc:9 + dc])
                        k.act(q_.t[:], q_.t[:], AF.Sqrt, [q_, self.oneT], [q_], scale=-1.0, bias=self.oneT.t[:, 0:1])
                        k.tt(q_.t[:], q_.t[:], i_.t[:], ALU.mult, [q_, i_], [q_])
                        k.tt(q_.t[:], q_.t[:], xs.t[:], ALU.mult, [q_, xs], [q_])
                        if d == 0:
                            k.scan(h_.t[:, 0:CTX], a_.t[:, 0:CTX], q_.t[:, 0:CTX], 0.0, [a_, q_], [h_])
                            k.scan(h_.t[:, CTX:T], a_.t[:, CTX:T], q_.t[:, CTX:T], h_.t[:, CTX - 1:CTX], [a_, q_, h_], [h_])
                            k.cp(ys.t[:], h_.t[:], [h_], [ys], eng="pool")
                        else:
                            k.scan(h_.t[:, 0:CTX][:, ::-1], a_.t[:, 0:CTX][:, ::-1], q_.t[:, 0:CTX][:, ::-1], 0.0, [a_, q_], [h_])
                            k.scan(h_.t[:, CTX:T][:, ::-1], a_.t[:, CTX:T][:, ::-1], q_.t[:, CTX:T][:, ::-1], h_.t[:, 0:1], [a_, q_, h_], [h_])
                            k.tt(ys.t[:], ys.t[:], h_.t[:], ALU.add, [ys, h_], [ys])
                    self.gelu_tanh(h_.t[:], gt.t[:], a_.t[:], q_.t[:], [h_, gt, a_, q_])
                    k.tt(ob.t[:], ys.t[:], h_.t[:], ALU.mult, [ys, h_], [ob])
                    k.dma("sp", self.cc[b, 1536 + c * 128:1536 + (c + 1) * 128, :], ob.t[:], [ob], [])

    def mixer_mlstm(self, l):
        k = self.k
        KS = 128 ** -0.5
        TRI3 = self.C.t[:, 256:640]
        MASK = [self.C.t[:, 640:768], self.C.t[:, 768:896]]
        for b in range(2):
            with k.scope():
                Hacc = k.sb("Hacc", [128, NT, 512], F32)
                with k.scope():
                    mlb = k.sb("mlb", [128, 16], F32)
                    k.dma("sp", mlb.t[:], self.mlb[l], [], [mlb])
                    sel = k.sb("sel", [16, 2048], F32)
                    nsel = k.sb("nsel", [16, 2048], F32)
                    k.dma("sp", sel.t[:], self.selc[:, :], [], [sel])
                    k.ts(nsel.t[:], sel.t[:], -1.0, None, ALU.mult, None, [sel], [nsel])
                    QT = k.sb("QT", [128, 4, T], BF16)
                    KT = k.sb("KT", [128, 4, T], BF16)
                    Kt = k.sb("Kt", [128, NT, 512], BF16)
                    Va = k.sb("Va", [128, NT, 4, 129], BF16)
                    G16 = k.sb("G16", [128, NT, 16], F32)
                    R = k.sb("R", [16, NT, 384], F32)
                    with k.scope():
                        st = k.ring("st", [128, T], F32, 2)
                        zr = k.ring("zr", [128, 1552], F32, 2)
                        pr = k.psring("pr", 2)
                        for h in range(4):
                            s_ = st.next()
                            k.dma("sp", s_.t[:], self.zf[b, 512 + h * 128:512 + (h + 1) * 128, :], [], [s_])
                            k.cp(QT.t[:, h, :], s_.t[:], [s_], [QT], eng="act")
                            s_ = st.next()
                            k.dma("sp", s_.t[:], self.zf[b, 1024 + h * 128:1024 + (h + 1) * 128, :], [], [s_])
                            k.act(KT.t[:, h, :], s_.t[:], AF.Copy, [s_], [KT], scale=KS)
                        k.op("pool", lambda e: e.memset(Va.t[:], 1.0), [], [Va])
                        for tt in range(NT):
                            z = zr.next()
                            k.dma("sp", z.t[:], self.zt[b, tt * 128:(tt + 1) * 128, 0:1552], [], [z])
                            k.act(Kt.t[:, tt, :], z.t[:, 0:512], AF.Copy, [z], [Kt], scale=KS)
                            k.cp(Va.t[:, tt, :, 0:128], z.t[:, 512:1024].rearrange("p (h e) -> p h e", h=4), [z], [Va])
                            k.tt(G16.t[:, tt, :], z.t[:, 1536:1552], mlb.t[:], ALU.add, [z, mlb], [G16])
                        for d in range(2):
                            gv = G16.t[:, :, d * 8 + 4:d * 8 + 8]
                            k.act(gv, gv, AF.Exp, [G16], [G16], scale=-1.0)
                            k.act(gv, gv, AF.Ln, [G16, self.oneT], [G16], bias=self.oneT.t[:, 0:1])
                            k.ts(gv, gv, -1.0, None, ALU.mult, None, [G16], [G16])
                        for tt in range(NT):
                            p = pr.next()
                            k.mm(p.t[0:16, 0:384], G16.t[:, tt, :], TRI3, True, True, [G16, self.C], [p])
                            k.cp(R.t[:, tt, :], p.t[0:16, 0:384], [p], [R], eng="act")
                    CT32 = k.sb("CT32", [128, 129], F32)
                    CTb = k.sb("CTb", [128, 129], BF16)
                    EDr = k.ring("ED", [128, 128], F32, 2)
                    EBr = k.ring("EB", [128, 128], F32, 2)
                    STr = k.ring("ST", [128, 128], BF16, 2)
                    QSr = k.ring("QS", [128, 128], BF16, 2)
                    VWr = k.ring("VW", [128, 129], BF16, 2)
                    dnr = k.ring("dn", [128, 2], F32, 2)
                    pD = k.psring("pD", 2)
                    pB = k.psring("pB", 1)
                    pS = k.psring("pS", 2)
                    pN = k.psring("pN", 2)
                    pC = k.psring("pC", 1)
                    for d in range(2):
                        order = list(range(NT)) if d == 0 else [1, 0] + list(range(NT - 1, 1, -1))
                        bsl = slice(0, 128) if d == 0 else slice(128, 256)
                        last = 127 if d == 0 else 0
                        for h in range(4):
                            kli = d * 8 + h
                            klf = d * 8 + 4 + h
                            SLI = sel.t[0:16, kli * 128:(kli + 1) * 128]
                            SLF = sel.t[0:16, klf * 128:(klf + 1) * 128]
                            NLF = nsel.t[0:16, klf * 128:(klf + 1) * 128]
                            k.op("dve", lambda e: e.memset(CT32.t[:], 0.0), [], [CT32])
                            k.op("dve", lambda e: e.memset(CTb.t[:], 0.0), [], [CTb])
                            for tt in order:
                                tsl = slice(tt * 128, (tt + 1) * 128)
                                Rb = R.t[0:16, tt, bsl]
                                Rg = R.t[0:16, tt, 256:384]
                                pd_ = pD.next()
                                k.mm(pd_.t[:, 0:128], Rg, SLI, True, False, [R, sel], [pd_])
                                k.mm(pd_.t[:, 0:128], Rb, NLF, False, False, [R, nsel], [pd_])
                                k.mm(pd_.t[:, 0:128], SLF, Rb, False, False, [R, sel], [pd_])
                                k.mm(pd_.t[:, 0:128], self.identF, MASK[d], False, True, [self.C], [pd_])
                                ED = EDr.next()
                                k.act(ED.t[:], pd_.t[:, 0:128], AF.Exp, [pd_], [ED])
                                pb_ = pB.next()
                                k.mm(pb_.t[:, 0:128], SLF, Rb, True, True, [R, sel], [pb_])
                                EB = EBr.next()
                                k.act(EB.t[:], pb_.t[:, 0:128], AF.Exp, [pb_], [EB])
                                ps_ = pS.next()
                                k.mm(ps_.t[:, 0:128], KT.t[:, h, tsl], QT.t[:, h, tsl], True, True, [KT, QT], [ps_])
                                ST = STr.next()
                                k.tt(ST.t[:], ps_.t[:, 0:128], ED.t[:], ALU.mult, [ps_, ED], [ST])
                                QS = QSr.next()
                                k.tt(QS.t[:], QT.t[:, h, tsl], EB.t[:], ALU.mult, [QT, EB], [QS])
                                pn_ = pN.next()
                                k.mm(pn_.t[:, 0:129], QS.t[:], CTb.t[:], True, False, [QS, CTb], [pn_])
                                k.mm(pn_.t[:, 0:129], ST.t[:], Va.t[:, tt, h, :], False, True, [ST, Va], [pn_])
                                dn = dnr.next()
                                k.act(dn.t[:, 0:1], pn_.t[:, 128:129], AF.Abs, [pn_], [dn])
                                k.ts(dn.t[:, 0:1], dn.t[:, 0:1], 1.0, None, ALU.max, None, [dn], [dn])
                                k.op("dve", lambda e: e.reciprocal(out=dn.t[:, 1:2], in_=dn.t[:, 0:1]), [dn], [dn])
                                hs = Hacc.t[:, tt, h * 128:(h + 1) * 128]
                                if d == 0:
                                    k.ts(hs, pn_.t[:, 0:128], dn.t[:, 1:2], None, ALU.mult, None, [pn_, dn], [Hacc])
                                else:
                                    k.stt(hs, pn_.t[:, 0:128], dn.t[:, 1:2], hs, ALU.mult, ALU.add, [pn_, dn, Hacc], [Hacc])
                                VW = VWr.next()
                                k.act(VW.t[:], Va.t[:, tt, h, :], AF.Identity, [Va, ED], [VW], scale=ED.t[:, last:last + 1])
                                pc_ = pC.next()
                                k.mm(pc_.t[:, 0:129], Kt.t[:, tt, h * 128:(h + 1) * 128], VW.t[:], True, True, [Kt, VW], [pc_])
                                k.stt(CT32.t[:], CT32.t[:], EB.t[:, last:last + 1], pc_.t[:, 0:129], ALU.mult, ALU.add,
                                      [CT32, EB, pc_], [CT32])
                                k.cp(CTb.t[:], CT32.t[:], [CT32], [CTb], eng="act")
                with k.scope():
                    onw = k.sb("onw", [128, 512], F32)
                    k.dma("sp", onw.t[:], self.onw[l], [], [onw])
                    OT = k.sb("OT", [128, 4, T], BF16)
                    mor = k.ring("mo", [128, 512], F32, 2)
                    sqr = k.ring("sqh", [128, 512], F32, 2)
                    ssr = k.ring("ss4", [128, 4], F32, 2)
                    pT = k.psring("pT", 2)
                    for tt in range(NT):
                        H = Hacc.t[:, tt, :]
                        sq = sqr.next()
                        k.tt(sq.t[:], H, H, ALU.mult, [Hacc], [sq])
                        ss = ssr.next()
                        k.op("dve", lambda e: e.tensor_reduce(out=ss.t[:], in_=sq.t[:].rearrange("p (h e) -> p h e", h=4), axis=AX.X, op=ALU.add), [sq], [ss])
                        k.act(ss.t[:], ss.t[:], AF.Sqrt, [ss, self.epsT], [ss], scale=1.0 / 128, bias=self.epsT.t[:, 0:1])
                        k.op("dve", lambda e: e.reciprocal(out=ss.t[:], in_=ss.t[:]), [ss], [ss])
                        for h in range(4):
                            hsl = slice(h * 128, (h + 1) * 128)
                            k.stt(sq.t[:, hsl], H[:, hsl], ss.t[:, h:h + 1], onw.t[:, hsl], ALU.mult, ALU.mult, [Hacc, ss, onw], [sq])
                        mo = mor.next()
                        k.dma("sp", mo.t[:], self.zt[b, tt * 128:(tt + 1) * 128, 1024:1536], [], [mo])
                        k.act(mo.t[:], mo.t[:], AF.Sigmoid, [mo], [mo])
                        k.tt(sq.t[:], sq.t[:], mo.t[:], ALU.mult, [sq, mo], [sq])
                        p = pT.next()
                        for h in range(4):
                            hsl = slice(h * 128, (h + 1) * 128)
                            k.tr(p.t[:, hsl], sq.t[:, hsl], self.identF, [sq, self.C], [p])
                        k.cp(OT.t[:, :, tt * 128:(tt + 1) * 128], p.t[:, 0:512].rearrange("p (h e) -> p h e", h=4), [p], [OT], eng="act")
                    for h in range(4):
                        k.dma("sp", self.cc[b, 512 + h * 128:512 + (h + 1) * 128, :], OT.t[:, h, :], [OT], [])

    def mixer_mla(self, l):
        k = self.k
        SC = 192 ** -0.5
        for b in range(2):
            with k.scope():
                QT = k.sb("aQT", [128, 4, T], BF16)
                QT2 = k.sb("aQT2", [64, 4, T], BF16)
                KT = k.sb("aKT", [128, 4, T], BF16)
                KT2 = k.sb("aKT2", [64, 4, T], BF16)
                Va = k.sb("aVa", [128, NT, 4, 129], BF16)
                k.op("pool", lambda e: e.memset(Va.t[:], 1.0), [], [Va])
                with k.scope():
                    nv = k.sb("mlav", [128, 896], F32)
                    k.dma("sp", nv.t[:], self.mlav[l], [], [nv])
                    QAW = nv.t[:, 0:384]
                    KVAW = nv.t[:, 384:512]
                    NW = [nv.t[:, 512:704], nv.t[:, 704:896]]
                    rc = k.sb("ropec", [128, NT, 32], F32)
                    rs = k.sb("ropes", [128, NT, 32], F32)
                    k.dma("sp", rc.t[:], self.ropec[:, :, :], [], [rc])
                    k.dma("sp", rs.t[:], self.ropes[:, :, :], [], [rs])
                    wq32 = k.sb("wq32", [128, 3, 768], F32)
                    wq = k.sb("wq", [128, 3, 768], BF16)
                    wkv32 = k.sb("wkv32", [128, 1024], F32)
                    wkv = k.sb("wkv", [128, 1024], BF16)
                    k.dma("sp", wq32.t[:], self.w_q_up[l].rearrange("(kc p) o -> p kc o", p=128), [], [wq32])
                    k.dma("sp", wkv32.t[:], self.w_kv_up[l], [], [wkv32])
                    k.cp(wq.t[:], wq32.t[:], [wq32], [wq], eng="pool")
                    k.cp(wkv.t[:], wkv32.t[:], [wkv32], [wkv], eng="pool")
                    Zr = k.ring("Z", [128, 576], F32, 2)
                    jk = k.sb("junk", [128, 384], F32)
                    ssr = k.ring("ss2", [128, 2], F32, 2)
                    cnr = k.ring("cn", [128, 512], F32, 2)
                    cTr = k.ring("cT", [128, 4, 128], BF16, 2)
                    Xr = [k.ring("X0", [128, 4, 192], F32, 2), k.ring("X1", [128, 4, 192], F32, 2)]
                    sqx = k.sb("sqx", [128, 768], F32)
                    s4r = k.ring("s4", [128, 4], F32, 2)
                    tmp = [k.sb("rt", [128, 4, 2, 16], F32) for _ in range(4)]
                    pT = k.psring("apT", 1)
                    pq = k.psring("apq", 2)
                    pk = k.psring("apk", 2)
                    pX = k.psring("apX", 2)
                    for tt in range(NT):
                        tsl = slice(tt * 128, (tt + 1) * 128)
                        Z = Zr.next()
                        k.dma("sp", Z.t[:], self.zt[b, tsl, 1552:2128], [], [Z])
                        ss = ssr.next()
                        k.act(jk.t[:, 0:384], Z.t[:, 0:384], AF.Square, [Z], [jk, ss], accum_out=ss.t[:, 0:1])
                        k.act(jk.t[:, 0:128], Z.t[:, 384:512], AF.Square, [Z], [jk, ss], accum_out=ss.t[:, 1:2])
                        k.act(ss.t[:, 0:1], ss.t[:, 0:1], AF.Sqrt, [ss, self.epsT], [ss], scale=1.0 / 384, bias=self.epsT.t[:, 0:1])
                        k.act(ss.t[:, 1:2], ss.t[:, 1:2], AF.Sqrt, [ss, self.epsT], [ss], scale=1.0 / 128, bias=self.epsT.t[:, 0:1])
                        k.op("dve", lambda e: e.reciprocal(out=ss.t[:], in_=ss.t[:]), [ss], [ss])
                        cn = cnr.next()
                        k.stt(cn.t[:, 0:384], Z.t[:, 0:384], ss.t[:, 0:1], QAW, ALU.mult, ALU.mult, [Z, ss, nv], [cn])
                        k.stt(cn.t[:, 384:512], Z.t[:, 384:512], ss.t[:, 1:2], KVAW, ALU.mult, ALU.mult, [Z, ss, nv], [cn])
                        p = pT.next()
                        for c4 in range(4):
                            k.tr(p.t[:, c4 * 128:(c4 + 1) * 128], cn.t[:, c4 * 128:(c4 + 1) * 128], self.identF, [cn, self.C], [p])
                        cT = cTr.next()
                        k.cp(cT.t[:], p.t[:, 0:512].rearrange("p (c e) -> p c e", c=4), [p], [cT], eng="act")
                        Xq = Xr[0].next()
                        Xk = Xr[1].next()
                        for nb in range(2):
                            p = pq.next()
                            for kc in range(3):
                                k.mm(p.t[:, 0:384], cT.t[:, kc, :], wq.t[:, kc, nb * 384:(nb + 1) * 384], kc == 0, kc == 2, [cT, wq], [p])
                            k.cp(Xq.t[:, 2 * nb:2 * nb + 2, :], p.t[:, 0:384].rearrange("p (h e) -> p h e", h=2), [p], [Xq], eng="act")
                        for nb in range(2):
                            p = pk.next()
                            k.mm(p.t[:, 0:512], cT.t[:, 3, :], wkv.t[:, nb * 512:(nb + 1) * 512], True, True, [cT, wkv], [p])
                            pv4 = p.t[:, 0:512].rearrange("p (h e) -> p h e", h=2)
                            k.cp(Xk.t[:, 2 * nb:2 * nb + 2, 0:128], pv4[:, :, 0:128], [p], [Xk])
                            k.cp(Va.t[:, tt, 2 * nb:2 * nb + 2, 0:128], pv4[:, :, 128:256], [p], [Va], eng="act")
                        for h in range(4):
                            k.cp(Xk.t[:, h, 128:192], Z.t[:, 512:576], [Z], [Xk], eng="pool")
                        for qi, X in enumerate((Xq, Xk)):
                            Xf = X.t[:].rearrange("p h e -> p (h e)")
                            k.tt(sqx.t[:], Xf, Xf, ALU.mult, [X], [sqx])
                            s4 = s4r.next()
                            k.op("dve", lambda e: e.tensor_reduce(out=s4.t[:], in_=sqx.t[:].rearrange("p (h e) -> p h e", h=4), axis=AX.X, op=ALU.add), [sqx], [s4])
                            k.act(s4.t[:], s4.t[:], AF.Sqrt, [s4, self.epsT], [s4], scale=1.0 / 192, bias=self.epsT.t[:, 0:1])
                            k.op("dve", lambda e: e.reciprocal(out=s4.t[:], in_=s4.t[:]), [s4], [s4])
                            for h in range(4):
                                k.stt(X.t[:, h, :], X.t[:, h, :], s4.t[:, h:h + 1], NW[qi], ALU.mult, ALU.mult, [X, s4, nv], [X])
                            rp = X.t[:, :, 128:192].rearrange("p h (a b f) -> p h a b f", a=2, b=2)
                            x1 = rp[:, :, :, 0, :]
                            x2 = rp[:, :, :, 1, :]
                            cosb = rc.t[:, tt, :].rearrange("p (o a f) -> p o a f", o=1, a=2).to_broadcast([128, 4, 2, 16])
                            sinb = rs.t[:, tt, :].rearrange("p (o a f) -> p o a f", o=1, a=2).to_broadcast([128, 4, 2, 16])
                            TT = tmp
                            k.tt(TT[0].t[:], x1, cosb, ALU.mult, [X, rc], [TT[0]])
                            k.tt(TT[1].t[:], x2, sinb, ALU.mult, [X, rs], [TT[1]])
                            k.tt(TT[2].t[:], x2, cosb, ALU.mult, [X, rc], [TT[2]])
                            k.tt(TT[3].t[:], x1, sinb, ALU.mult, [X, rs], [TT[3]])
                            k.tt(x1, TT[0].t[:], TT[1].t[:], ALU.subtract, [TT[0], TT[1], X], [X])
                            k.tt(x2, TT[2].t[:], TT[3].t[:], ALU.add, [TT[2], TT[3], X], [X])
                            dst, dst2 = (QT, QT2) if qi == 0 else (KT, KT2)
                            for hp in range(2):
                                p = pX.next()
                                for hh in range(2):
                                    h = 2 * hp + hh
                                    k.tr(p.t[:, hh * 256:hh * 256 + 128], X.t[:, h, 0:128], self.identF, [X, self.C], [p])
                                    k.tr(p.t[0:64, hh * 256 + 128:hh * 256 + 256], X.t[:, h, 128:192], self.identF, [X, self.C], [p])
                                pv_ = p.t[:, 0:512].rearrange("p (h e) -> p h e", h=2)
                                k.cp(dst.t[:, 2 * hp:2 * hp + 2, tsl], pv_[:, :, 0:128], [p], [dst], eng="act")
                                k.cp(dst2.t[0:64, 2 * hp:2 * hp + 2, tsl], p.t[0:64, 0:512].rearrange("p (h e) -> p h e", h=2)[:, :, 128:256], [p], [dst2])
                if self.dbg.get("mla_noattn"):
                    continue
                with k.scope():
                    Pr = k.ring("P", [128, 512], BF16, 3)
                    MT = k.ring("MT", [128, T], BF16, 2)
                    o32 = k.ring("o32", [128, 128], F32, 2)
                    rdr = k.ring("rd", [128, 1], F32, 2)
                    pS = k.psring("aS", 2)
                    po = [k.ps("apo%d" % i) for i in range(4)]
                    pT = k.psring("aT", 1)
                    blocks = [(0, 256, [0, 1])] + [(CTX + 512 * i, 512, list(range(NT))) for i in range(4)]
                    for h in range(4):
                        mt = MT.next()
                        for (q0, nq, kts) in blocks:
                            nsub = nq // 128
                            for idx, kt in enumerate(kts):
                                ksl = slice(kt * 128, (kt + 1) * 128)
                                ps_ = pS.next()
                                k.mm(ps_.t[:, 0:nq], KT.t[:, h, ksl], QT.t[:, h, q0:q0 + nq], True, False, [KT, QT], [ps_])
                                k.mm(ps_.t[:, 0:nq], KT2.t[0:64, h, ksl], QT2.t[0:64, h, q0:q0 + nq], False, True, [KT2, QT2], [ps_])
                                P = Pr.next()
                                k.act(P.t[:, 0:nq], ps_.t[:, 0:nq], AF.Exp, [ps_], [P], scale=SC)
                                for qs in range(nsub):
                                    k.mm(po[qs].t[:, 0:129], P.t[:, qs * 128:(qs + 1) * 128], Va.t[:, kt, h, :],
                                         idx == 0, idx == len(kts) - 1, [P, Va], [po[qs]])
                            for qs in range(nsub):
                                rd = rdr.next()
                                k.op("dve", lambda e: e.reciprocal(out=rd.t[:], in_=po[qs].t[:, 128:129]), [po[qs]], [rd])
                                o = o32.next()
                                k.ts(o.t[:], po[qs].t[:, 0:128], rd.t[:, 0:1], None, ALU.mult, None, [po[qs], rd], [o])
                                p = pT.next()
                                k.tr(p.t[:, 0:128], o.t[:], self.identF, [o, self.C], [p])
                                k.cp(mt.t[:, q0 + qs * 128:q0 + (qs + 1) * 128], p.t[:, 0:128], [p], [mt], eng="act")
                        k.dma("sp", self.cc[b, 1024 + h * 128:1024 + (h + 1) * 128, :], mt.t[:], [mt], [])


def make_consts():
    c = np.zeros((128, 2048), np.float32)
    i = np.arange(128)
    c[:, 0:128] = np.eye(128)
    c[:, 128:256] = 1.0
    c[:, 256:384] = (i[:, None] <= i[None, :])
    c[:, 384:512] = (i[:, None] >= i[None, :])
    c[:, 512:640] = np.eye(128)
    c[:, 640:768] = np.where(i[:, None] <= i[None, :], 0.0, -30000.0)
    c[:, 768:896] = np.where(i[:, None] >= i[None, :], 0.0, -30000.0)
    return c


def prep_common(inp):
    m = {}
    f = np.float32
    m["ada_w"] = inp["ada_w"]
    m["ada_bT"] = np.ascontiguousarray(inp["ada_b"].reshape(4, 96, 128).transpose(0, 2, 1))
    m["n1w"] = np.ascontiguousarray(inp["norm1_w"].reshape(4, 16, 128).transpose(0, 2, 1))
    m["n2w"] = np.ascontiguousarray(inp["norm2_w"].reshape(4, 16, 128).transpose(0, 2, 1))
    m["w_in"] = inp["w_in"]
    m["w_out"] = inp["w_out"]
    m["mlp_w1"] = inp["mlp_w1"]
    m["mlp_w2"] = inp["mlp_w2"]
    m["consts"] = make_consts()
    prep_mixers(inp, m)
    return m


def prep_core(inp, common, core):
    b0 = 2 * core
    m = dict(common)
    xin = np.concatenate([inp["ctx"][b0:b0 + 2], inp["x"][b0:b0 + 2]], axis=1)
    m["xin"] = np.ascontiguousarray(xin.transpose(0, 2, 1))
    c3 = np.stack([inp["c"][b0], inp["c"][b0 + 1], inp["c_ctx"]], axis=1)
    m["cT"] = np.ascontiguousarray(c3.reshape(16, 128, 3).transpose(1, 0, 2))
    return m


def _dup(a):
    return np.concatenate([a, a], axis=0)


def prep_mixers(inp, m):
    L = DEPTH
    c = m["consts"]
    c[:64, 896] = -1.0
    c[64:, 896] = 1.0
    lre = inp["s5_lam_re"].transpose(0, 3, 1, 2).reshape(L, 64, 64)
    lim = inp["s5_lam_im"].transpose(0, 3, 1, 2).reshape(L, 64, 64)
    ldt = np.broadcast_to(inp["s5_log_dt"].reshape(L, 1, 64), (L, 64, 64))
    s5v = np.concatenate([lre, lim, ldt], axis=2)
    m["s5v"] = np.ascontiguousarray(np.concatenate([s5v, s5v], axis=1))
    bre = inp["s5_b_re"].transpose(0, 3, 1, 2, 4).reshape(L, 64, 64, 16)
    bim = inp["s5_b_im"].transpose(0, 3, 1, 2, 4).reshape(L, 64, 64, 16)
    m["s5A"] = np.ascontiguousarray(np.concatenate([bre, bim], axis=1))
    m["s5B"] = np.ascontiguousarray(np.concatenate([bim, bre], axis=1))
    cre = inp["s5_c_re"].transpose(0, 4, 1, 2, 3).reshape(L, 64, 64, 16)
    cim = inp["s5_c_im"].transpose(0, 4, 1, 2, 3).reshape(L, 64, 64, 16)
    m["s5CA"] = np.ascontiguousarray(np.concatenate([cre, cim], axis=1))
    m["s5CB"] = np.ascontiguousarray(np.concatenate([cim, cre], axis=1))
    dsk = inp["s5_d"].reshape(L, 4, 128).transpose(0, 2, 1)
    glb = inp["s5_glu_b"].reshape(L, 4, 128).transpose(0, 2, 1)
    m["s5w"] = np.ascontiguousarray(np.concatenate([dsk, glb], axis=2))
    m["glu_w"] = inp["s5_glu_w"]
    n0 = np.arange(T, dtype=np.float32)
    n1 = np.concatenate([CTX - 1 - np.arange(CTX), CTX + (LAT - 1 - np.arange(LAT))]).astype(np.float32)
    m["nidx"] = np.ascontiguousarray(np.broadcast_to(np.stack([n0, n1])[:, None, :], (2, 128, T)))
    cw = inp["lru_conv_w"].reshape(L, 4, 4, 128).transpose(0, 3, 2, 1).reshape(L, 128, 16)
    cb = inp["lru_conv_b"].reshape(L, 4, 128).transpose(0, 2, 1)
    def dc(a):
        return a.reshape(L, 2, 4, 128).transpose(0, 3, 1, 2).reshape(L, 128, 8)
    m["lruv"] = np.ascontiguousarray(np.concatenate([cw, cb, dc(inp["lru_ba"]), dc(inp["lru_bx"]), dc(inp["lru_lam"])], axis=2))
    m["lru_wa"] = inp["lru_wa"]
    m["lru_wx"] = inp["lru_wx"]
    gb = np.concatenate([inp["ml_ig_bias"], inp["ml_fg_bias"]], axis=2).reshape(L, 1, 16)
    m["mlb"] = np.ascontiguousarray(np.broadcast_to(gb, (L, 128, 16)))
    m["onw"] = np.ascontiguousarray(np.broadcast_to(inp["ml_out_norm"].reshape(L, 1, 512), (L, 128, 512)))
    selc = np.zeros((16, 16, 128), np.float32)
    for kk in range(16):
        selc[kk, kk, :] = 1.0
    m["selc"] = selc.reshape(16, 2048)
    nv = np.concatenate([inp["mla_q_a_norm"], inp["mla_kv_a_norm"], inp["mla_q_norm"], inp["mla_k_norm"]], axis=1)
    m["mlav"] = np.ascontiguousarray(np.broadcast_to(nv.reshape(L, 1, 896), (L, 128, 896)))
    m["w_q_up"] = inp["mla_w_q_up"]
    m["w_kv_up"] = inp["mla_w_kv_up"]
    q = np.arange(LAT)
    inv = (np.float32(10000.0) ** (-np.arange(16, dtype=np.float32) / np.float32(16))).astype(np.float32)
    ang = np.concatenate([(q // 64).astype(np.float32)[:, None] * inv, (q % 64).astype(np.float32)[:, None] * inv], axis=1)
    cosf = np.ones((T, 32), np.float32)
    sinf = np.zeros((T, 32), np.float32)
    cosf[CTX:] = np.cos(ang.astype(np.float32))
    sinf[CTX:] = np.sin(ang.astype(np.float32))
    m["ropec"] = np.ascontiguousarray(cosf.reshape(NT, 128, 32).transpose(1, 0, 2))
    m["ropes"] = np.ascontiguousarray(sinf.reshape(NT, 128, 32).transpose(1, 0, 2))


_PROG = None


def kernel(**inputs):
    global _PROG
    inp = {k_: np.asarray(v) for k_, v in inputs.items()}
    if _PROG is None:
        _PROG = Prog()
    prog = _PROG
    common = prep_common(inp)
    in_maps = []
    for core in range(8):
        m = prep_core(inp, common, core)
        in_maps.append({n: m[n] for n in prog.inp})
    res = run_bass_kernel_spmd(prog.nc, in_maps, core_ids=list(range(8)))
    outs = [r["yout"] for r in res.results]
    y = np.concatenate(outs, axis=0)
    return np.ascontiguousarray(y.transpose(0, 2, 1)).astype(np.float32)
```

```python
import numpy as np
import ml_dtypes
from contextlib import ExitStack, contextmanager
import concourse.bass as bass
import concourse.mybir as mybir
from concourse.bass_utils import run_bass_kernel_spmd

F32 = mybir.dt.float32
BF16 = mybir.dt.bfloat16
I32 = mybir.dt.int32
AF = mybir.ActivationFunctionType
ALU = mybir.AluOpType
AX = mybir.AxisListType

D = 2048
T = 2304
CTX = 256
LAT = 2048
NT = 18
DEPTH = 4
DFF = 8192
INC = 4176
EPS = 1e-6
TB = [(0, 256), (256, 512), (768, 512), (1280, 512), (1792, 512)]
TWO_PI = float(2 * np.pi)
SAME_SYNC = True
MERGE_WAIT = True
CAP = 30000


class Res:
    __slots__ = ("w", "r")

    def __init__(self):
        self.w = None
        self.r = {}


class Buf:
    def __init__(self, t, psum=False):
        self.t = t
        self.res = Res()
        self.psum = psum

    def __getitem__(self, key):
        return self.t[key]


class Ring:
    def __init__(self, bufs):
        self.bufs = bufs
        self.i = 0

    def next(self):
        b = self.bufs[self.i]
        self.i = (self.i + 1) % len(self.bufs)
        return b


class KB:
    def __init__(self, nc):
        self.nc = nc
        self.E = {"pe": nc.tensor, "dve": nc.vector, "act": nc.scalar, "pool": nc.gpsimd, "sp": nc.sync}
        self.sem = {}
        self.cnt = {}
        self.owner = {}
        self.nsem = 0
        for e in self.E:
            self._fresh(e)
        self.waited = {e: {} for e in self.E}
        self.dq = {}
        self.dqi = {}
        for q, n in (("sp", 12), ("pool", 4), ("act", 4)):
            self.dq[q] = [[self._newsem("d"), 0] for _ in range(n)]
            self.dqi[q] = 0
        self.stack = []
        self.uid = 0

    def _newsem(self, pfx):
        self.nsem += 1
        return self.nc.alloc_semaphore(f"{pfx}{self.nsem}")

    def _fresh(self, e):
        s = self._newsem("e")
        self.sem[e] = s
        self.cnt[e] = 0
        self.owner[s] = e

    @contextmanager
    def scope(self):
        st = ExitStack()
        self.stack.append(st)
        try:
            yield
        finally:
            self.barrier()
            self.stack.pop()
            st.close()

    def _name(self, n):
        self.uid += 1
        return f"{n}_{self.uid}"

    def sb(self, name, shape, dtype):
        t = self.stack[-1].enter_context(self.nc.sbuf_tensor(self._name(name), list(shape), dtype))
        return Buf(t)

    def ps(self, name, shape=(128, 512), dtype=F32):
        t = self.stack[-1].enter_context(self.nc.psum_tensor(self._name(name), list(shape), dtype))
        return Buf(t, psum=True)

    def ring(self, name, shape, dtype, n):
        return Ring([self.sb(name, shape, dtype) for _ in range(n)])

    def psring(self, name, n, shape=(128, 512), dtype=F32):
        return Ring([self.ps(name, shape, dtype) for _ in range(n)])

    def _need(self, e, tok, out):
        if tok is None:
            return
        sem, val = tok
        own = self.owner.get(sem)
        if own == e and (e == "pe" or not SAME_SYNC):
            return
        w = self.waited[e]
        if w.get(sem, 0) >= val:
            return
        w[sem] = val
        for i, (s_, v_) in enumerate(out):
            if s_ is sem or s_ == sem:
                out[i] = (sem, max(v_, val))
                return
        out.append((sem, val))

    def _wait(self, e, tok):
        out = []
        self._need(e, tok, out)
        for (s_, v_) in out:
            self.E[e].wait_ge(s_, v_)

    def _deps(self, e, reads, writes):
        out = []
        for r in reads:
            r = r.res if isinstance(r, Buf) else r
            self._need(e, r.w, out)
        for wr in writes:
            wr = wr.res if isinstance(wr, Buf) else wr
            self._need(e, wr.w, out)
            for s_, v_ in wr.r.items():
                self._need(e, (s_, v_), out)
        return out

    def _commit(self, tok, reads, writes):
        sem, val = tok
        for r in reads:
            r = r.res if isinstance(r, Buf) else r
            if r.r.get(sem, 0) < val:
                r.r[sem] = val
        for wr in writes:
            wr = wr.res if isinstance(wr, Buf) else wr
            wr.w = tok
            wr.r = {}

    def op(self, e, fn, reads=(), writes=(), merge=True):
        pr = [r for r in reads if isinstance(r, Buf) and r.psum]
        if pr:
            reads = [r for r in reads if not (isinstance(r, Buf) and r.psum)]
            writes = list(writes) + pr
        need = self._deps(e, reads, writes)
        last = None
        if merge and MERGE_WAIT and need:
            last = need.pop()
        for (s_, v_) in need:
            self.E[e].wait_ge(s_, v_)
        ins = fn(self.E[e])
        if last is not None:
            ins._wait_ge(last[0], last[1])
        self.cnt[e] += 1
        ins.then_inc(self.sem[e], 1)
        tok = (self.sem[e], self.cnt[e])
        self._commit(tok, reads, writes)
        if self.cnt[e] >= CAP:
            self._fresh(e)

    def dma(self, q, out, in_, reads=(), writes=()):
        slots = self.dq[q]
        slot = slots[self.dqi[q]]
        self.dqi[q] = (self.dqi[q] + 1) % len(slots)
        if slot[1] > 0:
            self._wait(q, (slot[0], slot[1]))
        if slot[1] >= CAP:
            slot[0] = self._newsem("d")
            slot[1] = 0
        for (s_, v_) in self._deps(q, reads, writes):
            self.E[q].wait_ge(s_, v_)
        ins = self.E[q].dma_start(out=out, in_=in_)
        slot[1] += 16
        ins.then_inc(slot[0], 16)
        self._commit((slot[0], slot[1]), reads, writes)

    def barrier(self):
        toks = [(self.sem[e], self.cnt[e]) for e in self.E if self.cnt[e] > 0]
        for q in self.dq:
            for slot in self.dq[q]:
                if slot[1] > 0:
                    toks.append((slot[0], slot[1]))
        for e in self.E:
            for tok in toks:
                if self.owner.get(tok[0]) == e:
                    continue
                self._wait(e, tok)

    def mm(self, out, lhsT, rhs, start, stop, reads, writes):
        self.op("pe", lambda e: e.matmul(out, lhsT, rhs, start=start, stop=stop), reads, writes)

    def tr(self, out, in_, ident, reads, writes):
        self.op("pe", lambda e: e.transpose(out, in_, ident), reads, writes)

    def act(self, out, in_, func, reads, writes, bias=0.0, scale=1.0, **kw):
        self.op("act", lambda e: e.activation(out=out, in_=in_, func=func, bias=bias, scale=scale, **kw), reads, writes,
                merge=("accum_out" not in kw))

    def ts(self, out, in0, s1, s2, op0, op1, reads, writes, eng="dve"):
        if s2 is None:
            self.op(eng, lambda e: e.tensor_scalar(out=out, in0=in0, scalar1=s1, scalar2=None, op0=op0), reads, writes)
        else:
            self.op(eng, lambda e: e.tensor_scalar(out=out, in0=in0, scalar1=s1, scalar2=s2, op0=op0, op1=op1), reads, writes)

    def tt(self, out, in0, in1, op, reads, writes, eng="dve"):
        self.op(eng, lambda e: e.tensor_tensor(out=out, in0=in0, in1=in1, op=op), reads, writes)

    def stt(self, out, in0, scalar, in1, op0, op1, reads, writes):
        self.op("dve", lambda e: e.scalar_tensor_tensor(out=out, in0=in0, scalar=scalar, in1=in1, op0=op0, op1=op1), reads, writes)

    def cp(self, out, in_, reads, writes, eng="dve"):
        if eng == "act":
            self.op("act", lambda e: e.copy(out=out, in_=in_), reads, writes)
        else:
            self.op(eng, lambda e: e.tensor_copy(out=out, in_=in_), reads, writes)

    def scan(self, out, d0, d1, init, reads, writes):
        self.op("dve", lambda e: e.tensor_tensor_scan(out=out, data0=d0, data1=d1, initial=init, op0=ALU.mult, op1=ALU.add), reads, writes)


class Prog:
    def __init__(self, n_layers=DEPTH, dbg=None, layers=None):
        self.dbg = dbg or {}
        self.layers = list(range(n_layers)) if layers is None else layers
        nc = bass.Bass("TRN2", target_bir_lowering=False)
        self.nc = nc
        self.k = KB(nc)
        self.inp = {}
        self.build()

    def din(self, name, shape, dtype=F32):
        t = self.nc.dram_tensor(name, list(shape), dtype, kind="ExternalInput")
        self.inp[name] = (tuple(shape), dtype)
        return t.ap()

    def dscr(self, name, shape, dtype=F32):
        kind = "ExternalOutput" if name in self.dbg else "Internal"
        return self.nc.dram_tensor(name, list(shape), dtype, kind=kind).ap()

    def build(self):
        nc, k = self.nc, self.k
        L = DEPTH
        self.xin = self.din("xin", [2, D, T])
        self.cT = self.din("cT", [128, 16, 3])
        self.ada_w = self.din("ada_w", [L, D, 6 * D])
        self.ada_bT = self.din("ada_bT", [L, 128, 96])
        self.n1w = self.din("n1w", [L, 128, 16])
        self.n2w = self.din("n2w", [L, 128, 16])
        self.w_in = self.din("w_in", [L, D, INC])
        self.w_out = self.din("w_out", [L, D, D])
        self.mlp_w1 = self.din("mlp_w1", [L, D, DFF])
        self.mlp_w2 = self.din("mlp_w2", [L, DFF, D])
        self.consts = self.din("consts", [128, 2048])
        self.declare_mixer_inputs()
        self.yout = nc.dram_tensor("yout", [2, D, LAT], F32, kind="ExternalOutput").ap()
        self.xs = self.dscr("xs", [2, D, T])
        self.zf = self.dscr("zf", [2, 2560, T])
        self.zt = self.dscr("zt", [2, T, 2128])
        self.W1t = self.dscr("W1t", [64, 128, 16, 128], BF16)
        self.W2t = self.dscr("W2t", [4, 16, 128, 16, 128], BF16)
        if self.dbg.get("cc_in"):
            self.cc = self.din("cc", [2, D, T], BF16)
        else:
            self.cc = self.dscr("cc", [2, D, T], BF16)
        if "modv_o" in self.dbg:
            self.modv_o = self.dscr("modv_o", [128, 288])

        with k.scope():
            self.setup_consts()
            for b in range(2):
                for c in range(16):
                    k.dma("sp", self.xs[b, c * 128:(c + 1) * 128, :], self.xin[b, c * 128:(c + 1) * 128, :])
            k.barrier()
            for l in self.layers:
                self.layer(l)
            for b in range(2):
                for c in range(16):
                    k.dma("sp", self.yout[b, c * 128:(c + 1) * 128, :], self.xs[b, c * 128:(c + 1) * 128, CTX:T])

    def setup_consts(self):
        k = self.k
        self.C = k.sb("consts", [128, 2048], F32)
        k.dma("sp", self.C.t[:], self.consts[:, :], [], [self.C])
        self.identF = self.C.t[:, 0:128]
        self.onesF = self.C.t[:, 128:256]
        self.identB = k.sb("identB", [128, 128], BF16)
        k.cp(self.identB.t[:], self.identF, [self.C], [self.identB])
        self.cs = k.sb("cs", [128, 16, 3], F32)
        k.dma("sp", self.cs.t[:], self.cT[:, :, :], [], [self.cs])
        k.act(self.cs.t[:], self.cs.t[:], AF.Silu, [self.cs], [self.cs])
        self.modv = k.sb("modv", [128, 96, 3], F32)
        self.g1 = k.sb("g1", [128, 16, 3], F32)
        self.g2 = k.sb("g2", [128, 16, 3], F32)
        self.epsT = k.sb("epsT", [128, 1], F32)
        k.op("dve", lambda e: e.memset(self.epsT.t[:], EPS), [], [self.epsT])
        self.setup_mixer_consts()

    def layer(self, l):
        k = self.k
        self.wcast_done = False
        self.stage_mod(l)
        for b in range(2):
            self.stage_A(l, b)
        self.mixers(l)
        for b in range(2):
            self.stage_proj_res(l, b, which="out")
        if not self.wcast_done:
            self.stage_wcast(l)
        for b in range(2):
            self.stage_mlp(l, b)

    def stage_mod(self, l):
        k = self.k
        with k.scope():
            wr = k.ring("adaw", [128, 16, 128], F32, 3)
            pm = k.ps("pmod")
            adab = k.sb("adab", [128, 96], F32)
            nw1 = k.sb("nw1", [128, 16], F32)
            nw2 = k.sb("nw2", [128, 16], F32)
            k.dma("sp", adab.t[:], self.ada_bT[l], [], [adab])
            k.dma("sp", nw1.t[:], self.n1w[l], [], [nw1])
            k.dma("sp", nw2.t[:], self.n2w[l], [], [nw2])
            wv = self.ada_w[l].rearrange("(kc p) f -> p kc f", p=128)
            for j in range(96):
                w = wr.next()
                k.dma("sp", w.t[:], wv[:, :, j * 128:(j + 1) * 128], [], [w])
                for kc in range(16):
                    k.mm(pm.t[:, 3 * j:3 * j + 3], w.t[:, kc, :], self.cs.t[:, kc, :], kc == 0, kc == 15,
                         [w, self.cs], [pm])
            pv = pm.t[:, 0:288].rearrange("p (j r) -> p j r", r=3)
            for r in range(3):
                k.tt(self.modv.t[:, :, r], pv[:, :, r], adab.t[:], ALU.add, [pm, adab], [self.modv])
            if "modv_o" in self.dbg:
                k.dma("sp", self.modv_o[:, :], self.modv.t[:].rearrange("p j r -> p (j r)"), [self.modv], [])
            for r in range(3):
                k.stt(self.g1.t[:, :, r], self.modv.t[:, 16:32, r], 1.0, nw1.t[:], ALU.add, ALU.mult,
                      [self.modv, nw1], [self.g1])
                k.stt(self.g2.t[:, :, r], self.modv.t[:, 64:80, r], 1.0, nw2.t[:], ALU.add, ALU.mult,
                      [self.modv, nw2], [self.g2])

    def make_hT(self, hT, b, g, shift_base, blocks=TB, rel=False, nx=2, base=None, nmax=512):
        k = self.k
        with k.scope():
            xr = k.ring("xblk", [128, 16, nmax], F32, nx)
            sqr = k.ring("sq", [128, nmax], F32, 3)
            rsr = k.ring("rstd", [128, nmax], F32, 2)
            tmr = k.ring("tmp", [128, nmax], F32, 3)
            pss = k.psring("ss", 2)
            xv = self.xs[b].rearrange("(c p) t -> p c t", p=128)
            for (t0, n) in blocks:
                r = 2 if t0 < CTX else b
                o0 = (t0 - base) if base is not None else (0 if rel else t0)
                xb = xr.next()
                for c in range(16):
                    k.dma("sp", xb.t[:, c, 0:n], xv[:, c, t0:t0 + n], [], [xb])
                ss = pss.next()
                for c in range(16):
                    sq = sqr.next()
                    k.act(sq.t[:, 0:n], xb.t[:, c, 0:n], AF.Square, [xb], [sq])
                    k.mm(ss.t[:, 0:n], self.onesF, sq.t[:, 0:n], c == 0, c == 15, [sq, self.C], [ss])
                rs = rsr.next()
                k.act(rs.t[:, 0:n], ss.t[:, 0:n], AF.Sqrt, [ss, self.epsT], [rs], bias=self.epsT.t[:, 0:1], scale=1.0 / D)
                k.op("dve", lambda e: e.reciprocal(out=rs.t[:, 0:n], in_=rs.t[:, 0:n]), [rs], [rs])
                for c in range(16):
                    tm = tmr.next()
                    k.stt(tm.t[:, 0:n], xb.t[:, c, 0:n], g.t[:, c, r:r + 1], rs.t[:, 0:n], ALU.mult, ALU.mult,
                          [xb, g, rs], [tm])
                    k.act(hT.t[:, c, o0:o0 + n], tm.t[:, 0:n], AF.Identity, [tm, self.modv], [hT],
                          bias=self.modv.t[:, shift_base + c, r:r + 1], scale=1.0)

    FM_CHUNKS = [0, 128, 256, 384, 512, 640, 768, 896, 1024, 1152, 1280, 1408,
                 3152, 3280, 3408, 3536, 3664, 3792, 3920, 4048]
    TM_BLOCKS = [(1024, 512), (1536, 512), (2048, 512), (2560, 512), (3072, 80)]

    def stage_A(self, l, b):
        k = self.k
        with k.scope():
            hT = k.sb("hT", [128, 16, T], BF16)
            self.make_hT(hT, b, self.g1, 0)
            wv = self.w_in[l].rearrange("(kc p) c -> p kc c", p=128)
            with k.scope():
                wf = k.ring("wf", [128, 16, 128], F32, 2)
                wb = k.ring("wb", [128, 16, 128], BF16, 2)
                ob = k.ring("ob", [128, 512], F32, 3)
                pp = k.psring("pp", 3)
                ei = 0
                for ci, c0 in enumerate(self.FM_CHUNKS):
                    w32 = wf.next()
                    k.dma("sp", w32.t[:], wv[:, :, c0:c0 + 128], [], [w32])
                    w16 = wb.next()
                    k.cp(w16.t[:], w32.t[:], [w32], [w16], eng="pool")
                    for (t0, n) in TB:
                        p = pp.next()
                        for kc in range(16):
                            k.mm(p.t[:, 0:n], w16.t[:, kc, :], hT.t[:, kc, t0:t0 + n], kc == 0, kc == 15, [w16, hT], [p])
                        o = ob.next()
                        k.cp(o.t[:, 0:n], p.t[:, 0:n], [p], [o], eng=("act" if ei % 2 else "dve"))
                        ei += 1
                        k.dma("sp", self.zf[b, ci * 128:(ci + 1) * 128, t0:t0 + n], o.t[:, 0:n], [o], [])
            with k.scope():
                wf = k.ring("wf2", [128, 8, 512], F32, 2)
                wb = k.ring("wb2", [128, 16, 512], BF16, 2)
                ob = k.ring("ob2", [128, 512], F32, 3)
                pp = k.psring("pp2", 3)
                ei = 0
                for (c0, w) in self.TM_BLOCKS:
                    w16 = wb.next()
                    for hf in range(2):
                        w32 = wf.next()
                        k.dma("sp", w32.t[:, :, 0:w], wv[:, hf * 8:(hf + 1) * 8, c0:c0 + w], [], [w32])
                        k.cp(w16.t[:, hf * 8:(hf + 1) * 8, 0:w], w32.t[:, :, 0:w], [w32], [w16], eng="pool")
                    for tt in range(NT):
                        p = pp.next()
                        for kc in range(16):
                            k.mm(p.t[:, 0:w], hT.t[:, kc, tt * 128:(tt + 1) * 128], w16.t[:, kc, 0:w], kc == 0, kc == 15,
                                 [w16, hT], [p])
                        o = ob.next()
                        k.cp(o.t[:, 0:w], p.t[:, 0:w], [p], [o], eng=("act" if ei % 2 else "dve"))
                        ei += 1
                        k.dma("sp", self.zt[b, tt * 128:(tt + 1) * 128, c0 - 1024:c0 - 1024 + w], o.t[:, 0:w], [o], [])

    def stage_proj_res(self, l, b, which):
        k = self.k
        with k.scope():
            cT = k.sb("ccT", [128, 16, T], BF16)
            cv = self.cc[b].rearrange("(c p) t -> p c t", p=128)
            for c in range(16):
                k.dma("sp", cT.t[:, c, :], cv[:, c, :], [], [cT])
            wv = self.w_out[l].rearrange("(kc p) c -> p kc c", p=128)
            xv = self.xs[b].rearrange("(c p) t -> p c t", p=128)
            wf = k.ring("wf", [128, 16, 128], F32, 2)
            wb = k.ring("wb", [128, 16, 128], BF16, 2)
            xr = k.ring("xo", [128, 512], F32, 3)
            pp = k.psring("pp", 3)
            for fc in range(16):
                w32 = wf.next()
                k.dma("sp", w32.t[:], wv[:, :, fc * 128:(fc + 1) * 128], [], [w32])
                w16 = wb.next()
                k.cp(w16.t[:], w32.t[:], [w32], [w16], eng="pool")
                for (t0, n) in TB:
                    r = 2 if t0 < CTX else b
                    xo = xr.next()
                    k.dma("sp", xo.t[:, 0:n], xv[:, fc, t0:t0 + n], [], [xo])
                    p = pp.next()
                    for kc in range(16):
                        k.mm(p.t[:, 0:n], w16.t[:, kc, :], cT.t[:, kc, t0:t0 + n], kc == 0, kc == 15, [w16, cT], [p])
                    k.stt(xo.t[:, 0:n], p.t[:, 0:n], self.modv.t[:, 32 + fc, r:r + 1], xo.t[:, 0:n], ALU.mult, ALU.add,
                          [p, self.modv, xo], [xo])
                    k.dma("sp", xv[:, fc, t0:t0 + n], xo.t[:, 0:n], [xo], [])

    MLP_BLOCKS = [[(0, 256), (256, 256), (512, 256)], [(768, 256), (1024, 256), (1280, 256)],
                  [(1536, 256), (1792, 256), (2048, 256)]]

    def stage_wcast(self, l):
        k = self.k
        with k.scope():
            f32r = k.ring("wc32", [128, 8192], F32, 2)
            b16r = k.ring("wc16", [128, 8192], BF16, 2)
            engs = ["dve", "act", "pool"]
            ei = 0
            w1tv = self.W1t.rearrange("fc p kc j -> p fc kc j")
            for kc in range(16):
                a = f32r.next()
                k.dma("sp", a.t[:], self.mlp_w1[l, kc * 128:(kc + 1) * 128, :], [], [a])
                bb = b16r.next()
                for q in range(4):
                    k.cp(bb.t[:, q * 2048:(q + 1) * 2048], a.t[:, q * 2048:(q + 1) * 2048], [a], [bb], eng=engs[ei % 3])
                    ei += 1
                k.dma("sp", w1tv[:, :, kc, :], bb.t[:].rearrange("p (fc j) -> p fc j", j=128), [bb], [])
            w2v = self.mlp_w2[l].rearrange("(fg fc p) d -> fg fc p d", fc=16, p=128)
            w2tv = self.W2t.rearrange("fg dc p fc j -> fg fc p dc j")
            for fg in range(4):
                for f4 in range(4):
                    a = f32r.next()
                    for f in range(4):
                        k.dma("sp", a.t[:, f * 2048:(f + 1) * 2048], w2v[fg, f4 * 4 + f], [], [a])
                    bb = b16r.next()
                    for q in range(4):
                        k.cp(bb.t[:, q * 2048:(q + 1) * 2048], a.t[:, q * 2048:(q + 1) * 2048], [a], [bb], eng=engs[ei % 3])
                        ei += 1
                    for f in range(4):
                        k.dma("sp", w2tv[fg, f4 * 4 + f], bb.t[:, f * 2048:(f + 1) * 2048].rearrange("p (dc j) -> p dc j", j=128), [bb], [])

    def wcast_gen(self, l, f32r, b16r):
        k = self.k
        w1tv = self.W1t.rearrange("fc p kc j -> p fc kc j")
        w2v = self.mlp_w2[l].rearrange("(fg fc p) d -> fg fc p d", fc=16, p=128)
        w2tv = self.W2t.rearrange("fg dc p fc j -> fg fc p dc j")
        pieces = []
        for kc in range(16):
            for q in range(4):
                pieces.append((self.mlp_w1[l, kc * 128:(kc + 1) * 128, q * 2048:(q + 1) * 2048], w1tv[:, q * 16:(q + 1) * 16, kc, :]))
        for fg in range(4):
            for fc in range(16):
                pieces.append((w2v[fg, fc], w2tv[fg, fc]))
        loaded = {}

        def load(i):
            a = f32r.next()
            k.dma("sp", a.t[:], pieces[i][0], [], [a])
            loaded[i] = a

        load(0)
        for i in range(len(pieces)):
            if i + 1 < len(pieces):
                load(i + 1)
            a = loaded.pop(i)
            bb = b16r.next()
            k.cp(bb.t[:], a.t[:], [a], [bb], eng="act")
            k.dma("sp", pieces[i][1], bb.t[:].rearrange("p (c j) -> p c j", j=128), [bb], [])
            yield

    def stage_mlp(self, l, b):
        k = self.k
        with k.scope():
            hT = k.sb("hT2", [128, 16, 768], BF16)
            oacc = k.sb("oacc", [128, 16, 768], F32)
            aTr = k.ring("aT", [128, 16, 768], BF16, 2)
            w1r = k.ring("w1s", [128, 16, 128], BF16, 3)
            w2r = k.ring("w2s", [128, 16, 128], BF16, 3)
            rl = k.ring("rl", [128, 512], F32, 4)
            xr = k.ring("xo", [128, 512], F32, 3)
            pp = k.psring("pp", 3)
            pq = k.psring("pq", 2)
            xv = self.xs[b].rearrange("(c p) t -> p c t", p=128)
            MM = ((0, 512), (512, 256))
            for subs in self.MLP_BLOCKS:
                base = subs[0][0]
                self.make_hT(hT, b, self.g2, 48, subs, nx=1, base=base, nmax=256)
                for fg in range(4):
                    aT = aTr.next()
                    for fc in range(16):
                        w = w1r.next()
                        k.dma("sp", w.t[:], self.W1t[fg * 16 + fc], [], [w])
                        for (o, n) in MM:
                            p = pp.next()
                            for kc in range(16):
                                k.mm(p.t[:, 0:n], w.t[:, kc, :], hT.t[:, kc, o:o + n], kc == 0, kc == 15, [w, hT], [p])
                            rr = rl.next()
                            k.act(rr.t[:, 0:n], p.t[:, 0:n], AF.Relu, [p], [rr])
                            k.tt(aT.t[:, fc, o:o + n], rr.t[:, 0:n], rr.t[:, 0:n], ALU.mult, [rr], [aT])
                    for dc in range(16):
                        w = w2r.next()
                        k.dma("sp", w.t[:], self.W2t[fg, dc], [], [w])
                        for (o, n) in MM:
                            p = pq.next()
                            for fc in range(16):
                                k.mm(p.t[:, 0:n], w.t[:, fc, :], aT.t[:, fc, o:o + n], fc == 0, fc == 15, [w, aT], [p])
                            if fg == 0:
                                k.cp(oacc.t[:, dc, o:o + n], p.t[:, 0:n], [p], [oacc], eng="act")
                            elif fg < 3:
                                k.tt(oacc.t[:, dc, o:o + n], p.t[:, 0:n], oacc.t[:, dc, o:o + n], ALU.add, [p, oacc], [oacc])
                            else:
                                tm = rl.next()
                                k.tt(tm.t[:, 0:n], p.t[:, 0:n], oacc.t[:, dc, o:o + n], ALU.add, [p, oacc], [tm])
                                xo = xr.next()
                                k.dma("sp", xo.t[:, 0:n], xv[:, dc, base + o:base + o + n], [], [xo])
                                a0 = base + o
                                cuts = [a0] + ([CTX] if a0 < CTX < a0 + n else []) + [a0 + n]
                                for ci in range(len(cuts) - 1):
                                    c0, c1 = cuts[ci] - a0, cuts[ci + 1] - a0
                                    r = 2 if cuts[ci] < CTX else b
                                    k.stt(xo.t[:, c0:c1], tm.t[:, c0:c1], self.modv.t[:, 80 + dc, r:r + 1], xo.t[:, c0:c1],
                                          ALU.mult, ALU.add, [tm, self.modv, xo], [xo])
                                k.dma("sp", xv[:, dc, base + o:base + o + n], xo.t[:, 0:n], [xo], [])

    def declare_mixer_inputs(self):
        L = DEPTH
        self.s5v = self.din("s5v", [L, 128, 192])
        self.s5A = self.din("s5A", [L, 128, 64, 16])
        self.s5B = self.din("s5B", [L, 128, 64, 16])
        self.s5CA = self.din("s5CA", [L, 128, 64, 16])
        self.s5CB = self.din("s5CB", [L, 128, 64, 16])
        self.s5w = self.din("s5w", [L, 128, 8])
        self.glu_w = self.din("glu_w", [L, 512, 512])
        self.nidx = self.din("nidx", [2, 128, T])
        self.lruv = self.din("lruv", [L, 128, 44])
        self.lru_wa = self.din("lru_wa", [L, 2, 4, 128, 128])
        self.lru_wx = self.din("lru_wx", [L, 2, 4, 128, 128])
        self.ygd = self.dscr("ygd", [2, 512, T])
        self.mlb = self.din("mlb", [L, 128, 16])
        self.onw = self.din("onw", [L, 128, 512])
        self.selc = self.din("selc", [16, 2048])
        self.mlav = self.din("mlav", [L, 128, 896])
        self.w_q_up = self.din("w_q_up", [L, 384, 768])
        self.w_kv_up = self.din("w_kv_up", [L, 128, 1024])
        self.ropec = self.din("ropec", [128, NT, 32])
        self.ropes = self.din("ropes", [128, NT, 32])

    def setup_mixer_consts(self):
        k = self.k
        self.sgn = self.C.t[:, 896:897]
        self.oneT = k.sb("oneT", [128, 1], F32)
        k.op("dve", lambda e: e.memset(self.oneT.t[:], 1.0), [], [self.oneT])
        self.hpiT = k.sb("hpiT", [128, 1], F32)
        k.op("dve", lambda e: e.memset(self.hpiT.t[:], float(np.pi / 2)), [], [self.hpiT])

    def mixers(self, l):
        which = self.dbg.get("mixers", ("s5", "lru", "mlstm", "mla"))
        if "s5" in which:
            self.mixer_s5(l)
        if "lru" in which:
            self.mixer_lru(l)
        if "mlstm" in which:
            self.mixer_mlstm(l)
        if "mla" in which:
            self.mixer_mla(l)

    def frac_centered(self, out, u, ki, tmp, n, bufs):
        k = self.k
        k.cp(ki, u, bufs, bufs)
        k.tt(tmp, u, ki, ALU.subtract, bufs, bufs)
        k.stt(out, tmp, 0.5, tmp, ALU.is_gt, ALU.subtract, bufs, bufs)
        k.stt(out, out, 0.5, out, ALU.is_gt, ALU.subtract, bufs, bufs)

    def sincos(self, sin_out, cos_out, r, tmp, bufs):
        k = self.k
        k.act(sin_out, r, AF.Sin, bufs, bufs, scale=TWO_PI)
        k.stt(tmp, r, 0.25, r, ALU.is_gt, ALU.subtract, bufs, bufs)
        k.act(cos_out, tmp, AF.Sin, bufs + [self.hpiT], bufs, scale=-TWO_PI, bias=self.hpiT.t[:, 0:1])

    def gelu_tanh(self, out, y, t1, s1, bufs):
        k = self.k
        k.tt(t1, y, y, ALU.mult, bufs, bufs)
        k.ts(t1, t1, 0.044715, 1.0, ALU.mult, ALU.add, bufs, bufs)
        k.tt(t1, t1, y, ALU.mult, bufs, bufs)
        k.act(s1, t1, AF.Sigmoid, bufs, bufs, scale=1.5957691216057308)
        k.tt(out, y, s1, ALU.mult, bufs, bufs)

    def mixer_s5(self, l):
        k = self.k
        with k.scope():
            pv = k.sb("s5pv", [128, 192], F32)
            k.dma("sp", pv.t[:], self.s5v[l], [], [pv])
            A = k.sb("s5A", [128, 64, 16], F32)
            Bm = k.sb("s5B", [128, 64, 16], F32)
            CA = k.sb("s5CA", [128, 64, 16], F32)
            CB = k.sb("s5CB", [128, 64, 16], F32)
            k.dma("sp", A.t[:], self.s5A[l], [], [A])
            k.dma("sp", Bm.t[:], self.s5B[l], [], [Bm])
            k.dma("sp", CA.t[:], self.s5CA[l], [], [CA])
            k.dma("sp", CB.t[:], self.s5CB[l], [], [CB])
            sw = k.sb("s5w", [128, 8], F32)
            k.dma("sp", sw.t[:], self.s5w[l], [], [sw])
            nid = k.sb("nidx", [128, 2, T], F32)
            for d in range(2):
                k.dma("sp", nid.t[:, d, :], self.nidx[d], [], [nid])
            W = k.sb("s5work", [128, 16, 64], F32)
            WI = k.sb("s5worki", [128, 64], I32)
            Wb = [W]
            lr, li, dt, mag, ang, fT, sn, cs_, t0_, t1_, fr, fi, den, fis, frs, ar1 = [W.t[:, i, :] for i in range(16)]
            k.ts(lr, pv.t[:, 0:64], -1e-4, None, ALU.min, None, [pv], Wb)
            k.cp(li, pv.t[:, 64:128], [pv], Wb)
            k.act(dt, pv.t[:, 128:192], AF.Exp, [pv], Wb)
            k.tt(t0_, lr, dt, ALU.mult, Wb, Wb)
            k.act(mag, t0_, AF.Exp, Wb, Wb)
            k.tt(ang, li, dt, ALU.mult, Wb, Wb)
            k.ts(t0_, ang, 1.0 / TWO_PI, None, ALU.mult, None, Wb, Wb)
            self.frac_centered(fT, t0_, WI.t[:], t1_, 64, Wb + [WI])
            self.sincos(sn, cs_, fT, t1_, Wb)
            k.tt(t0_, mag, cs_, ALU.mult, Wb, Wb)
            k.ts(ar1, t0_, -1.0, None, ALU.add, None, Wb, Wb)
            k.tt(t1_, mag, sn, ALU.mult, Wb, Wb)
            k.tt(den, lr, lr, ALU.mult, Wb, Wb)
            k.tt(t0_, li, li, ALU.mult, Wb, Wb)
            k.tt(den, den, t0_, ALU.add, Wb, Wb)
            k.op("dve", lambda e: e.reciprocal(out=den, in_=den), Wb, Wb)
            k.tt(fr, ar1, lr, ALU.mult, Wb, Wb)
            k.tt(t0_, t1_, li, ALU.mult, Wb, Wb)
            k.tt(fr, fr, t0_, ALU.add, Wb, Wb)
            k.tt(fr, fr, den, ALU.mult, Wb, Wb)
            k.tt(fi, t1_, lr, ALU.mult, Wb, Wb)
            k.tt(t0_, ar1, li, ALU.mult, Wb, Wb)
            k.tt(fi, fi, t0_, ALU.subtract, Wb, Wb)
            k.tt(fi, fi, den, ALU.mult, Wb, Wb)
            k.ts(fis, fi, self.sgn, None, ALU.mult, None, Wb + [self.C], Wb)
            k.ts(frs, fr, self.sgn, -1.0, ALU.mult, ALU.mult, Wb + [self.C], Wb)
            nsgn = k.sb("nsgn", [128, 1], F32)
            k.ts(nsgn.t[:], self.sgn, -1.0, None, ALU.mult, None, [self.C], [nsgn])

            X1 = k.sb("X1", [128, 128], F32)
            X2 = k.sb("X2", [128, 128], F32)
            xt = k.sb("xtmp", [128, 16], F32)
            BP1 = k.ring("BP1", [128, 128], BF16, 2)
            BP2 = k.ring("BP2", [128, 128], BF16, 2)
            W1p = k.ring("W1p", [128, 128], BF16, 2)
            W2p = k.ring("W2p", [128, 128], BF16, 2)
            SIN = k.sb("SIN", [128, T], F32)
            COS = k.sb("COS", [128, T], F32)
            U_ = k.sb("U", [128, T], F32)
            KI = k.sb("KI", [128, T], I32)
            R_ = k.sb("R", [128, T], F32)
            ub = [k.sb("ub", [128, T], BF16) for _ in range(2)]
            u32 = k.sb("u32", [128, T], F32)
            bt_ = [k.sb("bt", [128, T], F32)] * 2
            G_ = [k.sb("G", [128, T], F32)] * 2
            V1_ = [k.sb("V1", [128, T], BF16)] * 2
            V2_ = [k.sb("V2", [128, T], BF16)] * 2
            yacc = [k.sb("yacc", [128, T], F32) for _ in range(2)]
            t1r = k.ring("t1", [128, 512], F32, 3)
            wc32 = k.ring("wc32", [128, 2048], F32, 3)
            wc16 = k.ring("wc16", [128, 2048], BF16, 2)
            wgen = self.wcast_gen(l, wc32, wc16)
            pd = k.psring("pd", 4)
            py = k.psring("py", 2)
            ptr = k.ps("ptr")
            for c in range(4):
                for b in range(2):
                    k.dma("sp", u32.t[:], self.zf[b, c * 128:(c + 1) * 128, :], [], [u32])
                    k.cp(ub[b].t[:], u32.t[:], [u32], [ub[b]], eng="act")
                first = True
                for j in range(8):
                    g = 8 * c + j
                    for d in range(2):
                        dg = d * 32 + g
                        cols = slice(16 * j, 16 * j + 16)
                        k.op("dve", lambda e: e.memset(X1.t[:], 0.0), [], [X1])
                        k.op("dve", lambda e: e.memset(X2.t[:], 0.0), [], [X2])
                        k.ts(xt.t[:], A.t[:, dg, :], fr[:, dg:dg + 1], None, ALU.mult, None, [A] + Wb, [xt])
                        k.stt(X1.t[:, cols], Bm.t[:, dg, :], fis[:, dg:dg + 1], xt.t[:], ALU.mult, ALU.add, [Bm, xt] + Wb, [X1])
                        k.ts(xt.t[:], A.t[:, dg, :], fi[:, dg:dg + 1], None, ALU.mult, None, [A] + Wb, [xt])
                        k.stt(X2.t[:, cols], Bm.t[:, dg, :], frs[:, dg:dg + 1], xt.t[:], ALU.mult, ALU.add, [Bm, xt] + Wb, [X2])
                        bp1 = BP1.next()
                        bp2 = BP2.next()
                        k.tr(ptr.t[:, 0:128], X1.t[:], self.identF, [X1, self.C], [ptr])
                        k.cp(bp1.t[:], ptr.t[:, 0:128], [ptr], [bp1], eng="act")
                        k.tr(ptr.t[:, 128:256], X2.t[:], self.identF, [X2, self.C], [ptr])
                        k.cp(bp2.t[:], ptr.t[:, 128:256], [ptr], [bp2], eng="act")
                        w1 = W1p.next()
                        w2 = W2p.next()
                        k.op("pool", lambda e: e.memset(w1.t[:], 0.0), [], [w1])
                        k.op("pool", lambda e: e.memset(w2.t[:], 0.0), [], [w2])
                        k.ts(w1.t[:, cols], CA.t[:, dg, :], nsgn.t[:, 0:1], None, ALU.mult, None, [CA, nsgn], [w1])
                        k.ts(w2.t[:, cols], CB.t[:, dg, :], -1.0, None, ALU.mult, None, [CB], [w2])
                        k.ts(U_.t[:], nid.t[:, d, :], fT[:, dg:dg + 1], None, ALU.mult, None, [nid] + Wb, [U_])
                        self.frac_centered(R_.t[:], U_.t[:], KI.t[:], U_.t[:], T, [U_, KI, R_])
                        self.sincos(SIN.t[:], COS.t[:], R_.t[:], U_.t[:], [R_, U_, SIN, COS])
                        for b in range(2):
                            bt, G, V1, V2 = bt_[b], G_[b], V1_[b], V2_[b]
                            next(wgen, None)
                            for (t0, n) in TB:
                                p1 = pd.next()
                                p2 = pd.next()
                                k.mm(p1.t[:, 0:n], bp1.t[:], ub[b].t[:, t0:t0 + n], True, True, [bp1, ub[b]], [p1])
                                k.mm(p2.t[:, 0:n], bp2.t[:], ub[b].t[:, t0:t0 + n], True, True, [bp2, ub[b]], [p2])
                                t1 = t1r.next()
                                k.tt(t1.t[:, 0:n], p1.t[:, 0:n], COS.t[:, t0:t0 + n], ALU.mult, [p1, COS], [t1])
                                k.tt(bt.t[:, t0:t0 + n], p2.t[:, 0:n], SIN.t[:, t0:t0 + n], ALU.mult, [p2, SIN], [bt])
                                k.tt(bt.t[:, t0:t0 + n], bt.t[:, t0:t0 + n], t1.t[:, 0:n], ALU.add, [bt, t1], [bt])
                            rm = mag[:, dg:dg + 1]
                            if d == 0:
                                k.scan(G.t[:, 0:CTX], rm.to_broadcast([128, CTX]), bt.t[:, 0:CTX], 0.0, [bt] + Wb, [G])
                                k.scan(G.t[:, CTX:T], rm.to_broadcast([128, LAT]), bt.t[:, CTX:T], G.t[:, CTX - 1:CTX], [bt, G] + Wb, [G])
                            else:
                                k.scan(G.t[:, 0:CTX][:, ::-1], rm.to_broadcast([128, CTX]), bt.t[:, 0:CTX][:, ::-1], 0.0, [bt] + Wb, [G])
                                k.scan(G.t[:, CTX:T][:, ::-1], rm.to_broadcast([128, LAT]), bt.t[:, CTX:T][:, ::-1], G.t[:, 0:1], [bt, G] + Wb, [G])
                            k.tt(V1.t[:], G.t[:], COS.t[:], ALU.mult, [G, COS], [V1], eng="pool")
                            k.tt(V2.t[:], G.t[:], SIN.t[:], ALU.mult, [G, SIN], [V2], eng="pool")
                            for (t0, n) in TB:
                                p = py.next()
                                k.mm(p.t[:, 0:n], w1.t[:], V1.t[:, t0:t0 + n], True, False, [w1, V1], [p])
                                k.mm(p.t[:, 0:n], w2.t[:], V2.t[:, t0:t0 + n], False, True, [w2, V2], [p])
                                if first:
                                    k.cp(yacc[b].t[:, t0:t0 + n], p.t[:, 0:n], [p], [yacc[b]])
                                else:
                                    k.tt(yacc[b].t[:, t0:t0 + n], p.t[:, 0:n], yacc[b].t[:, t0:t0 + n], ALU.add, [p, yacc[b]], [yacc[b]])
                        first = False
                for b in range(2):
                    bt, G = bt_[b], G_[b]
                    k.dma("sp", u32.t[:], self.zf[b, c * 128:(c + 1) * 128, :], [], [u32])
                    k.stt(yacc[b].t[:], u32.t[:], sw.t[:, c:c + 1], yacc[b].t[:], ALU.mult, ALU.add, [u32, sw, yacc[b]], [yacc[b]])
                    self.gelu_tanh(G.t[:], yacc[b].t[:], bt.t[:], U_.t[:], [G, yacc[b], bt, U_])
                    k.dma("sp", self.ygd[b, c * 128:(c + 1) * 128, :], G.t[:], [G], [])
            for _ in wgen:
                pass
            self.wcast_done = True
        with k.scope():
            sw = k.sb("s5w", [128, 8], F32)
            k.dma("sp", sw.t[:], self.s5w[l], [], [sw])
            gw32 = k.sb("gw32", [128, 4, 512], F32)
            gw = k.sb("gw", [128, 4, 512], BF16)
            k.dma("sp", gw32.t[:], self.glu_w[l].rearrange("(kc p) o -> p kc o", p=128), [], [gw32])
            k.cp(gw.t[:], gw32.t[:], [gw32], [gw], eng="pool")
            yg = k.sb("yg", [128, 4, T], F32)
            ygb = k.sb("ygb", [128, 4, T], BF16)
            sg = k.ring("sg", [128, 512], F32, 2)
            ob = k.ring("ob", [128, 512], BF16, 3)
            pp = k.psring("pg", 3)
            for b in range(2):
                for c in range(4):
                    k.dma("sp", yg.t[:, c, :], self.ygd[b, c * 128:(c + 1) * 128, :], [], [yg])
                    k.cp(ygb.t[:, c, :], yg.t[:, c, :], [yg], [ygb], eng="act")
                for co in range(4):
                    for (t0, n) in TB:
                        p = pp.next()
                        for kc in range(4):
                            k.mm(p.t[:, 0:n], gw.t[:, kc, co * 128:(co + 1) * 128], ygb.t[:, kc, t0:t0 + n], kc == 0, kc == 3, [gw, ygb], [p])
                        s_ = sg.next()
                        k.act(s_.t[:, 0:n], p.t[:, 0:n], AF.Sigmoid, [p, sw], [s_], bias=sw.t[:, 4 + co:5 + co])
                        o = ob.next()
                        k.tt(o.t[:, 0:n], yg.t[:, co, t0:t0 + n], s_.t[:, 0:n], ALU.mult, [yg, s_], [o])
                        k.dma("sp", self.cc[b, co * 128:(co + 1) * 128, t0:t0 + n], o.t[:, 0:n], [o], [])

    def mixer_lru(self, l):
        k = self.k
        with k.scope():
            lv = k.sb("lruv", [128, 44], F32)
            k.dma("sp", lv.t[:], self.lruv[l], [], [lv])
            cw = lv.t[:, 0:16].rearrange("p (c j) -> p c j", j=4)
            cb = lv.t[:, 16:20]
            ba = lv.t[:, 20:28].rearrange("p (d c) -> p d c", c=4)
            bx = lv.t[:, 28:36].rearrange("p (d c) -> p d c", c=4)
            lam = lv.t[:, 36:44]
            sp = k.sb("lrusp", [128, 16], F32)
            k.act(sp.t[:, 0:8], lam, AF.Exp, [lv], [sp], scale=-1.0)
            k.act(sp.t[:, 0:8], sp.t[:, 0:8], AF.Ln, [sp, self.oneT], [sp], bias=self.oneT.t[:, 0:1])
            k.ts(sp.t[:, 8:16], sp.t[:, 0:8], -16.0, None, ALU.mult, None, [sp], [sp])
            k.ts(sp.t[:, 0:8], sp.t[:, 0:8], -8.0, None, ALU.mult, None, [sp], [sp])
            wa32 = k.sb("wa32", [128, 8, 128], F32)
            wx32 = k.sb("wx32", [128, 8, 128], F32)
            wa = k.sb("wa", [128, 8, 128], BF16)
            wx = k.sb("wx", [128, 8, 128], BF16)
            k.dma("sp", wa32.t[:], self.lru_wa[l].rearrange("d n c o -> c (d n) o"), [], [wa32])
            k.dma("sp", wx32.t[:], self.lru_wx[l].rearrange("d n c o -> c (d n) o"), [], [wx32])
            k.cp(wa.t[:], wa32.t[:], [wa32], [wa])
            k.cp(wx.t[:], wx32.t[:], [wx32], [wx])
            x = k.sb("lx", [128, T], F32)
            gt = k.sb("lg", [128, T], F32)
            xs = k.sb("lxs", [128, T], F32)
            xsb = k.sb("lxsb", [128, T], BF16)
            r_ = k.sb("lr", [128, T], F32)
            i_ = k.sb("li", [128, T], F32)
            a_ = k.sb("la", [128, T], F32)
            q_ = k.sb("lq", [128, T], F32)
            h_ = k.sb("lh", [128, T], F32)
            ys = k.sb("lys", [128, T], F32)
            ob = k.sb("lob", [128, T], BF16)
            pp = k.psring("pl", 4)
            for b in range(2):
                for c in range(4):
                    k.dma("sp", x.t[:], self.zf[b, 1536 + c * 128:1536 + (c + 1) * 128, :], [], [x])
                    k.dma("sp", gt.t[:], self.zf[b, 2048 + c * 128:2048 + (c + 1) * 128, :], [], [gt])
                    k.ts(xs.t[:], x.t[:], cw[:, c, 2:3], cb[:, c:c + 1], ALU.mult, ALU.add, [x, lv], [xs])
                    for jtap in (0, 1, 3):
                        o = jtap - 2
                        for (r0, r1) in ((0, CTX), (CTX, T)):
                            a0 = r0 + max(0, -o)
                            a1 = r1 - max(0, o)
                            k.stt(xs.t[:, a0:a1], x.t[:, a0 + o:a1 + o], cw[:, c, jtap:jtap + 1], xs.t[:, a0:a1],
                                  ALU.mult, ALU.add, [x, lv, xs], [xs])
                    k.cp(xsb.t[:], xs.t[:], [xs], [xsb], eng="act")
                    for d in range(2):
                        for (t0, n) in TB:
                            p = pp.next()
                            k.mm(p.t[:, 0:n], wa.t[:, d * 4 + c, :], xsb.t[:, t0:t0 + n], True, True, [wa, xsb], [p])
                            k.act(r_.t[:, t0:t0 + n], p.t[:, 0:n], AF.Sigmoid, [p, lv], [r_], bias=ba[:, d, c:c + 1])
                            p = pp.next()
                            k.mm(p.t[:, 0:n], wx.t[:, d * 4 + c, :], xsb.t[:, t0:t0 + n], True, True, [wx, xsb], [p])
                            k.act(i_.t[:, t0:t0 + n], p.t[:, 0:n], AF.Sigmoid, [p, lv], [i_], bias=bx[:, d, c:c + 1])
                        dc = d * 4 + c
                        k.act(a_.t[:], r_.t[:], AF.Exp, [r_, sp], [a_], scale=sp.t[:, dc:dc + 1])
                        k.act(q_.t[:], r_.t[:], AF.Exp, [r_, sp], [q_], scale=sp.t[:, 8 + dc:9 + dc])
                        k.act(q_.t[:], q_.t[:], AF.Sqrt, [q_, self.oneT], [q_], scale=-1.0, bias=self.oneT.t[:, 0:1])
                        k.tt(q_.t[:], q_.t[:], i_.t[:], ALU.mult, [q_, i_], [q_])
                        k.tt(q_.t[:], q_.t[:], xs.t[:], ALU.mult, [q_, xs], [q_])
                        if d == 0:
                            k.scan(h_.t[:, 0:CTX], a_.t[:, 0:CTX], q_.t[:, 0:CTX], 0.0, [a_, q_], [h_])
                            k.scan(h_.t[:, CTX:T], a_.t[:, CTX:T], q_.t[:, CTX:T], h_.t[:, CTX - 1:CTX], [a_, q_, h_], [h_])
                            k.cp(ys.t[:], h_.t[:], [h_], [ys], eng="pool")
                        else:
                            k.scan(h_.t[:, 0:CTX][:, ::-1], a_.t[:, 0:CTX][:, ::-1], q_.t[:, 0:CTX][:, ::-1], 0.0, [a_, q_], [h_])
                            k.scan(h_.t[:, CTX:T][:, ::-1], a_.t[:, CTX:T][:, ::-1], q_.t[:, CTX:T][:, ::-1], h_.t[:, 0:1], [a_, q_, h_], [h_])
                            k.tt(ys.t[:], ys.t[:], h_.t[:], ALU.add, [ys, h_], [ys])
                    self.gelu_tanh(h_.t[:], gt.t[:], a_.t[:], q_.t[:], [h_, gt, a_, q_])
                    k.tt(ob.t[:], ys.t[:], h_.t[:], ALU.mult, [ys, h_], [ob])
                    k.dma("sp", self.cc[b, 1536 + c * 128:1536 + (c + 1) * 128, :], ob.t[:], [ob], [])

    def mixer_mlstm(self, l):
        k = self.k
        KS = 128 ** -0.5
        TRI3 = self.C.t[:, 256:640]
        MASK = [self.C.t[:, 640:768], self.C.t[:, 768:896]]
        for b in range(2):
            with k.scope():
                Hacc = k.sb("Hacc", [128, NT, 512], F32)
                k.op("pool", lambda e: e.memset(Hacc.t[:], 0.0), [], [Hacc])
                with k.scope():
                    mlb = k.sb("mlb", [128, 16], F32)
                    k.dma("sp", mlb.t[:], self.mlb[l], [], [mlb])
                    sel = k.sb("sel", [16, 2048], F32)
                    nsel = k.sb("nsel", [16, 2048], F32)
                    k.dma("sp", sel.t[:], self.selc[:, :], [], [sel])
                    k.ts(nsel.t[:], sel.t[:], -1.0, None, ALU.mult, None, [sel], [nsel])
                    QT = k.sb("QT", [128, 4, T], BF16)
                    KT = k.sb("KT", [128, 4, T], BF16)
                    Kt = k.sb("Kt", [128, NT, 512], BF16)
                    Va = k.sb("Va", [128, NT, 4, 129], BF16)
                    G16 = k.sb("G16", [128, NT, 16], F32)
                    R = k.sb("R", [16, NT, 384], F32)
                    with k.scope():
                        st = k.ring("st", [128, T], F32, 2)
                        zr = k.ring("zr", [128, 1552], F32, 2)
                        pr = k.psring("pr", 2)
                        for h in range(4):
                            s_ = st.next()
                            k.dma("sp", s_.t[:], self.zf[b, 512 + h * 128:512 + (h + 1) * 128, :], [], [s_])
                            k.cp(QT.t[:, h, :], s_.t[:], [s_], [QT], eng="act")
                            s_ = st.next()
                            k.dma("sp", s_.t[:], self.zf[b, 1024 + h * 128:1024 + (h + 1) * 128, :], [], [s_])
                            k.act(KT.t[:, h, :], s_.t[:], AF.Copy, [s_], [KT], scale=KS)
                        k.op("pool", lambda e: e.memset(Va.t[:], 1.0), [], [Va])
                        for tt in range(NT):
                            z = zr.next()
                            k.dma("sp", z.t[:], self.zt[b, tt * 128:(tt + 1) * 128, 0:1552], [], [z])
                            k.act(Kt.t[:, tt, :], z.t[:, 0:512], AF.Copy, [z], [Kt], scale=KS)
                            k.cp(Va.t[:, tt, :, 0:128], z.t[:, 512:1024].rearrange("p (h e) -> p h e", h=4), [z], [Va])
                            k.tt(G16.t[:, tt, :], z.t[:, 1536:1552], mlb.t[:], ALU.add, [z, mlb], [G16])
                        for d in range(2):
                            gv = G16.t[:, :, d * 8 + 4:d * 8 + 8]
                            k.act(gv, gv, AF.Exp, [G16], [G16], scale=-1.0)
                            k.act(gv, gv, AF.Ln, [G16, self.oneT], [G16], bias=self.oneT.t[:, 0:1])
                            k.ts(gv, gv, -1.0, None, ALU.mult, None, [G16], [G16])
                        for tt in range(NT):
                            p = pr.next()
                            k.mm(p.t[0:16, 0:384], G16.t[:, tt, :], TRI3, True, True, [G16, self.C], [p])
                            k.cp(R.t[:, tt, :], p.t[0:16, 0:384], [p], [R], eng="act")
                    CT32_ = {}
                    CTb_ = {}
                    for d in range(2):
                        for h in range(4):
                            CT32_[d, h] = k.sb("CT32", [128, 129], F32)
                            CTb_[d, h] = k.sb("CTb", [128, 129], BF16)
                            k.op("dve", lambda e: e.memset(CT32_[d, h].t[:], 0.0), [], [CT32_[d, h]])
                            k.op("dve", lambda e: e.memset(CTb_[d, h].t[:], 0.0), [], [CTb_[d, h]])
                    EDr = k.ring("ED", [128, 128], F32, 4)
                    EBr = k.ring("EB", [128, 128], F32, 4)
                    STr = k.ring("ST", [128, 128], BF16, 4)
                    QSr = k.ring("QS", [128, 128], BF16, 4)
                    VWr = k.ring("VW", [128, 129], BF16, 4)
                    dnr = k.ring("dn", [128, 2], F32, 4)
                    pD = k.psring("pD", 2)
                    pB = k.psring("pB", 1)
                    pS = k.psring("pS", 2)
                    pN = k.psring("pN", 2)
                    pC = k.psring("pC", 1)
                    orders = [list(range(NT)), [1, 0] + list(range(NT - 1, 1, -1))]
                    for step in range(NT):
                        for d in range(2):
                            tt = orders[d][step]
                            bsl = slice(0, 128) if d == 0 else slice(128, 256)
                            last = 127 if d == 0 else 0
                            for h in range(4):
                                CT32 = CT32_[d, h]
                                CTb = CTb_[d, h]
                                kli = d * 8 + h
                                klf = d * 8 + 4 + h
                                SLI = sel.t[0:16, kli * 128:(kli + 1) * 128]
                                SLF = sel.t[0:16, klf * 128:(klf + 1) * 128]
                                NLF = nsel.t[0:16, klf * 128:(klf + 1) * 128]
                                tsl = slice(tt * 128, (tt + 1) * 128)
                                Rb = R.t[0:16, tt, bsl]
                                Rg = R.t[0:16, tt, 256:384]
                                pd_ = pD.next()
                                k.mm(pd_.t[:, 0:128], Rg, SLI, True, False, [R, sel], [pd_])
                                k.mm(pd_.t[:, 0:128], Rb, NLF, False, False, [R, nsel], [pd_])
                                k.mm(pd_.t[:, 0:128], SLF, Rb, False, False, [R, sel], [pd_])
                                k.mm(pd_.t[:, 0:128], self.identF, MASK[d], False, True, [self.C], [pd_])
                                ED = EDr.next()
                                k.act(ED.t[:], pd_.t[:, 0:128], AF.Exp, [pd_], [ED])
                                pb_ = pB.next()
                                k.mm(pb_.t[:, 0:128], SLF, Rb, True, True, [R, sel], [pb_])
                                EB = EBr.next()
                                k.act(EB.t[:], pb_.t[:, 0:128], AF.Exp, [pb_], [EB])
                                ps_ = pS.next()
                                k.mm(ps_.t[:, 0:128], KT.t[:, h, tsl], QT.t[:, h, tsl], True, True, [KT, QT], [ps_])
                                ST = STr.next()
                                k.tt(ST.t[:], ps_.t[:, 0:128], ED.t[:], ALU.mult, [ps_, ED], [ST])
                                QS = QSr.next()
                                k.tt(QS.t[:], QT.t[:, h, tsl], EB.t[:], ALU.mult, [QT, EB], [QS])
                                pn_ = pN.next()
                                k.mm(pn_.t[:, 0:129], QS.t[:], CTb.t[:], True, False, [QS, CTb], [pn_])
                                k.mm(pn_.t[:, 0:129], ST.t[:], Va.t[:, tt, h, :], False, True, [ST, Va], [pn_])
                                dn = dnr.next()
                                k.act(dn.t[:, 0:1], pn_.t[:, 128:129], AF.Abs, [pn_], [dn])
                                k.ts(dn.t[:, 0:1], dn.t[:, 0:1], 1.0, None, ALU.max, None, [dn], [dn])
                                k.op("dve", lambda e: e.reciprocal(out=dn.t[:, 1:2], in_=dn.t[:, 0:1]), [dn], [dn])
                                hs = Hacc.t[:, tt, h * 128:(h + 1) * 128]
                                k.stt(hs, pn_.t[:, 0:128], dn.t[:, 1:2], hs, ALU.mult, ALU.add, [pn_, dn, Hacc], [Hacc])
                                VW = VWr.next()
                                k.act(VW.t[:], Va.t[:, tt, h, :], AF.Identity, [Va, ED], [VW], scale=ED.t[:, last:last + 1])
                                pc_ = pC.next()
                                k.mm(pc_.t[:, 0:129], Kt.t[:, tt, h * 128:(h + 1) * 128], VW.t[:], True, True, [Kt, VW], [pc_])
                                k.stt(CT32.t[:], CT32.t[:], EB.t[:, last:last + 1], pc_.t[:, 0:129], ALU.mult, ALU.add,
                                      [CT32, EB, pc_], [CT32])
                                k.cp(CTb.t[:], CT32.t[:], [CT32], [CTb], eng="act")
                with k.scope():
                    onw = k.sb("onw", [128, 512], F32)
                    k.dma("sp", onw.t[:], self.onw[l], [], [onw])
                    OT = k.sb("OT", [128, 4, T], BF16)
                    mor = k.ring("mo", [128, 512], F32, 2)
                    sqr = k.ring("sqh", [128, 512], F32, 2)
                    ssr = k.ring("ss4", [128, 4], F32, 2)
                    pT = k.psring("pT", 2)
                    for tt in range(NT):
                        H = Hacc.t[:, tt, :]
                        sq = sqr.next()
                        k.tt(sq.t[:], H, H, ALU.mult, [Hacc], [sq])
                        ss = ssr.next()
                        k.op("dve", lambda e: e.tensor_reduce(out=ss.t[:], in_=sq.t[:].rearrange("p (h e) -> p h e", h=4), axis=AX.X, op=ALU.add), [sq], [ss])
                        k.act(ss.t[:], ss.t[:], AF.Sqrt, [ss, self.epsT], [ss], scale=1.0 / 128, bias=self.epsT.t[:, 0:1])
                        k.op("dve", lambda e: e.reciprocal(out=ss.t[:], in_=ss.t[:]), [ss], [ss])
                        for h in range(4):
                            hsl = slice(h * 128, (h + 1) * 128)
                            k.stt(sq.t[:, hsl], H[:, hsl], ss.t[:, h:h + 1], onw.t[:, hsl], ALU.mult, ALU.mult, [Hacc, ss, onw], [sq])
                        mo = mor.next()
                        k.dma("sp", mo.t[:], self.zt[b, tt * 128:(tt + 1) * 128, 1024:1536], [], [mo])
                        k.act(mo.t[:], mo.t[:], AF.Sigmoid, [mo], [mo])
                        k.tt(sq.t[:], sq.t[:], mo.t[:], ALU.mult, [sq, mo], [sq])
                        p = pT.next()
                        for h in range(4):
                            hsl = slice(h * 128, (h + 1) * 128)
                            k.tr(p.t[:, hsl], sq.t[:, hsl], self.identF, [sq, self.C], [p])
                        k.cp(OT.t[:, :, tt * 128:(tt + 1) * 128], p.t[:, 0:512].rearrange("p (h e) -> p h e", h=4), [p], [OT], eng="act")
                    for h in range(4):
                        k.dma("sp", self.cc[b, 512 + h * 128:512 + (h + 1) * 128, :], OT.t[:, h, :], [OT], [])

    def mixer_mla(self, l):
        k = self.k
        SC = 192 ** -0.5
        for b in range(2):
            with k.scope():
                QT = k.sb("aQT", [128, 4, T], BF16)
                QT2 = k.sb("aQT2", [64, 4, T], BF16)
                KT = k.sb("aKT", [128, 4, T], BF16)
                KT2 = k.sb("aKT2", [64, 4, T], BF16)
                Va = k.sb("aVa", [128, NT, 4, 129], BF16)
                k.op("pool", lambda e: e.memset(Va.t[:], 1.0), [], [Va])
                with k.scope():
                    nv = k.sb("mlav", [128, 896], F32)
                    k.dma("sp", nv.t[:], self.mlav[l], [], [nv])
                    QAW = nv.t[:, 0:384]
                    KVAW = nv.t[:, 384:512]
                    NW = [nv.t[:, 512:704], nv.t[:, 704:896]]
                    rc = k.sb("ropec", [128, NT, 32], F32)
                    rs = k.sb("ropes", [128, NT, 32], F32)
                    k.dma("sp", rc.t[:], self.ropec[:, :, :], [], [rc])
                    k.dma("sp", rs.t[:], self.ropes[:, :, :], [], [rs])
                    wq32 = k.sb("wq32", [128, 3, 768], F32)
                    wq = k.sb("wq", [128, 3, 768], BF16)
                    wkv32 = k.sb("wkv32", [128, 1024], F32)
                    wkv = k.sb("wkv", [128, 1024], BF16)
                    k.dma("sp", wq32.t[:], self.w_q_up[l].rearrange("(kc p) o -> p kc o", p=128), [], [wq32])
                    k.dma("sp", wkv32.t[:], self.w_kv_up[l], [], [wkv32])
                    k.cp(wq.t[:], wq32.t[:], [wq32], [wq], eng="pool")
                    k.cp(wkv.t[:], wkv32.t[:], [wkv32], [wkv], eng="pool")
                    Zr = k.ring("Z", [128, 576], F32, 2)
                    jk = k.sb("junk", [128, 384], F32)
                    ssr = k.ring("ss2", [128, 2], F32, 2)
                    cnr = k.ring("cn", [128, 512], F32, 2)
                    cTr = k.ring("cT", [128, 4, 128], BF16, 2)
                    Xr = [k.ring("X0", [128, 4, 192], F32, 2), k.ring("X1", [128, 4, 192], F32, 2)]
                    sqx = k.sb("sqx", [128, 768], F32)
                    s4r = k.ring("s4", [128, 4], F32, 2)
                    tmp = [k.sb("rt", [128, 4, 2, 16], F32) for _ in range(4)]
                    pT = k.psring("apT", 1)
                    pq = k.psring("apq", 2)
                    pk = k.psring("apk", 2)
                    pX = k.psring("apX", 2)
                    for tt in range(NT):
                        tsl = slice(tt * 128, (tt + 1) * 128)
                        Z = Zr.next()
                        k.dma("sp", Z.t[:], self.zt[b, tsl, 1552:2128], [], [Z])
                        ss = ssr.next()
                        k.act(jk.t[:, 0:384], Z.t[:, 0:384], AF.Square, [Z], [jk, ss], accum_out=ss.t[:, 0:1])
                        k.act(jk.t[:, 0:128], Z.t[:, 384:512], AF.Square, [Z], [jk, ss], accum_out=ss.t[:, 1:2])
                        k.act(ss.t[:, 0:1], ss.t[:, 0:1], AF.Sqrt, [ss, self.epsT], [ss], scale=1.0 / 384, bias=self.epsT.t[:, 0:1])
                        k.act(ss.t[:, 1:2], ss.t[:, 1:2], AF.Sqrt, [ss, self.epsT], [ss], scale=1.0 / 128, bias=self.epsT.t[:, 0:1])
                        k.op("dve", lambda e: e.reciprocal(out=ss.t[:], in_=ss.t[:]), [ss], [ss])
                        cn = cnr.next()
                        k.stt(cn.t[:, 0:384], Z.t[:, 0:384], ss.t[:, 0:1], QAW, ALU.mult, ALU.mult, [Z, ss, nv], [cn])
                        k.stt(cn.t[:, 384:512], Z.t[:, 384:512], ss.t[:, 1:2], KVAW, ALU.mult, ALU.mult, [Z, ss, nv], [cn])
                        p = pT.next()
                        for c4 in range(4):
                            k.tr(p.t[:, c4 * 128:(c4 + 1) * 128], cn.t[:, c4 * 128:(c4 + 1) * 128], self.identF, [cn, self.C], [p])
                        cT = cTr.next()
                        k.cp(cT.t[:], p.t[:, 0:512].rearrange("p (c e) -> p c e", c=4), [p], [cT], eng="act")
                        Xq = Xr[0].next()
                        Xk = Xr[1].next()
                        for nb in range(2):
                            p = pq.next()
                            for kc in range(3):
                                k.mm(p.t[:, 0:384], cT.t[:, kc, :], wq.t[:, kc, nb * 384:(nb + 1) * 384], kc == 0, kc == 2, [cT, wq], [p])
                            k.cp(Xq.t[:, 2 * nb:2 * nb + 2, :], p.t[:, 0:384].rearrange("p (h e) -> p h e", h=2), [p], [Xq], eng="act")
                        for nb in range(2):
                            p = pk.next()
                            k.mm(p.t[:, 0:512], cT.t[:, 3, :], wkv.t[:, nb * 512:(nb + 1) * 512], True, True, [cT, wkv], [p])
                            pv4 = p.t[:, 0:512].rearrange("p (h e) -> p h e", h=2)
                            k.cp(Xk.t[:, 2 * nb:2 * nb + 2, 0:128], pv4[:, :, 0:128], [p], [Xk])
                            k.cp(Va.t[:, tt, 2 * nb:2 * nb + 2, 0:128], pv4[:, :, 128:256], [p], [Va], eng="act")
                        for h in range(4):
                            k.cp(Xk.t[:, h, 128:192], Z.t[:, 512:576], [Z], [Xk], eng="pool")
                        for qi, X in enumerate((Xq, Xk)):
                            Xf = X.t[:].rearrange("p h e -> p (h e)")
                            k.tt(sqx.t[:], Xf, Xf, ALU.mult, [X], [sqx])
                            s4 = s4r.next()
                            k.op("dve", lambda e: e.tensor_reduce(out=s4.t[:], in_=sqx.t[:].rearrange("p (h e) -> p h e", h=4), axis=AX.X, op=ALU.add), [sqx], [s4])
                            k.act(s4.t[:], s4.t[:], AF.Sqrt, [s4, self.epsT], [s4], scale=1.0 / 192, bias=self.epsT.t[:, 0:1])
                            k.op("dve", lambda e: e.reciprocal(out=s4.t[:], in_=s4.t[:]), [s4], [s4])
                            for h in range(4):
                                k.stt(X.t[:, h, :], X.t[:, h, :], s4.t[:, h:h + 1], NW[qi], ALU.mult, ALU.mult, [X, s4, nv], [X])
                            rp = X.t[:, :, 128:192].rearrange("p h (a b f) -> p h a b f", a=2, b=2)
                            x1 = rp[:, :, :, 0, :]
                            x2 = rp[:, :, :, 1, :]
                            cosb = rc.t[:, tt, :].rearrange("p (o a f) -> p o a f", o=1, a=2).to_broadcast([128, 4, 2, 16])
                            sinb = rs.t[:, tt, :].rearrange("p (o a f) -> p o a f", o=1, a=2).to_broadcast([128, 4, 2, 16])
                            TT = tmp
                            k.tt(TT[0].t[:], x1, cosb, ALU.mult, [X, rc], [TT[0]])
                            k.tt(TT[1].t[:], x2, sinb, ALU.mult, [X, rs], [TT[1]])
                            k.tt(TT[2].t[:], x2, cosb, ALU.mult, [X, rc], [TT[2]])
                            k.tt(TT[3].t[:], x1, sinb, ALU.mult, [X, rs], [TT[3]])
                            k.tt(x1, TT[0].t[:], TT[1].t[:], ALU.subtract, [TT[0], TT[1], X], [X])
                            k.tt(x2, TT[2].t[:], TT[3].t[:], ALU.add, [TT[2], TT[3], X], [X])
                            dst, dst2 = (QT, QT2) if qi == 0 else (KT, KT2)
                            for hp in range(2):
                                p = pX.next()
                                for hh in range(2):
                                    h = 2 * hp + hh
                                    k.tr(p.t[:, hh * 256:hh * 256 + 128], X.t[:, h, 0:128], self.identF, [X, self.C], [p])
                                    k.tr(p.t[0:64, hh * 256 + 128:hh * 256 + 256], X.t[:, h, 128:192], self.identF, [X, self.C], [p])
                                pv_ = p.t[:, 0:512].rearrange("p (h e) -> p h e", h=2)
                                k.cp(dst.t[:, 2 * hp:2 * hp + 2, tsl], pv_[:, :, 0:128], [p], [dst], eng="act")
                                k.cp(dst2.t[0:64, 2 * hp:2 * hp + 2, tsl], p.t[0:64, 0:512].rearrange("p (h e) -> p h e", h=2)[:, :, 128:256], [p], [dst2])
                if self.dbg.get("mla_noattn"):
                    continue
                with k.scope():
                    Pr = k.ring("P", [128, 512], BF16, 3)
                    MT = k.ring("MT", [128, T], BF16, 2)
                    o32 = k.ring("o32", [128, 128], F32, 2)
                    rdr = k.ring("rd", [128, 1], F32, 2)
                    pS = k.psring("aS", 2)
                    po = [k.ps("apo%d" % i) for i in range(4)]
                    pT = k.psring("aT", 1)
                    blocks = [(0, 256, [0, 1])] + [(CTX + 512 * i, 512, list(range(NT))) for i in range(4)]
                    for h in range(4):
                        mt = MT.next()
                        for (q0, nq, kts) in blocks:
                            nsub = nq // 128
                            for idx, kt in enumerate(kts):
                                ksl = slice(kt * 128, (kt + 1) * 128)
                                ps_ = pS.next()
                                k.mm(ps_.t[:, 0:nq], KT.t[:, h, ksl], QT.t[:, h, q0:q0 + nq], True, False, [KT, QT], [ps_])
                                k.mm(ps_.t[:, 0:nq], KT2.t[0:64, h, ksl], QT2.t[0:64, h, q0:q0 + nq], False, True, [KT2, QT2], [ps_])
                                P = Pr.next()
                                k.act(P.t[:, 0:nq], ps_.t[:, 0:nq], AF.Exp, [ps_], [P], scale=SC)
                                for qs in range(nsub):
                                    k.mm(po[qs].t[:, 0:129], P.t[:, qs * 128:(qs + 1) * 128], Va.t[:, kt, h, :],
                                         idx == 0, idx == len(kts) - 1, [P, Va], [po[qs]])
                            for qs in range(nsub):
                                rd = rdr.next()
                                k.op("dve", lambda e: e.reciprocal(out=rd.t[:], in_=po[qs].t[:, 128:129]), [po[qs]], [rd])
                                o = o32.next()
                                k.ts(o.t[:], po[qs].t[:, 0:128], rd.t[:, 0:1], None, ALU.mult, None, [po[qs], rd], [o])
                                p = pT.next()
                                k.tr(p.t[:, 0:128], o.t[:], self.identF, [o, self.C], [p])
                                k.cp(mt.t[:, q0 + qs * 128:q0 + (qs + 1) * 128], p.t[:, 0:128], [p], [mt], eng="act")
                        k.dma("sp", self.cc[b, 1024 + h * 128:1024 + (h + 1) * 128, :], mt.t[:], [mt], [])


def make_consts():
    c = np.zeros((128, 2048), np.float32)
    i = np.arange(128)
    c[:, 0:128] = np.eye(128)
    c[:, 128:256] = 1.0
    c[:, 256:384] = (i[:, None] <= i[None, :])
    c[:, 384:512] = (i[:, None] >= i[None, :])
    c[:, 512:640] = np.eye(128)
    c[:, 640:768] = np.where(i[:, None] <= i[None, :], 0.0, -30000.0)
    c[:, 768:896] = np.where(i[:, None] >= i[None, :], 0.0, -30000.0)
    return c


def prep_common(inp):
    m = {}
    f = np.float32
    m["ada_w"] = inp["ada_w"]
    m["ada_bT"] = np.ascontiguousarray(inp["ada_b"].reshape(4, 96, 128).transpose(0, 2, 1))
    m["n1w"] = np.ascontiguousarray(inp["norm1_w"].reshape(4, 16, 128).transpose(0, 2, 1))
    m["n2w"] = np.ascontiguousarray(inp["norm2_w"].reshape(4, 16, 128).transpose(0, 2, 1))
    m["w_in"] = inp["w_in"]
    m["w_out"] = inp["w_out"]
    m["mlp_w1"] = inp["mlp_w1"]
    m["mlp_w2"] = inp["mlp_w2"]
    m["consts"] = make_consts()
    prep_mixers(inp, m)
    return m


def prep_core(inp, common, core):
    b0 = 2 * core
    m = dict(common)
    xin = np.concatenate([inp["ctx"][b0:b0 + 2], inp["x"][b0:b0 + 2]], axis=1)
    m["xin"] = np.ascontiguousarray(xin.transpose(0, 2, 1))
    c3 = np.stack([inp["c"][b0], inp["c"][b0 + 1], inp["c_ctx"]], axis=1)
    m["cT"] = np.ascontiguousarray(c3.reshape(16, 128, 3).transpose(1, 0, 2))
    return m


def _dup(a):
    return np.concatenate([a, a], axis=0)


def prep_mixers(inp, m):
    L = DEPTH
    c = m["consts"]
    c[:64, 896] = -1.0
    c[64:, 896] = 1.0
    lre = inp["s5_lam_re"].transpose(0, 3, 1, 2).reshape(L, 64, 64)
    lim = inp["s5_lam_im"].transpose(0, 3, 1, 2).reshape(L, 64, 64)
    ldt = np.broadcast_to(inp["s5_log_dt"].reshape(L, 1, 64), (L, 64, 64))
    s5v = np.concatenate([lre, lim, ldt], axis=2)
    m["s5v"] = np.ascontiguousarray(np.concatenate([s5v, s5v], axis=1))
    bre = inp["s5_b_re"].transpose(0, 3, 1, 2, 4).reshape(L, 64, 64, 16)
    bim = inp["s5_b_im"].transpose(0, 3, 1, 2, 4).reshape(L, 64, 64, 16)
    m["s5A"] = np.ascontiguousarray(np.concatenate([bre, bim], axis=1))
    m["s5B"] = np.ascontiguousarray(np.concatenate([bim, bre], axis=1))
    cre = inp["s5_c_re"].transpose(0, 4, 1, 2, 3).reshape(L, 64, 64, 16)
    cim = inp["s5_c_im"].transpose(0, 4, 1, 2, 3).reshape(L, 64, 64, 16)
    m["s5CA"] = np.ascontiguousarray(np.concatenate([cre, cim], axis=1))
    m["s5CB"] = np.ascontiguousarray(np.concatenate([cim, cre], axis=1))
    dsk = inp["s5_d"].reshape(L, 4, 128).transpose(0, 2, 1)
    glb = inp["s5_glu_b"].reshape(L, 4, 128).transpose(0, 2, 1)
    m["s5w"] = np.ascontiguousarray(np.concatenate([dsk, glb], axis=2))
    m["glu_w"] = inp["s5_glu_w"]
    n0 = np.arange(T, dtype=np.float32)
    n1 = np.concatenate([CTX - 1 - np.arange(CTX), CTX + (LAT - 1 - np.arange(LAT))]).astype(np.float32)
    m["nidx"] = np.ascontiguousarray(np.broadcast_to(np.stack([n0, n1])[:, None, :], (2, 128, T)))
    cw = inp["lru_conv_w"].reshape(L, 4, 4, 128).transpose(0, 3, 2, 1).reshape(L, 128, 16)
    cb = inp["lru_conv_b"].reshape(L, 4, 128).transpose(0, 2, 1)
    def dc(a):
        return a.reshape(L, 2, 4, 128).transpose(0, 3, 1, 2).reshape(L, 128, 8)
    m["lruv"] = np.ascontiguousarray(np.concatenate([cw, cb, dc(inp["lru_ba"]), dc(inp["lru_bx"]), dc(inp["lru_lam"])], axis=2))
    m["lru_wa"] = inp["lru_wa"]
    m["lru_wx"] = inp["lru_wx"]
    gb = np.concatenate([inp["ml_ig_bias"], inp["ml_fg_bias"]], axis=2).reshape(L, 1, 16)
    m["mlb"] = np.ascontiguousarray(np.broadcast_to(gb, (L, 128, 16)))
    m["onw"] = np.ascontiguousarray(np.broadcast_to(inp["ml_out_norm"].reshape(L, 1, 512), (L, 128, 512)))
    selc = np.zeros((16, 16, 128), np.float32)
    for kk in range(16):
        selc[kk, kk, :] = 1.0
    m["selc"] = selc.reshape(16, 2048)
    nv = np.concatenate([inp["mla_q_a_norm"], inp["mla_kv_a_norm"], inp["mla_q_norm"], inp["mla_k_norm"]], axis=1)
    m["mlav"] = np.ascontiguousarray(np.broadcast_to(nv.reshape(L, 1, 896), (L, 128, 896)))
    m["w_q_up"] = inp["mla_w_q_up"]
    m["w_kv_up"] = inp["mla_w_kv_up"]
    q = np.arange(LAT)
    inv = (np.float32(10000.0) ** (-np.arange(16, dtype=np.float32) / np.float32(16))).astype(np.float32)
    ang = np.concatenate([(q // 64).astype(np.float32)[:, None] * inv, (q % 64).astype(np.float32)[:, None] * inv], axis=1)
    cosf = np.ones((T, 32), np.float32)
    sinf = np.zeros((T, 32), np.float32)
    cosf[CTX:] = np.cos(ang.astype(np.float32))
    sinf[CTX:] = np.sin(ang.astype(np.float32))
    m["ropec"] = np.ascontiguousarray(cosf.reshape(NT, 128, 32).transpose(1, 0, 2))
    m["ropes"] = np.ascontiguousarray(sinf.reshape(NT, 128, 32).transpose(1, 0, 2))


_PROG = None


def kernel(**inputs):
    global _PROG
    inp = {k_: np.asarray(v) for k_, v in inputs.items()}
    if _PROG is None:
        _PROG = Prog()
    prog = _PROG
    common = prep_common(inp)
    in_maps = []
    for core in range(8):
        m = prep_core(inp, common, core)
        in_maps.append({n: m[n] for n in prog.inp})
    res = run_bass_kernel_spmd(prog.nc, in_maps, core_ids=list(range(8)))
    outs = [r["yout"] for r in res.results]
    y = np.concatenate(outs, axis=0)
    return np.ascontiguousarray(y.transpose(0, 2, 1)).astype(np.float32)
```

```python
import numpy as np
import ml_dtypes
from contextlib import ExitStack, contextmanager
import concourse.bass as bass
import concourse.mybir as mybir
from concourse.bass_utils import run_bass_kernel_spmd

F32 = mybir.dt.float32
BF16 = mybir.dt.bfloat16
I32 = mybir.dt.int32
AF = mybir.ActivationFunctionType
ALU = mybir.AluOpType
AX = mybir.AxisListType

D = 2048
T = 2304
CTX = 256
LAT = 2048
NT = 18
DEPTH = 4
DFF = 8192
INC = 4176
EPS = 1e-6
TB = [(0, 256), (256, 512), (768, 512), (1280, 512), (1792, 512)]
TWO_PI = float(2 * np.pi)
SAME_SYNC = True
NO_SELF_SYNC = ("pe",)
MERGE_WAIT = True
CAP = 30000


class Res:
    __slots__ = ("w", "r")

    def __init__(self):
        self.w = None
        self.r = {}


class Buf:
    def __init__(self, t, psum=False):
        self.t = t
        self.res = Res()
        self.psum = psum

    def __getitem__(self, key):
        return self.t[key]


class Ring:
    def __init__(self, bufs):
        self.bufs = bufs
        self.i = 0

    def next(self):
        b = self.bufs[self.i]
        self.i = (self.i + 1) % len(self.bufs)
        return b


class KB:
    def __init__(self, nc):
        self.nc = nc
        self.E = {"pe": nc.tensor, "dve": nc.vector, "act": nc.scalar, "pool": nc.gpsimd, "sp": nc.sync}
        self.sem = {}
        self.cnt = {}
        self.owner = {}
        self.nsem = 0
        for e in self.E:
            self._fresh(e)
        self.waited = {e: {} for e in self.E}
        self.dq = {}
        self.dqi = {}
        for q, n in (("sp", 12), ("pool", 4), ("act", 4)):
            self.dq[q] = [[self._newsem("d"), 0] for _ in range(n)]
            self.dqi[q] = 0
        self.stack = []
        self.uid = 0

    def _newsem(self, pfx):
        self.nsem += 1
        return self.nc.alloc_semaphore(f"{pfx}{self.nsem}")

    def _fresh(self, e):
        s = self._newsem("e")
        self.sem[e] = s
        self.cnt[e] = 0
        self.owner[s] = e

    @contextmanager
    def scope(self):
        st = ExitStack()
        self.stack.append(st)
        try:
            yield
        finally:
            self.barrier()
            self.stack.pop()
            st.close()

    def _name(self, n):
        self.uid += 1
        return f"{n}_{self.uid}"

    def sb(self, name, shape, dtype):
        t = self.stack[-1].enter_context(self.nc.sbuf_tensor(self._name(name), list(shape), dtype))
        return Buf(t)

    def ps(self, name, shape=(128, 512), dtype=F32):
        t = self.stack[-1].enter_context(self.nc.psum_tensor(self._name(name), list(shape), dtype))
        return Buf(t, psum=True)

    def ring(self, name, shape, dtype, n):
        return Ring([self.sb(name, shape, dtype) for _ in range(n)])

    def psring(self, name, n, shape=(128, 512), dtype=F32):
        return Ring([self.ps(name, shape, dtype) for _ in range(n)])

    def _need(self, e, tok, out):
        if tok is None:
            return
        sem, val = tok
        own = self.owner.get(sem)
        if own == e and (e in NO_SELF_SYNC or not SAME_SYNC):
            return
        w = self.waited[e]
        if w.get(sem, 0) >= val:
            return
        w[sem] = val
        for i, (s_, v_) in enumerate(out):
            if s_ is sem or s_ == sem:
                out[i] = (sem, max(v_, val))
                return
        out.append((sem, val))

    def _wait(self, e, tok):
        out = []
        self._need(e, tok, out)
        for (s_, v_) in out:
            self.E[e].wait_ge(s_, v_)

    def _deps(self, e, reads, writes):
        out = []
        for r in reads:
            r = r.res if isinstance(r, Buf) else r
            self._need(e, r.w, out)
        for wr in writes:
            wr = wr.res if isinstance(wr, Buf) else wr
            self._need(e, wr.w, out)
            for s_, v_ in wr.r.items():
                self._need(e, (s_, v_), out)
        return out

    def _commit(self, tok, reads, writes):
        sem, val = tok
        for r in reads:
            r = r.res if isinstance(r, Buf) else r
            if r.r.get(sem, 0) < val:
                r.r[sem] = val
        for wr in writes:
            wr = wr.res if isinstance(wr, Buf) else wr
            wr.w = tok
            wr.r = {}

    def op(self, e, fn, reads=(), writes=(), merge=True):
        pr = [r for r in reads if isinstance(r, Buf) and r.psum]
        if pr:
            reads = [r for r in reads if not (isinstance(r, Buf) and r.psum)]
            writes = list(writes) + pr
        need = self._deps(e, reads, writes)
        last = None
        if merge and MERGE_WAIT and need:
            last = need.pop()
        for (s_, v_) in need:
            self.E[e].wait_ge(s_, v_)
        ins = fn(self.E[e])
        if last is not None:
            ins._wait_ge(last[0], last[1])
        self.cnt[e] += 1
        ins.then_inc(self.sem[e], 1)
        tok = (self.sem[e], self.cnt[e])
        self._commit(tok, reads, writes)
        if self.cnt[e] >= CAP:
            self._fresh(e)

    def dma(self, q, out, in_, reads=(), writes=()):
        slots = self.dq[q]
        slot = slots[self.dqi[q]]
        self.dqi[q] = (self.dqi[q] + 1) % len(slots)
        if slot[1] > 0:
            self._wait(q, (slot[0], slot[1]))
        if slot[1] >= CAP:
            slot[0] = self._newsem("d")
            slot[1] = 0
        for (s_, v_) in self._deps(q, reads, writes):
            self.E[q].wait_ge(s_, v_)
        ins = self.E[q].dma_start(out=out, in_=in_)
        slot[1] += 16
        ins.then_inc(slot[0], 16)
        self._commit((slot[0], slot[1]), reads, writes)

    def barrier(self):
        toks = [(self.sem[e], self.cnt[e]) for e in self.E if self.cnt[e] > 0]
        for q in self.dq:
            for slot in self.dq[q]:
                if slot[1] > 0:
                    toks.append((slot[0], slot[1]))
        for e in self.E:
            for tok in toks:
                if self.owner.get(tok[0]) == e:
                    continue
                self._wait(e, tok)

    def mm(self, out, lhsT, rhs, start, stop, reads, writes):
        self.op("pe", lambda e: e.matmul(out, lhsT, rhs, start=start, stop=stop), reads, writes)

    def tr(self, out, in_, ident, reads, writes):
        self.op("pe", lambda e: e.transpose(out, in_, ident), reads, writes)

    def act(self, out, in_, func, reads, writes, bias=0.0, scale=1.0, **kw):
        self.op("act", lambda e: e.activation(out=out, in_=in_, func=func, bias=bias, scale=scale, **kw), reads, writes,
                merge=("accum_out" not in kw))

    def ts(self, out, in0, s1, s2, op0, op1, reads, writes, eng="dve"):
        if s2 is None:
            self.op(eng, lambda e: e.tensor_scalar(out=out, in0=in0, scalar1=s1, scalar2=None, op0=op0), reads, writes)
        else:
            self.op(eng, lambda e: e.tensor_scalar(out=out, in0=in0, scalar1=s1, scalar2=s2, op0=op0, op1=op1), reads, writes)

    def tt(self, out, in0, in1, op, reads, writes, eng="dve"):
        self.op(eng, lambda e: e.tensor_tensor(out=out, in0=in0, in1=in1, op=op), reads, writes)

    def stt(self, out, in0, scalar, in1, op0, op1, reads, writes):
        self.op("dve", lambda e: e.scalar_tensor_tensor(out=out, in0=in0, scalar=scalar, in1=in1, op0=op0, op1=op1), reads, writes)

    def cp(self, out, in_, reads, writes, eng="dve"):
        if eng == "act":
            self.op("act", lambda e: e.copy(out=out, in_=in_), reads, writes)
        else:
            self.op(eng, lambda e: e.tensor_copy(out=out, in_=in_), reads, writes)

    def scan(self, out, d0, d1, init, reads, writes):
        self.op("dve", lambda e: e.tensor_tensor_scan(out=out, data0=d0, data1=d1, initial=init, op0=ALU.mult, op1=ALU.add), reads, writes)


class Prog:
    def __init__(self, n_layers=DEPTH, dbg=None, layers=None):
        self.dbg = dbg or {}
        self.layers = list(range(n_layers)) if layers is None else layers
        nc = bass.Bass("TRN2", target_bir_lowering=False)
        self.nc = nc
        self.k = KB(nc)
        self.inp = {}
        self.build()

    def din(self, name, shape, dtype=F32):
        t = self.nc.dram_tensor(name, list(shape), dtype, kind="ExternalInput")
        self.inp[name] = (tuple(shape), dtype)
        return t.ap()

    def dscr(self, name, shape, dtype=F32):
        kind = "ExternalOutput" if name in self.dbg else "Internal"
        return self.nc.dram_tensor(name, list(shape), dtype, kind=kind).ap()

    def build(self):
        nc, k = self.nc, self.k
        L = DEPTH
        self.xin = self.din("xin", [2, D, T])
        self.cT = self.din("cT", [128, 16, 3])
        self.ada_w = self.din("ada_w", [L, D, 6 * D])
        self.ada_bT = self.din("ada_bT", [L, 128, 96])
        self.n1w = self.din("n1w", [L, 128, 16])
        self.n2w = self.din("n2w", [L, 128, 16])
        self.w_in = self.din("w_in", [L, D, INC])
        self.w_out = self.din("w_out", [L, D, D])
        self.mlp_w1 = self.din("mlp_w1", [L, D, DFF])
        self.mlp_w2 = self.din("mlp_w2", [L, DFF, D])
        self.consts = self.din("consts", [128, 2048])
        self.declare_mixer_inputs()
        self.yout = nc.dram_tensor("yout", [2, D, LAT], F32, kind="ExternalOutput").ap()
        self.xs = self.dscr("xs", [2, D, T])
        self.zf = self.dscr("zf", [2, 2560, T])
        self.zt = self.dscr("zt", [2, T, 2128])
        self.W1t = self.dscr("W1t", [64, 128, 16, 128], BF16)
        self.W2t = self.dscr("W2t", [4, 16, 128, 16, 128], BF16)
        if self.dbg.get("cc_in"):
            self.cc = self.din("cc", [2, D, T], BF16)
        else:
            self.cc = self.dscr("cc", [2, D, T], BF16)
        if "modv_o" in self.dbg:
            self.modv_o = self.dscr("modv_o", [128, 288])

        with k.scope():
            self.setup_consts()
            for b in range(2):
                for c in range(16):
                    k.dma("sp", self.xs[b, c * 128:(c + 1) * 128, :], self.xin[b, c * 128:(c + 1) * 128, :])
            k.barrier()
            for l in self.layers:
                self.layer(l)
            for b in range(2):
                for c in range(16):
                    k.dma("sp", self.yout[b, c * 128:(c + 1) * 128, :], self.xs[b, c * 128:(c + 1) * 128, CTX:T])

    def setup_consts(self):
        k = self.k
        self.C = k.sb("consts", [128, 2048], F32)
        k.dma("sp", self.C.t[:], self.consts[:, :], [], [self.C])
        self.identF = self.C.t[:, 0:128]
        self.onesF = self.C.t[:, 128:256]
        self.identB = k.sb("identB", [128, 128], BF16)
        k.cp(self.identB.t[:], self.identF, [self.C], [self.identB])
        self.cs = k.sb("cs", [128, 16, 3], F32)
        k.dma("sp", self.cs.t[:], self.cT[:, :, :], [], [self.cs])
        k.act(self.cs.t[:], self.cs.t[:], AF.Silu, [self.cs], [self.cs])
        self.modv = k.sb("modv", [128, 96, 3], F32)
        self.g1 = k.sb("g1", [128, 16, 3], F32)
        self.g2 = k.sb("g2", [128, 16, 3], F32)
        self.epsT = k.sb("epsT", [128, 1], F32)
        k.op("dve", lambda e: e.memset(self.epsT.t[:], EPS), [], [self.epsT])
        self.setup_mixer_consts()

    def layer(self, l):
        k = self.k
        self.wcast_done = False
        self.stage_mod(l)
        for b in range(2):
            self.stage_A(l, b)
        self.mixers(l)
        for b in range(2):
            self.stage_proj_res(l, b, which="out")
        if not self.wcast_done:
            self.stage_wcast(l)
        for b in range(2):
            self.stage_mlp(l, b)

    def stage_mod(self, l):
        k = self.k
        with k.scope():
            wr = k.ring("adaw", [128, 16, 128], F32, 3)
            pm = k.ps("pmod")
            adab = k.sb("adab", [128, 96], F32)
            nw1 = k.sb("nw1", [128, 16], F32)
            nw2 = k.sb("nw2", [128, 16], F32)
            k.dma("sp", adab.t[:], self.ada_bT[l], [], [adab])
            k.dma("sp", nw1.t[:], self.n1w[l], [], [nw1])
            k.dma("sp", nw2.t[:], self.n2w[l], [], [nw2])
            wv = self.ada_w[l].rearrange("(kc p) f -> p kc f", p=128)
            for j in range(96):
                w = wr.next()
                k.dma("sp", w.t[:], wv[:, :, j * 128:(j + 1) * 128], [], [w])
                for kc in range(16):
                    k.mm(pm.t[:, 3 * j:3 * j + 3], w.t[:, kc, :], self.cs.t[:, kc, :], kc == 0, kc == 15,
                         [w, self.cs], [pm])
            pv = pm.t[:, 0:288].rearrange("p (j r) -> p j r", r=3)
            for r in range(3):
                k.tt(self.modv.t[:, :, r], pv[:, :, r], adab.t[:], ALU.add, [pm, adab], [self.modv])
            if "modv_o" in self.dbg:
                k.dma("sp", self.modv_o[:, :], self.modv.t[:].rearrange("p j r -> p (j r)"), [self.modv], [])
            for r in range(3):
                k.stt(self.g1.t[:, :, r], self.modv.t[:, 16:32, r], 1.0, nw1.t[:], ALU.add, ALU.mult,
                      [self.modv, nw1], [self.g1])
                k.stt(self.g2.t[:, :, r], self.modv.t[:, 64:80, r], 1.0, nw2.t[:], ALU.add, ALU.mult,
                      [self.modv, nw2], [self.g2])

    def make_hT(self, hT, b, g, shift_base, blocks=TB, rel=False, nx=2, base=None, nmax=512):
        k = self.k
        with k.scope():
            xr = k.ring("xblk", [128, 16, nmax], F32, nx)
            sqr = k.ring("sq", [128, nmax], F32, 3)
            rsr = k.ring("rstd", [128, nmax], F32, 2)
            tmr = k.ring("tmp", [128, nmax], F32, 3)
            pss = k.psring("ss", 2)
            xv = self.xs[b].rearrange("(c p) t -> p c t", p=128)
            for (t0, n) in blocks:
                r = 2 if t0 < CTX else b
                o0 = (t0 - base) if base is not None else (0 if rel else t0)
                xb = xr.next()
                for c in range(16):
                    k.dma("sp", xb.t[:, c, 0:n], xv[:, c, t0:t0 + n], [], [xb])
                ss = pss.next()
                for c in range(16):
                    sq = sqr.next()
                    k.act(sq.t[:, 0:n], xb.t[:, c, 0:n], AF.Square, [xb], [sq])
                    k.mm(ss.t[:, 0:n], self.onesF, sq.t[:, 0:n], c == 0, c == 15, [sq, self.C], [ss])
                rs = rsr.next()
                k.act(rs.t[:, 0:n], ss.t[:, 0:n], AF.Sqrt, [ss, self.epsT], [rs], bias=self.epsT.t[:, 0:1], scale=1.0 / D)
                k.op("dve", lambda e: e.reciprocal(out=rs.t[:, 0:n], in_=rs.t[:, 0:n]), [rs], [rs])
                for c in range(16):
                    tm = tmr.next()
                    k.stt(tm.t[:, 0:n], xb.t[:, c, 0:n], g.t[:, c, r:r + 1], rs.t[:, 0:n], ALU.mult, ALU.mult,
                          [xb, g, rs], [tm])
                    k.act(hT.t[:, c, o0:o0 + n], tm.t[:, 0:n], AF.Identity, [tm, self.modv], [hT],
                          bias=self.modv.t[:, shift_base + c, r:r + 1], scale=1.0)

    FM_CHUNKS = [0, 128, 256, 384, 512, 640, 768, 896, 1024, 1152, 1280, 1408,
                 3152, 3280, 3408, 3536, 3664, 3792, 3920, 4048]
    TM_BLOCKS = [(1024, 512), (1536, 512), (2048, 512), (2560, 512), (3072, 80)]

    def stage_A(self, l, b):
        k = self.k
        with k.scope():
            hT = k.sb("hT", [128, 16, T], BF16)
            self.make_hT(hT, b, self.g1, 0)
            wv = self.w_in[l].rearrange("(kc p) c -> p kc c", p=128)
            with k.scope():
                wf = k.ring("wf", [128, 16, 128], F32, 2)
                wb = k.ring("wb", [128, 16, 128], BF16, 2)
                ob = k.ring("ob", [128, 512], F32, 3)
                pp = k.psring("pp", 3)
                ei = 0
                for ci, c0 in enumerate(self.FM_CHUNKS):
                    w32 = wf.next()
                    k.dma("sp", w32.t[:], wv[:, :, c0:c0 + 128], [], [w32])
                    w16 = wb.next()
                    k.cp(w16.t[:], w32.t[:], [w32], [w16], eng="pool")
                    for (t0, n) in TB:
                        p = pp.next()
                        for kc in range(16):
                            k.mm(p.t[:, 0:n], w16.t[:, kc, :], hT.t[:, kc, t0:t0 + n], kc == 0, kc == 15, [w16, hT], [p])
                        o = ob.next()
                        k.cp(o.t[:, 0:n], p.t[:, 0:n], [p], [o], eng=("act" if ei % 2 else "dve"))
                        ei += 1
                        k.dma("sp", self.zf[b, ci * 128:(ci + 1) * 128, t0:t0 + n], o.t[:, 0:n], [o], [])
            with k.scope():
                wf = k.ring("wf2", [128, 8, 512], F32, 2)
                wb = k.ring("wb2", [128, 16, 512], BF16, 2)
                ob = k.ring("ob2", [128, 512], F32, 3)
                pp = k.psring("pp2", 3)
                ei = 0
                for (c0, w) in self.TM_BLOCKS:
                    w16 = wb.next()
                    for hf in range(2):
                        w32 = wf.next()
                        k.dma("sp", w32.t[:, :, 0:w], wv[:, hf * 8:(hf + 1) * 8, c0:c0 + w], [], [w32])
                        k.cp(w16.t[:, hf * 8:(hf + 1) * 8, 0:w], w32.t[:, :, 0:w], [w32], [w16], eng="pool")
                    for tt in range(NT):
                        p = pp.next()
                        for kc in range(16):
                            k.mm(p.t[:, 0:w], hT.t[:, kc, tt * 128:(tt + 1) * 128], w16.t[:, kc, 0:w], kc == 0, kc == 15,
                                 [w16, hT], [p])
                        o = ob.next()
                        k.cp(o.t[:, 0:w], p.t[:, 0:w], [p], [o], eng=("act" if ei % 2 else "dve"))
                        ei += 1
                        k.dma("sp", self.zt[b, tt * 128:(tt + 1) * 128, c0 - 1024:c0 - 1024 + w], o.t[:, 0:w], [o], [])

    def stage_proj_res(self, l, b, which):
        k = self.k
        with k.scope():
            cT = k.sb("ccT", [128, 16, T], BF16)
            cv = self.cc[b].rearrange("(c p) t -> p c t", p=128)
            for c in range(16):
                k.dma("sp", cT.t[:, c, :], cv[:, c, :], [], [cT])
            wv = self.w_out[l].rearrange("(kc p) c -> p kc c", p=128)
            xv = self.xs[b].rearrange("(c p) t -> p c t", p=128)
            wf = k.ring("wf", [128, 16, 128], F32, 2)
            wb = k.ring("wb", [128, 16, 128], BF16, 2)
            xr = k.ring("xo", [128, 512], F32, 3)
            pp = k.psring("pp", 3)
            for fc in range(16):
                w32 = wf.next()
                k.dma("sp", w32.t[:], wv[:, :, fc * 128:(fc + 1) * 128], [], [w32])
                w16 = wb.next()
                k.cp(w16.t[:], w32.t[:], [w32], [w16], eng="pool")
                for (t0, n) in TB:
                    r = 2 if t0 < CTX else b
                    xo = xr.next()
                    k.dma("sp", xo.t[:, 0:n], xv[:, fc, t0:t0 + n], [], [xo])
                    p = pp.next()
                    for kc in range(16):
                        k.mm(p.t[:, 0:n], w16.t[:, kc, :], cT.t[:, kc, t0:t0 + n], kc == 0, kc == 15, [w16, cT], [p])
                    k.stt(xo.t[:, 0:n], p.t[:, 0:n], self.modv.t[:, 32 + fc, r:r + 1], xo.t[:, 0:n], ALU.mult, ALU.add,
                          [p, self.modv, xo], [xo])
                    k.dma("sp", xv[:, fc, t0:t0 + n], xo.t[:, 0:n], [xo], [])

    MLP_BLOCKS = [[(0, 256), (256, 256), (512, 256)], [(768, 256), (1024, 256), (1280, 256)],
                  [(1536, 256), (1792, 256), (2048, 256)]]

    def stage_wcast(self, l):
        k = self.k
        with k.scope():
            f32r = k.ring("wc32", [128, 8192], F32, 2)
            b16r = k.ring("wc16", [128, 8192], BF16, 2)
            engs = ["dve", "act", "pool"]
            ei = 0
            w1tv = self.W1t.rearrange("fc p kc j -> p fc kc j")
            for kc in range(16):
                a = f32r.next()
                k.dma("sp", a.t[:], self.mlp_w1[l, kc * 128:(kc + 1) * 128, :], [], [a])
                bb = b16r.next()
                for q in range(4):
                    k.cp(bb.t[:, q * 2048:(q + 1) * 2048], a.t[:, q * 2048:(q + 1) * 2048], [a], [bb], eng=engs[ei % 3])
                    ei += 1
                k.dma("sp", w1tv[:, :, kc, :], bb.t[:].rearrange("p (fc j) -> p fc j", j=128), [bb], [])
            w2v = self.mlp_w2[l].rearrange("(fg fc p) d -> fg fc p d", fc=16, p=128)
            w2tv = self.W2t.rearrange("fg dc p fc j -> fg fc p dc j")
            for fg in range(4):
                for f4 in range(4):
                    a = f32r.next()
                    for f in range(4):
                        k.dma("sp", a.t[:, f * 2048:(f + 1) * 2048], w2v[fg, f4 * 4 + f], [], [a])
                    bb = b16r.next()
                    for q in range(4):
                        k.cp(bb.t[:, q * 2048:(q + 1) * 2048], a.t[:, q * 2048:(q + 1) * 2048], [a], [bb], eng=engs[ei % 3])
                        ei += 1
                    for f in range(4):
                        k.dma("sp", w2tv[fg, f4 * 4 + f], bb.t[:, f * 2048:(f + 1) * 2048].rearrange("p (dc j) -> p dc j", j=128), [bb], [])

    def wcast_gen(self, l, f32r, b16r):
        k = self.k
        w1tv = self.W1t.rearrange("fc p kc j -> p fc kc j")
        w2v = self.mlp_w2[l].rearrange("(fg fc p) d -> fg fc p d", fc=16, p=128)
        w2tv = self.W2t.rearrange("fg dc p fc j -> fg fc p dc j")
        pieces = []
        for kc in range(16):
            for q in range(4):
                pieces.append((self.mlp_w1[l, kc * 128:(kc + 1) * 128, q * 2048:(q + 1) * 2048], w1tv[:, q * 16:(q + 1) * 16, kc, :]))
        for fg in range(4):
            for fc in range(16):
                pieces.append((w2v[fg, fc], w2tv[fg, fc]))
        loaded = {}

        def load(i):
            a = f32r.next()
            k.dma("sp", a.t[:], pieces[i][0], [], [a])
            loaded[i] = a

        load(0)
        for i in range(len(pieces)):
            if i + 1 < len(pieces):
                load(i + 1)
            a = loaded.pop(i)
            bb = b16r.next()
            k.cp(bb.t[:], a.t[:], [a], [bb], eng="act")
            k.dma("sp", pieces[i][1], bb.t[:].rearrange("p (c j) -> p c j", j=128), [bb], [])
            yield

    def stage_mlp(self, l, b):
        k = self.k
        with k.scope():
            hT = k.sb("hT2", [128, 16, 768], BF16)
            oacc = k.sb("oacc", [128, 16, 768], F32)
            aTr = k.ring("aT", [128, 16, 768], BF16, 2)
            w1r = k.ring("w1s", [128, 16, 128], BF16, 3)
            w2r = k.ring("w2s", [128, 16, 128], BF16, 3)
            rl = k.ring("rl", [128, 512], F32, 4)
            xr = k.ring("xo", [128, 512], F32, 3)
            pp = k.psring("pp", 3)
            pq = k.psring("pq", 2)
            xv = self.xs[b].rearrange("(c p) t -> p c t", p=128)
            MM = ((0, 512), (512, 256))
            for subs in self.MLP_BLOCKS:
                base = subs[0][0]
                self.make_hT(hT, b, self.g2, 48, subs, nx=1, base=base, nmax=256)
                for fg in range(4):
                    aT = aTr.next()
                    for fc in range(16):
                        w = w1r.next()
                        k.dma("sp", w.t[:], self.W1t[fg * 16 + fc], [], [w])
                        for (o, n) in MM:
                            p = pp.next()
                            for kc in range(16):
                                k.mm(p.t[:, 0:n], w.t[:, kc, :], hT.t[:, kc, o:o + n], kc == 0, kc == 15, [w, hT], [p])
                            rr = rl.next()
                            k.act(rr.t[:, 0:n], p.t[:, 0:n], AF.Relu, [p], [rr])
                            k.tt(aT.t[:, fc, o:o + n], rr.t[:, 0:n], rr.t[:, 0:n], ALU.mult, [rr], [aT])
                    for dc in range(16):
                        w = w2r.next()
                        k.dma("sp", w.t[:], self.W2t[fg, dc], [], [w])
                        for (o, n) in MM:
                            p = pq.next()
                            for fc in range(16):
                                k.mm(p.t[:, 0:n], w.t[:, fc, :], aT.t[:, fc, o:o + n], fc == 0, fc == 15, [w, aT], [p])
                            if fg == 0:
                                k.cp(oacc.t[:, dc, o:o + n], p.t[:, 0:n], [p], [oacc], eng="act")
                            elif fg < 3:
                                k.tt(oacc.t[:, dc, o:o + n], p.t[:, 0:n], oacc.t[:, dc, o:o + n], ALU.add, [p, oacc], [oacc])
                            else:
                                tm = rl.next()
                                k.tt(tm.t[:, 0:n], p.t[:, 0:n], oacc.t[:, dc, o:o + n], ALU.add, [p, oacc], [tm])
                                xo = xr.next()
                                k.dma("sp", xo.t[:, 0:n], xv[:, dc, base + o:base + o + n], [], [xo])
                                a0 = base + o
                                cuts = [a0] + ([CTX] if a0 < CTX < a0 + n else []) + [a0 + n]
                                for ci in range(len(cuts) - 1):
                                    c0, c1 = cuts[ci] - a0, cuts[ci + 1] - a0
                                    r = 2 if cuts[ci] < CTX else b
                                    k.stt(xo.t[:, c0:c1], tm.t[:, c0:c1], self.modv.t[:, 80 + dc, r:r + 1], xo.t[:, c0:c1],
                                          ALU.mult, ALU.add, [tm, self.modv, xo], [xo])
                                k.dma("sp", xv[:, dc, base + o:base + o + n], xo.t[:, 0:n], [xo], [])

    def declare_mixer_inputs(self):
        L = DEPTH
        self.s5v = self.din("s5v", [L, 128, 192])
        self.s5A = self.din("s5A", [L, 128, 64, 16])
        self.s5B = self.din("s5B", [L, 128, 64, 16])
        self.s5CA = self.din("s5CA", [L, 128, 64, 16])
        self.s5CB = self.din("s5CB", [L, 128, 64, 16])
        self.s5w = self.din("s5w", [L, 128, 8])
        self.glu_w = self.din("glu_w", [L, 512, 512])
        self.nidx8 = self.din("nidx8", [2, 128, T // 8])
        self.selu = self.din("selu", [128, 64, 128], BF16)
        self.selt = self.din("selt", [128, 64, 128], BF16)
        self.bmask = self.din("bmask", [128, 256])
        self.lruv = self.din("lruv", [L, 128, 44])
        self.lru_wa = self.din("lru_wa", [L, 2, 4, 128, 128])
        self.lru_wx = self.din("lru_wx", [L, 2, 4, 128, 128])
        self.ygd = self.dscr("ygd", [2, 512, T])
        self.mlb = self.din("mlb", [L, 128, 16])
        self.onw = self.din("onw", [L, 128, 512])
        self.selc = self.din("selc", [16, 2048])
        self.mlav = self.din("mlav", [L, 128, 896])
        self.w_q_up = self.din("w_q_up", [L, 384, 768])
        self.w_kv_up = self.din("w_kv_up", [L, 128, 1024])
        self.ropec = self.din("ropec", [128, NT, 32])
        self.ropes = self.din("ropes", [128, NT, 32])

    def setup_mixer_consts(self):
        k = self.k
        self.sgn = self.C.t[:, 896:897]
        self.oneT = k.sb("oneT", [128, 1], F32)
        k.op("dve", lambda e: e.memset(self.oneT.t[:], 1.0), [], [self.oneT])
        self.hpiT = k.sb("hpiT", [128, 1], F32)
        k.op("dve", lambda e: e.memset(self.hpiT.t[:], float(np.pi / 2)), [], [self.hpiT])

    def mixers(self, l):
        which = self.dbg.get("mixers", ("s5", "lru", "mlstm", "mla"))
        if "s5" in which:
            self.mixer_s5(l)
        if "lru" in which:
            self.mixer_lru(l)
        if "mlstm" in which:
            self.mixer_mlstm(l)
        if "mla" in which:
            self.mixer_mla(l)

    def frac_centered(self, out, u, ki, tmp, n, bufs):
        k = self.k
        k.cp(ki, u, bufs, bufs)
        k.tt(tmp, u, ki, ALU.subtract, bufs, bufs)
        k.stt(out, tmp, 0.5, tmp, ALU.is_gt, ALU.subtract, bufs, bufs)
        k.stt(out, out, 0.5, out, ALU.is_gt, ALU.subtract, bufs, bufs)

    def sincos(self, sin_out, cos_out, r, tmp, bufs):
        k = self.k
        k.act(sin_out, r, AF.Sin, bufs, bufs, scale=TWO_PI)
        k.stt(tmp, r, 0.25, r, ALU.is_gt, ALU.subtract, bufs, bufs)
        k.act(cos_out, tmp, AF.Sin, bufs + [self.hpiT], bufs, scale=-TWO_PI, bias=self.hpiT.t[:, 0:1])

    def gelu_tanh(self, out, y, t1, s1, bufs):
        k = self.k
        k.tt(t1, y, y, ALU.mult, bufs, bufs)
        k.ts(t1, t1, 0.044715, 1.0, ALU.mult, ALU.add, bufs, bufs)
        k.tt(t1, t1, y, ALU.mult, bufs, bufs)
        k.act(s1, t1, AF.Sigmoid, bufs, bufs, scale=1.5957691216057308)
        k.tt(out, y, s1, ALU.mult, bufs, bufs)

    def mixer_s5(self, l):
        k = self.k
        NK = T // 8
        KC = CTX // 8
        with k.scope():
            pv = k.sb("s5pv", [128, 192], F32)
            k.dma("sp", pv.t[:], self.s5v[l], [], [pv])
            A = k.sb("s5A", [128, 64, 16], F32)
            Bm = k.sb("s5B", [128, 64, 16], F32)
            CA = k.sb("s5CA", [128, 64, 16], F32)
            CB = k.sb("s5CB", [128, 64, 16], F32)
            k.dma("sp", A.t[:], self.s5A[l], [], [A])
            k.dma("sp", Bm.t[:], self.s5B[l], [], [Bm])
            k.dma("sp", CA.t[:], self.s5CA[l], [], [CA])
            k.dma("sp", CB.t[:], self.s5CB[l], [], [CB])
            sw = k.sb("s5w", [128, 8], F32)
            k.dma("sp", sw.t[:], self.s5w[l], [], [sw])
            nid = k.sb("nidx8", [128, 2, NK], F32)
            for d in range(2):
                k.dma("sp", nid.t[:, d, :], self.nidx8[d], [], [nid])
            selu = k.sb("selu", [128, 64, 128], BF16)
            selt = k.sb("selt", [128, 64, 128], BF16)
            k.dma("sp", selu.t[:], self.selu[:, :, :], [], [selu])
            k.dma("sp", selt.t[:], self.selt[:, :, :], [], [selt])
            bmask = k.sb("bmask", [128, 256], F32)
            k.dma("sp", bmask.t[:], self.bmask[:, :], [], [bmask])
            W = k.sb("s5work", [128, 16, 64], F32)
            WI = k.sb("s5worki", [128, 64], I32)
            Wb = [W]
            lr, li, dt, mag, ang, fT, sn, cs_, t0_, t1_, fr, fi, den, lrdt, f8, ar1 = [W.t[:, i, :] for i in range(16)]
            k.ts(lr, pv.t[:, 0:64], -1e-4, None, ALU.min, None, [pv], Wb)
            k.cp(li, pv.t[:, 64:128], [pv], Wb)
            k.act(dt, pv.t[:, 128:192], AF.Exp, [pv], Wb)
            k.tt(lrdt, lr, dt, ALU.mult, Wb, Wb)
            k.act(mag, lrdt, AF.Exp, Wb, Wb)
            k.tt(ang, li, dt, ALU.mult, Wb, Wb)
            k.ts(t0_, ang, 1.0 / TWO_PI, None, ALU.mult, None, Wb, Wb)
            self.frac_centered(fT, t0_, WI.t[:], t1_, 64, Wb + [WI])
            self.sincos(sn, cs_, fT, t1_, Wb)
            k.tt(t0_, mag, cs_, ALU.mult, Wb, Wb)
            k.ts(ar1, t0_, -1.0, None, ALU.add, None, Wb, Wb)
            k.tt(t1_, mag, sn, ALU.mult, Wb, Wb)
            k.tt(den, lr, lr, ALU.mult, Wb, Wb)
            k.tt(t0_, li, li, ALU.mult, Wb, Wb)
            k.tt(den, den, t0_, ALU.add, Wb, Wb)
            k.op("dve", lambda e: e.reciprocal(out=den, in_=den), Wb, Wb)
            k.tt(fr, ar1, lr, ALU.mult, Wb, Wb)
            k.tt(t0_, t1_, li, ALU.mult, Wb, Wb)
            k.tt(fr, fr, t0_, ALU.add, Wb, Wb)
            k.tt(fr, fr, den, ALU.mult, Wb, Wb)
            k.tt(fi, t1_, lr, ALU.mult, Wb, Wb)
            k.tt(t0_, ar1, li, ALU.mult, Wb, Wb)
            k.tt(fi, fi, t0_, ALU.subtract, Wb, Wb)
            k.tt(fi, fi, den, ALU.mult, Wb, Wb)
            nsgn = k.sb("nsgn", [128, 1], F32)
            k.ts(nsgn.t[:], self.sgn, -1.0, None, ALU.mult, None, [self.C], [nsgn])
            M8 = k.sb("mag8", [128, 64], F32)
            k.act(M8.t[:], lrdt, AF.Exp, Wb, [M8], scale=8.0)
            k.ts(t0_, fT, 8.0, None, ALU.mult, None, Wb, Wb)
            self.frac_centered(f8, t0_, WI.t[:], t1_, 64, Wb + [WI])
            PR = k.sb("PR", [128, 16, 64], F32)
            PI = k.sb("PI", [128, 16, 64], F32)
            PRN = k.sb("PRN", [128, 16, 64], F32)
            PRM = k.sb("PRM", [128, 16, 64], F32)
            PIS = k.sb("PIS", [128, 16, 64], F32)
            PIM = k.sb("PIM", [128, 16, 64], F32)
            GR = k.sb("GR", [128, 16, 64], F32)
            GI = k.sb("GI", [128, 16, 64], F32)
            GRN = k.sb("GRN", [128, 16, 64], F32)
            GIS = k.sb("GIS", [128, 16, 64], F32)
            GIM = k.sb("GIM", [128, 16, 64], F32)
            PWs = [PR, PI, PRN, PRM, PIS, PIM, GR, GI, GRN, GIS, GIM]
            for m in range(-7, 9):
                mi = m + 7
                k.act(t0_, lrdt, AF.Exp, Wb, Wb, scale=float(m))
                k.ts(ang, fT, float(m), None, ALU.mult, None, Wb, Wb)
                self.frac_centered(den, ang, WI.t[:], t1_, 64, Wb + [WI])
                self.sincos(sn, cs_, den, t1_, Wb)
                k.tt(PR.t[:, mi, :], t0_, cs_, ALU.mult, Wb, [PR])
                k.tt(PI.t[:, mi, :], t0_, sn, ALU.mult, Wb, [PI])
                k.ts(PRN.t[:, mi, :], PR.t[:, mi, :], nsgn.t[:, 0:1], None, ALU.mult, None, [PR, nsgn], [PRN])
                k.ts(PRM.t[:, mi, :], PR.t[:, mi, :], -1.0, None, ALU.mult, None, [PR], [PRM])
                k.ts(PIS.t[:, mi, :], PI.t[:, mi, :], self.sgn, None, ALU.mult, None, [PI, self.C], [PIS])
                k.ts(PIM.t[:, mi, :], PI.t[:, mi, :], -1.0, None, ALU.mult, None, [PI], [PIM])
                k.tt(t0_, PR.t[:, mi, :], fr, ALU.mult, [PR] + Wb, Wb)
                k.tt(t1_, PI.t[:, mi, :], fi, ALU.mult, [PI] + Wb, Wb)
                k.tt(GR.t[:, mi, :], t0_, t1_, ALU.subtract, Wb, [GR])
                k.tt(t0_, PR.t[:, mi, :], fi, ALU.mult, [PR] + Wb, Wb)
                k.tt(t1_, PI.t[:, mi, :], fr, ALU.mult, [PI] + Wb, Wb)
                k.tt(GI.t[:, mi, :], t0_, t1_, ALU.add, Wb, [GI])
                k.ts(GRN.t[:, mi, :], GR.t[:, mi, :], nsgn.t[:, 0:1], None, ALU.mult, None, [GR, nsgn], [GRN])
                k.ts(GIS.t[:, mi, :], GI.t[:, mi, :], self.sgn, None, ALU.mult, None, [GI, self.C], [GIS])
                k.ts(GIM.t[:, mi, :], GI.t[:, mi, :], -1.0, None, ALU.mult, None, [GI], [GIM])

            LB = k.sb("LB", [128, 128], F32)
            RC = k.sb("RC", [128, 128], F32)
            LST = k.sb("LST", [128, 128], F32)
            LS2T = k.sb("LS2T", [128, 128], F32)
            W1f = k.sb("W1f", [128, 128], F32)
            W2f = k.sb("W2f", [128, 128], F32)
            tA = k.ring("tA", [128, 16], F32, 6)
            Mi_r = k.ring("Mi", [128, 128], BF16, 2)
            LS_r = k.ring("LS", [128, 128], BF16, 2)
            LS2_r = k.ring("LS2", [128, 128], BF16, 2)
            W1_r = k.ring("W1", [128, 128], BF16, 2)
            W2_r = k.ring("W2", [128, 128], BF16, 2)
            C8 = k.sb("C8", [128, NK], F32)
            S8 = k.sb("S8", [128, NK], F32)
            U8 = k.sb("U8", [128, NK], F32)
            K8 = k.sb("K8", [128, NK], I32)
            R8 = k.sb("R8", [128, NK], F32)
            Ug = [k.sb("Ug", [128, NK], BF16) for _ in range(2)]
            t1r = k.ring("t1", [128, NK], F32, 2)
            btr = k.ring("bt", [128, NK], F32, 2)
            Gr_ = k.ring("G", [128, NK], F32, 2)
            V1r = k.ring("V1", [128, NK], BF16, 2)
            V2r = k.ring("V2", [128, NK], BF16, 2)
            Ysb = [[k.sb("Ysb", [128, NK], BF16) for _ in range(2)] for _ in range(8)]
            ub = [k.sb("ub", [128, T], BF16) for _ in range(2)]
            yacc = [k.sb("yacc", [128, T], F32) for _ in range(2)]
            fin = k.ring("fin", [128, 512], F32, 6)
            wc32 = k.ring("wc32", [128, 2048], F32, 2)
            wc16 = k.ring("wc16", [128, 2048], BF16, 2)
            wgen = self.wcast_gen(l, wc32, wc16)
            pY = [k.ps("pY0"), k.ps("pY1")]
            pP = k.psring("pP", 2)
            pW = k.psring("pW", 2)
            pUn = k.psring("pUn", 2)
            ei = 0

            def build(dst, coefA, coefB, srcA, srcB, mlist, dg):
                for i, m in enumerate(mlist):
                    mi = m + 7
                    ta = tA.next()
                    k.act(ta.t[:], srcA.t[:, dg, :], AF.Identity, [srcA, coefA], [ta], scale=coefA.t[:, mi, dg:dg + 1])
                    k.stt(dst.t[:, 16 * i:16 * i + 16], srcB.t[:, dg, :], coefB.t[:, mi, dg:dg + 1], ta.t[:], ALU.mult, ALU.add,
                          [srcB, coefB, ta], [dst])

            for c in range(4):
                for b in range(2):
                    for (t0, n) in TB:
                        u_ = fin.next()
                        k.dma("sp", u_.t[:, 0:n], self.zf[b, c * 128:(c + 1) * 128, t0:t0 + n], [], [u_])
                        k.cp(ub[b].t[:, t0:t0 + n], u_.t[:, 0:n], [u_], [ub[b]], eng="act")
                for j in range(8):
                    g = 8 * c + j
                    for b in range(2):
                        p = pW.next()
                        for s_ in range(8):
                            k.mm(p.t[:, 0:NK], selu.t[:, j * 8 + s_, :], ub[b].t[:, s_:T:8], s_ == 0, s_ == 7, [selu, ub[b]], [p])
                        k.cp(Ug[b].t[:], p.t[:, 0:NK], [p], [Ug[b]], eng="act")
                    for d in range(2):
                        dg = d * 32 + g
                        if d == 0:
                            mLB = [-s_ for s_ in range(8)]
                            mLS = [7 - s_ for s_ in range(8)]
                            mRC = [t_ for t_ in range(8)]
                            mW = [t_ + 1 for t_ in range(8)]
                        else:
                            mLB = [s_ for s_ in range(8)]
                            mLS = [s_ for s_ in range(8)]
                            mRC = [-t_ for t_ in range(8)]
                            mW = [8 - t_ for t_ in range(8)]
                        build(LB, GRN, GIM, A, Bm, mLB, dg)
                        build(RC, PR, PIS, CA, CB, mRC, dg)
                        build(LST, GR, GIS, A, Bm, mLS, dg)
                        build(LS2T, GI, GRN, A, Bm, mLS, dg)
                        build(W1f, PRN, PIM, CA, CB, mW, dg)
                        build(W2f, PIS, PRM, CA, CB, mW, dg)
                        p = pW.next()
                        k.mm(p.t[:, 0:128], LB.t[:], RC.t[:], True, True, [LB, RC], [p])
                        Mi = Mi_r.next()
                        k.tt(Mi.t[:], p.t[:, 0:128], bmask.t[:, d * 128:(d + 1) * 128], ALU.mult, [p, bmask], [Mi])
                        p = pW.next()
                        k.tr(p.t[:, 0:128], LST.t[:], self.identF, [LST, self.C], [p])
                        k.tr(p.t[:, 128:256], LS2T.t[:], self.identF, [LS2T, self.C], [p])
                        LS = LS_r.next()
                        LS2 = LS2_r.next()
                        k.cp(LS.t[:], p.t[:, 0:128], [p], [LS], eng="act")
                        k.cp(LS2.t[:], p.t[:, 128:256], [p], [LS2], eng="act")
                        W1 = W1_r.next()
                        W2 = W2_r.next()
                        k.cp(W1.t[:], W1f.t[:], [W1f], [W1], eng="pool")
                        k.cp(W2.t[:], W2f.t[:], [W2f], [W2], eng="pool")
                        k.ts(U8.t[:], nid.t[:, d, :], f8[:, dg:dg + 1], None, ALU.mult, None, [nid] + Wb, [U8])
                        self.frac_centered(R8.t[:], U8.t[:], K8.t[:], U8.t[:], NK, [U8, K8, R8])
                        self.sincos(S8.t[:], C8.t[:], R8.t[:], U8.t[:], [R8, U8, S8, C8])
                        for b in range(2):
                            next(wgen, None)
                            p1 = pP.next()
                            p2 = pP.next()
                            k.mm(p1.t[:, 0:NK], LS.t[:], Ug[b].t[:], True, True, [LS, Ug[b]], [p1])
                            k.mm(p2.t[:, 0:NK], LS2.t[:], Ug[b].t[:], True, True, [LS2, Ug[b]], [p2])
                            t1 = t1r.next()
                            bt = btr.next()
                            k.tt(t1.t[:], p1.t[:, 0:NK], C8.t[:], ALU.mult, [p1, C8], [t1])
                            k.tt(bt.t[:], p2.t[:, 0:NK], S8.t[:], ALU.mult, [p2, S8], [bt])
                            k.tt(bt.t[:], bt.t[:], t1.t[:], ALU.add, [bt, t1], [bt])
                            G = Gr_.next()
                            rm = M8.t[:, dg:dg + 1]
                            if d == 0:
                                k.scan(G.t[:, 0:KC], rm.to_broadcast([128, KC]), bt.t[:, 0:KC], 0.0, [bt, M8], [G])
                                k.scan(G.t[:, KC:NK], rm.to_broadcast([128, NK - KC]), bt.t[:, KC:NK], G.t[:, KC - 1:KC], [bt, G, M8], [G])
                            else:
                                k.scan(G.t[:, 0:KC][:, ::-1], rm.to_broadcast([128, KC]), bt.t[:, 0:KC][:, ::-1], 0.0, [bt, M8], [G])
                                k.scan(G.t[:, KC:NK][:, ::-1], rm.to_broadcast([128, NK - KC]), bt.t[:, KC:NK][:, ::-1], G.t[:, 0:1], [bt, G, M8], [G])
                            V1 = V1r.next()
                            V2 = V2r.next()
                            k.tt(V1.t[:], G.t[:], C8.t[:], ALU.mult, [G, C8], [V1])
                            k.tt(V2.t[:], G.t[:], S8.t[:], ALU.mult, [G, S8], [V2])
                            py = pY[b]
                            k.mm(py.t[:, 0:NK], Mi.t[:], Ug[b].t[:], d == 0, False, [Mi, Ug[b]], [py])
                            if d == 0:
                                segs = [(1, NK, 0)]
                            else:
                                segs = [(0, KC - 1, 1), (KC, NK - 1, KC + 1), (NK - 1, NK, 0)]
                            for si, (o0, o1, s0) in enumerate(segs):
                                n_ = o1 - o0
                                lastmm = (d == 1 and si == len(segs) - 1)
                                k.mm(py.t[:, o0:o1], W1.t[:], V1.t[:, s0:s0 + n_], False, False, [W1, V1], [py])
                                k.mm(py.t[:, o0:o1], W2.t[:], V2.t[:, s0:s0 + n_], False, lastmm, [W2, V2], [py])
                    for b in range(2):
                        k.cp(Ysb[j][b].t[:], pY[b].t[:, 0:NK], [pY[b]], [Ysb[j][b]], eng=("act" if b else "dve"))
                for b in range(2):
                    for bb in range(5):
                        nk = 64 if bb < 4 else 32
                        p = pUn.next()
                        for t_ in range(8):
                            for j in range(8):
                                k.mm(p.t[:, t_:8 * nk:8], selt.t[:, j * 8 + t_, :], Ysb[j][b].t[:, 64 * bb:64 * bb + nk], j == 0, j == 7,
                                     [selt, Ysb[j][b]], [p])
                        k.cp(yacc[b].t[:, 512 * bb:512 * bb + 8 * nk], p.t[:, 0:8 * nk], [p], [yacc[b]], eng=("act" if bb % 2 else "dve"))
                for b in range(2):
                    for (t0, n) in TB:
                        u_ = fin.next()
                        k.dma("sp", u_.t[:, 0:n], self.zf[b, c * 128:(c + 1) * 128, t0:t0 + n], [], [u_])
                        y_ = fin.next()
                        k.stt(y_.t[:, 0:n], u_.t[:, 0:n], sw.t[:, c:c + 1], yacc[b].t[:, t0:t0 + n], ALU.mult, ALU.add,
                              [u_, sw, yacc[b]], [y_])
                        o_ = fin.next()
                        a_ = fin.next()
                        b_ = fin.next()
                        self.gelu_tanh(o_.t[:, 0:n], y_.t[:, 0:n], a_.t[:, 0:n], b_.t[:, 0:n], [o_, y_, a_, b_])
                        k.dma("sp", self.ygd[b, c * 128:(c + 1) * 128, t0:t0 + n], o_.t[:, 0:n], [o_], [])
            for _ in wgen:
                pass
            self.wcast_done = True
        with k.scope():
            sw = k.sb("s5w", [128, 8], F32)
            k.dma("sp", sw.t[:], self.s5w[l], [], [sw])
            gw32 = k.sb("gw32", [128, 4, 512], F32)
            gw = k.sb("gw", [128, 4, 512], BF16)
            k.dma("sp", gw32.t[:], self.glu_w[l].rearrange("(kc p) o -> p kc o", p=128), [], [gw32])
            k.cp(gw.t[:], gw32.t[:], [gw32], [gw], eng="pool")
            yg = k.sb("yg", [128, 4, T], F32)
            ygb = k.sb("ygb", [128, 4, T], BF16)
            sg = k.ring("sg", [128, 512], F32, 2)
            ob = k.ring("ob", [128, 512], BF16, 3)
            pp = k.psring("pg", 3)
            for b in range(2):
                for c in range(4):
                    k.dma("sp", yg.t[:, c, :], self.ygd[b, c * 128:(c + 1) * 128, :], [], [yg])
                    k.cp(ygb.t[:, c, :], yg.t[:, c, :], [yg], [ygb], eng="act")
                for co in range(4):
                    for (t0, n) in TB:
                        p = pp.next()
                        for kc in range(4):
                            k.mm(p.t[:, 0:n], gw.t[:, kc, co * 128:(co + 1) * 128], ygb.t[:, kc, t0:t0 + n], kc == 0, kc == 3, [gw, ygb], [p])
                        s_ = sg.next()
                        k.act(s_.t[:, 0:n], p.t[:, 0:n], AF.Sigmoid, [p, sw], [s_], bias=sw.t[:, 4 + co:5 + co])
                        o = ob.next()
                        k.tt(o.t[:, 0:n], yg.t[:, co, t0:t0 + n], s_.t[:, 0:n], ALU.mult, [yg, s_], [o])
                        k.dma("sp", self.cc[b, co * 128:(co + 1) * 128, t0:t0 + n], o.t[:, 0:n], [o], [])

    def mixer_lru(self, l):
        k = self.k
        with k.scope():
            lv = k.sb("lruv", [128, 44], F32)
            k.dma("sp", lv.t[:], self.lruv[l], [], [lv])
            cw = lv.t[:, 0:16].rearrange("p (c j) -> p c j", j=4)
            cb = lv.t[:, 16:20]
            ba = lv.t[:, 20:28].rearrange("p (d c) -> p d c", c=4)
            bx = lv.t[:, 28:36].rearrange("p (d c) -> p d c", c=4)
            lam = lv.t[:, 36:44]
            sp = k.sb("lrusp", [128, 16], F32)
            k.act(sp.t[:, 0:8], lam, AF.Exp, [lv], [sp], scale=-1.0)
            k.act(sp.t[:, 0:8], sp.t[:, 0:8], AF.Ln, [sp, self.oneT], [sp], bias=self.oneT.t[:, 0:1])
            k.ts(sp.t[:, 8:16], sp.t[:, 0:8], -16.0, None, ALU.mult, None, [sp], [sp])
            k.ts(sp.t[:, 0:8], sp.t[:, 0:8], -8.0, None, ALU.mult, None, [sp], [sp])
            wa32 = k.sb("wa32", [128, 8, 128], F32)
            wx32 = k.sb("wx32", [128, 8, 128], F32)
            wa = k.sb("wa", [128, 8, 128], BF16)
            wx = k.sb("wx", [128, 8, 128], BF16)
            k.dma("sp", wa32.t[:], self.lru_wa[l].rearrange("d n c o -> c (d n) o"), [], [wa32])
            k.dma("sp", wx32.t[:], self.lru_wx[l].rearrange("d n c o -> c (d n) o"), [], [wx32])
            k.cp(wa.t[:], wa32.t[:], [wa32], [wa])
            k.cp(wx.t[:], wx32.t[:], [wx32], [wx])
            x = k.sb("lx", [128, T], F32)
            gt = k.sb("lg", [128, T], F32)
            xs = k.sb("lxs", [128, T], F32)
            xsb = k.sb("lxsb", [128, T], BF16)
            r_ = k.sb("lr", [128, T], F32)
            i_ = k.sb("li", [128, T], F32)
            a_ = k.sb("la", [128, T], F32)
            q_ = k.sb("lq", [128, T], F32)
            h_ = k.sb("lh", [128, T], F32)
            ys = k.sb("lys", [128, T], F32)
            ob = k.sb("lob", [128, T], BF16)
            pp = k.psring("pl", 4)
            for b in range(2):
                for c in range(4):
                    k.dma("sp", x.t[:], self.zf[b, 1536 + c * 128:1536 + (c + 1) * 128, :], [], [x])
                    k.dma("sp", gt.t[:], self.zf[b, 2048 + c * 128:2048 + (c + 1) * 128, :], [], [gt])
                    k.ts(xs.t[:], x.t[:], cw[:, c, 2:3], cb[:, c:c + 1], ALU.mult, ALU.add, [x, lv], [xs])
                    for jtap in (0, 1, 3):
                        o = jtap - 2
                        for (r0, r1) in ((0, CTX), (CTX, T)):
                            a0 = r0 + max(0, -o)
                            a1 = r1 - max(0, o)
                            k.stt(xs.t[:, a0:a1], x.t[:, a0 + o:a1 + o], cw[:, c, jtap:jtap + 1], xs.t[:, a0:a1],
                                  ALU.mult, ALU.add, [x, lv, xs], [xs])
                    k.cp(xsb.t[:], xs.t[:], [xs], [xsb], eng="act")
                    for d in range(2):
                        for (t0, n) in TB:
                            p = pp.next()
                            k.mm(p.t[:, 0:n], wa.t[:, d * 4 + c, :], xsb.t[:, t0:t0 + n], True, True, [wa, xsb], [p])
                            k.act(r_.t[:, t0:t0 + n], p.t[:, 0:n], AF.Sigmoid, [p, lv], [r_], bias=ba[:, d, c:c + 1])
                            p = pp.next()
                            k.mm(p.t[:, 0:n], wx.t[:, d * 4 + c, :], xsb.t[:, t0:t0 + n], True, True, [wx, xsb], [p])
                            k.act(i_.t[:, t0:t0 + n], p.t[:, 0:n], AF.Sigmoid, [p, lv], [i_], bias=bx[:, d, c:c + 1])
                        dc = d * 4 + c
                        k.act(a_.t[:], r_.t[:], AF.Exp, [r_, sp], [a_], scale=sp.t[:, dc:dc + 1])
                        k.act(q_.t[:], r_.t[:], AF.Exp, [r_, sp], [q_], scale=sp.t[:, 8 + dc:9 + dc])
                        k.act(q_.t[:], q_.t[:], AF.Sqrt, [q_, self.oneT], [q_], scale=-1.0, bias=self.oneT.t[:, 0:1])
                        k.tt(q_.t[:], q_.t[:], i_.t[:], ALU.mult, [q_, i_], [q_])
                        k.tt(q_.t[:], q_.t[:], xs.t[:], ALU.mult, [q_, xs], [q_])
                        if d == 0:
                            k.scan(h_.t[:, 0:CTX], a_.t[:, 0:CTX], q_.t[:, 0:CTX], 0.0, [a_, q_], [h_])
                            k.scan(h_.t[:, CTX:T], a_.t[:, CTX:T], q_.t[:, CTX:T], h_.t[:, CTX - 1:CTX], [a_, q_, h_], [h_])
                            k.cp(ys.t[:], h_.t[:], [h_], [ys], eng="pool")
                        else:
                            k.scan(h_.t[:, 0:CTX][:, ::-1], a_.t[:, 0:CTX][:, ::-1], q_.t[:, 0:CTX][:, ::-1], 0.0, [a_, q_], [h_])
                            k.scan(h_.t[:, CTX:T][:, ::-1], a_.t[:, CTX:T][:, ::-1], q_.t[:, CTX:T][:, ::-1], h_.t[:, 0:1], [a_, q_, h_], [h_])
                            k.tt(ys.t[:], ys.t[:], h_.t[:], ALU.add, [ys, h_], [ys])
                    self.gelu_tanh(h_.t[:], gt.t[:], a_.t[:], q_.t[:], [h_, gt, a_, q_])
                    k.tt(ob.t[:], ys.t[:], h_.t[:], ALU.mult, [ys, h_], [ob])
                    k.dma("sp", self.cc[b, 1536 + c * 128:1536 + (c + 1) * 128, :], ob.t[:], [ob], [])

    def mixer_mlstm(self, l):
        k = self.k
        KS = 128 ** -0.5
        TRI3 = self.C.t[:, 256:640]
        MASK = [self.C.t[:, 640:768], self.C.t[:, 768:896]]
        for b in range(2):
            with k.scope():
                Hacc = k.sb("Hacc", [128, NT, 512], F32)
                k.op("pool", lambda e: e.memset(Hacc.t[:], 0.0), [], [Hacc])
                with k.scope():
                    mlb = k.sb("mlb", [128, 16], F32)
                    k.dma("sp", mlb.t[:], self.mlb[l], [], [mlb])
                    sel = k.sb("sel", [16, 2048], F32)
                    nsel = k.sb("nsel", [16, 2048], F32)
                    k.dma("sp", sel.t[:], self.selc[:, :], [], [sel])
                    k.ts(nsel.t[:], sel.t[:], -1.0, None, ALU.mult, None, [sel], [nsel])
                    QT = k.sb("QT", [128, 4, T], BF16)
                    KT = k.sb("KT", [128, 4, T], BF16)
                    Kt = k.sb("Kt", [128, NT, 512], BF16)
                    Va = k.sb("Va", [128, NT, 4, 129], BF16)
                    G16 = k.sb("G16", [128, NT, 16], F32)
                    R = k.sb("R", [16, NT, 384], F32)
                    with k.scope():
                        st = k.ring("st", [128, T], F32, 2)
                        zr = k.ring("zr", [128, 1552], F32, 2)
                        pr = k.psring("pr", 2)
                        for h in range(4):
                            s_ = st.next()
                            k.dma("sp", s_.t[:], self.zf[b, 512 + h * 128:512 + (h + 1) * 128, :], [], [s_])
                            k.cp(QT.t[:, h, :], s_.t[:], [s_], [QT], eng="act")
                            s_ = st.next()
                            k.dma("sp", s_.t[:], self.zf[b, 1024 + h * 128:1024 + (h + 1) * 128, :], [], [s_])
                            k.act(KT.t[:, h, :], s_.t[:], AF.Copy, [s_], [KT], scale=KS)
                        k.op("pool", lambda e: e.memset(Va.t[:], 1.0), [], [Va])
                        for tt in range(NT):
                            z = zr.next()
                            k.dma("sp", z.t[:], self.zt[b, tt * 128:(tt + 1) * 128, 0:1552], [], [z])
                            k.act(Kt.t[:, tt, :], z.t[:, 0:512], AF.Copy, [z], [Kt], scale=KS)
                            k.cp(Va.t[:, tt, :, 0:128], z.t[:, 512:1024].rearrange("p (h e) -> p h e", h=4), [z], [Va])
                            k.tt(G16.t[:, tt, :], z.t[:, 1536:1552], mlb.t[:], ALU.add, [z, mlb], [G16])
                        for d in range(2):
                            gv = G16.t[:, :, d * 8 + 4:d * 8 + 8]
                            k.act(gv, gv, AF.Exp, [G16], [G16], scale=-1.0)
                            k.act(gv, gv, AF.Ln, [G16, self.oneT], [G16], bias=self.oneT.t[:, 0:1])
                            k.ts(gv, gv, -1.0, None, ALU.mult, None, [G16], [G16])
                        for tt in range(NT):
                            p = pr.next()
                            k.mm(p.t[0:16, 0:384], G16.t[:, tt, :], TRI3, True, True, [G16, self.C], [p])
                            k.cp(R.t[:, tt, :], p.t[0:16, 0:384], [p], [R], eng="act")
                    CT32_ = {}
                    CTb_ = {}
                    for d in range(2):
                        for h in range(4):
                            CT32_[d, h] = k.sb("CT32", [128, 129], F32)
                            CTb_[d, h] = k.sb("CTb", [128, 129], BF16)
                            k.op("dve", lambda e: e.memset(CT32_[d, h].t[:], 0.0), [], [CT32_[d, h]])
                            k.op("dve", lambda e: e.memset(CTb_[d, h].t[:], 0.0), [], [CTb_[d, h]])
                    EDr = k.ring("ED", [128, 128], F32, 4)
                    EBr = k.ring("EB", [128, 128], F32, 4)
                    STr = k.ring("ST", [128, 128], BF16, 4)
                    QSr = k.ring("QS", [128, 128], BF16, 4)
                    VWr = k.ring("VW", [128, 129], BF16, 4)
                    dnr = k.ring("dn", [128, 2], F32, 4)
                    pD = k.psring("pD", 2)
                    pB = k.psring("pB", 1)
                    pS = k.psring("pS", 2)
                    pN = k.psring("pN", 2)
                    pC = k.psring("pC", 1)
                    orders = [list(range(NT)), [1, 0] + list(range(NT - 1, 1, -1))]
                    for step in range(NT):
                        for d in range(2):
                            tt = orders[d][step]
                            bsl = slice(0, 128) if d == 0 else slice(128, 256)
                            last = 127 if d == 0 else 0
                            for h in range(4):
                                CT32 = CT32_[d, h]
                                CTb = CTb_[d, h]
                                kli = d * 8 + h
                                klf = d * 8 + 4 + h
                                SLI = sel.t[0:16, kli * 128:(kli + 1) * 128]
                                SLF = sel.t[0:16, klf * 128:(klf + 1) * 128]
                                NLF = nsel.t[0:16, klf * 128:(klf + 1) * 128]
                                tsl = slice(tt * 128, (tt + 1) * 128)
                                Rb = R.t[0:16, tt, bsl]
                                Rg = R.t[0:16, tt, 256:384]
                                pd_ = pD.next()
                                k.mm(pd_.t[:, 0:128], Rg, SLI, True, False, [R, sel], [pd_])
                                k.mm(pd_.t[:, 0:128], Rb, NLF, False, False, [R, nsel], [pd_])
                                k.mm(pd_.t[:, 0:128], SLF, Rb, False, False, [R, sel], [pd_])
                                k.mm(pd_.t[:, 0:128], self.identF, MASK[d], False, True, [self.C], [pd_])
                                ED = EDr.next()
                                k.act(ED.t[:], pd_.t[:, 0:128], AF.Exp, [pd_], [ED])
                                pb_ = pB.next()
                                k.mm(pb_.t[:, 0:128], SLF, Rb, True, True, [R, sel], [pb_])
                                EB = EBr.next()
                                k.act(EB.t[:], pb_.t[:, 0:128], AF.Exp, [pb_], [EB])
                                ps_ = pS.next()
                                k.mm(ps_.t[:, 0:128], KT.t[:, h, tsl], QT.t[:, h, tsl], True, True, [KT, QT], [ps_])
                                ST = STr.next()
                                k.tt(ST.t[:], ps_.t[:, 0:128], ED.t[:], ALU.mult, [ps_, ED], [ST])
                                QS = QSr.next()
                                k.tt(QS.t[:], QT.t[:, h, tsl], EB.t[:], ALU.mult, [QT, EB], [QS])
                                pn_ = pN.next()
                                k.mm(pn_.t[:, 0:129], QS.t[:], CTb.t[:], True, False, [QS, CTb], [pn_])
                                k.mm(pn_.t[:, 0:129], ST.t[:], Va.t[:, tt, h, :], False, True, [ST, Va], [pn_])
                                dn = dnr.next()
                                k.act(dn.t[:, 0:1], pn_.t[:, 128:129], AF.Abs, [pn_], [dn])
                                k.ts(dn.t[:, 0:1], dn.t[:, 0:1], 1.0, None, ALU.max, None, [dn], [dn])
                                k.op("dve", lambda e: e.reciprocal(out=dn.t[:, 1:2], in_=dn.t[:, 0:1]), [dn], [dn])
                                hs = Hacc.t[:, tt, h * 128:(h + 1) * 128]
                                k.stt(hs, pn_.t[:, 0:128], dn.t[:, 1:2], hs, ALU.mult, ALU.add, [pn_, dn, Hacc], [Hacc])
                                VW = VWr.next()
                                k.act(VW.t[:], Va.t[:, tt, h, :], AF.Identity, [Va, ED], [VW], scale=ED.t[:, last:last + 1])
                                pc_ = pC.next()
                                k.mm(pc_.t[:, 0:129], Kt.t[:, tt, h * 128:(h + 1) * 128], VW.t[:], True, True, [Kt, VW], [pc_])
                                k.stt(CT32.t[:], CT32.t[:], EB.t[:, last:last + 1], pc_.t[:, 0:129], ALU.mult, ALU.add,
                                      [CT32, EB, pc_], [CT32])
                                k.cp(CTb.t[:], CT32.t[:], [CT32], [CTb], eng="act")
                with k.scope():
                    onw = k.sb("onw", [128, 512], F32)
                    k.dma("sp", onw.t[:], self.onw[l], [], [onw])
                    OT = k.sb("OT", [128, 4, T], BF16)
                    mor = k.ring("mo", [128, 512], F32, 2)
                    sqr = k.ring("sqh", [128, 512], F32, 2)
                    ssr = k.ring("ss4", [128, 4], F32, 2)
                    pT = k.psring("pT", 2)
                    for tt in range(NT):
                        H = Hacc.t[:, tt, :]
                        sq = sqr.next()
                        k.tt(sq.t[:], H, H, ALU.mult, [Hacc], [sq])
                        ss = ssr.next()
                        k.op("dve", lambda e: e.tensor_reduce(out=ss.t[:], in_=sq.t[:].rearrange("p (h e) -> p h e", h=4), axis=AX.X, op=ALU.add), [sq], [ss])
                        k.act(ss.t[:], ss.t[:], AF.Sqrt, [ss, self.epsT], [ss], scale=1.0 / 128, bias=self.epsT.t[:, 0:1])
                        k.op("dve", lambda e: e.reciprocal(out=ss.t[:], in_=ss.t[:]), [ss], [ss])
                        for h in range(4):
                            hsl = slice(h * 128, (h + 1) * 128)
                            k.stt(sq.t[:, hsl], H[:, hsl], ss.t[:, h:h + 1], onw.t[:, hsl], ALU.mult, ALU.mult, [Hacc, ss, onw], [sq])
                        mo = mor.next()
                        k.dma("sp", mo.t[:], self.zt[b, tt * 128:(tt + 1) * 128, 1024:1536], [], [mo])
                        k.act(mo.t[:], mo.t[:], AF.Sigmoid, [mo], [mo])
                        k.tt(sq.t[:], sq.t[:], mo.t[:], ALU.mult, [sq, mo], [sq])
                        p = pT.next()
                        for h in range(4):
                            hsl = slice(h * 128, (h + 1) * 128)
                            k.tr(p.t[:, hsl], sq.t[:, hsl], self.identF, [sq, self.C], [p])
                        k.cp(OT.t[:, :, tt * 128:(tt + 1) * 128], p.t[:, 0:512].rearrange("p (h e) -> p h e", h=4), [p], [OT], eng="act")
                    for h in range(4):
                        k.dma("sp", self.cc[b, 512 + h * 128:512 + (h + 1) * 128, :], OT.t[:, h, :], [OT], [])

    def mixer_mla(self, l):
        k = self.k
        SC = 192 ** -0.5
        for b in range(2):
            with k.scope():
                QT = k.sb("aQT", [128, 4, T], BF16)
                QT2 = k.sb("aQT2", [64, 4, T], BF16)
                KT = k.sb("aKT", [128, 4, T], BF16)
                KT2 = k.sb("aKT2", [64, 4, T], BF16)
                Va = k.sb("aVa", [128, NT, 4, 129], BF16)
                k.op("pool", lambda e: e.memset(Va.t[:], 1.0), [], [Va])
                with k.scope():
                    nv = k.sb("mlav", [128, 896], F32)
                    k.dma("sp", nv.t[:], self.mlav[l], [], [nv])
                    QAW = nv.t[:, 0:384]
                    KVAW = nv.t[:, 384:512]
                    NW = [nv.t[:, 512:704], nv.t[:, 704:896]]
                    rc = k.sb("ropec", [128, NT, 32], F32)
                    rs = k.sb("ropes", [128, NT, 32], F32)
                    k.dma("sp", rc.t[:], self.ropec[:, :, :], [], [rc])
                    k.dma("sp", rs.t[:], self.ropes[:, :, :], [], [rs])
                    wq32 = k.sb("wq32", [128, 3, 768], F32)
                    wq = k.sb("wq", [128, 3, 768], BF16)
                    wkv32 = k.sb("wkv32", [128, 1024], F32)
                    wkv = k.sb("wkv", [128, 1024], BF16)
                    k.dma("sp", wq32.t[:], self.w_q_up[l].rearrange("(kc p) o -> p kc o", p=128), [], [wq32])
                    k.dma("sp", wkv32.t[:], self.w_kv_up[l], [], [wkv32])
                    k.cp(wq.t[:], wq32.t[:], [wq32], [wq], eng="pool")
                    k.cp(wkv.t[:], wkv32.t[:], [wkv32], [wkv], eng="pool")
                    Zr = k.ring("Z", [128, 576], F32, 2)
                    jk = k.sb("junk", [128, 384], F32)
                    ssr = k.ring("ss2", [128, 2], F32, 2)
                    cnr = k.ring("cn", [128, 512], F32, 2)
                    cTr = k.ring("cT", [128, 4, 128], BF16, 2)
                    Xr = [k.ring("X0", [128, 4, 192], F32, 2), k.ring("X1", [128, 4, 192], F32, 2)]
                    sqx = k.sb("sqx", [128, 768], F32)
                    s4r = k.ring("s4", [128, 4], F32, 2)
                    tmp = [k.sb("rt", [128, 4, 2, 16], F32) for _ in range(4)]
                    pT = k.psring("apT", 1)
                    pq = k.psring("apq", 2)
                    pk = k.psring("apk", 2)
                    pX = k.psring("apX", 2)
                    for tt in range(NT):
                        tsl = slice(tt * 128, (tt + 1) * 128)
                        Z = Zr.next()
                        k.dma("sp", Z.t[:], self.zt[b, tsl, 1552:2128], [], [Z])
                        ss = ssr.next()
                        k.act(jk.t[:, 0:384], Z.t[:, 0:384], AF.Square, [Z], [jk, ss], accum_out=ss.t[:, 0:1])
                        k.act(jk.t[:, 0:128], Z.t[:, 384:512], AF.Square, [Z], [jk, ss], accum_out=ss.t[:, 1:2])
                        k.act(ss.t[:, 0:1], ss.t[:, 0:1], AF.Sqrt, [ss, self.epsT], [ss], scale=1.0 / 384, bias=self.epsT.t[:, 0:1])
                        k.act(ss.t[:, 1:2], ss.t[:, 1:2], AF.Sqrt, [ss, self.epsT], [ss], scale=1.0 / 128, bias=self.epsT.t[:, 0:1])
                        k.op("dve", lambda e: e.reciprocal(out=ss.t[:], in_=ss.t[:]), [ss], [ss])
                        cn = cnr.next()
                        k.stt(cn.t[:, 0:384], Z.t[:, 0:384], ss.t[:, 0:1], QAW, ALU.mult, ALU.mult, [Z, ss, nv], [cn])
                        k.stt(cn.t[:, 384:512], Z.t[:, 384:512], ss.t[:, 1:2], KVAW, ALU.mult, ALU.mult, [Z, ss, nv], [cn])
                        p = pT.next()
                        for c4 in range(4):
                            k.tr(p.t[:, c4 * 128:(c4 + 1) * 128], cn.t[:, c4 * 128:(c4 + 1) * 128], self.identF, [cn, self.C], [p])
                        cT = cTr.next()
                        k.cp(cT.t[:], p.t[:, 0:512].rearrange("p (c e) -> p c e", c=4), [p], [cT], eng="act")
                        Xq = Xr[0].next()
                        Xk = Xr[1].next()
                        for nb in range(2):
                            p = pq.next()
                            for kc in range(3):
                                k.mm(p.t[:, 0:384], cT.t[:, kc, :], wq.t[:, kc, nb * 384:(nb + 1) * 384], kc == 0, kc == 2, [cT, wq], [p])
                            k.cp(Xq.t[:, 2 * nb:2 * nb + 2, :], p.t[:, 0:384].rearrange("p (h e) -> p h e", h=2), [p], [Xq], eng="act")
                        for nb in range(2):
                            p = pk.next()
                            k.mm(p.t[:, 0:512], cT.t[:, 3, :], wkv.t[:, nb * 512:(nb + 1) * 512], True, True, [cT, wkv], [p])
                            pv4 = p.t[:, 0:512].rearrange("p (h e) -> p h e", h=2)
                            k.cp(Xk.t[:, 2 * nb:2 * nb + 2, 0:128], pv4[:, :, 0:128], [p], [Xk])
                            k.cp(Va.t[:, tt, 2 * nb:2 * nb + 2, 0:128], pv4[:, :, 128:256], [p], [Va], eng="act")
                        for h in range(4):
                            k.cp(Xk.t[:, h, 128:192], Z.t[:, 512:576], [Z], [Xk], eng="pool")
                        for qi, X in enumerate((Xq, Xk)):
                            Xf = X.t[:].rearrange("p h e -> p (h e)")
                            k.tt(sqx.t[:], Xf, Xf, ALU.mult, [X], [sqx])
                            s4 = s4r.next()
                            k.op("dve", lambda e: e.tensor_reduce(out=s4.t[:], in_=sqx.t[:].rearrange("p (h e) -> p h e", h=4), axis=AX.X, op=ALU.add), [sqx], [s4])
                            k.act(s4.t[:], s4.t[:], AF.Sqrt, [s4, self.epsT], [s4], scale=1.0 / 192, bias=self.epsT.t[:, 0:1])
                            k.op("dve", lambda e: e.reciprocal(out=s4.t[:], in_=s4.t[:]), [s4], [s4])
                            for h in range(4):
                                k.stt(X.t[:, h, :], X.t[:, h, :], s4.t[:, h:h + 1], NW[qi], ALU.mult, ALU.mult, [X, s4, nv], [X])
                            rp = X.t[:, :, 128:192].rearrange("p h (a b f) -> p h a b f", a=2, b=2)
                            x1 = rp[:, :, :, 0, :]
                            x2 = rp[:, :, :, 1, :]
                            cosb = rc.t[:, tt, :].rearrange("p (o a f) -> p o a f", o=1, a=2).to_broadcast([128, 4, 2, 16])
                            sinb = rs.t[:, tt, :].rearrange("p (o a f) -> p o a f", o=1, a=2).to_broadcast([128, 4, 2, 16])
                            TT = tmp
                            k.tt(TT[0].t[:], x1, cosb, ALU.mult, [X, rc], [TT[0]])
                            k.tt(TT[1].t[:], x2, sinb, ALU.mult, [X, rs], [TT[1]])
                            k.tt(TT[2].t[:], x2, cosb, ALU.mult, [X, rc], [TT[2]])
                            k.tt(TT[3].t[:], x1, sinb, ALU.mult, [X, rs], [TT[3]])
                            k.tt(x1, TT[0].t[:], TT[1].t[:], ALU.subtract, [TT[0], TT[1], X], [X])
                            k.tt(x2, TT[2].t[:], TT[3].t[:], ALU.add, [TT[2], TT[3], X], [X])
                            dst, dst2 = (QT, QT2) if qi == 0 else (KT, KT2)
                            for hp in range(2):
                                p = pX.next()
                                for hh in range(2):
                                    h = 2 * hp + hh
                                    k.tr(p.t[:, hh * 256:hh * 256 + 128], X.t[:, h, 0:128], self.identF, [X, self.C], [p])
                                    k.tr(p.t[0:64, hh * 256 + 128:hh * 256 + 256], X.t[:, h, 128:192], self.identF, [X, self.C], [p])
                                pv_ = p.t[:, 0:512].rearrange("p (h e) -> p h e", h=2)
                                k.cp(dst.t[:, 2 * hp:2 * hp + 2, tsl], pv_[:, :, 0:128], [p], [dst], eng="act")
                                k.cp(dst2.t[0:64, 2 * hp:2 * hp + 2, tsl], p.t[0:64, 0:512].rearrange("p (h e) -> p h e", h=2)[:, :, 128:256], [p], [dst2])
                if self.dbg.get("mla_noattn"):
                    continue
                with k.scope():
                    Pr = k.ring("P", [128, 512], BF16, 3)
                    MT = k.ring("MT", [128, T], BF16, 2)
                    o32 = k.ring("o32", [128, 128], F32, 2)
                    rdr = k.ring("rd", [128, 1], F32, 2)
                    pS = k.psring("aS", 2)
                    po = [k.ps("apo%d" % i) for i in range(4)]
                    pT = k.psring("aT", 1)
                    blocks = [(0, 256, [0, 1])] + [(CTX + 512 * i, 512, list(range(NT))) for i in range(4)]
                    for h in range(4):
                        mt = MT.next()
                        for (q0, nq, kts) in blocks:
                            nsub = nq // 128
                            for idx, kt in enumerate(kts):
                                ksl = slice(kt * 128, (kt + 1) * 128)
                                ps_ = pS.next()
                                k.mm(ps_.t[:, 0:nq], KT.t[:, h, ksl], QT.t[:, h, q0:q0 + nq], True, False, [KT, QT], [ps_])
                                k.mm(ps_.t[:, 0:nq], KT2.t[0:64, h, ksl], QT2.t[0:64, h, q0:q0 + nq], False, True, [KT2, QT2], [ps_])
                                P = Pr.next()
                                k.act(P.t[:, 0:nq], ps_.t[:, 0:nq], AF.Exp, [ps_], [P], scale=SC)
                                for qs in range(nsub):
                                    k.mm(po[qs].t[:, 0:129], P.t[:, qs * 128:(qs + 1) * 128], Va.t[:, kt, h, :],
                                         idx == 0, idx == len(kts) - 1, [P, Va], [po[qs]])
                            for qs in range(nsub):
                                rd = rdr.next()
                                k.op("dve", lambda e: e.reciprocal(out=rd.t[:], in_=po[qs].t[:, 128:129]), [po[qs]], [rd])
                                o = o32.next()
                                k.ts(o.t[:], po[qs].t[:, 0:128], rd.t[:, 0:1], None, ALU.mult, None, [po[qs], rd], [o])
                                p = pT.next()
                                k.tr(p.t[:, 0:128], o.t[:], self.identF, [o, self.C], [p])
                                k.cp(mt.t[:, q0 + qs * 128:q0 + (qs + 1) * 128], p.t[:, 0:128], [p], [mt], eng="act")
                        k.dma("sp", self.cc[b, 1024 + h * 128:1024 + (h + 1) * 128, :], mt.t[:], [mt], [])


def make_consts():
    c = np.zeros((128, 2048), np.float32)
    i = np.arange(128)
    c[:, 0:128] = np.eye(128)
    c[:, 128:256] = 1.0
    c[:, 256:384] = (i[:, None] <= i[None, :])
    c[:, 384:512] = (i[:, None] >= i[None, :])
    c[:, 512:640] = np.eye(128)
    c[:, 640:768] = np.where(i[:, None] <= i[None, :], 0.0, -30000.0)
    c[:, 768:896] = np.where(i[:, None] >= i[None, :], 0.0, -30000.0)
    return c


def prep_common(inp):
    m = {}
    f = np.float32
    m["ada_w"] = inp["ada_w"]
    m["ada_bT"] = np.ascontiguousarray(inp["ada_b"].reshape(4, 96, 128).transpose(0, 2, 1))
    m["n1w"] = np.ascontiguousarray(inp["norm1_w"].reshape(4, 16, 128).transpose(0, 2, 1))
    m["n2w"] = np.ascontiguousarray(inp["norm2_w"].reshape(4, 16, 128).transpose(0, 2, 1))
    m["w_in"] = inp["w_in"]
    m["w_out"] = inp["w_out"]
    m["mlp_w1"] = inp["mlp_w1"]
    m["mlp_w2"] = inp["mlp_w2"]
    m["consts"] = make_consts()
    prep_mixers(inp, m)
    return m


def prep_core(inp, common, core):
    b0 = 2 * core
    m = dict(common)
    xin = np.concatenate([inp["ctx"][b0:b0 + 2], inp["x"][b0:b0 + 2]], axis=1)
    m["xin"] = np.ascontiguousarray(xin.transpose(0, 2, 1))
    c3 = np.stack([inp["c"][b0], inp["c"][b0 + 1], inp["c_ctx"]], axis=1)
    m["cT"] = np.ascontiguousarray(c3.reshape(16, 128, 3).transpose(1, 0, 2))
    return m


def _dup(a):
    return np.concatenate([a, a], axis=0)


def prep_mixers(inp, m):
    L = DEPTH
    c = m["consts"]
    c[:64, 896] = -1.0
    c[64:, 896] = 1.0
    lre = inp["s5_lam_re"].transpose(0, 3, 1, 2).reshape(L, 64, 64)
    lim = inp["s5_lam_im"].transpose(0, 3, 1, 2).reshape(L, 64, 64)
    ldt = np.broadcast_to(inp["s5_log_dt"].reshape(L, 1, 64), (L, 64, 64))
    s5v = np.concatenate([lre, lim, ldt], axis=2)
    m["s5v"] = np.ascontiguousarray(np.concatenate([s5v, s5v], axis=1))
    bre = inp["s5_b_re"].transpose(0, 3, 1, 2, 4).reshape(L, 64, 64, 16)
    bim = inp["s5_b_im"].transpose(0, 3, 1, 2, 4).reshape(L, 64, 64, 16)
    m["s5A"] = np.ascontiguousarray(np.concatenate([bre, bim], axis=1))
    m["s5B"] = np.ascontiguousarray(np.concatenate([bim, bre], axis=1))
    cre = inp["s5_c_re"].transpose(0, 4, 1, 2, 3).reshape(L, 64, 64, 16)
    cim = inp["s5_c_im"].transpose(0, 4, 1, 2, 3).reshape(L, 64, 64, 16)
    m["s5CA"] = np.ascontiguousarray(np.concatenate([cre, cim], axis=1))
    m["s5CB"] = np.ascontiguousarray(np.concatenate([cim, cre], axis=1))
    dsk = inp["s5_d"].reshape(L, 4, 128).transpose(0, 2, 1)
    glb = inp["s5_glu_b"].reshape(L, 4, 128).transpose(0, 2, 1)
    m["s5w"] = np.ascontiguousarray(np.concatenate([dsk, glb], axis=2))
    m["glu_w"] = inp["s5_glu_w"]
    NK, KC = T // 8, CTX // 8
    n0 = np.arange(NK, dtype=np.float32)
    n1 = np.concatenate([KC - 1 - np.arange(KC), KC + (NK - KC - 1 - np.arange(NK - KC))]).astype(np.float32)
    m["nidx8"] = np.ascontiguousarray(np.broadcast_to(np.stack([n0, n1])[:, None, :], (2, 128, NK)))
    selu = np.zeros((128, 8, 8, 128), np.float32)
    selt = np.zeros((128, 8, 8, 128), np.float32)
    for jj in range(8):
        for ss in range(8):
            for ci in range(16):
                selu[16 * jj + ci, jj, ss, 16 * ss + ci] = 1.0
                selt[16 * ss + ci, jj, ss, 16 * jj + ci] = 1.0
    m["selu"] = selu.reshape(128, 64, 128).astype(ml_dtypes.bfloat16)
    m["selt"] = selt.reshape(128, 64, 128).astype(ml_dtypes.bfloat16)
    blk = np.arange(128) // 16
    m["bmask"] = np.concatenate([(blk[None, :] >= blk[:, None]), (blk[None, :] <= blk[:, None])], axis=1).astype(np.float32)
    cw = inp["lru_conv_w"].reshape(L, 4, 4, 128).transpose(0, 3, 2, 1).reshape(L, 128, 16)
    cb = inp["lru_conv_b"].reshape(L, 4, 128).transpose(0, 2, 1)
    def dc(a):
        return a.reshape(L, 2, 4, 128).transpose(0, 3, 1, 2).reshape(L, 128, 8)
    m["lruv"] = np.ascontiguousarray(np.concatenate([cw, cb, dc(inp["lru_ba"]), dc(inp["lru_bx"]), dc(inp["lru_lam"])], axis=2))
    m["lru_wa"] = inp["lru_wa"]
    m["lru_wx"] = inp["lru_wx"]
    gb = np.concatenate([inp["ml_ig_bias"], inp["ml_fg_bias"]], axis=2).reshape(L, 1, 16)
    m["mlb"] = np.ascontiguousarray(np.broadcast_to(gb, (L, 128, 16)))
    m["onw"] = np.ascontiguousarray(np.broadcast_to(inp["ml_out_norm"].reshape(L, 1, 512), (L, 128, 512)))
    selc = np.zeros((16, 16, 128), np.float32)
    for kk in range(16):
        selc[kk, kk, :] = 1.0
    m["selc"] = selc.reshape(16, 2048)
    nv = np.concatenate([inp["mla_q_a_norm"], inp["mla_kv_a_norm"], inp["mla_q_norm"], inp["mla_k_norm"]], axis=1)
    m["mlav"] = np.ascontiguousarray(np.broadcast_to(nv.reshape(L, 1, 896), (L, 128, 896)))
    m["w_q_up"] = inp["mla_w_q_up"]
    m["w_kv_up"] = inp["mla_w_kv_up"]
    q = np.arange(LAT)
    inv = (np.float32(10000.0) ** (-np.arange(16, dtype=np.float32) / np.float32(16))).astype(np.float32)
    ang = np.concatenate([(q // 64).astype(np.float32)[:, None] * inv, (q % 64).astype(np.float32)[:, None] * inv], axis=1)
    cosf = np.ones((T, 32), np.float32)
    sinf = np.zeros((T, 32), np.float32)
    cosf[CTX:] = np.cos(ang.astype(np.float32))
    sinf[CTX:] = np.sin(ang.astype(np.float32))
    m["ropec"] = np.ascontiguousarray(cosf.reshape(NT, 128, 32).transpose(1, 0, 2))
    m["ropes"] = np.ascontiguousarray(sinf.reshape(NT, 128, 32).transpose(1, 0, 2))


_PROG = None


def kernel(**inputs):
    global _PROG
    inp = {k_: np.asarray(v) for k_, v in inputs.items()}
    if _PROG is None:
        _PROG = Prog()
    prog = _PROG
    common = prep_common(inp)
    in_maps = []
    for core in range(8):
        m = prep_core(inp, common, core)
        in_maps.append({n: m[n] for n in prog.inp})
    res = run_bass_kernel_spmd(prog.nc, in_maps, core_ids=list(range(8)))
    outs = [r["yout"] for r in res.results]
    y = np.concatenate(outs, axis=0)
    return np.ascontiguousarray(y.transpose(0, 2, 1)).astype(np.float32)
```

```python
import numpy as np
import ml_dtypes
from contextlib import ExitStack, contextmanager
import concourse.bass as bass
import concourse.mybir as mybir
from concourse.bass_utils import run_bass_kernel_spmd

F32 = mybir.dt.float32
BF16 = mybir.dt.bfloat16
I32 = mybir.dt.int32
AF = mybir.ActivationFunctionType
ALU = mybir.AluOpType
AX = mybir.AxisListType

D = 2048
T = 2304
CTX = 256
LAT = 2048
NT = 18
DEPTH = 4
DFF = 8192
INC = 4176
EPS = 1e-6
TB = [(0, 256), (256, 512), (768, 512), (1280, 512), (1792, 512)]
TWO_PI = float(2 * np.pi)
SAME_SYNC = True
NO_SELF_SYNC = ("pe",)
MERGE_WAIT = True
CAP = 30000


class Res:
    __slots__ = ("w", "r")

    def __init__(self):
        self.w = None
        self.r = {}


class Buf:
    def __init__(self, t, psum=False):
        self.t = t
        self.res = Res()
        self.psum = psum

    def __getitem__(self, key):
        return self.t[key]


class Ring:
    def __init__(self, bufs):
        self.bufs = bufs
        self.i = 0

    def next(self):
        b = self.bufs[self.i]
        self.i = (self.i + 1) % len(self.bufs)
        return b


class KB:
    def __init__(self, nc):
        self.nc = nc
        self.E = {"pe": nc.tensor, "dve": nc.vector, "act": nc.scalar, "pool": nc.gpsimd, "sp": nc.sync}
        self.sem = {}
        self.cnt = {}
        self.owner = {}
        self.nsem = 0
        for e in self.E:
            self._fresh(e)
        self.waited = {e: {} for e in self.E}
        self.dq = {}
        self.dqi = {}
        for q, n in (("sp", 12), ("pool", 4), ("act", 4)):
            self.dq[q] = [[self._newsem("d"), 0] for _ in range(n)]
            self.dqi[q] = 0
        self.stack = []
        self.uid = 0

    def _newsem(self, pfx):
        self.nsem += 1
        return self.nc.alloc_semaphore(f"{pfx}{self.nsem}")

    def _fresh(self, e):
        s = self._newsem("e")
        self.sem[e] = s
        self.cnt[e] = 0
        self.owner[s] = e

    @contextmanager
    def scope(self):
        st = ExitStack()
        self.stack.append(st)
        try:
            yield
        finally:
            self.barrier()
            self.stack.pop()
            st.close()

    def _name(self, n):
        self.uid += 1
        return f"{n}_{self.uid}"

    def sb(self, name, shape, dtype):
        t = self.stack[-1].enter_context(self.nc.sbuf_tensor(self._name(name), list(shape), dtype))
        return Buf(t)

    def ps(self, name, shape=(128, 512), dtype=F32):
        t = self.stack[-1].enter_context(self.nc.psum_tensor(self._name(name), list(shape), dtype))
        return Buf(t, psum=True)

    def ring(self, name, shape, dtype, n):
        return Ring([self.sb(name, shape, dtype) for _ in range(n)])

    def psring(self, name, n, shape=(128, 512), dtype=F32):
        return Ring([self.ps(name, shape, dtype) for _ in range(n)])

    def _need(self, e, tok, out):
        if tok is None:
            return
        sem, val = tok
        own = self.owner.get(sem)
        if own == e and (e in NO_SELF_SYNC or not SAME_SYNC):
            return
        w = self.waited[e]
        if w.get(sem, 0) >= val:
            return
        w[sem] = val
        for i, (s_, v_) in enumerate(out):
            if s_ is sem or s_ == sem:
                out[i] = (sem, max(v_, val))
                return
        out.append((sem, val))

    def _wait(self, e, tok):
        out = []
        self._need(e, tok, out)
        for (s_, v_) in out:
            self.E[e].wait_ge(s_, v_)

    def _deps(self, e, reads, writes):
        out = []
        for r in reads:
            r = r.res if isinstance(r, Buf) else r
            self._need(e, r.w, out)
        for wr in writes:
            wr = wr.res if isinstance(wr, Buf) else wr
            self._need(e, wr.w, out)
            for s_, v_ in wr.r.items():
                self._need(e, (s_, v_), out)
        return out

    def _commit(self, tok, reads, writes):
        sem, val = tok
        for r in reads:
            r = r.res if isinstance(r, Buf) else r
            if r.r.get(sem, 0) < val:
                r.r[sem] = val
        for wr in writes:
            wr = wr.res if isinstance(wr, Buf) else wr
            wr.w = tok
            wr.r = {}

    def op(self, e, fn, reads=(), writes=(), merge=True):
        pr = [r for r in reads if isinstance(r, Buf) and r.psum]
        if pr:
            reads = [r for r in reads if not (isinstance(r, Buf) and r.psum)]
            writes = list(writes) + pr
        need = self._deps(e, reads, writes)
        last = None
        if merge and MERGE_WAIT and need:
            last = need.pop()
        for (s_, v_) in need:
            self.E[e].wait_ge(s_, v_)
        ins = fn(self.E[e])
        if last is not None:
            ins._wait_ge(last[0], last[1])
        self.cnt[e] += 1
        ins.then_inc(self.sem[e], 1)
        tok = (self.sem[e], self.cnt[e])
        self._commit(tok, reads, writes)
        if self.cnt[e] >= CAP:
            self._fresh(e)

    def dma(self, q, out, in_, reads=(), writes=()):
        slots = self.dq[q]
        slot = slots[self.dqi[q]]
        self.dqi[q] = (self.dqi[q] + 1) % len(slots)
        if slot[1] > 0:
            self._wait(q, (slot[0], slot[1]))
        if slot[1] >= CAP:
            slot[0] = self._newsem("d")
            slot[1] = 0
        for (s_, v_) in self._deps(q, reads, writes):
            self.E[q].wait_ge(s_, v_)
        ins = self.E[q].dma_start(out=out, in_=in_)
        slot[1] += 16
        ins.then_inc(slot[0], 16)
        self._commit((slot[0], slot[1]), reads, writes)

    def barrier(self):
        toks = [(self.sem[e], self.cnt[e]) for e in self.E if self.cnt[e] > 0]
        for q in self.dq:
            for slot in self.dq[q]:
                if slot[1] > 0:
                    toks.append((slot[0], slot[1]))
        for e in self.E:
            for tok in toks:
                if self.owner.get(tok[0]) == e:
                    continue
                self._wait(e, tok)

    def mm(self, out, lhsT, rhs, start, stop, reads, writes):
        self.op("pe", lambda e: e.matmul(out, lhsT, rhs, start=start, stop=stop), reads, writes)

    def tr(self, out, in_, ident, reads, writes):
        self.op("pe", lambda e: e.transpose(out, in_, ident), reads, writes)

    def act(self, out, in_, func, reads, writes, bias=0.0, scale=1.0, **kw):
        self.op("act", lambda e: e.activation(out=out, in_=in_, func=func, bias=bias, scale=scale, **kw), reads, writes,
                merge=("accum_out" not in kw))

    def ts(self, out, in0, s1, s2, op0, op1, reads, writes, eng="dve"):
        if s2 is None:
            self.op(eng, lambda e: e.tensor_scalar(out=out, in0=in0, scalar1=s1, scalar2=None, op0=op0), reads, writes)
        else:
            self.op(eng, lambda e: e.tensor_scalar(out=out, in0=in0, scalar1=s1, scalar2=s2, op0=op0, op1=op1), reads, writes)

    def tt(self, out, in0, in1, op, reads, writes, eng="dve"):
        self.op(eng, lambda e: e.tensor_tensor(out=out, in0=in0, in1=in1, op=op), reads, writes)

    def stt(self, out, in0, scalar, in1, op0, op1, reads, writes):
        self.op("dve", lambda e: e.scalar_tensor_tensor(out=out, in0=in0, scalar=scalar, in1=in1, op0=op0, op1=op1), reads, writes)

    def cp(self, out, in_, reads, writes, eng="dve"):
        if eng == "act":
            self.op("act", lambda e: e.copy(out=out, in_=in_), reads, writes)
        else:
            self.op(eng, lambda e: e.tensor_copy(out=out, in_=in_), reads, writes)

    def scan(self, out, d0, d1, init, reads, writes):
        self.op("dve", lambda e: e.tensor_tensor_scan(out=out, data0=d0, data1=d1, initial=init, op0=ALU.mult, op1=ALU.add), reads, writes)


class Prog:
    def __init__(self, n_layers=DEPTH, dbg=None, layers=None):
        self.dbg = dbg or {}
        self.layers = list(range(n_layers)) if layers is None else layers
        nc = bass.Bass("TRN2", target_bir_lowering=False)
        self.nc = nc
        self.k = KB(nc)
        self.inp = {}
        self.build()

    def din(self, name, shape, dtype=F32):
        t = self.nc.dram_tensor(name, list(shape), dtype, kind="ExternalInput")
        self.inp[name] = (tuple(shape), dtype)
        return t.ap()

    def dscr(self, name, shape, dtype=F32):
        kind = "ExternalOutput" if name in self.dbg else "Internal"
        return self.nc.dram_tensor(name, list(shape), dtype, kind=kind).ap()

    def build(self):
        nc, k = self.nc, self.k
        L = DEPTH
        self.xin = self.din("xin", [2, D, T])
        self.cT = self.din("cT", [128, 16, 3])
        self.ada_w = self.din("ada_w", [L, D, 6 * D])
        self.ada_bT = self.din("ada_bT", [L, 128, 96])
        self.n1w = self.din("n1w", [L, 128, 16])
        self.n2w = self.din("n2w", [L, 128, 16])
        self.w_in = self.din("w_in", [L, D, INC])
        self.w_out = self.din("w_out", [L, D, D])
        self.mlp_w1 = self.din("mlp_w1", [L, D, DFF])
        self.mlp_w2 = self.din("mlp_w2", [L, DFF, D])
        self.consts = self.din("consts", [128, 2048])
        self.declare_mixer_inputs()
        self.yout = nc.dram_tensor("yout", [2, D, LAT], F32, kind="ExternalOutput").ap()
        self.xs = self.dscr("xs", [2, D, T])
        self.zf = self.dscr("zf", [2, 2560, T])
        self.zt = self.dscr("zt", [2, T, 2128])
        self.W1t = self.dscr("W1t", [64, 128, 16, 128], BF16)
        self.W2t = self.dscr("W2t", [4, 16, 128, 16, 128], BF16)
        if self.dbg.get("cc_in"):
            self.cc = self.din("cc", [2, D, T], BF16)
        else:
            self.cc = self.dscr("cc", [2, D, T], BF16)
        if "modv_o" in self.dbg:
            self.modv_o = self.dscr("modv_o", [128, 288])

        with k.scope():
            self.setup_consts()
            for b in range(2):
                for c in range(16):
                    k.dma("sp", self.xs[b, c * 128:(c + 1) * 128, :], self.xin[b, c * 128:(c + 1) * 128, :])
            k.barrier()
            for l in self.layers:
                self.layer(l)
            for b in range(2):
                for c in range(16):
                    k.dma("sp", self.yout[b, c * 128:(c + 1) * 128, :], self.xs[b, c * 128:(c + 1) * 128, CTX:T])

    def setup_consts(self):
        k = self.k
        self.C = k.sb("consts", [128, 2048], F32)
        k.dma("sp", self.C.t[:], self.consts[:, :], [], [self.C])
        self.identF = self.C.t[:, 0:128]
        self.onesF = self.C.t[:, 128:256]
        self.identB = k.sb("identB", [128, 128], BF16)
        k.cp(self.identB.t[:], self.identF, [self.C], [self.identB])
        self.cs = k.sb("cs", [128, 16, 3], F32)
        k.dma("sp", self.cs.t[:], self.cT[:, :, :], [], [self.cs])
        k.act(self.cs.t[:], self.cs.t[:], AF.Silu, [self.cs], [self.cs])
        self.modv = k.sb("modv", [128, 96, 3], F32)
        self.g1 = k.sb("g1", [128, 16, 3], F32)
        self.g2 = k.sb("g2", [128, 16, 3], F32)
        self.epsT = k.sb("epsT", [128, 1], F32)
        k.op("dve", lambda e: e.memset(self.epsT.t[:], EPS), [], [self.epsT])
        self.setup_mixer_consts()

    def layer(self, l):
        k = self.k
        self.wcast_done = False
        self.stage_mod(l)
        for b in range(2):
            self.stage_A(l, b)
        self.mixers(l)
        for b in range(2):
            self.stage_proj_res(l, b, which="out")
        if not self.wcast_done:
            self.stage_wcast(l)
        for b in range(2):
            self.stage_mlp(l, b)

    def stage_mod(self, l):
        k = self.k
        with k.scope():
            wr = k.ring("adaw", [128, 16, 128], F32, 3)
            pm = k.ps("pmod")
            adab = k.sb("adab", [128, 96], F32)
            nw1 = k.sb("nw1", [128, 16], F32)
            nw2 = k.sb("nw2", [128, 16], F32)
            k.dma("sp", adab.t[:], self.ada_bT[l], [], [adab])
            k.dma("sp", nw1.t[:], self.n1w[l], [], [nw1])
            k.dma("sp", nw2.t[:], self.n2w[l], [], [nw2])
            wv = self.ada_w[l].rearrange("(kc p) f -> p kc f", p=128)
            for j in range(96):
                w = wr.next()
                k.dma("sp", w.t[:], wv[:, :, j * 128:(j + 1) * 128], [], [w])
                for kc in range(16):
                    k.mm(pm.t[:, 3 * j:3 * j + 3], w.t[:, kc, :], self.cs.t[:, kc, :], kc == 0, kc == 15,
                         [w, self.cs], [pm])
            pv = pm.t[:, 0:288].rearrange("p (j r) -> p j r", r=3)
            for r in range(3):
                k.tt(self.modv.t[:, :, r], pv[:, :, r], adab.t[:], ALU.add, [pm, adab], [self.modv])
            if "modv_o" in self.dbg:
                k.dma("sp", self.modv_o[:, :], self.modv.t[:].rearrange("p j r -> p (j r)"), [self.modv], [])
            for r in range(3):
                k.stt(self.g1.t[:, :, r], self.modv.t[:, 16:32, r], 1.0, nw1.t[:], ALU.add, ALU.mult,
                      [self.modv, nw1], [self.g1])
                k.stt(self.g2.t[:, :, r], self.modv.t[:, 64:80, r], 1.0, nw2.t[:], ALU.add, ALU.mult,
                      [self.modv, nw2], [self.g2])

    def make_hT(self, hT, b, g, shift_base, blocks=TB, rel=False, nx=2, base=None, nmax=512):
        k = self.k
        with k.scope():
            xr = k.ring("xblk", [128, 16, nmax], F32, nx)
            sqr = k.ring("sq", [128, nmax], F32, 3)
            rsr = k.ring("rstd", [128, nmax], F32, 2)
            tmr = k.ring("tmp", [128, nmax], F32, 3)
            pss = k.psring("ss", 2)
            xv = self.xs[b].rearrange("(c p) t -> p c t", p=128)
            for (t0, n) in blocks:
                r = 2 if t0 < CTX else b
                o0 = (t0 - base) if base is not None else (0 if rel else t0)
                xb = xr.next()
                for c in range(16):
                    k.dma("sp", xb.t[:, c, 0:n], xv[:, c, t0:t0 + n], [], [xb])
                ss = pss.next()
                for c in range(16):
                    sq = sqr.next()
                    k.act(sq.t[:, 0:n], xb.t[:, c, 0:n], AF.Square, [xb], [sq])
                    k.mm(ss.t[:, 0:n], self.onesF, sq.t[:, 0:n], c == 0, c == 15, [sq, self.C], [ss])
                rs = rsr.next()
                k.act(rs.t[:, 0:n], ss.t[:, 0:n], AF.Sqrt, [ss, self.epsT], [rs], bias=self.epsT.t[:, 0:1], scale=1.0 / D)
                k.op("dve", lambda e: e.reciprocal(out=rs.t[:, 0:n], in_=rs.t[:, 0:n]), [rs], [rs])
                for c in range(16):
                    tm = tmr.next()
                    k.stt(tm.t[:, 0:n], xb.t[:, c, 0:n], g.t[:, c, r:r + 1], rs.t[:, 0:n], ALU.mult, ALU.mult,
                          [xb, g, rs], [tm])
                    k.act(hT.t[:, c, o0:o0 + n], tm.t[:, 0:n], AF.Identity, [tm, self.modv], [hT],
                          bias=self.modv.t[:, shift_base + c, r:r + 1], scale=1.0)

    def hT_piece(self, hT, b, g, shift_base, t0, n, o0, R):
        k = self.k
        xv = self.xs[b].rearrange("(c p) t -> p c t", p=128)
        r = 2 if t0 < CTX else b
        xb = R["x"].next()
        for c in range(16):
            k.dma("sp", xb.t[:, c, 0:n], xv[:, c, t0:t0 + n], [], [xb])
        ss = R["ps"].next()
        for c in range(16):
            sq = R["sq"].next()
            k.act(sq.t[:, 0:n], xb.t[:, c, 0:n], AF.Square, [xb], [sq])
            k.mm(ss.t[:, 0:n], self.onesF, sq.t[:, 0:n], c == 0, c == 15, [sq, self.C], [ss])
        rs = R["rs"].next()
        k.act(rs.t[:, 0:n], ss.t[:, 0:n], AF.Sqrt, [ss, self.epsT], [rs], bias=self.epsT.t[:, 0:1], scale=1.0 / D)
        k.op("dve", lambda e: e.reciprocal(out=rs.t[:, 0:n], in_=rs.t[:, 0:n]), [rs], [rs])
        for c in range(16):
            tm = R["tm"].next()
            k.stt(tm.t[:, 0:n], xb.t[:, c, 0:n], g.t[:, c, r:r + 1], rs.t[:, 0:n], ALU.mult, ALU.mult, [xb, g, rs], [tm])
            k.act(hT.t[:, c, o0:o0 + n], tm.t[:, 0:n], AF.Identity, [tm, self.modv], [hT],
                  bias=self.modv.t[:, shift_base + c, r:r + 1], scale=1.0)

    FM_CHUNKS = [0, 128, 256, 384, 512, 640, 768, 896, 1024, 1152, 1280, 1408,
                 3152, 3280, 3408, 3536, 3664, 3792, 3920, 4048]
    TM_BLOCKS = [(1024, 512), (1536, 512), (2048, 512), (2560, 512), (3072, 80)]

    def stage_A(self, l, b):
        k = self.k
        with k.scope():
            hT = k.sb("hT", [128, 16, T], BF16)
            self.make_hT(hT, b, self.g1, 0)
            wv = self.w_in[l].rearrange("(kc p) c -> p kc c", p=128)
            with k.scope():
                wf = k.ring("wf", [128, 16, 128], F32, 2)
                wb = k.ring("wb", [128, 16, 128], BF16, 2)
                ob = k.ring("ob", [128, 512], F32, 3)
                pp = k.psring("pp", 3)
                ei = 0
                for ci, c0 in enumerate(self.FM_CHUNKS):
                    w32 = wf.next()
                    k.dma("sp", w32.t[:], wv[:, :, c0:c0 + 128], [], [w32])
                    w16 = wb.next()
                    k.cp(w16.t[:], w32.t[:], [w32], [w16], eng="pool")
                    for (t0, n) in TB:
                        p = pp.next()
                        for kc in range(16):
                            k.mm(p.t[:, 0:n], w16.t[:, kc, :], hT.t[:, kc, t0:t0 + n], kc == 0, kc == 15, [w16, hT], [p])
                        o = ob.next()
                        k.cp(o.t[:, 0:n], p.t[:, 0:n], [p], [o], eng=("act" if ei % 2 else "dve"))
                        ei += 1
                        k.dma("sp", self.zf[b, ci * 128:(ci + 1) * 128, t0:t0 + n], o.t[:, 0:n], [o], [])
            with k.scope():
                wf = k.ring("wf2", [128, 8, 512], F32, 2)
                wb = k.ring("wb2", [128, 16, 512], BF16, 2)
                ob = k.ring("ob2", [128, 512], F32, 3)
                pp = k.psring("pp2", 3)
                ei = 0
                for (c0, w) in self.TM_BLOCKS:
                    w16 = wb.next()
                    for hf in range(2):
                        w32 = wf.next()
                        k.dma("sp", w32.t[:, :, 0:w], wv[:, hf * 8:(hf + 1) * 8, c0:c0 + w], [], [w32])
                        k.cp(w16.t[:, hf * 8:(hf + 1) * 8, 0:w], w32.t[:, :, 0:w], [w32], [w16], eng="pool")
                    for tt in range(NT):
                        p = pp.next()
                        for kc in range(16):
                            k.mm(p.t[:, 0:w], hT.t[:, kc, tt * 128:(tt + 1) * 128], w16.t[:, kc, 0:w], kc == 0, kc == 15,
                                 [w16, hT], [p])
                        o = ob.next()
                        k.cp(o.t[:, 0:w], p.t[:, 0:w], [p], [o], eng=("act" if ei % 2 else "dve"))
                        ei += 1
                        k.dma("sp", self.zt[b, tt * 128:(tt + 1) * 128, c0 - 1024:c0 - 1024 + w], o.t[:, 0:w], [o], [])

    def stage_proj_res(self, l, b, which):
        k = self.k
        with k.scope():
            cT = k.sb("ccT", [128, 16, T], BF16)
            cv = self.cc[b].rearrange("(c p) t -> p c t", p=128)
            for c in range(16):
                k.dma("sp", cT.t[:, c, :], cv[:, c, :], [], [cT])
            wv = self.w_out[l].rearrange("(kc p) c -> p kc c", p=128)
            xv = self.xs[b].rearrange("(c p) t -> p c t", p=128)
            wf = k.ring("wf", [128, 16, 128], F32, 2)
            wb = k.ring("wb", [128, 16, 128], BF16, 2)
            xr = k.ring("xo", [128, 512], F32, 3)
            pp = k.psring("pp", 3)
            for fc in range(16):
                w32 = wf.next()
                k.dma("sp", w32.t[:], wv[:, :, fc * 128:(fc + 1) * 128], [], [w32])
                w16 = wb.next()
                k.cp(w16.t[:], w32.t[:], [w32], [w16], eng="pool")
                for (t0, n) in TB:
                    r = 2 if t0 < CTX else b
                    xo = xr.next()
                    k.dma("sp", xo.t[:, 0:n], xv[:, fc, t0:t0 + n], [], [xo])
                    p = pp.next()
                    for kc in range(16):
                        k.mm(p.t[:, 0:n], w16.t[:, kc, :], cT.t[:, kc, t0:t0 + n], kc == 0, kc == 15, [w16, cT], [p])
                    k.stt(xo.t[:, 0:n], p.t[:, 0:n], self.modv.t[:, 32 + fc, r:r + 1], xo.t[:, 0:n], ALU.mult, ALU.add,
                          [p, self.modv, xo], [xo])
                    k.dma("sp", xv[:, fc, t0:t0 + n], xo.t[:, 0:n], [xo], [])

    MLP_BLOCKS = [[(0, 256), (256, 256), (512, 256)], [(768, 256), (1024, 256), (1280, 256)],
                  [(1536, 256), (1792, 256), (2048, 256)]]

    def stage_wcast(self, l):
        k = self.k
        with k.scope():
            f32r = k.ring("wc32", [128, 8192], F32, 2)
            b16r = k.ring("wc16", [128, 8192], BF16, 2)
            engs = ["dve", "act", "pool"]
            ei = 0
            w1tv = self.W1t.rearrange("fc p kc j -> p fc kc j")
            for kc in range(16):
                a = f32r.next()
                k.dma("sp", a.t[:], self.mlp_w1[l, kc * 128:(kc + 1) * 128, :], [], [a])
                bb = b16r.next()
                for q in range(4):
                    k.cp(bb.t[:, q * 2048:(q + 1) * 2048], a.t[:, q * 2048:(q + 1) * 2048], [a], [bb], eng=engs[ei % 3])
                    ei += 1
                k.dma("sp", w1tv[:, :, kc, :], bb.t[:].rearrange("p (fc j) -> p fc j", j=128), [bb], [])
            w2v = self.mlp_w2[l].rearrange("(fg fc p) d -> fg fc p d", fc=16, p=128)
            w2tv = self.W2t.rearrange("fg dc p fc j -> fg fc p dc j")
            for fg in range(4):
                for f4 in range(4):
                    a = f32r.next()
                    for f in range(4):
                        k.dma("sp", a.t[:, f * 2048:(f + 1) * 2048], w2v[fg, f4 * 4 + f], [], [a])
                    bb = b16r.next()
                    for q in range(4):
                        k.cp(bb.t[:, q * 2048:(q + 1) * 2048], a.t[:, q * 2048:(q + 1) * 2048], [a], [bb], eng=engs[ei % 3])
                        ei += 1
                    for f in range(4):
                        k.dma("sp", w2tv[fg, f4 * 4 + f], bb.t[:, f * 2048:(f + 1) * 2048].rearrange("p (dc j) -> p dc j", j=128), [bb], [])

    def wcast_gen(self, l, f32r, b16r):
        k = self.k
        w1tv = self.W1t.rearrange("fc p kc j -> p fc kc j")
        w2v = self.mlp_w2[l].rearrange("(fg fc p) d -> fg fc p d", fc=16, p=128)
        w2tv = self.W2t.rearrange("fg dc p fc j -> fg fc p dc j")
        pieces = []
        for kc in range(16):
            for q in range(4):
                pieces.append((self.mlp_w1[l, kc * 128:(kc + 1) * 128, q * 2048:(q + 1) * 2048], w1tv[:, q * 16:(q + 1) * 16, kc, :]))
        for fg in range(4):
            for fc in range(16):
                pieces.append((w2v[fg, fc], w2tv[fg, fc]))
        loaded = {}

        def load(i):
            a = f32r.next()
            k.dma("sp", a.t[:], pieces[i][0], [], [a])
            loaded[i] = a

        load(0)
        for i in range(len(pieces)):
            if i + 1 < len(pieces):
                load(i + 1)
            a = loaded.pop(i)
            bb = b16r.next()
            k.cp(bb.t[:], a.t[:], [a], [bb], eng="act")
            k.dma("sp", pieces[i][1], bb.t[:].rearrange("p (c j) -> p c j", j=128), [bb], [])
            yield

    def stage_mlp(self, l, b):
        k = self.k
        with k.scope():
            hTs = [k.sb("hT2", [128, 16, 768], BF16) for _ in range(2)]
            oacc = k.sb("oacc", [128, 16, 768], F32)
            aTr = k.ring("aT", [128, 16, 768], BF16, 1)
            w1r = k.ring("w1s", [128, 16, 128], BF16, 3)
            w2r = k.ring("w2s", [128, 16, 128], BF16, 3)
            rl = k.ring("rl", [128, 512], F32, 4)
            xr = k.ring("xo", [128, 512], F32, 3)
            HR = {"x": k.ring("hx", [128, 16, 256], F32, 1), "sq": k.ring("hsq", [128, 256], F32, 3),
                  "rs": k.ring("hrs", [128, 256], F32, 2), "tm": k.ring("htm", [128, 256], F32, 3),
                  "ps": k.psring("hps", 1)}
            pp = k.psring("pp", 3)
            pq = k.psring("pq", 2)
            xv = self.xs[b].rearrange("(c p) t -> p c t", p=128)
            MM = ((0, 512), (512, 256))
            blocks = self.MLP_BLOCKS
            for (t0, n) in blocks[0]:
                self.hT_piece(hTs[0], b, self.g2, 48, t0, n, t0 - blocks[0][0][0], HR)
            for bi, subs in enumerate(blocks):
                base = subs[0][0]
                hT = hTs[bi % 2]
                nxt = blocks[bi + 1] if bi + 1 < len(blocks) else None
                for fg in range(4):
                    aT = aTr.next()
                    for fc in range(16):
                        w = w1r.next()
                        k.dma("sp", w.t[:], self.W1t[fg * 16 + fc], [], [w])
                        for (o, n) in MM:
                            p = pp.next()
                            for kc in range(16):
                                k.mm(p.t[:, 0:n], w.t[:, kc, :], hT.t[:, kc, o:o + n], kc == 0, kc == 15, [w, hT], [p])
                            rr = rl.next()
                            k.act(rr.t[:, 0:n], p.t[:, 0:n], AF.Relu, [p], [rr])
                            k.tt(aT.t[:, fc, o:o + n], rr.t[:, 0:n], rr.t[:, 0:n], ALU.mult, [rr], [aT])
                    if nxt is not None and fg < 3:
                        (t0n, nn) = nxt[fg]
                        self.hT_piece(hTs[(bi + 1) % 2], b, self.g2, 48, t0n, nn, t0n - nxt[0][0], HR)
                    for dc in range(16):
                        w = w2r.next()
                        k.dma("sp", w.t[:], self.W2t[fg, dc], [], [w])
                        for (o, n) in MM:
                            p = pq.next()
                            for fc in range(16):
                                k.mm(p.t[:, 0:n], w.t[:, fc, :], aT.t[:, fc, o:o + n], fc == 0, fc == 15, [w, aT], [p])
                            if fg == 0:
                                k.cp(oacc.t[:, dc, o:o + n], p.t[:, 0:n], [p], [oacc], eng="act")
                            elif fg < 3:
                                k.tt(oacc.t[:, dc, o:o + n], p.t[:, 0:n], oacc.t[:, dc, o:o + n], ALU.add, [p, oacc], [oacc])
                            else:
                                tm = rl.next()
                                k.tt(tm.t[:, 0:n], p.t[:, 0:n], oacc.t[:, dc, o:o + n], ALU.add, [p, oacc], [tm])
                                xo = xr.next()
                                k.dma("sp", xo.t[:, 0:n], xv[:, dc, base + o:base + o + n], [], [xo])
                                a0 = base + o
                                cuts = [a0] + ([CTX] if a0 < CTX < a0 + n else []) + [a0 + n]
                                for ci in range(len(cuts) - 1):
                                    c0, c1 = cuts[ci] - a0, cuts[ci + 1] - a0
                                    r = 2 if cuts[ci] < CTX else b
                                    k.stt(xo.t[:, c0:c1], tm.t[:, c0:c1], self.modv.t[:, 80 + dc, r:r + 1], xo.t[:, c0:c1],
                                          ALU.mult, ALU.add, [tm, self.modv, xo], [xo])
                                k.dma("sp", xv[:, dc, base + o:base + o + n], xo.t[:, 0:n], [xo], [])

    def declare_mixer_inputs(self):
        L = DEPTH
        self.s5v = self.din("s5v", [L, 128, 192])
        self.s5A = self.din("s5A", [L, 128, 64, 16])
        self.s5B = self.din("s5B", [L, 128, 64, 16])
        self.s5CA = self.din("s5CA", [L, 128, 64, 16])
        self.s5CB = self.din("s5CB", [L, 128, 64, 16])
        self.s5w = self.din("s5w", [L, 128, 8])
        self.glu_w = self.din("glu_w", [L, 512, 512])
        self.nidx8 = self.din("nidx8", [2, 128, T // 8])
        self.selu = self.din("selu", [128, 64, 128], BF16)
        self.selt = self.din("selt", [128, 64, 128], BF16)
        self.bmask = self.din("bmask", [128, 256])
        self.lruv = self.din("lruv", [L, 128, 44])
        self.lru_wa = self.din("lru_wa", [L, 2, 4, 128, 128])
        self.lru_wx = self.din("lru_wx", [L, 2, 4, 128, 128])
        self.ygd = self.dscr("ygd", [2, 512, T])
        self.mlb = self.din("mlb", [L, 128, 16])
        self.onw = self.din("onw", [L, 128, 512])
        self.selc = self.din("selc", [16, 2048])
        self.mlav = self.din("mlav", [L, 128, 896])
        self.w_q_up = self.din("w_q_up", [L, 384, 768])
        self.w_kv_up = self.din("w_kv_up", [L, 128, 1024])
        self.ropec = self.din("ropec", [128, NT, 32])
        self.ropes = self.din("ropes", [128, NT, 32])

    def setup_mixer_consts(self):
        k = self.k
        self.sgn = self.C.t[:, 896:897]
        self.oneT = k.sb("oneT", [128, 1], F32)
        k.op("dve", lambda e: e.memset(self.oneT.t[:], 1.0), [], [self.oneT])
        self.hpiT = k.sb("hpiT", [128, 1], F32)
        k.op("dve", lambda e: e.memset(self.hpiT.t[:], float(np.pi / 2)), [], [self.hpiT])

    def mixers(self, l):
        which = self.dbg.get("mixers", ("s5", "lru", "mlstm", "mla"))
        if "s5" in which:
            self.mixer_s5(l)
        if "lru" in which:
            self.mixer_lru(l)
        if "mlstm" in which:
            self.mixer_mlstm(l)
        if "mla" in which:
            self.mixer_mla(l)

    def frac_centered(self, out, u, ki, tmp, n, bufs):
        k = self.k
        k.cp(ki, u, bufs, bufs)
        k.tt(tmp, u, ki, ALU.subtract, bufs, bufs)
        k.stt(out, tmp, 0.5, tmp, ALU.is_gt, ALU.subtract, bufs, bufs)
        k.stt(out, out, 0.5, out, ALU.is_gt, ALU.subtract, bufs, bufs)

    def sincos(self, sin_out, cos_out, r, tmp, bufs):
        k = self.k
        k.act(sin_out, r, AF.Sin, bufs, bufs, scale=TWO_PI)
        k.stt(tmp, r, 0.25, r, ALU.is_gt, ALU.subtract, bufs, bufs)
        k.act(cos_out, tmp, AF.Sin, bufs + [self.hpiT], bufs, scale=-TWO_PI, bias=self.hpiT.t[:, 0:1])

    def gelu_tanh(self, out, y, t1, s1, bufs):
        k = self.k
        k.tt(t1, y, y, ALU.mult, bufs, bufs)
        k.ts(t1, t1, 0.044715, 1.0, ALU.mult, ALU.add, bufs, bufs)
        k.tt(t1, t1, y, ALU.mult, bufs, bufs)
        k.act(s1, t1, AF.Sigmoid, bufs, bufs, scale=1.5957691216057308)
        k.tt(out, y, s1, ALU.mult, bufs, bufs)

    def mixer_s5(self, l):
        k = self.k
        NK = T // 8
        KC = CTX // 8
        with k.scope():
            pv = k.sb("s5pv", [128, 192], F32)
            k.dma("sp", pv.t[:], self.s5v[l], [], [pv])
            A = k.sb("s5A", [128, 64, 16], F32)
            Bm = k.sb("s5B", [128, 64, 16], F32)
            CA = k.sb("s5CA", [128, 64, 16], F32)
            CB = k.sb("s5CB", [128, 64, 16], F32)
            k.dma("sp", A.t[:], self.s5A[l], [], [A])
            k.dma("sp", Bm.t[:], self.s5B[l], [], [Bm])
            k.dma("sp", CA.t[:], self.s5CA[l], [], [CA])
            k.dma("sp", CB.t[:], self.s5CB[l], [], [CB])
            sw = k.sb("s5w", [128, 8], F32)
            k.dma("sp", sw.t[:], self.s5w[l], [], [sw])
            nid = k.sb("nidx8", [128, 2, NK], F32)
            for d in range(2):
                k.dma("sp", nid.t[:, d, :], self.nidx8[d], [], [nid])
            selu = k.sb("selu", [128, 64, 128], BF16)
            selt = k.sb("selt", [128, 64, 128], BF16)
            k.dma("sp", selu.t[:], self.selu[:, :, :], [], [selu])
            k.dma("sp", selt.t[:], self.selt[:, :, :], [], [selt])
            bmask = k.sb("bmask", [128, 256], F32)
            k.dma("sp", bmask.t[:], self.bmask[:, :], [], [bmask])
            W = k.sb("s5work", [128, 16, 64], F32)
            WI = k.sb("s5worki", [128, 64], I32)
            Wb = [W]
            lr, li, dt, mag, ang, fT, sn, cs_, t0_, t1_, fr, fi, den, lrdt, f8, ar1 = [W.t[:, i, :] for i in range(16)]
            k.ts(lr, pv.t[:, 0:64], -1e-4, None, ALU.min, None, [pv], Wb)
            k.cp(li, pv.t[:, 64:128], [pv], Wb)
            k.act(dt, pv.t[:, 128:192], AF.Exp, [pv], Wb)
            k.tt(lrdt, lr, dt, ALU.mult, Wb, Wb)
            k.act(mag, lrdt, AF.Exp, Wb, Wb)
            k.tt(ang, li, dt, ALU.mult, Wb, Wb)
            k.ts(t0_, ang, 1.0 / TWO_PI, None, ALU.mult, None, Wb, Wb)
            self.frac_centered(fT, t0_, WI.t[:], t1_, 64, Wb + [WI])
            self.sincos(sn, cs_, fT, t1_, Wb)
            k.tt(t0_, mag, cs_, ALU.mult, Wb, Wb)
            k.ts(ar1, t0_, -1.0, None, ALU.add, None, Wb, Wb)
            k.tt(t1_, mag, sn, ALU.mult, Wb, Wb)
            k.tt(den, lr, lr, ALU.mult, Wb, Wb)
            k.tt(t0_, li, li, ALU.mult, Wb, Wb)
            k.tt(den, den, t0_, ALU.add, Wb, Wb)
            k.op("dve", lambda e: e.reciprocal(out=den, in_=den), Wb, Wb)
            k.tt(fr, ar1, lr, ALU.mult, Wb, Wb)
            k.tt(t0_, t1_, li, ALU.mult, Wb, Wb)
            k.tt(fr, fr, t0_, ALU.add, Wb, Wb)
            k.tt(fr, fr, den, ALU.mult, Wb, Wb)
            k.tt(fi, t1_, lr, ALU.mult, Wb, Wb)
            k.tt(t0_, ar1, li, ALU.mult, Wb, Wb)
            k.tt(fi, fi, t0_, ALU.subtract, Wb, Wb)
            k.tt(fi, fi, den, ALU.mult, Wb, Wb)
            nsgn = k.sb("nsgn", [128, 1], F32)
            k.ts(nsgn.t[:], self.sgn, -1.0, None, ALU.mult, None, [self.C], [nsgn])
            M8 = k.sb("mag8", [128, 64], F32)
            k.act(M8.t[:], lrdt, AF.Exp, Wb, [M8], scale=8.0)
            k.ts(t0_, fT, 8.0, None, ALU.mult, None, Wb, Wb)
            self.frac_centered(f8, t0_, WI.t[:], t1_, 64, Wb + [WI])
            PR = k.sb("PR", [128, 16, 64], F32)
            PI = k.sb("PI", [128, 16, 64], F32)
            PRN = k.sb("PRN", [128, 16, 64], F32)
            PRM = k.sb("PRM", [128, 16, 64], F32)
            PIS = k.sb("PIS", [128, 16, 64], F32)
            PIM = k.sb("PIM", [128, 16, 64], F32)
            GR = k.sb("GR", [128, 16, 64], F32)
            GI = k.sb("GI", [128, 16, 64], F32)
            GRN = k.sb("GRN", [128, 16, 64], F32)
            GIS = k.sb("GIS", [128, 16, 64], F32)
            GIM = k.sb("GIM", [128, 16, 64], F32)
            PWs = [PR, PI, PRN, PRM, PIS, PIM, GR, GI, GRN, GIS, GIM]
            for m in range(-7, 9):
                mi = m + 7
                k.act(t0_, lrdt, AF.Exp, Wb, Wb, scale=float(m))
                k.ts(ang, fT, float(m), None, ALU.mult, None, Wb, Wb)
                self.frac_centered(den, ang, WI.t[:], t1_, 64, Wb + [WI])
                self.sincos(sn, cs_, den, t1_, Wb)
                k.tt(PR.t[:, mi, :], t0_, cs_, ALU.mult, Wb, [PR])
                k.tt(PI.t[:, mi, :], t0_, sn, ALU.mult, Wb, [PI])
                k.ts(PRN.t[:, mi, :], PR.t[:, mi, :], nsgn.t[:, 0:1], None, ALU.mult, None, [PR, nsgn], [PRN])
                k.ts(PRM.t[:, mi, :], PR.t[:, mi, :], -1.0, None, ALU.mult, None, [PR], [PRM])
                k.ts(PIS.t[:, mi, :], PI.t[:, mi, :], self.sgn, None, ALU.mult, None, [PI, self.C], [PIS])
                k.ts(PIM.t[:, mi, :], PI.t[:, mi, :], -1.0, None, ALU.mult, None, [PI], [PIM])
                k.tt(t0_, PR.t[:, mi, :], fr, ALU.mult, [PR] + Wb, Wb)
                k.tt(t1_, PI.t[:, mi, :], fi, ALU.mult, [PI] + Wb, Wb)
                k.tt(GR.t[:, mi, :], t0_, t1_, ALU.subtract, Wb, [GR])
                k.tt(t0_, PR.t[:, mi, :], fi, ALU.mult, [PR] + Wb, Wb)
                k.tt(t1_, PI.t[:, mi, :], fr, ALU.mult, [PI] + Wb, Wb)
                k.tt(GI.t[:, mi, :], t0_, t1_, ALU.add, Wb, [GI])
                k.ts(GRN.t[:, mi, :], GR.t[:, mi, :], nsgn.t[:, 0:1], None, ALU.mult, None, [GR, nsgn], [GRN])
                k.ts(GIS.t[:, mi, :], GI.t[:, mi, :], self.sgn, None, ALU.mult, None, [GI, self.C], [GIS])
                k.ts(GIM.t[:, mi, :], GI.t[:, mi, :], -1.0, None, ALU.mult, None, [GI], [GIM])

            LB = k.sb("LB", [128, 128], F32)
            RC = k.sb("RC", [128, 128], F32)
            LST = k.sb("LST", [128, 128], F32)
            LS2T = k.sb("LS2T", [128, 128], F32)
            W1f = k.sb("W1f", [128, 128], F32)
            W2f = k.sb("W2f", [128, 128], F32)
            tA = k.ring("tA", [128, 8, 16], F32, 6)
            Mi_r = k.ring("Mi", [128, 128], BF16, 2)
            LS_r = k.ring("LS", [128, 128], BF16, 2)
            LS2_r = k.ring("LS2", [128, 128], BF16, 2)
            W1_r = k.ring("W1", [128, 128], BF16, 2)
            W2_r = k.ring("W2", [128, 128], BF16, 2)
            C8 = k.sb("C8", [128, NK], F32)
            S8 = k.sb("S8", [128, NK], F32)
            U8 = k.sb("U8", [128, NK], F32)
            K8 = k.sb("K8", [128, NK], I32)
            R8 = k.sb("R8", [128, NK], F32)
            Ug = [k.sb("Ug", [128, NK], BF16) for _ in range(2)]
            t1r = k.ring("t1", [128, NK], F32, 2)
            btr = k.ring("bt", [128, NK], F32, 2)
            Gr_ = k.ring("G", [128, NK], F32, 2)
            V1r = k.ring("V1", [128, NK], BF16, 2)
            V2r = k.ring("V2", [128, NK], BF16, 2)
            Ysb = [[k.sb("Ysb", [128, NK], BF16) for _ in range(2)] for _ in range(8)]
            ub = [k.sb("ub", [128, T], BF16) for _ in range(2)]
            yacc = [k.sb("yacc", [128, T], F32) for _ in range(2)]
            fin = k.ring("fin", [128, 512], F32, 6)
            wc32 = k.ring("wc32", [128, 2048], F32, 2)
            wc16 = k.ring("wc16", [128, 2048], BF16, 2)
            wgen = self.wcast_gen(l, wc32, wc16)
            pY = [k.ps("pY0"), k.ps("pY1")]
            pP = k.psring("pP", 2)
            pW = k.psring("pW", 2)
            pUn = k.psring("pUn", 2)
            ei = 0

            def build(dst, coefA, coefB, srcA, srcB, mlist, dg):
                mi0 = mlist[0] + 7
                if mlist[1] - mlist[0] == 1:
                    msl = slice(mi0, mi0 + 8)
                else:
                    msl = slice(mi0, (mi0 - 8) if mi0 - 8 >= 0 else None, -1)
                ca = coefA.t[:, msl, dg:dg + 1].to_broadcast([128, 8, 16])
                cb = coefB.t[:, msl, dg:dg + 1].to_broadcast([128, 8, 16])
                sa = srcA.t[:, dg:dg + 1, :].to_broadcast([128, 8, 16])
                sb_ = srcB.t[:, dg:dg + 1, :].to_broadcast([128, 8, 16])
                ta = tA.next()
                tb = tA.next()
                k.tt(ta.t[:], sa, ca, ALU.mult, [srcA, coefA], [ta])
                k.tt(tb.t[:], sb_, cb, ALU.mult, [srcB, coefB], [tb], eng="pool")
                k.tt(dst.t[:].rearrange("p (m c) -> p m c", c=16), ta.t[:], tb.t[:], ALU.add, [ta, tb], [dst])

            for c in range(4):
                for b in range(2):
                    for (t0, n) in TB:
                        u_ = fin.next()
                        k.dma("sp", u_.t[:, 0:n], self.zf[b, c * 128:(c + 1) * 128, t0:t0 + n], [], [u_])
                        k.cp(ub[b].t[:, t0:t0 + n], u_.t[:, 0:n], [u_], [ub[b]], eng="act")
                for j in range(8):
                    g = 8 * c + j
                    for b in range(2):
                        p = pW.next()
                        for s_ in range(8):
                            k.mm(p.t[:, 0:NK], selu.t[:, j * 8 + s_, :], ub[b].t[:, s_:T:8], s_ == 0, s_ == 7, [selu, ub[b]], [p])
                        k.cp(Ug[b].t[:], p.t[:, 0:NK], [p], [Ug[b]], eng="act")
                    for d in range(2):
                        dg = d * 32 + g
                        if d == 0:
                            mLB = [-s_ for s_ in range(8)]
                            mLS = [7 - s_ for s_ in range(8)]
                            mRC = [t_ for t_ in range(8)]
                            mW = [t_ + 1 for t_ in range(8)]
                        else:
                            mLB = [s_ for s_ in range(8)]
                            mLS = [s_ for s_ in range(8)]
                            mRC = [-t_ for t_ in range(8)]
                            mW = [8 - t_ for t_ in range(8)]
                        build(LB, GRN, GIM, A, Bm, mLB, dg)
                        build(RC, PR, PIS, CA, CB, mRC, dg)
                        build(LST, GR, GIS, A, Bm, mLS, dg)
                        build(LS2T, GI, GRN, A, Bm, mLS, dg)
                        build(W1f, PRN, PIM, CA, CB, mW, dg)
                        build(W2f, PIS, PRM, CA, CB, mW, dg)
                        p = pW.next()
                        k.mm(p.t[:, 0:128], LB.t[:], RC.t[:], True, True, [LB, RC], [p])
                        Mi = Mi_r.next()
                        k.tt(Mi.t[:], p.t[:, 0:128], bmask.t[:, d * 128:(d + 1) * 128], ALU.mult, [p, bmask], [Mi])
                        p = pW.next()
                        k.tr(p.t[:, 0:128], LST.t[:], self.identF, [LST, self.C], [p])
                        k.tr(p.t[:, 128:256], LS2T.t[:], self.identF, [LS2T, self.C], [p])
                        LS = LS_r.next()
                        LS2 = LS2_r.next()
                        k.cp(LS.t[:], p.t[:, 0:128], [p], [LS], eng="act")
                        k.cp(LS2.t[:], p.t[:, 128:256], [p], [LS2], eng="act")
                        W1 = W1_r.next()
                        W2 = W2_r.next()
                        k.cp(W1.t[:], W1f.t[:], [W1f], [W1], eng="pool")
                        k.cp(W2.t[:], W2f.t[:], [W2f], [W2], eng="pool")
                        k.ts(U8.t[:], nid.t[:, d, :], f8[:, dg:dg + 1], None, ALU.mult, None, [nid] + Wb, [U8])
                        self.frac_centered(R8.t[:], U8.t[:], K8.t[:], U8.t[:], NK, [U8, K8, R8])
                        self.sincos(S8.t[:], C8.t[:], R8.t[:], U8.t[:], [R8, U8, S8, C8])
                        for b in range(2):
                            next(wgen, None)
                            p1 = pP.next()
                            p2 = pP.next()
                            k.mm(p1.t[:, 0:NK], LS.t[:], Ug[b].t[:], True, True, [LS, Ug[b]], [p1])
                            k.mm(p2.t[:, 0:NK], LS2.t[:], Ug[b].t[:], True, True, [LS2, Ug[b]], [p2])
                            t1 = t1r.next()
                            bt = btr.next()
                            k.tt(t1.t[:], p1.t[:, 0:NK], C8.t[:], ALU.mult, [p1, C8], [t1])
                            k.tt(bt.t[:], p2.t[:, 0:NK], S8.t[:], ALU.mult, [p2, S8], [bt])
                            k.tt(bt.t[:], bt.t[:], t1.t[:], ALU.add, [bt, t1], [bt])
                            G = Gr_.next()
                            rm = M8.t[:, dg:dg + 1]
                            if d == 0:
                                k.scan(G.t[:, 0:KC], rm.to_broadcast([128, KC]), bt.t[:, 0:KC], 0.0, [bt, M8], [G])
                                k.scan(G.t[:, KC:NK], rm.to_broadcast([128, NK - KC]), bt.t[:, KC:NK], G.t[:, KC - 1:KC], [bt, G, M8], [G])
                            else:
                                k.scan(G.t[:, 0:KC][:, ::-1], rm.to_broadcast([128, KC]), bt.t[:, 0:KC][:, ::-1], 0.0, [bt, M8], [G])
                                k.scan(G.t[:, KC:NK][:, ::-1], rm.to_broadcast([128, NK - KC]), bt.t[:, KC:NK][:, ::-1], G.t[:, 0:1], [bt, G, M8], [G])
                            V1 = V1r.next()
                            V2 = V2r.next()
                            k.tt(V1.t[:], G.t[:], C8.t[:], ALU.mult, [G, C8], [V1])
                            k.tt(V2.t[:], G.t[:], S8.t[:], ALU.mult, [G, S8], [V2])
                            py = pY[b]
                            k.mm(py.t[:, 0:NK], Mi.t[:], Ug[b].t[:], d == 0, False, [Mi, Ug[b]], [py])
                            if d == 0:
                                segs = [(1, NK, 0)]
                            else:
                                segs = [(0, KC - 1, 1), (KC, NK - 1, KC + 1), (NK - 1, NK, 0)]
                            for si, (o0, o1, s0) in enumerate(segs):
                                n_ = o1 - o0
                                lastmm = (d == 1 and si == len(segs) - 1)
                                k.mm(py.t[:, o0:o1], W1.t[:], V1.t[:, s0:s0 + n_], False, False, [W1, V1], [py])
                                k.mm(py.t[:, o0:o1], W2.t[:], V2.t[:, s0:s0 + n_], False, lastmm, [W2, V2], [py])
                    for b in range(2):
                        k.cp(Ysb[j][b].t[:], pY[b].t[:, 0:NK], [pY[b]], [Ysb[j][b]], eng=("act" if b else "dve"))
                for b in range(2):
                    for bb in range(5):
                        nk = 64 if bb < 4 else 32
                        p = pUn.next()
                        for t_ in range(8):
                            for j in range(8):
                                k.mm(p.t[:, t_:8 * nk:8], selt.t[:, j * 8 + t_, :], Ysb[j][b].t[:, 64 * bb:64 * bb + nk], j == 0, j == 7,
                                     [selt, Ysb[j][b]], [p])
                        k.cp(yacc[b].t[:, 512 * bb:512 * bb + 8 * nk], p.t[:, 0:8 * nk], [p], [yacc[b]], eng=("act" if bb % 2 else "dve"))
                for b in range(2):
                    for (t0, n) in TB:
                        u_ = fin.next()
                        k.dma("sp", u_.t[:, 0:n], self.zf[b, c * 128:(c + 1) * 128, t0:t0 + n], [], [u_])
                        y_ = fin.next()
                        k.stt(y_.t[:, 0:n], u_.t[:, 0:n], sw.t[:, c:c + 1], yacc[b].t[:, t0:t0 + n], ALU.mult, ALU.add,
                              [u_, sw, yacc[b]], [y_])
                        o_ = fin.next()
                        a_ = fin.next()
                        b_ = fin.next()
                        self.gelu_tanh(o_.t[:, 0:n], y_.t[:, 0:n], a_.t[:, 0:n], b_.t[:, 0:n], [o_, y_, a_, b_])
                        k.dma("sp", self.ygd[b, c * 128:(c + 1) * 128, t0:t0 + n], o_.t[:, 0:n], [o_], [])
            for _ in wgen:
                pass
            self.wcast_done = True
        with k.scope():
            sw = k.sb("s5w", [128, 8], F32)
            k.dma("sp", sw.t[:], self.s5w[l], [], [sw])
            gw32 = k.sb("gw32", [128, 4, 512], F32)
            gw = k.sb("gw", [128, 4, 512], BF16)
            k.dma("sp", gw32.t[:], self.glu_w[l].rearrange("(kc p) o -> p kc o", p=128), [], [gw32])
            k.cp(gw.t[:], gw32.t[:], [gw32], [gw], eng="pool")
            yg = k.sb("yg", [128, 4, T], F32)
            ygb = k.sb("ygb", [128, 4, T], BF16)
            sg = k.ring("sg", [128, 512], F32, 2)
            ob = k.ring("ob", [128, 512], BF16, 3)
            pp = k.psring("pg", 3)
            for b in range(2):
                for c in range(4):
                    k.dma("sp", yg.t[:, c, :], self.ygd[b, c * 128:(c + 1) * 128, :], [], [yg])
                    k.cp(ygb.t[:, c, :], yg.t[:, c, :], [yg], [ygb], eng="act")
                for co in range(4):
                    for (t0, n) in TB:
                        p = pp.next()
                        for kc in range(4):
                            k.mm(p.t[:, 0:n], gw.t[:, kc, co * 128:(co + 1) * 128], ygb.t[:, kc, t0:t0 + n], kc == 0, kc == 3, [gw, ygb], [p])
                        s_ = sg.next()
                        k.act(s_.t[:, 0:n], p.t[:, 0:n], AF.Sigmoid, [p, sw], [s_], bias=sw.t[:, 4 + co:5 + co])
                        o = ob.next()
                        k.tt(o.t[:, 0:n], yg.t[:, co, t0:t0 + n], s_.t[:, 0:n], ALU.mult, [yg, s_], [o])
                        k.dma("sp", self.cc[b, co * 128:(co + 1) * 128, t0:t0 + n], o.t[:, 0:n], [o], [])

    def mixer_lru(self, l):
        k = self.k
        with k.scope():
            lv = k.sb("lruv", [128, 44], F32)
            k.dma("sp", lv.t[:], self.lruv[l], [], [lv])
            cw = lv.t[:, 0:16].rearrange("p (c j) -> p c j", j=4)
            cb = lv.t[:, 16:20]
            ba = lv.t[:, 20:28].rearrange("p (d c) -> p d c", c=4)
            bx = lv.t[:, 28:36].rearrange("p (d c) -> p d c", c=4)
            lam = lv.t[:, 36:44]
            sp = k.sb("lrusp", [128, 16], F32)
            k.act(sp.t[:, 0:8], lam, AF.Exp, [lv], [sp], scale=-1.0)
            k.act(sp.t[:, 0:8], sp.t[:, 0:8], AF.Ln, [sp, self.oneT], [sp], bias=self.oneT.t[:, 0:1])
            k.ts(sp.t[:, 8:16], sp.t[:, 0:8], -16.0, None, ALU.mult, None, [sp], [sp])
            k.ts(sp.t[:, 0:8], sp.t[:, 0:8], -8.0, None, ALU.mult, None, [sp], [sp])
            wa32 = k.sb("wa32", [128, 8, 128], F32)
            wx32 = k.sb("wx32", [128, 8, 128], F32)
            wa = k.sb("wa", [128, 8, 128], BF16)
            wx = k.sb("wx", [128, 8, 128], BF16)
            k.dma("sp", wa32.t[:], self.lru_wa[l].rearrange("d n c o -> c (d n) o"), [], [wa32])
            k.dma("sp", wx32.t[:], self.lru_wx[l].rearrange("d n c o -> c (d n) o"), [], [wx32])
            k.cp(wa.t[:], wa32.t[:], [wa32], [wa])
            k.cp(wx.t[:], wx32.t[:], [wx32], [wx])
            x = k.sb("lx", [128, T], F32)
            gt = k.sb("lg", [128, T], F32)
            xs = k.sb("lxs", [128, T], F32)
            xsb = k.sb("lxsb", [128, T], BF16)
            r_ = k.sb("lr", [128, T], F32)
            i_ = k.sb("li", [128, T], F32)
            a_ = k.sb("la", [128, T], F32)
            q_ = k.sb("lq", [128, T], F32)
            h_ = k.sb("lh", [128, T], F32)
            ys = k.sb("lys", [128, T], F32)
            ob = k.sb("lob", [128, T], BF16)
            pp = k.psring("pl", 4)
            for b in range(2):
                for c in range(4):
                    k.dma("sp", x.t[:], self.zf[b, 1536 + c * 128:1536 + (c + 1) * 128, :], [], [x])
                    k.dma("sp", gt.t[:], self.zf[b, 2048 + c * 128:2048 + (c + 1) * 128, :], [], [gt])
                    k.ts(xs.t[:], x.t[:], cw[:, c, 2:3], cb[:, c:c + 1], ALU.mult, ALU.add, [x, lv], [xs])
                    for jtap in (0, 1, 3):
                        o = jtap - 2
                        for (r0, r1) in ((0, CTX), (CTX, T)):
                            a0 = r0 + max(0, -o)
                            a1 = r1 - max(0, o)
                            k.stt(xs.t[:, a0:a1], x.t[:, a0 + o:a1 + o], cw[:, c, jtap:jtap + 1], xs.t[:, a0:a1],
                                  ALU.mult, ALU.add, [x, lv, xs], [xs])
                    k.cp(xsb.t[:], xs.t[:], [xs], [xsb], eng="act")
                    for d in range(2):
                        for (t0, n) in TB:
                            p = pp.next()
                            k.mm(p.t[:, 0:n], wa.t[:, d * 4 + c, :], xsb.t[:, t0:t0 + n], True, True, [wa, xsb], [p])
                            k.act(r_.t[:, t0:t0 + n], p.t[:, 0:n], AF.Sigmoid, [p, lv], [r_], bias=ba[:, d, c:c + 1])
                            p = pp.next()
                            k.mm(p.t[:, 0:n], wx.t[:, d * 4 + c, :], xsb.t[:, t0:t0 + n], True, True, [wx, xsb], [p])
                            k.act(i_.t[:, t0:t0 + n], p.t[:, 0:n], AF.Sigmoid, [p, lv], [i_], bias=bx[:, d, c:c + 1])
                        dc = d * 4 + c
                        k.act(a_.t[:], r_.t[:], AF.Exp, [r_, sp], [a_], scale=sp.t[:, dc:dc + 1])
                        k.act(q_.t[:], r_.t[:], AF.Exp, [r_, sp], [q_], scale=sp.t[:, 8 + dc:9 + dc])
                        k.act(q_.t[:], q_.t[:], AF.Sqrt, [q_, self.oneT], [q_], scale=-1.0, bias=self.oneT.t[:, 0:1])
                        k.tt(q_.t[:], q_.t[:], i_.t[:], ALU.mult, [q_, i_], [q_])
                        k.tt(q_.t[:], q_.t[:], xs.t[:], ALU.mult, [q_, xs], [q_])
                        if d == 0:
                            k.scan(h_.t[:, 0:CTX], a_.t[:, 0:CTX], q_.t[:, 0:CTX], 0.0, [a_, q_], [h_])
                            k.scan(h_.t[:, CTX:T], a_.t[:, CTX:T], q_.t[:, CTX:T], h_.t[:, CTX - 1:CTX], [a_, q_, h_], [h_])
                            k.cp(ys.t[:], h_.t[:], [h_], [ys], eng="pool")
                        else:
                            k.scan(h_.t[:, 0:CTX][:, ::-1], a_.t[:, 0:CTX][:, ::-1], q_.t[:, 0:CTX][:, ::-1], 0.0, [a_, q_], [h_])
                            k.scan(h_.t[:, CTX:T][:, ::-1], a_.t[:, CTX:T][:, ::-1], q_.t[:, CTX:T][:, ::-1], h_.t[:, 0:1], [a_, q_, h_], [h_])
                            k.tt(ys.t[:], ys.t[:], h_.t[:], ALU.add, [ys, h_], [ys])
                    self.gelu_tanh(h_.t[:], gt.t[:], a_.t[:], q_.t[:], [h_, gt, a_, q_])
                    k.tt(ob.t[:], ys.t[:], h_.t[:], ALU.mult, [ys, h_], [ob])
                    k.dma("sp", self.cc[b, 1536 + c * 128:1536 + (c + 1) * 128, :], ob.t[:], [ob], [])

    def mixer_mlstm(self, l):
        k = self.k
        KS = 128 ** -0.5
        TRI3 = self.C.t[:, 256:640]
        MASK = [self.C.t[:, 640:768], self.C.t[:, 768:896]]
        for b in range(2):
            with k.scope():
                Hacc = k.sb("Hacc", [128, NT, 512], F32)
                k.op("pool", lambda e: e.memset(Hacc.t[:], 0.0), [], [Hacc])
                with k.scope():
                    mlb = k.sb("mlb", [128, 16], F32)
                    k.dma("sp", mlb.t[:], self.mlb[l], [], [mlb])
                    sel = k.sb("sel", [16, 2048], F32)
                    nsel = k.sb("nsel", [16, 2048], F32)
                    k.dma("sp", sel.t[:], self.selc[:, :], [], [sel])
                    k.ts(nsel.t[:], sel.t[:], -1.0, None, ALU.mult, None, [sel], [nsel])
                    QT = k.sb("QT", [128, 4, T], BF16)
                    KT = k.sb("KT", [128, 4, T], BF16)
                    Kt = k.sb("Kt", [128, NT, 512], BF16)
                    Va = k.sb("Va", [128, NT, 4, 129], BF16)
                    G16 = k.sb("G16", [128, NT, 16], F32)
                    R = k.sb("R", [16, NT, 384], F32)
                    with k.scope():
                        st = k.ring("st", [128, T], F32, 2)
                        zr = k.ring("zr", [128, 1552], F32, 2)
                        pr = k.psring("pr", 2)
                        for h in range(4):
                            s_ = st.next()
                            k.dma("sp", s_.t[:], self.zf[b, 512 + h * 128:512 + (h + 1) * 128, :], [], [s_])
                            k.cp(QT.t[:, h, :], s_.t[:], [s_], [QT], eng="act")
                            s_ = st.next()
                            k.dma("sp", s_.t[:], self.zf[b, 1024 + h * 128:1024 + (h + 1) * 128, :], [], [s_])
                            k.act(KT.t[:, h, :], s_.t[:], AF.Copy, [s_], [KT], scale=KS)
                        k.op("pool", lambda e: e.memset(Va.t[:], 1.0), [], [Va])
                        for tt in range(NT):
                            z = zr.next()
                            k.dma("sp", z.t[:], self.zt[b, tt * 128:(tt + 1) * 128, 0:1552], [], [z])
                            k.act(Kt.t[:, tt, :], z.t[:, 0:512], AF.Copy, [z], [Kt], scale=KS)
                            k.cp(Va.t[:, tt, :, 0:128], z.t[:, 512:1024].rearrange("p (h e) -> p h e", h=4), [z], [Va])
                            k.tt(G16.t[:, tt, :], z.t[:, 1536:1552], mlb.t[:], ALU.add, [z, mlb], [G16])
                        for d in range(2):
                            gv = G16.t[:, :, d * 8 + 4:d * 8 + 8]
                            k.act(gv, gv, AF.Exp, [G16], [G16], scale=-1.0)
                            k.act(gv, gv, AF.Ln, [G16, self.oneT], [G16], bias=self.oneT.t[:, 0:1])
                            k.ts(gv, gv, -1.0, None, ALU.mult, None, [G16], [G16])
                        for tt in range(NT):
                            p = pr.next()
                            k.mm(p.t[0:16, 0:384], G16.t[:, tt, :], TRI3, True, True, [G16, self.C], [p])
                            k.cp(R.t[:, tt, :], p.t[0:16, 0:384], [p], [R], eng="act")
                    CT32_ = {}
                    CTb_ = {}
                    for d in range(2):
                        for h in range(4):
                            CT32_[d, h] = k.sb("CT32", [128, 129], F32)
                            CTb_[d, h] = k.sb("CTb", [128, 129], BF16)
                            k.op("dve", lambda e: e.memset(CT32_[d, h].t[:], 0.0), [], [CT32_[d, h]])
                            k.op("dve", lambda e: e.memset(CTb_[d, h].t[:], 0.0), [], [CTb_[d, h]])
                    EDr = k.ring("ED", [128, 128], F32, 4)
                    EBr = k.ring("EB", [128, 128], F32, 4)
                    STr = k.ring("ST", [128, 128], BF16, 4)
                    QSr = k.ring("QS", [128, 128], BF16, 4)
                    VWr = k.ring("VW", [128, 129], BF16, 4)
                    dnr = k.ring("dn", [128, 2], F32, 4)
                    pD = k.psring("pD", 2)
                    pB = k.psring("pB", 1)
                    pS = k.psring("pS", 2)
                    pN = k.psring("pN", 2)
                    pC = k.psring("pC", 1)
                    orders = [list(range(NT)), [1, 0] + list(range(NT - 1, 1, -1))]
                    for step in range(NT):
                        for d in range(2):
                            tt = orders[d][step]
                            bsl = slice(0, 128) if d == 0 else slice(128, 256)
                            last = 127 if d == 0 else 0
                            for h in range(4):
                                CT32 = CT32_[d, h]
                                CTb = CTb_[d, h]
                                kli = d * 8 + h
                                klf = d * 8 + 4 + h
                                SLI = sel.t[0:16, kli * 128:(kli + 1) * 128]
                                SLF = sel.t[0:16, klf * 128:(klf + 1) * 128]
                                NLF = nsel.t[0:16, klf * 128:(klf + 1) * 128]
                                tsl = slice(tt * 128, (tt + 1) * 128)
                                Rb = R.t[0:16, tt, bsl]
                                Rg = R.t[0:16, tt, 256:384]
                                pd_ = pD.next()
                                k.mm(pd_.t[:, 0:128], Rg, SLI, True, False, [R, sel], [pd_])
                                k.mm(pd_.t[:, 0:128], Rb, NLF, False, False, [R, nsel], [pd_])
                                k.mm(pd_.t[:, 0:128], SLF, Rb, False, False, [R, sel], [pd_])
                                k.mm(pd_.t[:, 0:128], self.identF, MASK[d], False, True, [self.C], [pd_])
                                ED = EDr.next()
                                k.act(ED.t[:], pd_.t[:, 0:128], AF.Exp, [pd_], [ED])
                                pb_ = pB.next()
                                k.mm(pb_.t[:, 0:128], SLF, Rb, True, True, [R, sel], [pb_])
                                EB = EBr.next()
                                k.act(EB.t[:], pb_.t[:, 0:128], AF.Exp, [pb_], [EB])
                                ps_ = pS.next()
                                k.mm(ps_.t[:, 0:128], KT.t[:, h, tsl], QT.t[:, h, tsl], True, True, [KT, QT], [ps_])
                                ST = STr.next()
                                k.tt(ST.t[:], ps_.t[:, 0:128], ED.t[:], ALU.mult, [ps_, ED], [ST])
                                QS = QSr.next()
                                k.tt(QS.t[:], QT.t[:, h, tsl], EB.t[:], ALU.mult, [QT, EB], [QS])
                                pn_ = pN.next()
                                k.mm(pn_.t[:, 0:129], QS.t[:], CTb.t[:], True, False, [QS, CTb], [pn_])
                                k.mm(pn_.t[:, 0:129], ST.t[:], Va.t[:, tt, h, :], False, True, [ST, Va], [pn_])
                                dn = dnr.next()
                                k.act(dn.t[:, 0:1], pn_.t[:, 128:129], AF.Abs, [pn_], [dn])
                                k.ts(dn.t[:, 0:1], dn.t[:, 0:1], 1.0, None, ALU.max, None, [dn], [dn])
                                k.op("dve", lambda e: e.reciprocal(out=dn.t[:, 1:2], in_=dn.t[:, 0:1]), [dn], [dn])
                                hs = Hacc.t[:, tt, h * 128:(h + 1) * 128]
                                k.stt(hs, pn_.t[:, 0:128], dn.t[:, 1:2], hs, ALU.mult, ALU.add, [pn_, dn, Hacc], [Hacc])
                                VW = VWr.next()
                                k.act(VW.t[:], Va.t[:, tt, h, :], AF.Identity, [Va, ED], [VW], scale=ED.t[:, last:last + 1])
                                pc_ = pC.next()
                                k.mm(pc_.t[:, 0:129], Kt.t[:, tt, h * 128:(h + 1) * 128], VW.t[:], True, True, [Kt, VW], [pc_])
                                k.stt(CT32.t[:], CT32.t[:], EB.t[:, last:last + 1], pc_.t[:, 0:129], ALU.mult, ALU.add,
                                      [CT32, EB, pc_], [CT32])
                                k.cp(CTb.t[:], CT32.t[:], [CT32], [CTb], eng="act")
                with k.scope():
                    onw = k.sb("onw", [128, 512], F32)
                    k.dma("sp", onw.t[:], self.onw[l], [], [onw])
                    OT = k.sb("OT", [128, 4, T], BF16)
                    mor = k.ring("mo", [128, 512], F32, 2)
                    sqr = k.ring("sqh", [128, 512], F32, 2)
                    ssr = k.ring("ss4", [128, 4], F32, 2)
                    pT = k.psring("pT", 2)
                    for tt in range(NT):
                        H = Hacc.t[:, tt, :]
                        sq = sqr.next()
                        k.tt(sq.t[:], H, H, ALU.mult, [Hacc], [sq])
                        ss = ssr.next()
                        k.op("dve", lambda e: e.tensor_reduce(out=ss.t[:], in_=sq.t[:].rearrange("p (h e) -> p h e", h=4), axis=AX.X, op=ALU.add), [sq], [ss])
                        k.act(ss.t[:], ss.t[:], AF.Sqrt, [ss, self.epsT], [ss], scale=1.0 / 128, bias=self.epsT.t[:, 0:1])
                        k.op("dve", lambda e: e.reciprocal(out=ss.t[:], in_=ss.t[:]), [ss], [ss])
                        for h in range(4):
                            hsl = slice(h * 128, (h + 1) * 128)
                            k.stt(sq.t[:, hsl], H[:, hsl], ss.t[:, h:h + 1], onw.t[:, hsl], ALU.mult, ALU.mult, [Hacc, ss, onw], [sq])
                        mo = mor.next()
                        k.dma("sp", mo.t[:], self.zt[b, tt * 128:(tt + 1) * 128, 1024:1536], [], [mo])
                        k.act(mo.t[:], mo.t[:], AF.Sigmoid, [mo], [mo])
                        k.tt(sq.t[:], sq.t[:], mo.t[:], ALU.mult, [sq, mo], [sq])
                        p = pT.next()
                        for h in range(4):
                            hsl = slice(h * 128, (h + 1) * 128)
                            k.tr(p.t[:, hsl], sq.t[:, hsl], self.identF, [sq, self.C], [p])
                        k.cp(OT.t[:, :, tt * 128:(tt + 1) * 128], p.t[:, 0:512].rearrange("p (h e) -> p h e", h=4), [p], [OT], eng="act")
                    for h in range(4):
                        k.dma("sp", self.cc[b, 512 + h * 128:512 + (h + 1) * 128, :], OT.t[:, h, :], [OT], [])

    def mixer_mla(self, l):
        k = self.k
        SC = 192 ** -0.5
        for b in range(2):
            with k.scope():
                QT = k.sb("aQT", [128, 4, T], BF16)
                QT2 = k.sb("aQT2", [64, 4, T], BF16)
                KT = k.sb("aKT", [128, 4, T], BF16)
                KT2 = k.sb("aKT2", [64, 4, T], BF16)
                Va = k.sb("aVa", [128, NT, 4, 129], BF16)
                k.op("pool", lambda e: e.memset(Va.t[:], 1.0), [], [Va])
                with k.scope():
                    nv = k.sb("mlav", [128, 896], F32)
                    k.dma("sp", nv.t[:], self.mlav[l], [], [nv])
                    QAW = nv.t[:, 0:384]
                    KVAW = nv.t[:, 384:512]
                    NW = [nv.t[:, 512:704], nv.t[:, 704:896]]
                    rc = k.sb("ropec", [128, NT, 32], F32)
                    rs = k.sb("ropes", [128, NT, 32], F32)
                    k.dma("sp", rc.t[:], self.ropec[:, :, :], [], [rc])
                    k.dma("sp", rs.t[:], self.ropes[:, :, :], [], [rs])
                    wq32 = k.sb("wq32", [128, 3, 768], F32)
                    wq = k.sb("wq", [128, 3, 768], BF16)
                    wkv32 = k.sb("wkv32", [128, 1024], F32)
                    wkv = k.sb("wkv", [128, 1024], BF16)
                    k.dma("sp", wq32.t[:], self.w_q_up[l].rearrange("(kc p) o -> p kc o", p=128), [], [wq32])
                    k.dma("sp", wkv32.t[:], self.w_kv_up[l], [], [wkv32])
                    k.cp(wq.t[:], wq32.t[:], [wq32], [wq], eng="pool")
                    k.cp(wkv.t[:], wkv32.t[:], [wkv32], [wkv], eng="pool")
                    Zr = k.ring("Z", [128, 576], F32, 2)
                    jk = k.sb("junk", [128, 384], F32)
                    ssr = k.ring("ss2", [128, 2], F32, 2)
                    cnr = k.ring("cn", [128, 512], F32, 2)
                    cTr = k.ring("cT", [128, 4, 128], BF16, 2)
                    Xr = [k.ring("X0", [128, 4, 192], F32, 2), k.ring("X1", [128, 4, 192], F32, 2)]
                    sqx = k.sb("sqx", [128, 768], F32)
                    s4r = k.ring("s4", [128, 4], F32, 2)
                    tmp = [k.sb("rt", [128, 4, 2, 16], F32) for _ in range(4)]
                    pT = k.psring("apT", 1)
                    pq = k.psring("apq", 2)
                    pk = k.psring("apk", 2)
                    pX = k.psring("apX", 2)
                    for tt in range(NT):
                        tsl = slice(tt * 128, (tt + 1) * 128)
                        Z = Zr.next()
                        k.dma("sp", Z.t[:], self.zt[b, tsl, 1552:2128], [], [Z])
                        ss = ssr.next()
                        k.act(jk.t[:, 0:384], Z.t[:, 0:384], AF.Square, [Z], [jk, ss], accum_out=ss.t[:, 0:1])
                        k.act(jk.t[:, 0:128], Z.t[:, 384:512], AF.Square, [Z], [jk, ss], accum_out=ss.t[:, 1:2])
                        k.act(ss.t[:, 0:1], ss.t[:, 0:1], AF.Sqrt, [ss, self.epsT], [ss], scale=1.0 / 384, bias=self.epsT.t[:, 0:1])
                        k.act(ss.t[:, 1:2], ss.t[:, 1:2], AF.Sqrt, [ss, self.epsT], [ss], scale=1.0 / 128, bias=self.epsT.t[:, 0:1])
                        k.op("dve", lambda e: e.reciprocal(out=ss.t[:], in_=ss.t[:]), [ss], [ss])
                        cn = cnr.next()
                        k.stt(cn.t[:, 0:384], Z.t[:, 0:384], ss.t[:, 0:1], QAW, ALU.mult, ALU.mult, [Z, ss, nv], [cn])
                        k.stt(cn.t[:, 384:512], Z.t[:, 384:512], ss.t[:, 1:2], KVAW, ALU.mult, ALU.mult, [Z, ss, nv], [cn])
                        p = pT.next()
                        for c4 in range(4):
                            k.tr(p.t[:, c4 * 128:(c4 + 1) * 128], cn.t[:, c4 * 128:(c4 + 1) * 128], self.identF, [cn, self.C], [p])
                        cT = cTr.next()
                        k.cp(cT.t[:], p.t[:, 0:512].rearrange("p (c e) -> p c e", c=4), [p], [cT], eng="act")
                        Xq = Xr[0].next()
                        Xk = Xr[1].next()
                        for nb in range(2):
                            p = pq.next()
                            for kc in range(3):
                                k.mm(p.t[:, 0:384], cT.t[:, kc, :], wq.t[:, kc, nb * 384:(nb + 1) * 384], kc == 0, kc == 2, [cT, wq], [p])
                            k.cp(Xq.t[:, 2 * nb:2 * nb + 2, :], p.t[:, 0:384].rearrange("p (h e) -> p h e", h=2), [p], [Xq], eng="act")
                        for nb in range(2):
                            p = pk.next()
                            k.mm(p.t[:, 0:512], cT.t[:, 3, :], wkv.t[:, nb * 512:(nb + 1) * 512], True, True, [cT, wkv], [p])
                            pv4 = p.t[:, 0:512].rearrange("p (h e) -> p h e", h=2)
                            k.cp(Xk.t[:, 2 * nb:2 * nb + 2, 0:128], pv4[:, :, 0:128], [p], [Xk])
                            k.cp(Va.t[:, tt, 2 * nb:2 * nb + 2, 0:128], pv4[:, :, 128:256], [p], [Va], eng="act")
                        for h in range(4):
                            k.cp(Xk.t[:, h, 128:192], Z.t[:, 512:576], [Z], [Xk], eng="pool")
                        for qi, X in enumerate((Xq, Xk)):
                            Xf = X.t[:].rearrange("p h e -> p (h e)")
                            k.tt(sqx.t[:], Xf, Xf, ALU.mult, [X], [sqx])
                            s4 = s4r.next()
                            k.op("dve", lambda e: e.tensor_reduce(out=s4.t[:], in_=sqx.t[:].rearrange("p (h e) -> p h e", h=4), axis=AX.X, op=ALU.add), [sqx], [s4])
                            k.act(s4.t[:], s4.t[:], AF.Sqrt, [s4, self.epsT], [s4], scale=1.0 / 192, bias=self.epsT.t[:, 0:1])
                            k.op("dve", lambda e: e.reciprocal(out=s4.t[:], in_=s4.t[:]), [s4], [s4])
                            for h in range(4):
                                k.stt(X.t[:, h, :], X.t[:, h, :], s4.t[:, h:h + 1], NW[qi], ALU.mult, ALU.mult, [X, s4, nv], [X])
                            rp = X.t[:, :, 128:192].rearrange("p h (a b f) -> p h a b f", a=2, b=2)
                            x1 = rp[:, :, :, 0, :]
                            x2 = rp[:, :, :, 1, :]
                            cosb = rc.t[:, tt, :].rearrange("p (o a f) -> p o a f", o=1, a=2).to_broadcast([128, 4, 2, 16])
                            sinb = rs.t[:, tt, :].rearrange("p (o a f) -> p o a f", o=1, a=2).to_broadcast([128, 4, 2, 16])
                            TT = tmp
                            k.tt(TT[0].t[:], x1, cosb, ALU.mult, [X, rc], [TT[0]])
                            k.tt(TT[1].t[:], x2, sinb, ALU.mult, [X, rs], [TT[1]])
                            k.tt(TT[2].t[:], x2, cosb, ALU.mult, [X, rc], [TT[2]])
                            k.tt(TT[3].t[:], x1, sinb, ALU.mult, [X, rs], [TT[3]])
                            k.tt(x1, TT[0].t[:], TT[1].t[:], ALU.subtract, [TT[0], TT[1], X], [X])
                            k.tt(x2, TT[2].t[:], TT[3].t[:], ALU.add, [TT[2], TT[3], X], [X])
                            dst, dst2 = (QT, QT2) if qi == 0 else (KT, KT2)
                            for hp in range(2):
                                p = pX.next()
                                for hh in range(2):
                                    h = 2 * hp + hh
                                    k.tr(p.t[:, hh * 256:hh * 256 + 128], X.t[:, h, 0:128], self.identF, [X, self.C], [p])
                                    k.tr(p.t[0:64, hh * 256 + 128:hh * 256 + 256], X.t[:, h, 128:192], self.identF, [X, self.C], [p])
                                pv_ = p.t[:, 0:512].rearrange("p (h e) -> p h e", h=2)
                                k.cp(dst.t[:, 2 * hp:2 * hp + 2, tsl], pv_[:, :, 0:128], [p], [dst], eng="act")
                                k.cp(dst2.t[0:64, 2 * hp:2 * hp + 2, tsl], p.t[0:64, 0:512].rearrange("p (h e) -> p h e", h=2)[:, :, 128:256], [p], [dst2])
                if self.dbg.get("mla_noattn"):
                    continue
                with k.scope():
                    Pr = k.ring("P", [128, 512], BF16, 3)
                    MT = k.ring("MT", [128, T], BF16, 2)
                    o32 = k.ring("o32", [128, 128], F32, 2)
                    rdr = k.ring("rd", [128, 1], F32, 2)
                    pS = k.psring("aS", 2)
                    po = [k.ps("apo%d" % i) for i in range(4)]
                    pT = k.psring("aT", 1)
                    blocks = [(0, 256, [0, 1])] + [(CTX + 512 * i, 512, list(range(NT))) for i in range(4)]
                    for h in range(4):
                        mt = MT.next()
                        for (q0, nq, kts) in blocks:
                            nsub = nq // 128
                            for idx, kt in enumerate(kts):
                                ksl = slice(kt * 128, (kt + 1) * 128)
                                ps_ = pS.next()
                                k.mm(ps_.t[:, 0:nq], KT.t[:, h, ksl], QT.t[:, h, q0:q0 + nq], True, False, [KT, QT], [ps_])
                                k.mm(ps_.t[:, 0:nq], KT2.t[0:64, h, ksl], QT2.t[0:64, h, q0:q0 + nq], False, True, [KT2, QT2], [ps_])
                                P = Pr.next()
                                k.act(P.t[:, 0:nq], ps_.t[:, 0:nq], AF.Exp, [ps_], [P], scale=SC)
                                for qs in range(nsub):
                                    k.mm(po[qs].t[:, 0:129], P.t[:, qs * 128:(qs + 1) * 128], Va.t[:, kt, h, :],
                                         idx == 0, idx == len(kts) - 1, [P, Va], [po[qs]])
                            for qs in range(nsub):
                                rd = rdr.next()
                                k.op("dve", lambda e: e.reciprocal(out=rd.t[:], in_=po[qs].t[:, 128:129]), [po[qs]], [rd])
                                o = o32.next()
                                k.ts(o.t[:], po[qs].t[:, 0:128], rd.t[:, 0:1], None, ALU.mult, None, [po[qs], rd], [o])
                                p = pT.next()
                                k.tr(p.t[:, 0:128], o.t[:], self.identF, [o, self.C], [p])
                                k.cp(mt.t[:, q0 + qs * 128:q0 + (qs + 1) * 128], p.t[:, 0:128], [p], [mt], eng="act")
                        k.dma("sp", self.cc[b, 1024 + h * 128:1024 + (h + 1) * 128, :], mt.t[:], [mt], [])


def make_consts():
    c = np.zeros((128, 2048), np.float32)
    i = np.arange(128)
    c[:, 0:128] = np.eye(128)
    c[:, 128:256] = 1.0
    c[:, 256:384] = (i[:, None] <= i[None, :])
    c[:, 384:512] = (i[:, None] >= i[None, :])
    c[:, 512:640] = np.eye(128)
    c[:, 640:768] = np.where(i[:, None] <= i[None, :], 0.0, -30000.0)
    c[:, 768:896] = np.where(i[:, None] >= i[None, :], 0.0, -30000.0)
    return c


def prep_common(inp):
    m = {}
    f = np.float32
    m["ada_w"] = inp["ada_w"]
    m["ada_bT"] = np.ascontiguousarray(inp["ada_b"].reshape(4, 96, 128).transpose(0, 2, 1))
    m["n1w"] = np.ascontiguousarray(inp["norm1_w"].reshape(4, 16, 128).transpose(0, 2, 1))
    m["n2w"] = np.ascontiguousarray(inp["norm2_w"].reshape(4, 16, 128).transpose(0, 2, 1))
    m["w_in"] = inp["w_in"]
    m["w_out"] = inp["w_out"]
    m["mlp_w1"] = inp["mlp_w1"]
    m["mlp_w2"] = inp["mlp_w2"]
    m["consts"] = make_consts()
    prep_mixers(inp, m)
    return m


def prep_core(inp, common, core):
    b0 = 2 * core
    m = dict(common)
    xin = np.concatenate([inp["ctx"][b0:b0 + 2], inp["x"][b0:b0 + 2]], axis=1)
    m["xin"] = np.ascontiguousarray(xin.transpose(0, 2, 1))
    c3 = np.stack([inp["c"][b0], inp["c"][b0 + 1], inp["c_ctx"]], axis=1)
    m["cT"] = np.ascontiguousarray(c3.reshape(16, 128, 3).transpose(1, 0, 2))
    return m


def _dup(a):
    return np.concatenate([a, a], axis=0)


def prep_mixers(inp, m):
    L = DEPTH
    c = m["consts"]
    c[:64, 896] = -1.0
    c[64:, 896] = 1.0
    lre = inp["s5_lam_re"].transpose(0, 3, 1, 2).reshape(L, 64, 64)
    lim = inp["s5_lam_im"].transpose(0, 3, 1, 2).reshape(L, 64, 64)
    ldt = np.broadcast_to(inp["s5_log_dt"].reshape(L, 1, 64), (L, 64, 64))
    s5v = np.concatenate([lre, lim, ldt], axis=2)
    m["s5v"] = np.ascontiguousarray(np.concatenate([s5v, s5v], axis=1))
    bre = inp["s5_b_re"].transpose(0, 3, 1, 2, 4).reshape(L, 64, 64, 16)
    bim = inp["s5_b_im"].transpose(0, 3, 1, 2, 4).reshape(L, 64, 64, 16)
    m["s5A"] = np.ascontiguousarray(np.concatenate([bre, bim], axis=1))
    m["s5B"] = np.ascontiguousarray(np.concatenate([bim, bre], axis=1))
    cre = inp["s5_c_re"].transpose(0, 4, 1, 2, 3).reshape(L, 64, 64, 16)
    cim = inp["s5_c_im"].transpose(0, 4, 1, 2, 3).reshape(L, 64, 64, 16)
    m["s5CA"] = np.ascontiguousarray(np.concatenate([cre, cim], axis=1))
    m["s5CB"] = np.ascontiguousarray(np.concatenate([cim, cre], axis=1))
    dsk = inp["s5_d"].reshape(L, 4, 128).transpose(0, 2, 1)
    glb = inp["s5_glu_b"].reshape(L, 4, 128).transpose(0, 2, 1)
    m["s5w"] = np.ascontiguousarray(np.concatenate([dsk, glb], axis=2))
    m["glu_w"] = inp["s5_glu_w"]
    NK, KC = T // 8, CTX // 8
    n0 = np.arange(NK, dtype=np.float32)
    n1 = np.concatenate([KC - 1 - np.arange(KC), KC + (NK - KC - 1 - np.arange(NK - KC))]).astype(np.float32)
    m["nidx8"] = np.ascontiguousarray(np.broadcast_to(np.stack([n0, n1])[:, None, :], (2, 128, NK)))
    selu = np.zeros((128, 8, 8, 128), np.float32)
    selt = np.zeros((128, 8, 8, 128), np.float32)
    for jj in range(8):
        for ss in range(8):
            for ci in range(16):
                selu[16 * jj + ci, jj, ss, 16 * ss + ci] = 1.0
                selt[16 * ss + ci, jj, ss, 16 * jj + ci] = 1.0
    m["selu"] = selu.reshape(128, 64, 128).astype(ml_dtypes.bfloat16)
    m["selt"] = selt.reshape(128, 64, 128).astype(ml_dtypes.bfloat16)
    blk = np.arange(128) // 16
    m["bmask"] = np.concatenate([(blk[None, :] >= blk[:, None]), (blk[None, :] <= blk[:, None])], axis=1).astype(np.float32)
    cw = inp["lru_conv_w"].reshape(L, 4, 4, 128).transpose(0, 3, 2, 1).reshape(L, 128, 16)
    cb = inp["lru_conv_b"].reshape(L, 4, 128).transpose(0, 2, 1)
    def dc(a):
        return a.reshape(L, 2, 4, 128).transpose(0, 3, 1, 2).reshape(L, 128, 8)
    m["lruv"] = np.ascontiguousarray(np.concatenate([cw, cb, dc(inp["lru_ba"]), dc(inp["lru_bx"]), dc(inp["lru_lam"])], axis=2))
    m["lru_wa"] = inp["lru_wa"]
    m["lru_wx"] = inp["lru_wx"]
    gb = np.concatenate([inp["ml_ig_bias"], inp["ml_fg_bias"]], axis=2).reshape(L, 1, 16)
    m["mlb"] = np.ascontiguousarray(np.broadcast_to(gb, (L, 128, 16)))
    m["onw"] = np.ascontiguousarray(np.broadcast_to(inp["ml_out_norm"].reshape(L, 1, 512), (L, 128, 512)))
    selc = np.zeros((16, 16, 128), np.float32)
    for kk in range(16):
        selc[kk, kk, :] = 1.0
    m["selc"] = selc.reshape(16, 2048)
    nv = np.concatenate([inp["mla_q_a_norm"], inp["mla_kv_a_norm"], inp["mla_q_norm"], inp["mla_k_norm"]], axis=1)
    m["mlav"] = np.ascontiguousarray(np.broadcast_to(nv.reshape(L, 1, 896), (L, 128, 896)))
    m["w_q_up"] = inp["mla_w_q_up"]
    m["w_kv_up"] = inp["mla_w_kv_up"]
    q = np.arange(LAT)
    inv = (np.float32(10000.0) ** (-np.arange(16, dtype=np.float32) / np.float32(16))).astype(np.float32)
    ang = np.concatenate([(q // 64).astype(np.float32)[:, None] * inv, (q % 64).astype(np.float32)[:, None] * inv], axis=1)
    cosf = np.ones((T, 32), np.float32)
    sinf = np.zeros((T, 32), np.float32)
    cosf[CTX:] = np.cos(ang.astype(np.float32))
    sinf[CTX:] = np.sin(ang.astype(np.float32))
    m["ropec"] = np.ascontiguousarray(cosf.reshape(NT, 128, 32).transpose(1, 0, 2))
    m["ropes"] = np.ascontiguousarray(sinf.reshape(NT, 128, 32).transpose(1, 0, 2))


_PROG = None


def kernel(**inputs):
    global _PROG
    inp = {k_: np.asarray(v) for k_, v in inputs.items()}
    if _PROG is None:
        _PROG = Prog()
    prog = _PROG
    common = prep_common(inp)
    in_maps = []
    for core in range(8):
        m = prep_core(inp, common, core)
        in_maps.append({n: m[n] for n in prog.inp})
    res = run_bass_kernel_spmd(prog.nc, in_maps, core_ids=list(range(8)))
    outs = [r["yout"] for r in res.results]
    y = np.concatenate(outs, axis=0)
    return np.ascontiguousarray(y.transpose(0, 2, 1)).astype(np.float32)
```

```python
import numpy as np
import ml_dtypes
from contextlib import ExitStack, contextmanager
import concourse.bass as bass
import concourse.mybir as mybir
from concourse.bass_utils import run_bass_kernel_spmd

F32 = mybir.dt.float32
BF16 = mybir.dt.bfloat16
I32 = mybir.dt.int32
AF = mybir.ActivationFunctionType
ALU = mybir.AluOpType
AX = mybir.AxisListType

D = 2048
T = 2304
CTX = 256
LAT = 2048
NT = 18
DEPTH = 4
DFF = 8192
INC = 4176
EPS = 1e-6
TB = [(0, 256), (256, 512), (768, 512), (1280, 512), (1792, 512)]
TWO_PI = float(2 * np.pi)
SAME_SYNC = True
NO_SELF_SYNC = ("pe",)
MERGE_WAIT = True
CAP = 30000


class Res:
    __slots__ = ("w", "r")

    def __init__(self):
        self.w = None
        self.r = {}


class Buf:
    def __init__(self, t, psum=False):
        self.t = t
        self.res = Res()
        self.psum = psum

    def __getitem__(self, key):
        return self.t[key]


class Ring:
    def __init__(self, bufs):
        self.bufs = bufs
        self.i = 0

    def next(self):
        b = self.bufs[self.i]
        self.i = (self.i + 1) % len(self.bufs)
        return b


class KB:
    def __init__(self, nc):
        self.nc = nc
        self.E = {"pe": nc.tensor, "dve": nc.vector, "act": nc.scalar, "pool": nc.gpsimd, "sp": nc.sync}
        self.sem = {}
        self.cnt = {}
        self.owner = {}
        self.nsem = 0
        for e in self.E:
            self._fresh(e)
        self.waited = {e: {} for e in self.E}
        self.dq = {}
        self.dqi = {}
        for q, n in (("sp", 12), ("pool", 4), ("act", 4)):
            self.dq[q] = [[self._newsem("d"), 0] for _ in range(n)]
            self.dqi[q] = 0
        self.stack = []
        self.uid = 0

    def _newsem(self, pfx):
        self.nsem += 1
        return self.nc.alloc_semaphore(f"{pfx}{self.nsem}")

    def _fresh(self, e):
        s = self._newsem("e")
        self.sem[e] = s
        self.cnt[e] = 0
        self.owner[s] = e

    @contextmanager
    def scope(self):
        st = ExitStack()
        self.stack.append(st)
        try:
            yield
        finally:
            self.barrier()
            self.stack.pop()
            st.close()

    def _name(self, n):
        self.uid += 1
        return f"{n}_{self.uid}"

    def sb(self, name, shape, dtype):
        t = self.stack[-1].enter_context(self.nc.sbuf_tensor(self._name(name), list(shape), dtype))
        return Buf(t)

    def ps(self, name, shape=(128, 512), dtype=F32):
        t = self.stack[-1].enter_context(self.nc.psum_tensor(self._name(name), list(shape), dtype))
        return Buf(t, psum=True)

    def ring(self, name, shape, dtype, n):
        return Ring([self.sb(name, shape, dtype) for _ in range(n)])

    def psring(self, name, n, shape=(128, 512), dtype=F32):
        return Ring([self.ps(name, shape, dtype) for _ in range(n)])

    def _need(self, e, tok, out):
        if tok is None:
            return
        sem, val = tok
        own = self.owner.get(sem)
        if own == e and (e in NO_SELF_SYNC or not SAME_SYNC):
            return
        w = self.waited[e]
        if w.get(sem, 0) >= val:
            return
        w[sem] = val
        for i, (s_, v_) in enumerate(out):
            if s_ is sem or s_ == sem:
                out[i] = (sem, max(v_, val))
                return
        out.append((sem, val))

    def _wait(self, e, tok):
        out = []
        self._need(e, tok, out)
        for (s_, v_) in out:
            self.E[e].wait_ge(s_, v_)

    def _deps(self, e, reads, writes):
        out = []
        for r in reads:
            r = r.res if isinstance(r, Buf) else r
            self._need(e, r.w, out)
        for wr in writes:
            wr = wr.res if isinstance(wr, Buf) else wr
            self._need(e, wr.w, out)
            for s_, v_ in wr.r.items():
                self._need(e, (s_, v_), out)
        return out

    def _commit(self, tok, reads, writes):
        sem, val = tok
        for r in reads:
            r = r.res if isinstance(r, Buf) else r
            if r.r.get(sem, 0) < val:
                r.r[sem] = val
        for wr in writes:
            wr = wr.res if isinstance(wr, Buf) else wr
            wr.w = tok
            wr.r = {}

    def op(self, e, fn, reads=(), writes=(), merge=True):
        pr = [r for r in reads if isinstance(r, Buf) and r.psum]
        if pr:
            reads = [r for r in reads if not (isinstance(r, Buf) and r.psum)]
            writes = list(writes) + pr
        need = self._deps(e, reads, writes)
        last = None
        if merge and MERGE_WAIT and need:
            last = need.pop()
        for (s_, v_) in need:
            self.E[e].wait_ge(s_, v_)
        ins = fn(self.E[e])
        if last is not None:
            ins._wait_ge(last[0], last[1])
        self.cnt[e] += 1
        ins.then_inc(self.sem[e], 1)
        tok = (self.sem[e], self.cnt[e])
        self._commit(tok, reads, writes)
        if self.cnt[e] >= CAP:
            self._fresh(e)

    def dma(self, q, out, in_, reads=(), writes=()):
        slots = self.dq[q]
        slot = slots[self.dqi[q]]
        self.dqi[q] = (self.dqi[q] + 1) % len(slots)
        if slot[1] > 0:
            self._wait(q, (slot[0], slot[1]))
        if slot[1] >= CAP:
            slot[0] = self._newsem("d")
            slot[1] = 0
        for (s_, v_) in self._deps(q, reads, writes):
            self.E[q].wait_ge(s_, v_)
        ins = self.E[q].dma_start(out=out, in_=in_)
        slot[1] += 16
        ins.then_inc(slot[0], 16)
        self._commit((slot[0], slot[1]), reads, writes)

    def barrier(self):
        toks = [(self.sem[e], self.cnt[e]) for e in self.E if self.cnt[e] > 0]
        for q in self.dq:
            for slot in self.dq[q]:
                if slot[1] > 0:
                    toks.append((slot[0], slot[1]))
        for e in self.E:
            for tok in toks:
                if self.owner.get(tok[0]) == e:
                    continue
                self._wait(e, tok)

    def mm(self, out, lhsT, rhs, start, stop, reads, writes):
        self.op("pe", lambda e: e.matmul(out, lhsT, rhs, start=start, stop=stop), reads, writes)

    def tr(self, out, in_, ident, reads, writes):
        self.op("pe", lambda e: e.transpose(out, in_, ident), reads, writes)

    def act(self, out, in_, func, reads, writes, bias=0.0, scale=1.0, **kw):
        self.op("act", lambda e: e.activation(out=out, in_=in_, func=func, bias=bias, scale=scale, **kw), reads, writes,
                merge=("accum_out" not in kw))

    def ts(self, out, in0, s1, s2, op0, op1, reads, writes, eng="dve"):
        if s2 is None:
            self.op(eng, lambda e: e.tensor_scalar(out=out, in0=in0, scalar1=s1, scalar2=None, op0=op0), reads, writes)
        else:
            self.op(eng, lambda e: e.tensor_scalar(out=out, in0=in0, scalar1=s1, scalar2=s2, op0=op0, op1=op1), reads, writes)

    def tt(self, out, in0, in1, op, reads, writes, eng="dve"):
        self.op(eng, lambda e: e.tensor_tensor(out=out, in0=in0, in1=in1, op=op), reads, writes)

    def stt(self, out, in0, scalar, in1, op0, op1, reads, writes):
        self.op("dve", lambda e: e.scalar_tensor_tensor(out=out, in0=in0, scalar=scalar, in1=in1, op0=op0, op1=op1), reads, writes)

    def cp(self, out, in_, reads, writes, eng="dve"):
        if eng == "act":
            self.op("act", lambda e: e.copy(out=out, in_=in_), reads, writes)
        else:
            self.op(eng, lambda e: e.tensor_copy(out=out, in_=in_), reads, writes)

    def scan(self, out, d0, d1, init, reads, writes):
        self.op("dve", lambda e: e.tensor_tensor_scan(out=out, data0=d0, data1=d1, initial=init, op0=ALU.mult, op1=ALU.add), reads, writes)


class Prog:
    def __init__(self, n_layers=DEPTH, dbg=None, layers=None):
        self.dbg = dbg or {}
        self.layers = list(range(n_layers)) if layers is None else layers
        nc = bass.Bass("TRN2", target_bir_lowering=False)
        self.nc = nc
        self.k = KB(nc)
        self.inp = {}
        self.build()

    @staticmethod
    def pf(items, load):
        items = list(items)
        nxt = load(items[0])
        for i, it in enumerate(items):
            cur = nxt
            if i + 1 < len(items):
                nxt = load(items[i + 1])
            yield it, cur

    def din(self, name, shape, dtype=F32):
        t = self.nc.dram_tensor(name, list(shape), dtype, kind="ExternalInput")
        self.inp[name] = (tuple(shape), dtype)
        return t.ap()

    def dscr(self, name, shape, dtype=F32):
        kind = "ExternalOutput" if name in self.dbg else "Internal"
        return self.nc.dram_tensor(name, list(shape), dtype, kind=kind).ap()

    def build(self):
        nc, k = self.nc, self.k
        L = DEPTH
        self.xin = self.din("xin", [2, D, T])
        self.cT = self.din("cT", [128, 16, 3])
        self.ada_w = self.din("ada_w", [L, D, 6 * D])
        self.ada_bT = self.din("ada_bT", [L, 128, 96])
        self.n1w = self.din("n1w", [L, 128, 16])
        self.n2w = self.din("n2w", [L, 128, 16])
        self.w_in = self.din("w_in", [L, D, INC])
        self.w_out = self.din("w_out", [L, D, D])
        self.mlp_w1 = self.din("mlp_w1", [L, D, DFF])
        self.mlp_w2 = self.din("mlp_w2", [L, DFF, D])
        self.consts = self.din("consts", [128, 2048])
        self.declare_mixer_inputs()
        self.yout = nc.dram_tensor("yout", [2, D, LAT], F32, kind="ExternalOutput").ap()
        self.xs = self.dscr("xs", [2, D, T])
        self.zf = self.dscr("zf", [2, 2560, T])
        self.zt = self.dscr("zt", [2, T, 2128])
        self.W1t = self.dscr("W1t", [64, 128, 16, 128], BF16)
        self.W2t = self.dscr("W2t", [4, 16, 128, 16, 128], BF16)
        if self.dbg.get("cc_in"):
            self.cc = self.din("cc", [2, D, T], BF16)
        else:
            self.cc = self.dscr("cc", [2, D, T], BF16)
        if "modv_o" in self.dbg:
            self.modv_o = self.dscr("modv_o", [128, 288])

        with k.scope():
            self.setup_consts()
            for b in range(2):
                for c in range(16):
                    k.dma("sp", self.xs[b, c * 128:(c + 1) * 128, :], self.xin[b, c * 128:(c + 1) * 128, :])
            k.barrier()
            for l in self.layers:
                self.layer(l)
            for b in range(2):
                for c in range(16):
                    k.dma("sp", self.yout[b, c * 128:(c + 1) * 128, :], self.xs[b, c * 128:(c + 1) * 128, CTX:T])

    def setup_consts(self):
        k = self.k
        self.C = k.sb("consts", [128, 2048], F32)
        k.dma("sp", self.C.t[:], self.consts[:, :], [], [self.C])
        self.identF = self.C.t[:, 0:128]
        self.onesF = self.C.t[:, 128:256]
        self.identB = k.sb("identB", [128, 128], BF16)
        k.cp(self.identB.t[:], self.identF, [self.C], [self.identB])
        self.cs = k.sb("cs", [128, 16, 3], F32)
        k.dma("sp", self.cs.t[:], self.cT[:, :, :], [], [self.cs])
        k.act(self.cs.t[:], self.cs.t[:], AF.Silu, [self.cs], [self.cs])
        self.modv = k.sb("modv", [128, 96, 3], F32)
        self.g1 = k.sb("g1", [128, 16, 3], F32)
        self.g2 = k.sb("g2", [128, 16, 3], F32)
        self.epsT = k.sb("epsT", [128, 1], F32)
        k.op("dve", lambda e: e.memset(self.epsT.t[:], EPS), [], [self.epsT])
        self.setup_mixer_consts()

    def layer(self, l):
        k = self.k
        self.wcast_done = False
        self.stage_mod(l)
        for b in range(2):
            self.stage_A(l, b)
        self.mixers(l)
        for b in range(2):
            self.stage_proj_res(l, b, which="out")
        if not self.wcast_done:
            self.stage_wcast(l)
        for b in range(2):
            self.stage_mlp(l, b)

    def stage_mod(self, l):
        k = self.k
        with k.scope():
            wr = k.ring("adaw", [128, 16, 128], F32, 3)
            pm = k.ps("pmod")
            adab = k.sb("adab", [128, 96], F32)
            nw1 = k.sb("nw1", [128, 16], F32)
            nw2 = k.sb("nw2", [128, 16], F32)
            k.dma("sp", adab.t[:], self.ada_bT[l], [], [adab])
            k.dma("sp", nw1.t[:], self.n1w[l], [], [nw1])
            k.dma("sp", nw2.t[:], self.n2w[l], [], [nw2])
            wv = self.ada_w[l].rearrange("(kc p) f -> p kc f", p=128)
            for j in range(96):
                w = wr.next()
                k.dma("sp", w.t[:], wv[:, :, j * 128:(j + 1) * 128], [], [w])
                for kc in range(16):
                    k.mm(pm.t[:, 3 * j:3 * j + 3], w.t[:, kc, :], self.cs.t[:, kc, :], kc == 0, kc == 15,
                         [w, self.cs], [pm])
            pv = pm.t[:, 0:288].rearrange("p (j r) -> p j r", r=3)
            for r in range(3):
                k.tt(self.modv.t[:, :, r], pv[:, :, r], adab.t[:], ALU.add, [pm, adab], [self.modv])
            if "modv_o" in self.dbg:
                k.dma("sp", self.modv_o[:, :], self.modv.t[:].rearrange("p j r -> p (j r)"), [self.modv], [])
            for r in range(3):
                k.stt(self.g1.t[:, :, r], self.modv.t[:, 16:32, r], 1.0, nw1.t[:], ALU.add, ALU.mult,
                      [self.modv, nw1], [self.g1])
                k.stt(self.g2.t[:, :, r], self.modv.t[:, 64:80, r], 1.0, nw2.t[:], ALU.add, ALU.mult,
                      [self.modv, nw2], [self.g2])

    def make_hT(self, hT, b, g, shift_base, blocks=TB, rel=False, nx=2, base=None, nmax=512):
        k = self.k
        with k.scope():
            xr = k.ring("xblk", [128, 16, nmax], F32, nx)
            sqr = k.ring("sq", [128, nmax], F32, 3)
            rsr = k.ring("rstd", [128, nmax], F32, 2)
            tmr = k.ring("tmp", [128, nmax], F32, 3)
            pss = k.psring("ss", 2)
            xv = self.xs[b].rearrange("(c p) t -> p c t", p=128)
            for (t0, n) in blocks:
                r = 2 if t0 < CTX else b
                o0 = (t0 - base) if base is not None else (0 if rel else t0)
                xb = xr.next()
                for c in range(16):
                    k.dma("sp", xb.t[:, c, 0:n], xv[:, c, t0:t0 + n], [], [xb])
                ss = pss.next()
                for c in range(16):
                    sq = sqr.next()
                    k.act(sq.t[:, 0:n], xb.t[:, c, 0:n], AF.Square, [xb], [sq])
                    k.mm(ss.t[:, 0:n], self.onesF, sq.t[:, 0:n], c == 0, c == 15, [sq, self.C], [ss])
                rs = rsr.next()
                k.act(rs.t[:, 0:n], ss.t[:, 0:n], AF.Sqrt, [ss, self.epsT], [rs], bias=self.epsT.t[:, 0:1], scale=1.0 / D)
                k.op("dve", lambda e: e.reciprocal(out=rs.t[:, 0:n], in_=rs.t[:, 0:n]), [rs], [rs])
                for c in range(16):
                    tm = tmr.next()
                    k.stt(tm.t[:, 0:n], xb.t[:, c, 0:n], g.t[:, c, r:r + 1], rs.t[:, 0:n], ALU.mult, ALU.mult,
                          [xb, g, rs], [tm])
                    k.act(hT.t[:, c, o0:o0 + n], tm.t[:, 0:n], AF.Identity, [tm, self.modv], [hT],
                          bias=self.modv.t[:, shift_base + c, r:r + 1], scale=1.0)

    def hT_piece(self, hT, b, g, shift_base, t0, n, o0, R):
        k = self.k
        xv = self.xs[b].rearrange("(c p) t -> p c t", p=128)
        r = 2 if t0 < CTX else b
        xb = R["x"].next()
        for c in range(16):
            k.dma("sp", xb.t[:, c, 0:n], xv[:, c, t0:t0 + n], [], [xb])
        ss = R["ps"].next()
        for c in range(16):
            sq = R["sq"].next()
            k.act(sq.t[:, 0:n], xb.t[:, c, 0:n], AF.Square, [xb], [sq])
            k.mm(ss.t[:, 0:n], self.onesF, sq.t[:, 0:n], c == 0, c == 15, [sq, self.C], [ss])
        rs = R["rs"].next()
        k.act(rs.t[:, 0:n], ss.t[:, 0:n], AF.Sqrt, [ss, self.epsT], [rs], bias=self.epsT.t[:, 0:1], scale=1.0 / D)
        k.op("dve", lambda e: e.reciprocal(out=rs.t[:, 0:n], in_=rs.t[:, 0:n]), [rs], [rs])
        for c in range(16):
            tm = R["tm"].next()
            k.stt(tm.t[:, 0:n], xb.t[:, c, 0:n], g.t[:, c, r:r + 1], rs.t[:, 0:n], ALU.mult, ALU.mult, [xb, g, rs], [tm])
            k.act(hT.t[:, c, o0:o0 + n], tm.t[:, 0:n], AF.Identity, [tm, self.modv], [hT],
                  bias=self.modv.t[:, shift_base + c, r:r + 1], scale=1.0)

    FM_CHUNKS = [0, 128, 256, 384, 512, 640, 768, 896, 1024, 1152, 1280, 1408,
                 3152, 3280, 3408, 3536, 3664, 3792, 3920, 4048]
    TM_BLOCKS = [(1024, 512), (1536, 512), (2048, 512), (2560, 512), (3072, 80)]

    def stage_A(self, l, b):
        k = self.k
        with k.scope():
            hT = k.sb("hT", [128, 16, T], BF16)
            self.make_hT(hT, b, self.g1, 0)
            wv = self.w_in[l].rearrange("(kc p) c -> p kc c", p=128)
            with k.scope():
                wf = k.ring("wf", [128, 16, 128], F32, 2)
                wb = k.ring("wb", [128, 16, 128], BF16, 2)
                ob = k.ring("ob", [128, 512], F32, 3)
                pp = k.psring("pp", 3)
                ei = 0
                def load_fm(item):
                    w32 = wf.next()
                    k.dma("sp", w32.t[:], wv[:, :, item[1]:item[1] + 128], [], [w32])
                    w16_ = wb.next()
                    k.cp(w16_.t[:], w32.t[:], [w32], [w16_], eng="pool")
                    return w16_

                for (ci, c0), w16 in self.pf(list(enumerate(self.FM_CHUNKS)), load_fm):
                    for (t0, n) in TB:
                        p = pp.next()
                        for kc in range(16):
                            k.mm(p.t[:, 0:n], w16.t[:, kc, :], hT.t[:, kc, t0:t0 + n], kc == 0, kc == 15, [w16, hT], [p])
                        o = ob.next()
                        k.cp(o.t[:, 0:n], p.t[:, 0:n], [p], [o], eng=("act" if ei % 2 else "dve"))
                        ei += 1
                        k.dma("sp", self.zf[b, ci * 128:(ci + 1) * 128, t0:t0 + n], o.t[:, 0:n], [o], [])
            with k.scope():
                wf = k.ring("wf2", [128, 8, 512], F32, 2)
                wb = k.ring("wb2", [128, 16, 512], BF16, 2)
                ob = k.ring("ob2", [128, 512], F32, 3)
                pp = k.psring("pp2", 3)
                ei = 0
                def load_tm(item):
                    (c0_, w_) = item
                    w16_ = wb.next()
                    for hf in range(2):
                        w32 = wf.next()
                        k.dma("sp", w32.t[:, :, 0:w_], wv[:, hf * 8:(hf + 1) * 8, c0_:c0_ + w_], [], [w32])
                        k.cp(w16_.t[:, hf * 8:(hf + 1) * 8, 0:w_], w32.t[:, :, 0:w_], [w32], [w16_], eng="pool")
                    return w16_

                for (c0, w), w16 in self.pf(self.TM_BLOCKS, load_tm):
                    for tt in range(NT):
                        p = pp.next()
                        for kc in range(16):
                            k.mm(p.t[:, 0:w], hT.t[:, kc, tt * 128:(tt + 1) * 128], w16.t[:, kc, 0:w], kc == 0, kc == 15,
                                 [w16, hT], [p])
                        o = ob.next()
                        k.cp(o.t[:, 0:w], p.t[:, 0:w], [p], [o], eng=("act" if ei % 2 else "dve"))
                        ei += 1
                        k.dma("sp", self.zt[b, tt * 128:(tt + 1) * 128, c0 - 1024:c0 - 1024 + w], o.t[:, 0:w], [o], [])

    def stage_proj_res(self, l, b, which):
        k = self.k
        with k.scope():
            cT = k.sb("ccT", [128, 16, T], BF16)
            cv = self.cc[b].rearrange("(c p) t -> p c t", p=128)
            for c in range(16):
                k.dma("sp", cT.t[:, c, :], cv[:, c, :], [], [cT])
            wv = self.w_out[l].rearrange("(kc p) c -> p kc c", p=128)
            xv = self.xs[b].rearrange("(c p) t -> p c t", p=128)
            wf = k.ring("wf", [128, 16, 128], F32, 2)
            wb = k.ring("wb", [128, 16, 128], BF16, 2)
            xr = k.ring("xo", [128, 512], F32, 3)
            pp = k.psring("pp", 3)
            def load_o(fc_):
                w32 = wf.next()
                k.dma("sp", w32.t[:], wv[:, :, fc_ * 128:(fc_ + 1) * 128], [], [w32])
                w16_ = wb.next()
                k.cp(w16_.t[:], w32.t[:], [w32], [w16_], eng="pool")
                return w16_

            for fc, w16 in self.pf(range(16), load_o):
                for (t0, n) in TB:
                    r = 2 if t0 < CTX else b
                    xo = xr.next()
                    k.dma("sp", xo.t[:, 0:n], xv[:, fc, t0:t0 + n], [], [xo])
                    p = pp.next()
                    for kc in range(16):
                        k.mm(p.t[:, 0:n], w16.t[:, kc, :], cT.t[:, kc, t0:t0 + n], kc == 0, kc == 15, [w16, cT], [p])
                    k.stt(xo.t[:, 0:n], p.t[:, 0:n], self.modv.t[:, 32 + fc, r:r + 1], xo.t[:, 0:n], ALU.mult, ALU.add,
                          [p, self.modv, xo], [xo])
                    k.dma("sp", xv[:, fc, t0:t0 + n], xo.t[:, 0:n], [xo], [])

    MLP_BLOCKS = [[(0, 256), (256, 256), (512, 256)], [(768, 256), (1024, 256), (1280, 256)],
                  [(1536, 256), (1792, 256), (2048, 256)]]

    def stage_wcast(self, l):
        k = self.k
        with k.scope():
            f32r = k.ring("wc32", [128, 8192], F32, 2)
            b16r = k.ring("wc16", [128, 8192], BF16, 2)
            engs = ["dve", "act", "pool"]
            ei = 0
            w1tv = self.W1t.rearrange("fc p kc j -> p fc kc j")
            for kc in range(16):
                a = f32r.next()
                k.dma("sp", a.t[:], self.mlp_w1[l, kc * 128:(kc + 1) * 128, :], [], [a])
                bb = b16r.next()
                for q in range(4):
                    k.cp(bb.t[:, q * 2048:(q + 1) * 2048], a.t[:, q * 2048:(q + 1) * 2048], [a], [bb], eng=engs[ei % 3])
                    ei += 1
                k.dma("sp", w1tv[:, :, kc, :], bb.t[:].rearrange("p (fc j) -> p fc j", j=128), [bb], [])
            w2v = self.mlp_w2[l].rearrange("(fg fc p) d -> fg fc p d", fc=16, p=128)
            w2tv = self.W2t.rearrange("fg dc p fc j -> fg fc p dc j")
            for fg in range(4):
                for f4 in range(4):
                    a = f32r.next()
                    for f in range(4):
                        k.dma("sp", a.t[:, f * 2048:(f + 1) * 2048], w2v[fg, f4 * 4 + f], [], [a])
                    bb = b16r.next()
                    for q in range(4):
                        k.cp(bb.t[:, q * 2048:(q + 1) * 2048], a.t[:, q * 2048:(q + 1) * 2048], [a], [bb], eng=engs[ei % 3])
                        ei += 1
                    for f in range(4):
                        k.dma("sp", w2tv[fg, f4 * 4 + f], bb.t[:, f * 2048:(f + 1) * 2048].rearrange("p (dc j) -> p dc j", j=128), [bb], [])

    def wcast_gen(self, l, f32r, b16r):
        k = self.k
        w1tv = self.W1t.rearrange("fc p kc j -> p fc kc j")
        w2v = self.mlp_w2[l].rearrange("(fg fc p) d -> fg fc p d", fc=16, p=128)
        w2tv = self.W2t.rearrange("fg dc p fc j -> fg fc p dc j")
        pieces = []
        for kc in range(16):
            for q in range(4):
                pieces.append((self.mlp_w1[l, kc * 128:(kc + 1) * 128, q * 2048:(q + 1) * 2048], w1tv[:, q * 16:(q + 1) * 16, kc, :]))
        for fg in range(4):
            for fc in range(16):
                pieces.append((w2v[fg, fc], w2tv[fg, fc]))
        loaded = {}

        def load(i):
            a = f32r.next()
            k.dma("sp", a.t[:], pieces[i][0], [], [a])
            loaded[i] = a

        load(0)
        for i in range(len(pieces)):
            if i + 1 < len(pieces):
                load(i + 1)
            a = loaded.pop(i)
            bb = b16r.next()
            k.cp(bb.t[:], a.t[:], [a], [bb], eng="act")
            k.dma("sp", pieces[i][1], bb.t[:].rearrange("p (c j) -> p c j", j=128), [bb], [])
            yield

    def stage_mlp(self, l, b):
        k = self.k
        with k.scope():
            hTs = [k.sb("hT2", [128, 16, 768], BF16) for _ in range(2)]
            oacc = k.sb("oacc", [128, 16, 768], F32)
            aTr = k.ring("aT", [128, 16, 768], BF16, 1)
            w1r = k.ring("w1s", [128, 16, 128], BF16, 3)
            w2r = k.ring("w2s", [128, 16, 128], BF16, 3)
            rl = k.ring("rl", [128, 512], F32, 4)
            xr = k.ring("xo", [128, 512], F32, 3)
            HR = {"x": k.ring("hx", [128, 16, 256], F32, 1), "sq": k.ring("hsq", [128, 256], F32, 3),
                  "rs": k.ring("hrs", [128, 256], F32, 2), "tm": k.ring("htm", [128, 256], F32, 3),
                  "ps": k.psring("hps", 1)}
            pp = k.psring("pp", 3)
            pq = k.psring("pq", 2)
            xv = self.xs[b].rearrange("(c p) t -> p c t", p=128)
            MM = ((0, 512), (512, 256))
            blocks = self.MLP_BLOCKS
            for (t0, n) in blocks[0]:
                self.hT_piece(hTs[0], b, self.g2, 48, t0, n, t0 - blocks[0][0][0], HR)
            for bi, subs in enumerate(blocks):
                base = subs[0][0]
                hT = hTs[bi % 2]
                nxt = blocks[bi + 1] if bi + 1 < len(blocks) else None
                for fg in range(4):
                    aT = aTr.next()
                    def load_w1(fc_):
                        w_ = w1r.next()
                        k.dma("sp", w_.t[:], self.W1t[fg * 16 + fc_], [], [w_])
                        return w_

                    for fc, w in self.pf(range(16), load_w1):
                        for (o, n) in MM:
                            p = pp.next()
                            for kc in range(16):
                                k.mm(p.t[:, 0:n], w.t[:, kc, :], hT.t[:, kc, o:o + n], kc == 0, kc == 15, [w, hT], [p])
                            rr = rl.next()
                            k.act(rr.t[:, 0:n], p.t[:, 0:n], AF.Relu, [p], [rr])
                            k.tt(aT.t[:, fc, o:o + n], rr.t[:, 0:n], rr.t[:, 0:n], ALU.mult, [rr], [aT])
                    if nxt is not None and fg < 3:
                        (t0n, nn) = nxt[fg]
                        self.hT_piece(hTs[(bi + 1) % 2], b, self.g2, 48, t0n, nn, t0n - nxt[0][0], HR)
                    def load_w2(dc_):
                        w_ = w2r.next()
                        k.dma("sp", w_.t[:], self.W2t[fg, dc_], [], [w_])
                        return w_

                    for dc, w in self.pf(range(16), load_w2):
                        for (o, n) in MM:
                            p = pq.next()
                            for fc in range(16):
                                k.mm(p.t[:, 0:n], w.t[:, fc, :], aT.t[:, fc, o:o + n], fc == 0, fc == 15, [w, aT], [p])
                            if fg == 0:
                                k.cp(oacc.t[:, dc, o:o + n], p.t[:, 0:n], [p], [oacc], eng="act")
                            elif fg < 3:
                                k.tt(oacc.t[:, dc, o:o + n], p.t[:, 0:n], oacc.t[:, dc, o:o + n], ALU.add, [p, oacc], [oacc])
                            else:
                                tm = rl.next()
                                k.tt(tm.t[:, 0:n], p.t[:, 0:n], oacc.t[:, dc, o:o + n], ALU.add, [p, oacc], [tm])
                                xo = xr.next()
                                k.dma("sp", xo.t[:, 0:n], xv[:, dc, base + o:base + o + n], [], [xo])
                                a0 = base + o
                                cuts = [a0] + ([CTX] if a0 < CTX < a0 + n else []) + [a0 + n]
                                for ci in range(len(cuts) - 1):
                                    c0, c1 = cuts[ci] - a0, cuts[ci + 1] - a0
                                    r = 2 if cuts[ci] < CTX else b
                                    k.stt(xo.t[:, c0:c1], tm.t[:, c0:c1], self.modv.t[:, 80 + dc, r:r + 1], xo.t[:, c0:c1],
                                          ALU.mult, ALU.add, [tm, self.modv, xo], [xo])
                                k.dma("sp", xv[:, dc, base + o:base + o + n], xo.t[:, 0:n], [xo], [])

    def declare_mixer_inputs(self):
        L = DEPTH
        self.s5v = self.din("s5v", [L, 128, 192])
        self.s5A = self.din("s5A", [L, 128, 64, 16])
        self.s5B = self.din("s5B", [L, 128, 64, 16])
        self.s5CA = self.din("s5CA", [L, 128, 64, 16])
        self.s5CB = self.din("s5CB", [L, 128, 64, 16])
        self.s5w = self.din("s5w", [L, 128, 8])
        self.glu_w = self.din("glu_w", [L, 512, 512])
        self.nidx8 = self.din("nidx8", [2, 128, T // 8])
        self.selu = self.din("selu", [128, 64, 128], BF16)
        self.selt = self.din("selt", [128, 64, 128], BF16)
        self.bmask = self.din("bmask", [128, 256])
        self.lruv = self.din("lruv", [L, 128, 44])
        self.lru_wa = self.din("lru_wa", [L, 2, 4, 128, 128])
        self.lru_wx = self.din("lru_wx", [L, 2, 4, 128, 128])
        self.ygd = self.dscr("ygd", [2, 512, T])
        self.mlb = self.din("mlb", [L, 128, 16])
        self.onw = self.din("onw", [L, 128, 512])
        self.selc = self.din("selc", [16, 2048])
        self.mlav = self.din("mlav", [L, 128, 896])
        self.w_q_up = self.din("w_q_up", [L, 384, 768])
        self.w_kv_up = self.din("w_kv_up", [L, 128, 1024])
        self.ropec = self.din("ropec", [128, NT, 32])
        self.ropes = self.din("ropes", [128, NT, 32])

    def setup_mixer_consts(self):
        k = self.k
        self.sgn = self.C.t[:, 896:897]
        self.oneT = k.sb("oneT", [128, 1], F32)
        k.op("dve", lambda e: e.memset(self.oneT.t[:], 1.0), [], [self.oneT])
        self.hpiT = k.sb("hpiT", [128, 1], F32)
        k.op("dve", lambda e: e.memset(self.hpiT.t[:], float(np.pi / 2)), [], [self.hpiT])

    def mixers(self, l):
        which = self.dbg.get("mixers", ("s5", "lru", "mlstm", "mla"))
        if "s5" in which:
            self.mixer_s5(l)
        if "lru" in which:
            self.mixer_lru(l)
        if "mlstm" in which:
            self.mixer_mlstm(l)
        if "mla" in which:
            self.mixer_mla(l)

    def frac_centered(self, out, u, ki, tmp, n, bufs):
        k = self.k
        k.cp(ki, u, bufs, bufs)
        k.tt(tmp, u, ki, ALU.subtract, bufs, bufs)
        k.stt(out, tmp, 0.5, tmp, ALU.is_gt, ALU.subtract, bufs, bufs)
        k.stt(out, out, 0.5, out, ALU.is_gt, ALU.subtract, bufs, bufs)

    def sincos(self, sin_out, cos_out, r, tmp, bufs):
        k = self.k
        k.act(sin_out, r, AF.Sin, bufs, bufs, scale=TWO_PI)
        k.stt(tmp, r, 0.25, r, ALU.is_gt, ALU.subtract, bufs, bufs)
        k.act(cos_out, tmp, AF.Sin, bufs + [self.hpiT], bufs, scale=-TWO_PI, bias=self.hpiT.t[:, 0:1])

    def gelu_tanh(self, out, y, t1, s1, bufs):
        k = self.k
        k.tt(t1, y, y, ALU.mult, bufs, bufs)
        k.ts(t1, t1, 0.044715, 1.0, ALU.mult, ALU.add, bufs, bufs)
        k.tt(t1, t1, y, ALU.mult, bufs, bufs)
        k.act(s1, t1, AF.Sigmoid, bufs, bufs, scale=1.5957691216057308)
        k.tt(out, y, s1, ALU.mult, bufs, bufs)

    def mixer_s5(self, l):
        k = self.k
        NK = T // 8
        KC = CTX // 8
        with k.scope():
            pv = k.sb("s5pv", [128, 192], F32)
            k.dma("sp", pv.t[:], self.s5v[l], [], [pv])
            A = k.sb("s5A", [128, 64, 16], F32)
            Bm = k.sb("s5B", [128, 64, 16], F32)
            CA = k.sb("s5CA", [128, 64, 16], F32)
            CB = k.sb("s5CB", [128, 64, 16], F32)
            k.dma("sp", A.t[:], self.s5A[l], [], [A])
            k.dma("sp", Bm.t[:], self.s5B[l], [], [Bm])
            k.dma("sp", CA.t[:], self.s5CA[l], [], [CA])
            k.dma("sp", CB.t[:], self.s5CB[l], [], [CB])
            sw = k.sb("s5w", [128, 8], F32)
            k.dma("sp", sw.t[:], self.s5w[l], [], [sw])
            nid = k.sb("nidx8", [128, 2, NK], F32)
            for d in range(2):
                k.dma("sp", nid.t[:, d, :], self.nidx8[d], [], [nid])
            selu = k.sb("selu", [128, 64, 128], BF16)
            selt = k.sb("selt", [128, 64, 128], BF16)
            k.dma("sp", selu.t[:], self.selu[:, :, :], [], [selu])
            k.dma("sp", selt.t[:], self.selt[:, :, :], [], [selt])
            bmask = k.sb("bmask", [128, 256], F32)
            k.dma("sp", bmask.t[:], self.bmask[:, :], [], [bmask])
            W = k.sb("s5work", [128, 16, 64], F32)
            WI = k.sb("s5worki", [128, 64], I32)
            Wb = [W]
            lr, li, dt, mag, ang, fT, sn, cs_, t0_, t1_, fr, fi, den, lrdt, f8, ar1 = [W.t[:, i, :] for i in range(16)]
            k.ts(lr, pv.t[:, 0:64], -1e-4, None, ALU.min, None, [pv], Wb)
            k.cp(li, pv.t[:, 64:128], [pv], Wb)
            k.act(dt, pv.t[:, 128:192], AF.Exp, [pv], Wb)
            k.tt(lrdt, lr, dt, ALU.mult, Wb, Wb)
            k.act(mag, lrdt, AF.Exp, Wb, Wb)
            k.tt(ang, li, dt, ALU.mult, Wb, Wb)
            k.ts(t0_, ang, 1.0 / TWO_PI, None, ALU.mult, None, Wb, Wb)
            self.frac_centered(fT, t0_, WI.t[:], t1_, 64, Wb + [WI])
            self.sincos(sn, cs_, fT, t1_, Wb)
            k.tt(t0_, mag, cs_, ALU.mult, Wb, Wb)
            k.ts(ar1, t0_, -1.0, None, ALU.add, None, Wb, Wb)
            k.tt(t1_, mag, sn, ALU.mult, Wb, Wb)
            k.tt(den, lr, lr, ALU.mult, Wb, Wb)
            k.tt(t0_, li, li, ALU.mult, Wb, Wb)
            k.tt(den, den, t0_, ALU.add, Wb, Wb)
            k.op("dve", lambda e: e.reciprocal(out=den, in_=den), Wb, Wb)
            k.tt(fr, ar1, lr, ALU.mult, Wb, Wb)
            k.tt(t0_, t1_, li, ALU.mult, Wb, Wb)
            k.tt(fr, fr, t0_, ALU.add, Wb, Wb)
            k.tt(fr, fr, den, ALU.mult, Wb, Wb)
            k.tt(fi, t1_, lr, ALU.mult, Wb, Wb)
            k.tt(t0_, ar1, li, ALU.mult, Wb, Wb)
            k.tt(fi, fi, t0_, ALU.subtract, Wb, Wb)
            k.tt(fi, fi, den, ALU.mult, Wb, Wb)
            nsgn = k.sb("nsgn", [128, 1], F32)
            k.ts(nsgn.t[:], self.sgn, -1.0, None, ALU.mult, None, [self.C], [nsgn])
            M8 = k.sb("mag8", [128, 64], F32)
            k.act(M8.t[:], lrdt, AF.Exp, Wb, [M8], scale=8.0)
            k.ts(t0_, fT, 8.0, None, ALU.mult, None, Wb, Wb)
            self.frac_centered(f8, t0_, WI.t[:], t1_, 64, Wb + [WI])
            PR = k.sb("PR", [128, 16, 64], F32)
            PI = k.sb("PI", [128, 16, 64], F32)
            PRN = k.sb("PRN", [128, 16, 64], F32)
            PRM = k.sb("PRM", [128, 16, 64], F32)
            PIS = k.sb("PIS", [128, 16, 64], F32)
            PIM = k.sb("PIM", [128, 16, 64], F32)
            GR = k.sb("GR", [128, 16, 64], F32)
            GI = k.sb("GI", [128, 16, 64], F32)
            GRN = k.sb("GRN", [128, 16, 64], F32)
            GIS = k.sb("GIS", [128, 16, 64], F32)
            GIM = k.sb("GIM", [128, 16, 64], F32)
            PWs = [PR, PI, PRN, PRM, PIS, PIM, GR, GI, GRN, GIS, GIM]
            for m in range(-7, 9):
                mi = m + 7
                k.act(t0_, lrdt, AF.Exp, Wb, Wb, scale=float(m))
                k.ts(ang, fT, float(m), None, ALU.mult, None, Wb, Wb)
                self.frac_centered(den, ang, WI.t[:], t1_, 64, Wb + [WI])
                self.sincos(sn, cs_, den, t1_, Wb)
                k.tt(PR.t[:, mi, :], t0_, cs_, ALU.mult, Wb, [PR])
                k.tt(PI.t[:, mi, :], t0_, sn, ALU.mult, Wb, [PI])
                k.ts(PRN.t[:, mi, :], PR.t[:, mi, :], nsgn.t[:, 0:1], None, ALU.mult, None, [PR, nsgn], [PRN])
                k.ts(PRM.t[:, mi, :], PR.t[:, mi, :], -1.0, None, ALU.mult, None, [PR], [PRM])
                k.ts(PIS.t[:, mi, :], PI.t[:, mi, :], self.sgn, None, ALU.mult, None, [PI, self.C], [PIS])
                k.ts(PIM.t[:, mi, :], PI.t[:, mi, :], -1.0, None, ALU.mult, None, [PI], [PIM])
                k.tt(t0_, PR.t[:, mi, :], fr, ALU.mult, [PR] + Wb, Wb)
                k.tt(t1_, PI.t[:, mi, :], fi, ALU.mult, [PI] + Wb, Wb)
                k.tt(GR.t[:, mi, :], t0_, t1_, ALU.subtract, Wb, [GR])
                k.tt(t0_, PR.t[:, mi, :], fi, ALU.mult, [PR] + Wb, Wb)
                k.tt(t1_, PI.t[:, mi, :], fr, ALU.mult, [PI] + Wb, Wb)
                k.tt(GI.t[:, mi, :], t0_, t1_, ALU.add, Wb, [GI])
                k.ts(GRN.t[:, mi, :], GR.t[:, mi, :], nsgn.t[:, 0:1], None, ALU.mult, None, [GR, nsgn], [GRN])
                k.ts(GIS.t[:, mi, :], GI.t[:, mi, :], self.sgn, None, ALU.mult, None, [GI, self.C], [GIS])
                k.ts(GIM.t[:, mi, :], GI.t[:, mi, :], -1.0, None, ALU.mult, None, [GI], [GIM])

            LB = k.sb("LB", [128, 128], F32)
            RC = k.sb("RC", [128, 128], F32)
            LST = k.sb("LST", [128, 128], F32)
            LS2T = k.sb("LS2T", [128, 128], F32)
            W1f = k.sb("W1f", [128, 128], F32)
            W2f = k.sb("W2f", [128, 128], F32)
            tA = k.ring("tA", [128, 8, 16], F32, 6)
            Mi_r = k.ring("Mi", [128, 128], BF16, 2)
            LS_r = k.ring("LS", [128, 128], BF16, 2)
            LS2_r = k.ring("LS2", [128, 128], BF16, 2)
            W1_r = k.ring("W1", [128, 128], BF16, 2)
            W2_r = k.ring("W2", [128, 128], BF16, 2)
            C8 = k.sb("C8", [128, NK], F32)
            S8 = k.sb("S8", [128, NK], F32)
            U8 = k.sb("U8", [128, NK], F32)
            K8 = k.sb("K8", [128, NK], I32)
            R8 = k.sb("R8", [128, NK], F32)
            Ug = [k.sb("Ug", [128, NK], BF16) for _ in range(2)]
            t1r = k.ring("t1", [128, NK], F32, 2)
            btr = k.ring("bt", [128, NK], F32, 2)
            Gr_ = k.ring("G", [128, NK], F32, 2)
            V1r = k.ring("V1", [128, NK], BF16, 2)
            V2r = k.ring("V2", [128, NK], BF16, 2)
            Ysb = [[k.sb("Ysb", [128, NK], BF16) for _ in range(2)] for _ in range(8)]
            ub = [k.sb("ub", [128, T], BF16) for _ in range(2)]
            yacc = [k.sb("yacc", [128, T], F32) for _ in range(2)]
            fin = k.ring("fin", [128, 512], F32, 6)
            wc32 = k.ring("wc32", [128, 2048], F32, 2)
            wc16 = k.ring("wc16", [128, 2048], BF16, 2)
            wgen = self.wcast_gen(l, wc32, wc16)
            pY = [k.ps("pY0"), k.ps("pY1")]
            pP = k.psring("pP", 2)
            pW = k.psring("pW", 2)
            pUn = k.psring("pUn", 2)
            ei = 0

            def build(dst, coefA, coefB, srcA, srcB, mlist, dg):
                mi0 = mlist[0] + 7
                if mlist[1] - mlist[0] == 1:
                    msl = slice(mi0, mi0 + 8)
                else:
                    msl = slice(mi0, (mi0 - 8) if mi0 - 8 >= 0 else None, -1)
                ca = coefA.t[:, msl, dg:dg + 1].to_broadcast([128, 8, 16])
                cb = coefB.t[:, msl, dg:dg + 1].to_broadcast([128, 8, 16])
                sa = srcA.t[:, dg:dg + 1, :].to_broadcast([128, 8, 16])
                sb_ = srcB.t[:, dg:dg + 1, :].to_broadcast([128, 8, 16])
                ta = tA.next()
                tb = tA.next()
                k.tt(ta.t[:], sa, ca, ALU.mult, [srcA, coefA], [ta])
                k.tt(tb.t[:], sb_, cb, ALU.mult, [srcB, coefB], [tb], eng="pool")
                k.tt(dst.t[:].rearrange("p (m c) -> p m c", c=16), ta.t[:], tb.t[:], ALU.add, [ta, tb], [dst])

            for c in range(4):
                for b in range(2):
                    for (t0, n) in TB:
                        u_ = fin.next()
                        k.dma("sp", u_.t[:, 0:n], self.zf[b, c * 128:(c + 1) * 128, t0:t0 + n], [], [u_])
                        k.cp(ub[b].t[:, t0:t0 + n], u_.t[:, 0:n], [u_], [ub[b]], eng="act")
                for j in range(8):
                    g = 8 * c + j
                    for b in range(2):
                        p = pW.next()
                        for s_ in range(8):
                            k.mm(p.t[:, 0:NK], selu.t[:, j * 8 + s_, :], ub[b].t[:, s_:T:8], s_ == 0, s_ == 7, [selu, ub[b]], [p])
                        k.cp(Ug[b].t[:], p.t[:, 0:NK], [p], [Ug[b]], eng="act")
                    for d in range(2):
                        dg = d * 32 + g
                        if d == 0:
                            mLB = [-s_ for s_ in range(8)]
                            mLS = [7 - s_ for s_ in range(8)]
                            mRC = [t_ for t_ in range(8)]
                            mW = [t_ + 1 for t_ in range(8)]
                        else:
                            mLB = [s_ for s_ in range(8)]
                            mLS = [s_ for s_ in range(8)]
                            mRC = [-t_ for t_ in range(8)]
                            mW = [8 - t_ for t_ in range(8)]
                        build(LB, GRN, GIM, A, Bm, mLB, dg)
                        build(RC, PR, PIS, CA, CB, mRC, dg)
                        build(LST, GR, GIS, A, Bm, mLS, dg)
                        build(LS2T, GI, GRN, A, Bm, mLS, dg)
                        build(W1f, PRN, PIM, CA, CB, mW, dg)
                        build(W2f, PIS, PRM, CA, CB, mW, dg)
                        p = pW.next()
                        k.mm(p.t[:, 0:128], LB.t[:], RC.t[:], True, True, [LB, RC], [p])
                        Mi = Mi_r.next()
                        k.tt(Mi.t[:], p.t[:, 0:128], bmask.t[:, d * 128:(d + 1) * 128], ALU.mult, [p, bmask], [Mi])
                        p = pW.next()
                        k.tr(p.t[:, 0:128], LST.t[:], self.identF, [LST, self.C], [p])
                        k.tr(p.t[:, 128:256], LS2T.t[:], self.identF, [LS2T, self.C], [p])
                        LS = LS_r.next()
                        LS2 = LS2_r.next()
                        k.cp(LS.t[:], p.t[:, 0:128], [p], [LS], eng="act")
                        k.cp(LS2.t[:], p.t[:, 128:256], [p], [LS2], eng="act")
                        W1 = W1_r.next()
                        W2 = W2_r.next()
                        k.cp(W1.t[:], W1f.t[:], [W1f], [W1], eng="pool")
                        k.cp(W2.t[:], W2f.t[:], [W2f], [W2], eng="pool")
                        k.ts(U8.t[:], nid.t[:, d, :], f8[:, dg:dg + 1], None, ALU.mult, None, [nid] + Wb, [U8])
                        self.frac_centered(R8.t[:], U8.t[:], K8.t[:], U8.t[:], NK, [U8, K8, R8])
                        self.sincos(S8.t[:], C8.t[:], R8.t[:], U8.t[:], [R8, U8, S8, C8])
                        for b in range(2):
                            next(wgen, None)
                            p1 = pP.next()
                            p2 = pP.next()
                            k.mm(p1.t[:, 0:NK], LS.t[:], Ug[b].t[:], True, True, [LS, Ug[b]], [p1])
                            k.mm(p2.t[:, 0:NK], LS2.t[:], Ug[b].t[:], True, True, [LS2, Ug[b]], [p2])
                            t1 = t1r.next()
                            bt = btr.next()
                            k.tt(t1.t[:], p1.t[:, 0:NK], C8.t[:], ALU.mult, [p1, C8], [t1])
                            k.tt(bt.t[:], p2.t[:, 0:NK], S8.t[:], ALU.mult, [p2, S8], [bt])
                            k.tt(bt.t[:], bt.t[:], t1.t[:], ALU.add, [bt, t1], [bt])
                            G = Gr_.next()
                            rm = M8.t[:, dg:dg + 1]
                            if d == 0:
                                k.scan(G.t[:, 0:KC], rm.to_broadcast([128, KC]), bt.t[:, 0:KC], 0.0, [bt, M8], [G])
                                k.scan(G.t[:, KC:NK], rm.to_broadcast([128, NK - KC]), bt.t[:, KC:NK], G.t[:, KC - 1:KC], [bt, G, M8], [G])
                            else:
                                k.scan(G.t[:, 0:KC][:, ::-1], rm.to_broadcast([128, KC]), bt.t[:, 0:KC][:, ::-1], 0.0, [bt, M8], [G])
                                k.scan(G.t[:, KC:NK][:, ::-1], rm.to_broadcast([128, NK - KC]), bt.t[:, KC:NK][:, ::-1], G.t[:, 0:1], [bt, G, M8], [G])
                            V1 = V1r.next()
                            V2 = V2r.next()
                            k.tt(V1.t[:], G.t[:], C8.t[:], ALU.mult, [G, C8], [V1])
                            k.tt(V2.t[:], G.t[:], S8.t[:], ALU.mult, [G, S8], [V2])
                            py = pY[b]
                            k.mm(py.t[:, 0:NK], Mi.t[:], Ug[b].t[:], d == 0, False, [Mi, Ug[b]], [py])
                            if d == 0:
                                segs = [(1, NK, 0)]
                            else:
                                segs = [(0, KC - 1, 1), (KC, NK - 1, KC + 1), (NK - 1, NK, 0)]
                            for si, (o0, o1, s0) in enumerate(segs):
                                n_ = o1 - o0
                                lastmm = (d == 1 and si == len(segs) - 1)
                                k.mm(py.t[:, o0:o1], W1.t[:], V1.t[:, s0:s0 + n_], False, False, [W1, V1], [py])
                                k.mm(py.t[:, o0:o1], W2.t[:], V2.t[:, s0:s0 + n_], False, lastmm, [W2, V2], [py])
                    for b in range(2):
                        k.cp(Ysb[j][b].t[:], pY[b].t[:, 0:NK], [pY[b]], [Ysb[j][b]], eng=("act" if b else "dve"))
                for b in range(2):
                    for bb in range(5):
                        nk = 64 if bb < 4 else 32
                        p = pUn.next()
                        for t_ in range(8):
                            for j in range(8):
                                k.mm(p.t[:, t_:8 * nk:8], selt.t[:, j * 8 + t_, :], Ysb[j][b].t[:, 64 * bb:64 * bb + nk], j == 0, j == 7,
                                     [selt, Ysb[j][b]], [p])
                        k.cp(yacc[b].t[:, 512 * bb:512 * bb + 8 * nk], p.t[:, 0:8 * nk], [p], [yacc[b]], eng=("act" if bb % 2 else "dve"))
                for b in range(2):
                    for (t0, n) in TB:
                        u_ = fin.next()
                        k.dma("sp", u_.t[:, 0:n], self.zf[b, c * 128:(c + 1) * 128, t0:t0 + n], [], [u_])
                        y_ = fin.next()
                        k.stt(y_.t[:, 0:n], u_.t[:, 0:n], sw.t[:, c:c + 1], yacc[b].t[:, t0:t0 + n], ALU.mult, ALU.add,
                              [u_, sw, yacc[b]], [y_])
                        o_ = fin.next()
                        a_ = fin.next()
                        b_ = fin.next()
                        self.gelu_tanh(o_.t[:, 0:n], y_.t[:, 0:n], a_.t[:, 0:n], b_.t[:, 0:n], [o_, y_, a_, b_])
                        k.dma("sp", self.ygd[b, c * 128:(c + 1) * 128, t0:t0 + n], o_.t[:, 0:n], [o_], [])
            for _ in wgen:
                pass
            self.wcast_done = True
        with k.scope():
            sw = k.sb("s5w", [128, 8], F32)
            k.dma("sp", sw.t[:], self.s5w[l], [], [sw])
            gw32 = k.sb("gw32", [128, 4, 512], F32)
            gw = k.sb("gw", [128, 4, 512], BF16)
            k.dma("sp", gw32.t[:], self.glu_w[l].rearrange("(kc p) o -> p kc o", p=128), [], [gw32])
            k.cp(gw.t[:], gw32.t[:], [gw32], [gw], eng="pool")
            yg = k.sb("yg", [128, 4, T], F32)
            ygb = k.sb("ygb", [128, 4, T], BF16)
            sg = k.ring("sg", [128, 512], F32, 2)
            ob = k.ring("ob", [128, 512], BF16, 3)
            pp = k.psring("pg", 3)
            for b in range(2):
                for c in range(4):
                    k.dma("sp", yg.t[:, c, :], self.ygd[b, c * 128:(c + 1) * 128, :], [], [yg])
                    k.cp(ygb.t[:, c, :], yg.t[:, c, :], [yg], [ygb], eng="act")
                for co in range(4):
                    for (t0, n) in TB:
                        p = pp.next()
                        for kc in range(4):
                            k.mm(p.t[:, 0:n], gw.t[:, kc, co * 128:(co + 1) * 128], ygb.t[:, kc, t0:t0 + n], kc == 0, kc == 3, [gw, ygb], [p])
                        s_ = sg.next()
                        k.act(s_.t[:, 0:n], p.t[:, 0:n], AF.Sigmoid, [p, sw], [s_], bias=sw.t[:, 4 + co:5 + co])
                        o = ob.next()
                        k.tt(o.t[:, 0:n], yg.t[:, co, t0:t0 + n], s_.t[:, 0:n], ALU.mult, [yg, s_], [o])
                        k.dma("sp", self.cc[b, co * 128:(co + 1) * 128, t0:t0 + n], o.t[:, 0:n], [o], [])

    def mixer_lru(self, l):
        k = self.k
        with k.scope():
            lv = k.sb("lruv", [128, 44], F32)
            k.dma("sp", lv.t[:], self.lruv[l], [], [lv])
            cw = lv.t[:, 0:16].rearrange("p (c j) -> p c j", j=4)
            cb = lv.t[:, 16:20]
            ba = lv.t[:, 20:28].rearrange("p (d c) -> p d c", c=4)
            bx = lv.t[:, 28:36].rearrange("p (d c) -> p d c", c=4)
            lam = lv.t[:, 36:44]
            sp = k.sb("lrusp", [128, 16], F32)
            k.act(sp.t[:, 0:8], lam, AF.Exp, [lv], [sp], scale=-1.0)
            k.act(sp.t[:, 0:8], sp.t[:, 0:8], AF.Ln, [sp, self.oneT], [sp], bias=self.oneT.t[:, 0:1])
            k.ts(sp.t[:, 8:16], sp.t[:, 0:8], -16.0, None, ALU.mult, None, [sp], [sp])
            k.ts(sp.t[:, 0:8], sp.t[:, 0:8], -8.0, None, ALU.mult, None, [sp], [sp])
            wa32 = k.sb("wa32", [128, 8, 128], F32)
            wx32 = k.sb("wx32", [128, 8, 128], F32)
            wa = k.sb("wa", [128, 8, 128], BF16)
            wx = k.sb("wx", [128, 8, 128], BF16)
            k.dma("sp", wa32.t[:], self.lru_wa[l].rearrange("d n c o -> c (d n) o"), [], [wa32])
            k.dma("sp", wx32.t[:], self.lru_wx[l].rearrange("d n c o -> c (d n) o"), [], [wx32])
            k.cp(wa.t[:], wa32.t[:], [wa32], [wa])
            k.cp(wx.t[:], wx32.t[:], [wx32], [wx])
            x = k.sb("lx", [128, T], F32)
            gt = k.sb("lg", [128, T], F32)
            xs = k.sb("lxs", [128, T], F32)
            xsb = k.sb("lxsb", [128, T], BF16)
            r_ = k.sb("lr", [128, T], F32)
            i_ = k.sb("li", [128, T], F32)
            a_ = k.sb("la", [128, T], F32)
            q_ = k.sb("lq", [128, T], F32)
            h_ = k.sb("lh", [128, T], F32)
            ys = k.sb("lys", [128, T], F32)
            ob = k.sb("lob", [128, T], BF16)
            pp = k.psring("pl", 4)
            for b in range(2):
                for c in range(4):
                    k.dma("sp", x.t[:], self.zf[b, 1536 + c * 128:1536 + (c + 1) * 128, :], [], [x])
                    k.dma("sp", gt.t[:], self.zf[b, 2048 + c * 128:2048 + (c + 1) * 128, :], [], [gt])
                    k.ts(xs.t[:], x.t[:], cw[:, c, 2:3], cb[:, c:c + 1], ALU.mult, ALU.add, [x, lv], [xs])
                    for jtap in (0, 1, 3):
                        o = jtap - 2
                        for (r0, r1) in ((0, CTX), (CTX, T)):
                            a0 = r0 + max(0, -o)
                            a1 = r1 - max(0, o)
                            k.stt(xs.t[:, a0:a1], x.t[:, a0 + o:a1 + o], cw[:, c, jtap:jtap + 1], xs.t[:, a0:a1],
                                  ALU.mult, ALU.add, [x, lv, xs], [xs])
                    k.cp(xsb.t[:], xs.t[:], [xs], [xsb], eng="act")
                    for d in range(2):
                        for (t0, n) in TB:
                            p = pp.next()
                            k.mm(p.t[:, 0:n], wa.t[:, d * 4 + c, :], xsb.t[:, t0:t0 + n], True, True, [wa, xsb], [p])
                            k.act(r_.t[:, t0:t0 + n], p.t[:, 0:n], AF.Sigmoid, [p, lv], [r_], bias=ba[:, d, c:c + 1])
                            p = pp.next()
                            k.mm(p.t[:, 0:n], wx.t[:, d * 4 + c, :], xsb.t[:, t0:t0 + n], True, True, [wx, xsb], [p])
                            k.act(i_.t[:, t0:t0 + n], p.t[:, 0:n], AF.Sigmoid, [p, lv], [i_], bias=bx[:, d, c:c + 1])
                        dc = d * 4 + c
                        k.act(a_.t[:], r_.t[:], AF.Exp, [r_, sp], [a_], scale=sp.t[:, dc:dc + 1])
                        k.act(q_.t[:], r_.t[:], AF.Exp, [r_, sp], [q_], scale=sp.t[:, 8 + dc:9 + dc])
                        k.act(q_.t[:], q_.t[:], AF.Sqrt, [q_, self.oneT], [q_], scale=-1.0, bias=self.oneT.t[:, 0:1])
                        k.tt(q_.t[:], q_.t[:], i_.t[:], ALU.mult, [q_, i_], [q_])
                        k.tt(q_.t[:], q_.t[:], xs.t[:], ALU.mult, [q_, xs], [q_])
                        if d == 0:
                            k.scan(h_.t[:, 0:CTX], a_.t[:, 0:CTX], q_.t[:, 0:CTX], 0.0, [a_, q_], [h_])
                            k.scan(h_.t[:, CTX:T], a_.t[:, CTX:T], q_.t[:, CTX:T], h_.t[:, CTX - 1:CTX], [a_, q_, h_], [h_])
                            k.cp(ys.t[:], h_.t[:], [h_], [ys], eng="pool")
                        else:
                            k.scan(h_.t[:, 0:CTX][:, ::-1], a_.t[:, 0:CTX][:, ::-1], q_.t[:, 0:CTX][:, ::-1], 0.0, [a_, q_], [h_])
                            k.scan(h_.t[:, CTX:T][:, ::-1], a_.t[:, CTX:T][:, ::-1], q_.t[:, CTX:T][:, ::-1], h_.t[:, 0:1], [a_, q_, h_], [h_])
                            k.tt(ys.t[:], ys.t[:], h_.t[:], ALU.add, [ys, h_], [ys])
                    self.gelu_tanh(h_.t[:], gt.t[:], a_.t[:], q_.t[:], [h_, gt, a_, q_])
                    k.tt(ob.t[:], ys.t[:], h_.t[:], ALU.mult, [ys, h_], [ob])
                    k.dma("sp", self.cc[b, 1536 + c * 128:1536 + (c + 1) * 128, :], ob.t[:], [ob], [])

    def mixer_mlstm(self, l):
        k = self.k
        KS = 128 ** -0.5
        TRI3 = self.C.t[:, 256:640]
        MASK = [self.C.t[:, 640:768], self.C.t[:, 768:896]]
        for b in range(2):
            with k.scope():
                Hacc = k.sb("Hacc", [128, NT, 512], F32)
                k.op("pool", lambda e: e.memset(Hacc.t[:], 0.0), [], [Hacc])
                with k.scope():
                    mlb = k.sb("mlb", [128, 16], F32)
                    k.dma("sp", mlb.t[:], self.mlb[l], [], [mlb])
                    sel = k.sb("sel", [16, 2048], F32)
                    nsel = k.sb("nsel", [16, 2048], F32)
                    k.dma("sp", sel.t[:], self.selc[:, :], [], [sel])
                    k.ts(nsel.t[:], sel.t[:], -1.0, None, ALU.mult, None, [sel], [nsel])
                    QT = k.sb("QT", [128, 4, T], BF16)
                    KT = k.sb("KT", [128, 4, T], BF16)
                    Kt = k.sb("Kt", [128, NT, 512], BF16)
                    Va = k.sb("Va", [128, NT, 4, 129], BF16)
                    G16 = k.sb("G16", [128, NT, 16], F32)
                    R = k.sb("R", [16, NT, 384], F32)
                    with k.scope():
                        st = k.ring("st", [128, T], F32, 2)
                        zr = k.ring("zr", [128, 1552], F32, 2)
                        pr = k.psring("pr", 2)
                        for h in range(4):
                            s_ = st.next()
                            k.dma("sp", s_.t[:], self.zf[b, 512 + h * 128:512 + (h + 1) * 128, :], [], [s_])
                            k.cp(QT.t[:, h, :], s_.t[:], [s_], [QT], eng="act")
                            s_ = st.next()
                            k.dma("sp", s_.t[:], self.zf[b, 1024 + h * 128:1024 + (h + 1) * 128, :], [], [s_])
                            k.act(KT.t[:, h, :], s_.t[:], AF.Copy, [s_], [KT], scale=KS)
                        k.op("pool", lambda e: e.memset(Va.t[:], 1.0), [], [Va])
                        for tt in range(NT):
                            z = zr.next()
                            k.dma("sp", z.t[:], self.zt[b, tt * 128:(tt + 1) * 128, 0:1552], [], [z])
                            k.act(Kt.t[:, tt, :], z.t[:, 0:512], AF.Copy, [z], [Kt], scale=KS)
                            k.cp(Va.t[:, tt, :, 0:128], z.t[:, 512:1024].rearrange("p (h e) -> p h e", h=4), [z], [Va])
                            k.tt(G16.t[:, tt, :], z.t[:, 1536:1552], mlb.t[:], ALU.add, [z, mlb], [G16])
                        for d in range(2):
                            gv = G16.t[:, :, d * 8 + 4:d * 8 + 8]
                            k.act(gv, gv, AF.Exp, [G16], [G16], scale=-1.0)
                            k.act(gv, gv, AF.Ln, [G16, self.oneT], [G16], bias=self.oneT.t[:, 0:1])
                            k.ts(gv, gv, -1.0, None, ALU.mult, None, [G16], [G16])
                        for tt in range(NT):
                            p = pr.next()
                            k.mm(p.t[0:16, 0:384], G16.t[:, tt, :], TRI3, True, True, [G16, self.C], [p])
                            k.cp(R.t[:, tt, :], p.t[0:16, 0:384], [p], [R], eng="act")
                    CT32_ = {}
                    CTb_ = {}
                    for d in range(2):
                        for h in range(4):
                            CT32_[d, h] = k.sb("CT32", [128, 129], F32)
                            CTb_[d, h] = k.sb("CTb", [128, 129], BF16)
                            k.op("dve", lambda e: e.memset(CT32_[d, h].t[:], 0.0), [], [CT32_[d, h]])
                            k.op("dve", lambda e: e.memset(CTb_[d, h].t[:], 0.0), [], [CTb_[d, h]])
                    EDr = k.ring("ED", [128, 128], F32, 4)
                    EBr = k.ring("EB", [128, 128], F32, 4)
                    STr = k.ring("ST", [128, 128], BF16, 4)
                    QSr = k.ring("QS", [128, 128], BF16, 4)
                    VWr = k.ring("VW", [128, 129], BF16, 4)
                    dnr = k.ring("dn", [128, 2], F32, 4)
                    pD = k.psring("pD", 2)
                    pB = k.psring("pB", 1)
                    pS = k.psring("pS", 2)
                    pN = k.psring("pN", 2)
                    pC = k.psring("pC", 1)
                    orders = [list(range(NT)), [1, 0] + list(range(NT - 1, 1, -1))]
                    for step in range(NT):
                        for d in range(2):
                            tt = orders[d][step]
                            bsl = slice(0, 128) if d == 0 else slice(128, 256)
                            last = 127 if d == 0 else 0
                            for h in range(4):
                                CT32 = CT32_[d, h]
                                CTb = CTb_[d, h]
                                kli = d * 8 + h
                                klf = d * 8 + 4 + h
                                SLI = sel.t[0:16, kli * 128:(kli + 1) * 128]
                                SLF = sel.t[0:16, klf * 128:(klf + 1) * 128]
                                NLF = nsel.t[0:16, klf * 128:(klf + 1) * 128]
                                tsl = slice(tt * 128, (tt + 1) * 128)
                                Rb = R.t[0:16, tt, bsl]
                                Rg = R.t[0:16, tt, 256:384]
                                pd_ = pD.next()
                                k.mm(pd_.t[:, 0:128], Rg, SLI, True, False, [R, sel], [pd_])
                                k.mm(pd_.t[:, 0:128], Rb, NLF, False, False, [R, nsel], [pd_])
                                k.mm(pd_.t[:, 0:128], SLF, Rb, False, False, [R, sel], [pd_])
                                k.mm(pd_.t[:, 0:128], self.identF, MASK[d], False, True, [self.C], [pd_])
                                ED = EDr.next()
                                k.act(ED.t[:], pd_.t[:, 0:128], AF.Exp, [pd_], [ED])
                                pb_ = pB.next()
                                k.mm(pb_.t[:, 0:128], SLF, Rb, True, True, [R, sel], [pb_])
                                EB = EBr.next()
                                k.act(EB.t[:], pb_.t[:, 0:128], AF.Exp, [pb_], [EB])
                                ps_ = pS.next()
                                k.mm(ps_.t[:, 0:128], KT.t[:, h, tsl], QT.t[:, h, tsl], True, True, [KT, QT], [ps_])
                                ST = STr.next()
                                k.tt(ST.t[:], ps_.t[:, 0:128], ED.t[:], ALU.mult, [ps_, ED], [ST])
                                QS = QSr.next()
                                k.tt(QS.t[:], QT.t[:, h, tsl], EB.t[:], ALU.mult, [QT, EB], [QS])
                                pn_ = pN.next()
                                k.mm(pn_.t[:, 0:129], QS.t[:], CTb.t[:], True, False, [QS, CTb], [pn_])
                                k.mm(pn_.t[:, 0:129], ST.t[:], Va.t[:, tt, h, :], False, True, [ST, Va], [pn_])
                                dn = dnr.next()
                                k.act(dn.t[:, 0:1], pn_.t[:, 128:129], AF.Abs, [pn_], [dn])
                                k.ts(dn.t[:, 0:1], dn.t[:, 0:1], 1.0, None, ALU.max, None, [dn], [dn])
                                k.op("dve", lambda e: e.reciprocal(out=dn.t[:, 1:2], in_=dn.t[:, 0:1]), [dn], [dn])
                                hs = Hacc.t[:, tt, h * 128:(h + 1) * 128]
                                k.stt(hs, pn_.t[:, 0:128], dn.t[:, 1:2], hs, ALU.mult, ALU.add, [pn_, dn, Hacc], [Hacc])
                                VW = VWr.next()
                                k.act(VW.t[:], Va.t[:, tt, h, :], AF.Identity, [Va, ED], [VW], scale=ED.t[:, last:last + 1])
                                pc_ = pC.next()
                                k.mm(pc_.t[:, 0:129], Kt.t[:, tt, h * 128:(h + 1) * 128], VW.t[:], True, True, [Kt, VW], [pc_])
                                k.stt(CT32.t[:], CT32.t[:], EB.t[:, last:last + 1], pc_.t[:, 0:129], ALU.mult, ALU.add,
                                      [CT32, EB, pc_], [CT32])
                                k.cp(CTb.t[:], CT32.t[:], [CT32], [CTb], eng="act")
                with k.scope():
                    onw = k.sb("onw", [128, 512], F32)
                    k.dma("sp", onw.t[:], self.onw[l], [], [onw])
                    OT = k.sb("OT", [128, 4, T], BF16)
                    mor = k.ring("mo", [128, 512], F32, 2)
                    sqr = k.ring("sqh", [128, 512], F32, 2)
                    ssr = k.ring("ss4", [128, 4], F32, 2)
                    pT = k.psring("pT", 2)
                    for tt in range(NT):
                        H = Hacc.t[:, tt, :]
                        sq = sqr.next()
                        k.tt(sq.t[:], H, H, ALU.mult, [Hacc], [sq])
                        ss = ssr.next()
                        k.op("dve", lambda e: e.tensor_reduce(out=ss.t[:], in_=sq.t[:].rearrange("p (h e) -> p h e", h=4), axis=AX.X, op=ALU.add), [sq], [ss])
                        k.act(ss.t[:], ss.t[:], AF.Sqrt, [ss, self.epsT], [ss], scale=1.0 / 128, bias=self.epsT.t[:, 0:1])
                        k.op("dve", lambda e: e.reciprocal(out=ss.t[:], in_=ss.t[:]), [ss], [ss])
                        for h in range(4):
                            hsl = slice(h * 128, (h + 1) * 128)
                            k.stt(sq.t[:, hsl], H[:, hsl], ss.t[:, h:h + 1], onw.t[:, hsl], ALU.mult, ALU.mult, [Hacc, ss, onw], [sq])
                        mo = mor.next()
                        k.dma("sp", mo.t[:], self.zt[b, tt * 128:(tt + 1) * 128, 1024:1536], [], [mo])
                        k.act(mo.t[:], mo.t[:], AF.Sigmoid, [mo], [mo])
                        k.tt(sq.t[:], sq.t[:], mo.t[:], ALU.mult, [sq, mo], [sq])
                        p = pT.next()
                        for h in range(4):
                            hsl = slice(h * 128, (h + 1) * 128)
                            k.tr(p.t[:, hsl], sq.t[:, hsl], self.identF, [sq, self.C], [p])
                        k.cp(OT.t[:, :, tt * 128:(tt + 1) * 128], p.t[:, 0:512].rearrange("p (h e) -> p h e", h=4), [p], [OT], eng="act")
                    for h in range(4):
                        k.dma("sp", self.cc[b, 512 + h * 128:512 + (h + 1) * 128, :], OT.t[:, h, :], [OT], [])

    def mixer_mla(self, l):
        k = self.k
        SC = 192 ** -0.5
        for b in range(2):
            with k.scope():
                QT = k.sb("aQT", [128, 4, T], BF16)
                QT2 = k.sb("aQT2", [64, 4, T], BF16)
                KT = k.sb("aKT", [128, 4, T], BF16)
                KT2 = k.sb("aKT2", [64, 4, T], BF16)
                Va = k.sb("aVa", [128, NT, 4, 129], BF16)
                k.op("pool", lambda e: e.memset(Va.t[:], 1.0), [], [Va])
                with k.scope():
                    nv = k.sb("mlav", [128, 896], F32)
                    k.dma("sp", nv.t[:], self.mlav[l], [], [nv])
                    QAW = nv.t[:, 0:384]
                    KVAW = nv.t[:, 384:512]
                    NW = [nv.t[:, 512:704], nv.t[:, 704:896]]
                    rc = k.sb("ropec", [128, NT, 32], F32)
                    rs = k.sb("ropes", [128, NT, 32], F32)
                    k.dma("sp", rc.t[:], self.ropec[:, :, :], [], [rc])
                    k.dma("sp", rs.t[:], self.ropes[:, :, :], [], [rs])
                    wq32 = k.sb("wq32", [128, 3, 768], F32)
                    wq = k.sb("wq", [128, 3, 768], BF16)
                    wkv32 = k.sb("wkv32", [128, 1024], F32)
                    wkv = k.sb("wkv", [128, 1024], BF16)
                    k.dma("sp", wq32.t[:], self.w_q_up[l].rearrange("(kc p) o -> p kc o", p=128), [], [wq32])
                    k.dma("sp", wkv32.t[:], self.w_kv_up[l], [], [wkv32])
                    k.cp(wq.t[:], wq32.t[:], [wq32], [wq], eng="pool")
                    k.cp(wkv.t[:], wkv32.t[:], [wkv32], [wkv], eng="pool")
                    Zr = k.ring("Z", [128, 576], F32, 2)
                    jk = k.sb("junk", [128, 384], F32)
                    ssr = k.ring("ss2", [128, 2], F32, 2)
                    cnr = k.ring("cn", [128, 512], F32, 2)
                    cTr = k.ring("cT", [128, 4, 128], BF16, 2)
                    Xr = [k.ring("X0", [128, 4, 192], F32, 2), k.ring("X1", [128, 4, 192], F32, 2)]
                    sqx = k.sb("sqx", [128, 768], F32)
                    s4r = k.ring("s4", [128, 4], F32, 2)
                    tmp = [k.sb("rt", [128, 4, 2, 16], F32) for _ in range(4)]
                    pT = k.psring("apT", 1)
                    pq = k.psring("apq", 2)
                    pk = k.psring("apk", 2)
                    pX = k.psring("apX", 2)
                    for tt in range(NT):
                        tsl = slice(tt * 128, (tt + 1) * 128)
                        Z = Zr.next()
                        k.dma("sp", Z.t[:], self.zt[b, tsl, 1552:2128], [], [Z])
                        ss = ssr.next()
                        k.act(jk.t[:, 0:384], Z.t[:, 0:384], AF.Square, [Z], [jk, ss], accum_out=ss.t[:, 0:1])
                        k.act(jk.t[:, 0:128], Z.t[:, 384:512], AF.Square, [Z], [jk, ss], accum_out=ss.t[:, 1:2])
                        k.act(ss.t[:, 0:1], ss.t[:, 0:1], AF.Sqrt, [ss, self.epsT], [ss], scale=1.0 / 384, bias=self.epsT.t[:, 0:1])
                        k.act(ss.t[:, 1:2], ss.t[:, 1:2], AF.Sqrt, [ss, self.epsT], [ss], scale=1.0 / 128, bias=self.epsT.t[:, 0:1])
                        k.op("dve", lambda e: e.reciprocal(out=ss.t[:], in_=ss.t[:]), [ss], [ss])
                        cn = cnr.next()
                        k.stt(cn.t[:, 0:384], Z.t[:, 0:384], ss.t[:, 0:1], QAW, ALU.mult, ALU.mult, [Z, ss, nv], [cn])
                        k.stt(cn.t[:, 384:512], Z.t[:, 384:512], ss.t[:, 1:2], KVAW, ALU.mult, ALU.mult, [Z, ss, nv], [cn])
                        p = pT.next()
                        for c4 in range(4):
                            k.tr(p.t[:, c4 * 128:(c4 + 1) * 128], cn.t[:, c4 * 128:(c4 + 1) * 128], self.identF, [cn, self.C], [p])
                        cT = cTr.next()
                        k.cp(cT.t[:], p.t[:, 0:512].rearrange("p (c e) -> p c e", c=4), [p], [cT], eng="act")
                        Xq = Xr[0].next()
                        Xk = Xr[1].next()
                        for nb in range(2):
                            p = pq.next()
                            for kc in range(3):
                                k.mm(p.t[:, 0:384], cT.t[:, kc, :], wq.t[:, kc, nb * 384:(nb + 1) * 384], kc == 0, kc == 2, [cT, wq], [p])
                            k.cp(Xq.t[:, 2 * nb:2 * nb + 2, :], p.t[:, 0:384].rearrange("p (h e) -> p h e", h=2), [p], [Xq], eng="act")
                        for nb in range(2):
                            p = pk.next()
                            k.mm(p.t[:, 0:512], cT.t[:, 3, :], wkv.t[:, nb * 512:(nb + 1) * 512], True, True, [cT, wkv], [p])
                            pv4 = p.t[:, 0:512].rearrange("p (h e) -> p h e", h=2)
                            k.cp(Xk.t[:, 2 * nb:2 * nb + 2, 0:128], pv4[:, :, 0:128], [p], [Xk])
                            k.cp(Va.t[:, tt, 2 * nb:2 * nb + 2, 0:128], pv4[:, :, 128:256], [p], [Va], eng="act")
                        for h in range(4):
                            k.cp(Xk.t[:, h, 128:192], Z.t[:, 512:576], [Z], [Xk], eng="pool")
                        for qi, X in enumerate((Xq, Xk)):
                            Xf = X.t[:].rearrange("p h e -> p (h e)")
                            k.tt(sqx.t[:], Xf, Xf, ALU.mult, [X], [sqx])
                            s4 = s4r.next()
                            k.op("dve", lambda e: e.tensor_reduce(out=s4.t[:], in_=sqx.t[:].rearrange("p (h e) -> p h e", h=4), axis=AX.X, op=ALU.add), [sqx], [s4])
                            k.act(s4.t[:], s4.t[:], AF.Sqrt, [s4, self.epsT], [s4], scale=1.0 / 192, bias=self.epsT.t[:, 0:1])
                            k.op("dve", lambda e: e.reciprocal(out=s4.t[:], in_=s4.t[:]), [s4], [s4])
                            for h in range(4):
                                k.stt(X.t[:, h, :], X.t[:, h, :], s4.t[:, h:h + 1], NW[qi], ALU.mult, ALU.mult, [X, s4, nv], [X])
                            rp = X.t[:, :, 128:192].rearrange("p h (a b f) -> p h a b f", a=2, b=2)
                            x1 = rp[:, :, :, 0, :]
                            x2 = rp[:, :, :, 1, :]
                            cosb = rc.t[:, tt, :].rearrange("p (o a f) -> p o a f", o=1, a=2).to_broadcast([128, 4, 2, 16])
                            sinb = rs.t[:, tt, :].rearrange("p (o a f) -> p o a f", o=1, a=2).to_broadcast([128, 4, 2, 16])
                            TT = tmp
                            k.tt(TT[0].t[:], x1, cosb, ALU.mult, [X, rc], [TT[0]])
                            k.tt(TT[1].t[:], x2, sinb, ALU.mult, [X, rs], [TT[1]])
                            k.tt(TT[2].t[:], x2, cosb, ALU.mult, [X, rc], [TT[2]])
                            k.tt(TT[3].t[:], x1, sinb, ALU.mult, [X, rs], [TT[3]])
                            k.tt(x1, TT[0].t[:], TT[1].t[:], ALU.subtract, [TT[0], TT[1], X], [X])
                            k.tt(x2, TT[2].t[:], TT[3].t[:], ALU.add, [TT[2], TT[3], X], [X])
                            dst, dst2 = (QT, QT2) if qi == 0 else (KT, KT2)
                            for hp in range(2):
                                p = pX.next()
                                for hh in range(2):
                                    h = 2 * hp + hh
                                    k.tr(p.t[:, hh * 256:hh * 256 + 128], X.t[:, h, 0:128], self.identF, [X, self.C], [p])
                                    k.tr(p.t[0:64, hh * 256 + 128:hh * 256 + 256], X.t[:, h, 128:192], self.identF, [X, self.C], [p])
                                pv_ = p.t[:, 0:512].rearrange("p (h e) -> p h e", h=2)
                                k.cp(dst.t[:, 2 * hp:2 * hp + 2, tsl], pv_[:, :, 0:128], [p], [dst], eng="act")
                                k.cp(dst2.t[0:64, 2 * hp:2 * hp + 2, tsl], p.t[0:64, 0:512].rearrange("p (h e) -> p h e", h=2)[:, :, 128:256], [p], [dst2])
                if self.dbg.get("mla_noattn"):
                    continue
                with k.scope():
                    Pr = k.ring("P", [128, 512], BF16, 3)
                    MT = k.ring("MT", [128, T], BF16, 2)
                    o32 = k.ring("o32", [128, 128], F32, 2)
                    rdr = k.ring("rd", [128, 1], F32, 2)
                    pS = k.psring("aS", 2)
                    po = [k.ps("apo%d" % i) for i in range(4)]
                    pT = k.psring("aT", 1)
                    blocks = [(0, 256, [0, 1])] + [(CTX + 512 * i, 512, list(range(NT))) for i in range(4)]
                    for h in range(4):
                        mt = MT.next()
                        for (q0, nq, kts) in blocks:
                            nsub = nq // 128
                            for idx, kt in enumerate(kts):
                                ksl = slice(kt * 128, (kt + 1) * 128)
                                ps_ = pS.next()
                                k.mm(ps_.t[:, 0:nq], KT.t[:, h, ksl], QT.t[:, h, q0:q0 + nq], True, False, [KT, QT], [ps_])
                                k.mm(ps_.t[:, 0:nq], KT2.t[0:64, h, ksl], QT2.t[0:64, h, q0:q0 + nq], False, True, [KT2, QT2], [ps_])
                                P = Pr.next()
                                k.act(P.t[:, 0:nq], ps_.t[:, 0:nq], AF.Exp, [ps_], [P], scale=SC)
                                for qs in range(nsub):
                                    k.mm(po[qs].t[:, 0:129], P.t[:, qs * 128:(qs + 1) * 128], Va.t[:, kt, h, :],
                                         idx == 0, idx == len(kts) - 1, [P, Va], [po[qs]])
                            for qs in range(nsub):
                                rd = rdr.next()
                                k.op("dve", lambda e: e.reciprocal(out=rd.t[:], in_=po[qs].t[:, 128:129]), [po[qs]], [rd])
                                o = o32.next()
                                k.ts(o.t[:], po[qs].t[:, 0:128], rd.t[:, 0:1], None, ALU.mult, None, [po[qs], rd], [o])
                                p = pT.next()
                                k.tr(p.t[:, 0:128], o.t[:], self.identF, [o, self.C], [p])
                                k.cp(mt.t[:, q0 + qs * 128:q0 + (qs + 1) * 128], p.t[:, 0:128], [p], [mt], eng="act")
                        k.dma("sp", self.cc[b, 1024 + h * 128:1024 + (h + 1) * 128, :], mt.t[:], [mt], [])


def make_consts():
    c = np.zeros((128, 2048), np.float32)
    i = np.arange(128)
    c[:, 0:128] = np.eye(128)
    c[:, 128:256] = 1.0
    c[:, 256:384] = (i[:, None] <= i[None, :])
    c[:, 384:512] = (i[:, None] >= i[None, :])
    c[:, 512:640] = np.eye(128)
    c[:, 640:768] = np.where(i[:, None] <= i[None, :], 0.0, -30000.0)
    c[:, 768:896] = np.where(i[:, None] >= i[None, :], 0.0, -30000.0)
    return c


def prep_common(inp):
    m = {}
    f = np.float32
    m["ada_w"] = inp["ada_w"]
    m["ada_bT"] = np.ascontiguousarray(inp["ada_b"].reshape(4, 96, 128).transpose(0, 2, 1))
    m["n1w"] = np.ascontiguousarray(inp["norm1_w"].reshape(4, 16, 128).transpose(0, 2, 1))
    m["n2w"] = np.ascontiguousarray(inp["norm2_w"].reshape(4, 16, 128).transpose(0, 2, 1))
    m["w_in"] = inp["w_in"]
    m["w_out"] = inp["w_out"]
    m["mlp_w1"] = inp["mlp_w1"]
    m["mlp_w2"] = inp["mlp_w2"]
    m["consts"] = make_consts()
    prep_mixers(inp, m)
    return m


def prep_core(inp, common, core):
    b0 = 2 * core
    m = dict(common)
    xin = np.concatenate([inp["ctx"][b0:b0 + 2], inp["x"][b0:b0 + 2]], axis=1)
    m["xin"] = np.ascontiguousarray(xin.transpose(0, 2, 1))
    c3 = np.stack([inp["c"][b0], inp["c"][b0 + 1], inp["c_ctx"]], axis=1)
    m["cT"] = np.ascontiguousarray(c3.reshape(16, 128, 3).transpose(1, 0, 2))
    return m


def _dup(a):
    return np.concatenate([a, a], axis=0)


def prep_mixers(inp, m):
    L = DEPTH
    c = m["consts"]
    c[:64, 896] = -1.0
    c[64:, 896] = 1.0
    lre = inp["s5_lam_re"].transpose(0, 3, 1, 2).reshape(L, 64, 64)
    lim = inp["s5_lam_im"].transpose(0, 3, 1, 2).reshape(L, 64, 64)
    ldt = np.broadcast_to(inp["s5_log_dt"].reshape(L, 1, 64), (L, 64, 64))
    s5v = np.concatenate([lre, lim, ldt], axis=2)
    m["s5v"] = np.ascontiguousarray(np.concatenate([s5v, s5v], axis=1))
    bre = inp["s5_b_re"].transpose(0, 3, 1, 2, 4).reshape(L, 64, 64, 16)
    bim = inp["s5_b_im"].transpose(0, 3, 1, 2, 4).reshape(L, 64, 64, 16)
    m["s5A"] = np.ascontiguousarray(np.concatenate([bre, bim], axis=1))
    m["s5B"] = np.ascontiguousarray(np.concatenate([bim, bre], axis=1))
    cre = inp["s5_c_re"].transpose(0, 4, 1, 2, 3).reshape(L, 64, 64, 16)
    cim = inp["s5_c_im"].transpose(0, 4, 1, 2, 3).reshape(L, 64, 64, 16)
    m["s5CA"] = np.ascontiguousarray(np.concatenate([cre, cim], axis=1))
    m["s5CB"] = np.ascontiguousarray(np.concatenate([cim, cre], axis=1))
    dsk = inp["s5_d"].reshape(L, 4, 128).transpose(0, 2, 1)
    glb = inp["s5_glu_b"].reshape(L, 4, 128).transpose(0, 2, 1)
    m["s5w"] = np.ascontiguousarray(np.concatenate([dsk, glb], axis=2))
    m["glu_w"] = inp["s5_glu_w"]
    NK, KC = T // 8, CTX // 8
    n0 = np.arange(NK, dtype=np.float32)
    n1 = np.concatenate([KC - 1 - np.arange(KC), KC + (NK - KC - 1 - np.arange(NK - KC))]).astype(np.float32)
    m["nidx8"] = np.ascontiguousarray(np.broadcast_to(np.stack([n0, n1])[:, None, :], (2, 128, NK)))
    selu = np.zeros((128, 8, 8, 128), np.float32)
    selt = np.zeros((128, 8, 8, 128), np.float32)
    for jj in range(8):
        for ss in range(8):
            for ci in range(16):
                selu[16 * jj + ci, jj, ss, 16 * ss + ci] = 1.0
                selt[16 * ss + ci, jj, ss, 16 * jj + ci] = 1.0
    m["selu"] = selu.reshape(128, 64, 128).astype(ml_dtypes.bfloat16)
    m["selt"] = selt.reshape(128, 64, 128).astype(ml_dtypes.bfloat16)
    blk = np.arange(128) // 16
    m["bmask"] = np.concatenate([(blk[None, :] >= blk[:, None]), (blk[None, :] <= blk[:, None])], axis=1).astype(np.float32)
    cw = inp["lru_conv_w"].reshape(L, 4, 4, 128).transpose(0, 3, 2, 1).reshape(L, 128, 16)
    cb = inp["lru_conv_b"].reshape(L, 4, 128).transpose(0, 2, 1)
    def dc(a):
        return a.reshape(L, 2, 4, 128).transpose(0, 3, 1, 2).reshape(L, 128, 8)
    m["lruv"] = np.ascontiguousarray(np.concatenate([cw, cb, dc(inp["lru_ba"]), dc(inp["lru_bx"]), dc(inp["lru_lam"])], axis=2))
    m["lru_wa"] = inp["lru_wa"]
    m["lru_wx"] = inp["lru_wx"]
    gb = np.concatenate([inp["ml_ig_bias"], inp["ml_fg_bias"]], axis=2).reshape(L, 1, 16)
    m["mlb"] = np.ascontiguousarray(np.broadcast_to(gb, (L, 128, 16)))
    m["onw"] = np.ascontiguousarray(np.broadcast_to(inp["ml_out_norm"].reshape(L, 1, 512), (L, 128, 512)))
    selc = np.zeros((16, 16, 128), np.float32)
    for kk in range(16):
        selc[kk, kk, :] = 1.0
    m["selc"] = selc.reshape(16, 2048)
    nv = np.concatenate([inp["mla_q_a_norm"], inp["mla_kv_a_norm"], inp["mla_q_norm"], inp["mla_k_norm"]], axis=1)
    m["mlav"] = np.ascontiguousarray(np.broadcast_to(nv.reshape(L, 1, 896), (L, 128, 896)))
    m["w_q_up"] = inp["mla_w_q_up"]
    m["w_kv_up"] = inp["mla_w_kv_up"]
    q = np.arange(LAT)
    inv = (np.float32(10000.0) ** (-np.arange(16, dtype=np.float32) / np.float32(16))).astype(np.float32)
    ang = np.concatenate([(q // 64).astype(np.float32)[:, None] * inv, (q % 64).astype(np.float32)[:, None] * inv], axis=1)
    cosf = np.ones((T, 32), np.float32)
    sinf = np.zeros((T, 32), np.float32)
    cosf[CTX:] = np.cos(ang.astype(np.float32))
    sinf[CTX:] = np.sin(ang.astype(np.float32))
    m["ropec"] = np.ascontiguousarray(cosf.reshape(NT, 128, 32).transpose(1, 0, 2))
    m["ropes"] = np.ascontiguousarray(sinf.reshape(NT, 128, 32).transpose(1, 0, 2))


_PROG = None


def kernel(**inputs):
    global _PROG
    inp = {k_: np.asarray(v) for k_, v in inputs.items()}
    if _PROG is None:
        _PROG = Prog()
    prog = _PROG
    common = prep_common(inp)
    in_maps = []
    for core in range(8):
        m = prep_core(inp, common, core)
        in_maps.append({n: m[n] for n in prog.inp})
    res = run_bass_kernel_spmd(prog.nc, in_maps, core_ids=list(range(8)))
    outs = [r["yout"] for r in res.results]
    y = np.concatenate(outs, axis=0)
    return np.ascontiguousarray(y.transpose(0, 2, 1)).astype(np.float32)
```

```python
import numpy as np
import ml_dtypes
from contextlib import ExitStack, contextmanager
import concourse.bass as bass
import concourse.mybir as mybir
from concourse.bass_utils import run_bass_kernel_spmd

F32 = mybir.dt.float32
BF16 = mybir.dt.bfloat16
I32 = mybir.dt.int32
AF = mybir.ActivationFunctionType
ALU = mybir.AluOpType
AX = mybir.AxisListType

D = 2048
T = 2304
CTX = 256
LAT = 2048
NT = 18
DEPTH = 4
DFF = 8192
INC = 4176
EPS = 1e-6
TB = [(0, 256), (256, 512), (768, 512), (1280, 512), (1792, 512)]
TWO_PI = float(2 * np.pi)
SAME_SYNC = True
NO_SELF_SYNC = ("pe",)
MERGE_WAIT = True
CAP = 30000


class Res:
    __slots__ = ("w", "r")

    def __init__(self):
        self.w = None
        self.r = {}


class Buf:
    def __init__(self, t, psum=False):
        self.t = t
        self.res = Res()
        self.psum = psum

    def __getitem__(self, key):
        return self.t[key]


class Ring:
    def __init__(self, bufs):
        self.bufs = bufs
        self.i = 0

    def next(self):
        b = self.bufs[self.i]
        self.i = (self.i + 1) % len(self.bufs)
        return b


class KB:
    def __init__(self, nc):
        self.nc = nc
        self.E = {"pe": nc.tensor, "dve": nc.vector, "act": nc.scalar, "pool": nc.gpsimd, "sp": nc.sync}
        self.sem = {}
        self.cnt = {}
        self.owner = {}
        self.nsem = 0
        for e in self.E:
            self._fresh(e)
        self.waited = {e: {} for e in self.E}
        self.dq = {}
        self.dqi = {}
        for q, n in (("sp", 12), ("pool", 4), ("act", 4)):
            self.dq[q] = [[self._newsem("d"), 0] for _ in range(n)]
            self.dqi[q] = 0
        self.stack = []
        self.uid = 0

    def _newsem(self, pfx):
        self.nsem += 1
        return self.nc.alloc_semaphore(f"{pfx}{self.nsem}")

    def _fresh(self, e):
        s = self._newsem("e")
        self.sem[e] = s
        self.cnt[e] = 0
        self.owner[s] = e

    @contextmanager
    def scope(self):
        st = ExitStack()
        self.stack.append(st)
        try:
            yield
        finally:
            self.barrier()
            self.stack.pop()
            st.close()

    def _name(self, n):
        self.uid += 1
        return f"{n}_{self.uid}"

    def sb(self, name, shape, dtype):
        t = self.stack[-1].enter_context(self.nc.sbuf_tensor(self._name(name), list(shape), dtype))
        return Buf(t)

    def ps(self, name, shape=(128, 512), dtype=F32):
        t = self.stack[-1].enter_context(self.nc.psum_tensor(self._name(name), list(shape), dtype))
        return Buf(t, psum=True)

    def ring(self, name, shape, dtype, n):
        return Ring([self.sb(name, shape, dtype) for _ in range(n)])

    def psring(self, name, n, shape=(128, 512), dtype=F32):
        return Ring([self.ps(name, shape, dtype) for _ in range(n)])

    def _need(self, e, tok, out):
        if tok is None:
            return
        sem, val = tok
        own = self.owner.get(sem)
        if own == e and (e in NO_SELF_SYNC or not SAME_SYNC):
            return
        w = self.waited[e]
        if w.get(sem, 0) >= val:
            return
        w[sem] = val
        for i, (s_, v_) in enumerate(out):
            if s_ is sem or s_ == sem:
                out[i] = (sem, max(v_, val))
                return
        out.append((sem, val))

    def _wait(self, e, tok):
        out = []
        self._need(e, tok, out)
        for (s_, v_) in out:
            self.E[e].wait_ge(s_, v_)

    def _deps(self, e, reads, writes):
        out = []
        for r in reads:
            r = r.res if isinstance(r, Buf) else r
            self._need(e, r.w, out)
        for wr in writes:
            wr = wr.res if isinstance(wr, Buf) else wr
            self._need(e, wr.w, out)
            for s_, v_ in wr.r.items():
                self._need(e, (s_, v_), out)
        return out

    def _commit(self, tok, reads, writes):
        sem, val = tok
        for r in reads:
            r = r.res if isinstance(r, Buf) else r
            if r.r.get(sem, 0) < val:
                r.r[sem] = val
        for wr in writes:
            wr = wr.res if isinstance(wr, Buf) else wr
            wr.w = tok
            wr.r = {}

    def op(self, e, fn, reads=(), writes=(), merge=True):
        pr = [r for r in reads if isinstance(r, Buf) and r.psum]
        if pr:
            reads = [r for r in reads if not (isinstance(r, Buf) and r.psum)]
            writes = list(writes) + pr
        need = self._deps(e, reads, writes)
        last = None
        if merge and MERGE_WAIT and need:
            last = need.pop()
        for (s_, v_) in need:
            self.E[e].wait_ge(s_, v_)
        ins = fn(self.E[e])
        if last is not None:
            ins._wait_ge(last[0], last[1])
        self.cnt[e] += 1
        ins.then_inc(self.sem[e], 1)
        tok = (self.sem[e], self.cnt[e])
        self._commit(tok, reads, writes)
        if self.cnt[e] >= CAP:
            self._fresh(e)

    def dma(self, q, out, in_, reads=(), writes=()):
        slots = self.dq[q]
        slot = slots[self.dqi[q]]
        self.dqi[q] = (self.dqi[q] + 1) % len(slots)
        if slot[1] > 0:
            self._wait(q, (slot[0], slot[1]))
        if slot[1] >= CAP:
            slot[0] = self._newsem("d")
            slot[1] = 0
        for (s_, v_) in self._deps(q, reads, writes):
            self.E[q].wait_ge(s_, v_)
        ins = self.E[q].dma_start(out=out, in_=in_)
        slot[1] += 16
        ins.then_inc(slot[0], 16)
        self._commit((slot[0], slot[1]), reads, writes)

    def barrier(self):
        toks = [(self.sem[e], self.cnt[e]) for e in self.E if self.cnt[e] > 0]
        for q in self.dq:
            for slot in self.dq[q]:
                if slot[1] > 0:
                    toks.append((slot[0], slot[1]))
        for e in self.E:
            for tok in toks:
                if self.owner.get(tok[0]) == e:
                    continue
                self._wait(e, tok)

    def mm(self, out, lhsT, rhs, start, stop, reads, writes):
        self.op("pe", lambda e: e.matmul(out, lhsT, rhs, start=start, stop=stop), reads, writes)

    def tr(self, out, in_, ident, reads, writes):
        self.op("pe", lambda e: e.transpose(out, in_, ident), reads, writes)

    def act(self, out, in_, func, reads, writes, bias=0.0, scale=1.0, **kw):
        self.op("act", lambda e: e.activation(out=out, in_=in_, func=func, bias=bias, scale=scale, **kw), reads, writes,
                merge=("accum_out" not in kw))

    def ts(self, out, in0, s1, s2, op0, op1, reads, writes, eng="dve"):
        if s2 is None:
            self.op(eng, lambda e: e.tensor_scalar(out=out, in0=in0, scalar1=s1, scalar2=None, op0=op0), reads, writes)
        else:
            self.op(eng, lambda e: e.tensor_scalar(out=out, in0=in0, scalar1=s1, scalar2=s2, op0=op0, op1=op1), reads, writes)

    def tt(self, out, in0, in1, op, reads, writes, eng="dve"):
        self.op(eng, lambda e: e.tensor_tensor(out=out, in0=in0, in1=in1, op=op), reads, writes)

    def stt(self, out, in0, scalar, in1, op0, op1, reads, writes):
        self.op("dve", lambda e: e.scalar_tensor_tensor(out=out, in0=in0, scalar=scalar, in1=in1, op0=op0, op1=op1), reads, writes)

    def cp(self, out, in_, reads, writes, eng="dve"):
        if eng == "act":
            self.op("act", lambda e: e.copy(out=out, in_=in_), reads, writes)
        else:
            self.op(eng, lambda e: e.tensor_copy(out=out, in_=in_), reads, writes)

    def scan(self, out, d0, d1, init, reads, writes):
        self.op("dve", lambda e: e.tensor_tensor_scan(out=out, data0=d0, data1=d1, initial=init, op0=ALU.mult, op1=ALU.add), reads, writes)


class Prog:
    def __init__(self, n_layers=DEPTH, dbg=None, layers=None):
        self.dbg = dbg or {}
        self.layers = list(range(n_layers)) if layers is None else layers
        nc = bass.Bass("TRN2", target_bir_lowering=False)
        self.nc = nc
        self.k = KB(nc)
        self.inp = {}
        self.build()

    @staticmethod
    def pf(items, load):
        items = list(items)
        nxt = load(items[0])
        for i, it in enumerate(items):
            cur = nxt
            if i + 1 < len(items):
                nxt = load(items[i + 1])
            yield it, cur

    def din(self, name, shape, dtype=F32):
        t = self.nc.dram_tensor(name, list(shape), dtype, kind="ExternalInput")
        self.inp[name] = (tuple(shape), dtype)
        return t.ap()

    def dscr(self, name, shape, dtype=F32):
        kind = "ExternalOutput" if name in self.dbg else "Internal"
        return self.nc.dram_tensor(name, list(shape), dtype, kind=kind).ap()

    def build(self):
        nc, k = self.nc, self.k
        L = DEPTH
        self.xin = self.din("xin", [2, D, T])
        self.cT = self.din("cT", [128, 16, 3])
        self.ada_w = self.din("ada_w", [L, D, 6 * D])
        self.ada_bT = self.din("ada_bT", [L, 128, 96])
        self.n1w = self.din("n1w", [L, 128, 16])
        self.n2w = self.din("n2w", [L, 128, 16])
        self.w_in = self.din("w_in", [L, D, INC])
        self.w_out = self.din("w_out", [L, D, D])
        self.mlp_w1 = self.din("mlp_w1", [L, D, DFF])
        self.mlp_w2 = self.din("mlp_w2", [L, DFF, D])
        self.consts = self.din("consts", [128, 2048])
        self.declare_mixer_inputs()
        self.yout = nc.dram_tensor("yout", [2, D, LAT], F32, kind="ExternalOutput").ap()
        self.xs = self.dscr("xs", [2, D, T])
        self.zf = self.dscr("zf", [2, 2560, T])
        self.zt = self.dscr("zt", [2, T, 2128])
        self.W1t = self.dscr("W1t", [64, 128, 16, 128], BF16)
        self.W2t = self.dscr("W2t", [4, 16, 128, 16, 128], BF16)
        if self.dbg.get("cc_in"):
            self.cc = self.din("cc", [2, D, T], BF16)
        else:
            self.cc = self.dscr("cc", [2, D, T], BF16)
        if "modv_o" in self.dbg:
            self.modv_o = self.dscr("modv_o", [128, 288])

        with k.scope():
            self.setup_consts()
            for b in range(2):
                for c in range(16):
                    k.dma("sp", self.xs[b, c * 128:(c + 1) * 128, :], self.xin[b, c * 128:(c + 1) * 128, :])
            k.barrier()
            for l in self.layers:
                self.layer(l)
            for b in range(2):
                for c in range(16):
                    k.dma("sp", self.yout[b, c * 128:(c + 1) * 128, :], self.xs[b, c * 128:(c + 1) * 128, CTX:T])

    def setup_consts(self):
        k = self.k
        self.C = k.sb("consts", [128, 2048], F32)
        k.dma("sp", self.C.t[:], self.consts[:, :], [], [self.C])
        self.identF = self.C.t[:, 0:128]
        self.onesF = self.C.t[:, 128:256]
        self.identB = k.sb("identB", [128, 128], BF16)
        k.cp(self.identB.t[:], self.identF, [self.C], [self.identB])
        self.cs = k.sb("cs", [128, 16, 3], F32)
        k.dma("sp", self.cs.t[:], self.cT[:, :, :], [], [self.cs])
        k.act(self.cs.t[:], self.cs.t[:], AF.Silu, [self.cs], [self.cs])
        self.modv = k.sb("modv", [128, 96, 3], F32)
        self.g1 = k.sb("g1", [128, 16, 3], F32)
        self.g2 = k.sb("g2", [128, 16, 3], F32)
        self.epsT = k.sb("epsT", [128, 1], F32)
        k.op("dve", lambda e: e.memset(self.epsT.t[:], EPS), [], [self.epsT])
        self.setup_mixer_consts()

    def layer(self, l):
        k = self.k
        self.wcast_done = False
        self.stage_mod(l)
        for b in range(2):
            self.stage_A(l, b)
        self.mixers(l)
        for b in range(2):
            self.stage_proj_res(l, b, which="out")
        if not self.wcast_done:
            self.stage_wcast(l)
        for b in range(2):
            self.stage_mlp(l, b)

    def stage_mod(self, l):
        k = self.k
        with k.scope():
            wr = k.ring("adaw", [128, 16, 512], F32, 2)
            mtm = k.sb("modtm", [3, 6 * D], F32)
            pr = k.psring("pmodr", 2)
            pm = k.ps("pmod")
            adab = k.sb("adab", [128, 96], F32)
            nw1 = k.sb("nw1", [128, 16], F32)
            nw2 = k.sb("nw2", [128, 16], F32)
            k.dma("sp", adab.t[:], self.ada_bT[l], [], [adab])
            k.dma("sp", nw1.t[:], self.n1w[l], [], [nw1])
            k.dma("sp", nw2.t[:], self.n2w[l], [], [nw2])
            wv = self.ada_w[l].rearrange("(kc p) f -> p kc f", p=128)

            def load_w(nb_):
                w_ = wr.next()
                for hf in range(2):
                    k.dma("sp", w_.t[:, hf * 8:(hf + 1) * 8, :], wv[:, hf * 8:(hf + 1) * 8, nb_ * 512:(nb_ + 1) * 512], [], [w_])
                return w_

            for nb, w in self.pf(range(24), load_w):
                p = pr.next()
                for kc in range(16):
                    k.mm(p.t[0:3, 0:512], self.cs.t[:, kc, :], w.t[:, kc, :], kc == 0, kc == 15, [w, self.cs], [p])
                k.cp(mtm.t[:, nb * 512:(nb + 1) * 512], p.t[0:3, 0:512], [p], [mtm], eng=("act" if nb % 2 else "dve"))
            for j in range(96):
                k.tr(pm.t[:, 3 * j:3 * j + 3], mtm.t[0:3, j * 128:(j + 1) * 128], self.C.t[0:3, 0:3], [mtm, self.C], [pm])
            pv = pm.t[:, 0:288].rearrange("p (j r) -> p j r", r=3)
            for r in range(3):
                k.tt(self.modv.t[:, :, r], pv[:, :, r], adab.t[:], ALU.add, [pm, adab], [self.modv])
            if "modv_o" in self.dbg:
                k.dma("sp", self.modv_o[:, :], self.modv.t[:].rearrange("p j r -> p (j r)"), [self.modv], [])
            for r in range(3):
                k.stt(self.g1.t[:, :, r], self.modv.t[:, 16:32, r], 1.0, nw1.t[:], ALU.add, ALU.mult,
                      [self.modv, nw1], [self.g1])
                k.stt(self.g2.t[:, :, r], self.modv.t[:, 64:80, r], 1.0, nw2.t[:], ALU.add, ALU.mult,
                      [self.modv, nw2], [self.g2])

    def make_hT(self, hT, b, g, shift_base, blocks=TB, rel=False, nx=2, base=None, nmax=512):
        k = self.k
        with k.scope():
            xr = k.ring("xblk", [128, 16, nmax], F32, nx)
            sqr = k.ring("sq", [128, nmax], F32, 3)
            rsr = k.ring("rstd", [128, nmax], F32, 2)
            tmr = k.ring("tmp", [128, nmax], F32, 3)
            pss = k.psring("ss", 2)
            xv = self.xs[b].rearrange("(c p) t -> p c t", p=128)
            for (t0, n) in blocks:
                r = 2 if t0 < CTX else b
                o0 = (t0 - base) if base is not None else (0 if rel else t0)
                xb = xr.next()
                for c in range(16):
                    k.dma("sp", xb.t[:, c, 0:n], xv[:, c, t0:t0 + n], [], [xb])
                ss = pss.next()
                for c in range(16):
                    sq = sqr.next()
                    k.act(sq.t[:, 0:n], xb.t[:, c, 0:n], AF.Square, [xb], [sq])
                    k.mm(ss.t[:, 0:n], self.onesF, sq.t[:, 0:n], c == 0, c == 15, [sq, self.C], [ss])
                rs = rsr.next()
                k.act(rs.t[:, 0:n], ss.t[:, 0:n], AF.Sqrt, [ss, self.epsT], [rs], bias=self.epsT.t[:, 0:1], scale=1.0 / D)
                k.op("dve", lambda e: e.reciprocal(out=rs.t[:, 0:n], in_=rs.t[:, 0:n]), [rs], [rs])
                for c in range(16):
                    tm = tmr.next()
                    k.stt(tm.t[:, 0:n], xb.t[:, c, 0:n], g.t[:, c, r:r + 1], rs.t[:, 0:n], ALU.mult, ALU.mult,
                          [xb, g, rs], [tm])
                    k.act(hT.t[:, c, o0:o0 + n], tm.t[:, 0:n], AF.Identity, [tm, self.modv], [hT],
                          bias=self.modv.t[:, shift_base + c, r:r + 1], scale=1.0)

    def hT_piece(self, hT, b, g, shift_base, t0, n, o0, R):
        k = self.k
        xv = self.xs[b].rearrange("(c p) t -> p c t", p=128)
        r = 2 if t0 < CTX else b
        xb = R["x"].next()
        for c in range(16):
            k.dma("sp", xb.t[:, c, 0:n], xv[:, c, t0:t0 + n], [], [xb])
        ss = R["ps"].next()
        for c in range(16):
            sq = R["sq"].next()
            k.act(sq.t[:, 0:n], xb.t[:, c, 0:n], AF.Square, [xb], [sq])
            k.mm(ss.t[:, 0:n], self.onesF, sq.t[:, 0:n], c == 0, c == 15, [sq, self.C], [ss])
        rs = R["rs"].next()
        k.act(rs.t[:, 0:n], ss.t[:, 0:n], AF.Sqrt, [ss, self.epsT], [rs], bias=self.epsT.t[:, 0:1], scale=1.0 / D)
        k.op("dve", lambda e: e.reciprocal(out=rs.t[:, 0:n], in_=rs.t[:, 0:n]), [rs], [rs])
        for c in range(16):
            tm = R["tm"].next()
            k.stt(tm.t[:, 0:n], xb.t[:, c, 0:n], g.t[:, c, r:r + 1], rs.t[:, 0:n], ALU.mult, ALU.mult, [xb, g, rs], [tm])
            k.act(hT.t[:, c, o0:o0 + n], tm.t[:, 0:n], AF.Identity, [tm, self.modv], [hT],
                  bias=self.modv.t[:, shift_base + c, r:r + 1], scale=1.0)

    FM_CHUNKS = [0, 128, 256, 384, 512, 640, 768, 896, 1024, 1152, 1280, 1408,
                 3152, 3280, 3408, 3536, 3664, 3792, 3920, 4048]
    TM_BLOCKS = [(1024, 512), (1536, 512), (2048, 512), (2560, 512), (3072, 80)]

    def stage_A(self, l, b):
        k = self.k
        with k.scope():
            hT = k.sb("hT", [128, 16, T], BF16)
            self.make_hT(hT, b, self.g1, 0)
            wv = self.w_in[l].rearrange("(kc p) c -> p kc c", p=128)
            with k.scope():
                wf = k.ring("wf", [128, 16, 128], F32, 2)
                wb = k.ring("wb", [128, 16, 128], BF16, 2)
                ob = k.ring("ob", [128, 512], F32, 3)
                pp = k.psring("pp", 3)
                ei = 0
                def load_fm(item):
                    w32 = wf.next()
                    k.dma("sp", w32.t[:], wv[:, :, item[1]:item[1] + 128], [], [w32])
                    w16_ = wb.next()
                    k.cp(w16_.t[:], w32.t[:], [w32], [w16_], eng="pool")
                    return w16_

                for (ci, c0), w16 in self.pf(list(enumerate(self.FM_CHUNKS)), load_fm):
                    for (t0, n) in TB:
                        p = pp.next()
                        for kc in range(16):
                            k.mm(p.t[:, 0:n], w16.t[:, kc, :], hT.t[:, kc, t0:t0 + n], kc == 0, kc == 15, [w16, hT], [p])
                        o = ob.next()
                        k.cp(o.t[:, 0:n], p.t[:, 0:n], [p], [o], eng=("act" if ei % 2 else "dve"))
                        ei += 1
                        k.dma("sp", self.zf[b, ci * 128:(ci + 1) * 128, t0:t0 + n], o.t[:, 0:n], [o], [])
            with k.scope():
                wf = k.ring("wf2", [128, 8, 512], F32, 2)
                wb = k.ring("wb2", [128, 16, 512], BF16, 2)
                ob = k.ring("ob2", [128, 512], F32, 3)
                pp = k.psring("pp2", 3)
                ei = 0
                def load_tm(item):
                    (c0_, w_) = item
                    w16_ = wb.next()
                    for hf in range(2):
                        w32 = wf.next()
                        k.dma("sp", w32.t[:, :, 0:w_], wv[:, hf * 8:(hf + 1) * 8, c0_:c0_ + w_], [], [w32])
                        k.cp(w16_.t[:, hf * 8:(hf + 1) * 8, 0:w_], w32.t[:, :, 0:w_], [w32], [w16_], eng="pool")
                    return w16_

                for (c0, w), w16 in self.pf(self.TM_BLOCKS, load_tm):
                    for tt in range(NT):
                        p = pp.next()
                        for kc in range(16):
                            k.mm(p.t[:, 0:w], hT.t[:, kc, tt * 128:(tt + 1) * 128], w16.t[:, kc, 0:w], kc == 0, kc == 15,
                                 [w16, hT], [p])
                        o = ob.next()
                        k.cp(o.t[:, 0:w], p.t[:, 0:w], [p], [o], eng=("act" if ei % 2 else "dve"))
                        ei += 1
                        k.dma("sp", self.zt[b, tt * 128:(tt + 1) * 128, c0 - 1024:c0 - 1024 + w], o.t[:, 0:w], [o], [])

    def stage_proj_res(self, l, b, which):
        k = self.k
        with k.scope():
            cT = k.sb("ccT", [128, 16, T], BF16)
            cv = self.cc[b].rearrange("(c p) t -> p c t", p=128)
            for c in range(16):
                k.dma("sp", cT.t[:, c, :], cv[:, c, :], [], [cT])
            wv = self.w_out[l].rearrange("(kc p) c -> p kc c", p=128)
            xv = self.xs[b].rearrange("(c p) t -> p c t", p=128)
            wf = k.ring("wf", [128, 16, 128], F32, 2)
            wb = k.ring("wb", [128, 16, 128], BF16, 2)
            xr = k.ring("xo", [128, 512], F32, 3)
            pp = k.psring("pp", 3)
            def load_o(fc_):
                w32 = wf.next()
                k.dma("sp", w32.t[:], wv[:, :, fc_ * 128:(fc_ + 1) * 128], [], [w32])
                w16_ = wb.next()
                k.cp(w16_.t[:], w32.t[:], [w32], [w16_], eng="pool")
                return w16_

            for fc, w16 in self.pf(range(16), load_o):
                for (t0, n) in TB:
                    r = 2 if t0 < CTX else b
                    xo = xr.next()
                    k.dma("sp", xo.t[:, 0:n], xv[:, fc, t0:t0 + n], [], [xo])
                    p = pp.next()
                    for kc in range(16):
                        k.mm(p.t[:, 0:n], w16.t[:, kc, :], cT.t[:, kc, t0:t0 + n], kc == 0, kc == 15, [w16, cT], [p])
                    k.stt(xo.t[:, 0:n], p.t[:, 0:n], self.modv.t[:, 32 + fc, r:r + 1], xo.t[:, 0:n], ALU.mult, ALU.add,
                          [p, self.modv, xo], [xo])
                    k.dma("sp", xv[:, fc, t0:t0 + n], xo.t[:, 0:n], [xo], [])

    MLP_BLOCKS = [[(0, 256), (256, 256), (512, 256)], [(768, 256), (1024, 256), (1280, 256)],
                  [(1536, 256), (1792, 256), (2048, 256)]]

    def stage_wcast(self, l):
        k = self.k
        with k.scope():
            f32r = k.ring("wc32", [128, 8192], F32, 2)
            b16r = k.ring("wc16", [128, 8192], BF16, 2)
            engs = ["dve", "act", "pool"]
            ei = 0
            w1tv = self.W1t.rearrange("fc p kc j -> p fc kc j")
            for kc in range(16):
                a = f32r.next()
                k.dma("sp", a.t[:], self.mlp_w1[l, kc * 128:(kc + 1) * 128, :], [], [a])
                bb = b16r.next()
                for q in range(4):
                    k.cp(bb.t[:, q * 2048:(q + 1) * 2048], a.t[:, q * 2048:(q + 1) * 2048], [a], [bb], eng=engs[ei % 3])
                    ei += 1
                k.dma("sp", w1tv[:, :, kc, :], bb.t[:].rearrange("p (fc j) -> p fc j", j=128), [bb], [])
            w2v = self.mlp_w2[l].rearrange("(fg fc p) d -> fg fc p d", fc=16, p=128)
            w2tv = self.W2t.rearrange("fg dc p fc j -> fg fc p dc j")
            for fg in range(4):
                for f4 in range(4):
                    a = f32r.next()
                    for f in range(4):
                        k.dma("sp", a.t[:, f * 2048:(f + 1) * 2048], w2v[fg, f4 * 4 + f], [], [a])
                    bb = b16r.next()
                    for q in range(4):
                        k.cp(bb.t[:, q * 2048:(q + 1) * 2048], a.t[:, q * 2048:(q + 1) * 2048], [a], [bb], eng=engs[ei % 3])
                        ei += 1
                    for f in range(4):
                        k.dma("sp", w2tv[fg, f4 * 4 + f], bb.t[:, f * 2048:(f + 1) * 2048].rearrange("p (dc j) -> p dc j", j=128), [bb], [])

    def wcast_gen(self, l, f32r, b16r):
        k = self.k
        w1tv = self.W1t.rearrange("fc p kc j -> p fc kc j")
        w2v = self.mlp_w2[l].rearrange("(fg fc p) d -> fg fc p d", fc=16, p=128)
        w2tv = self.W2t.rearrange("fg dc p fc j -> fg fc p dc j")
        pieces = []
        for kc in range(16):
            for q in range(4):
                pieces.append((self.mlp_w1[l, kc * 128:(kc + 1) * 128, q * 2048:(q + 1) * 2048], w1tv[:, q * 16:(q + 1) * 16, kc, :]))
        for fg in range(4):
            for fc in range(16):
                pieces.append((w2v[fg, fc], w2tv[fg, fc]))
        loaded = {}

        def load(i):
            a = f32r.next()
            k.dma("sp", a.t[:], pieces[i][0], [], [a])
            loaded[i] = a

        load(0)
        for i in range(len(pieces)):
            if i + 1 < len(pieces):
                load(i + 1)
            a = loaded.pop(i)
            bb = b16r.next()
            k.cp(bb.t[:], a.t[:], [a], [bb], eng="act")
            k.dma("sp", pieces[i][1], bb.t[:].rearrange("p (c j) -> p c j", j=128), [bb], [])
            yield

    def stage_mlp(self, l, b):
        k = self.k
        with k.scope():
            hTs = [k.sb("hT2", [128, 16, 768], BF16) for _ in range(2)]
            oacc = k.sb("oacc", [128, 16, 768], F32)
            aTr = k.ring("aT", [128, 16, 768], BF16, 1)
            w1r = k.ring("w1s", [128, 16, 128], BF16, 3)
            w2r = k.ring("w2s", [128, 16, 128], BF16, 3)
            rl = k.ring("rl", [128, 512], F32, 4)
            xr = k.ring("xo", [128, 512], F32, 3)
            HR = {"x": k.ring("hx", [128, 16, 256], F32, 1), "sq": k.ring("hsq", [128, 256], F32, 3),
                  "rs": k.ring("hrs", [128, 256], F32, 2), "tm": k.ring("htm", [128, 256], F32, 3),
                  "ps": k.psring("hps", 1)}
            pp = k.psring("pp", 3)
            pq = k.psring("pq", 2)
            xv = self.xs[b].rearrange("(c p) t -> p c t", p=128)
            MM = ((0, 512), (512, 256))
            blocks = self.MLP_BLOCKS
            for (t0, n) in blocks[0]:
                self.hT_piece(hTs[0], b, self.g2, 48, t0, n, t0 - blocks[0][0][0], HR)
            for bi, subs in enumerate(blocks):
                base = subs[0][0]
                hT = hTs[bi % 2]
                nxt = blocks[bi + 1] if bi + 1 < len(blocks) else None
                for fg in range(4):
                    aT = aTr.next()
                    def load_w1(fc_):
                        w_ = w1r.next()
                        k.dma("sp", w_.t[:], self.W1t[fg * 16 + fc_], [], [w_])
                        return w_

                    for fc, w in self.pf(range(16), load_w1):
                        for (o, n) in MM:
                            p = pp.next()
                            for kc in range(16):
                                k.mm(p.t[:, 0:n], w.t[:, kc, :], hT.t[:, kc, o:o + n], kc == 0, kc == 15, [w, hT], [p])
                            rr = rl.next()
                            k.act(rr.t[:, 0:n], p.t[:, 0:n], AF.Relu, [p], [rr])
                            k.tt(aT.t[:, fc, o:o + n], rr.t[:, 0:n], rr.t[:, 0:n], ALU.mult, [rr], [aT])
                    if nxt is not None and fg < 3:
                        (t0n, nn) = nxt[fg]
                        self.hT_piece(hTs[(bi + 1) % 2], b, self.g2, 48, t0n, nn, t0n - nxt[0][0], HR)
                    def load_w2(dc_):
                        w_ = w2r.next()
                        k.dma("sp", w_.t[:], self.W2t[fg, dc_], [], [w_])
                        return w_

                    for dc, w in self.pf(range(16), load_w2):
                        for (o, n) in MM:
                            p = pq.next()
                            for fc in range(16):
                                k.mm(p.t[:, 0:n], w.t[:, fc, :], aT.t[:, fc, o:o + n], fc == 0, fc == 15, [w, aT], [p])
                            if fg == 0:
                                k.cp(oacc.t[:, dc, o:o + n], p.t[:, 0:n], [p], [oacc], eng="act")
                            elif fg < 3:
                                k.tt(oacc.t[:, dc, o:o + n], p.t[:, 0:n], oacc.t[:, dc, o:o + n], ALU.add, [p, oacc], [oacc])
                            else:
                                tm = rl.next()
                                k.tt(tm.t[:, 0:n], p.t[:, 0:n], oacc.t[:, dc, o:o + n], ALU.add, [p, oacc], [tm])
                                xo = xr.next()
                                k.dma("sp", xo.t[:, 0:n], xv[:, dc, base + o:base + o + n], [], [xo])
                                a0 = base + o
                                cuts = [a0] + ([CTX] if a0 < CTX < a0 + n else []) + [a0 + n]
                                for ci in range(len(cuts) - 1):
                                    c0, c1 = cuts[ci] - a0, cuts[ci + 1] - a0
                                    r = 2 if cuts[ci] < CTX else b
                                    k.stt(xo.t[:, c0:c1], tm.t[:, c0:c1], self.modv.t[:, 80 + dc, r:r + 1], xo.t[:, c0:c1],
                                          ALU.mult, ALU.add, [tm, self.modv, xo], [xo])
                                k.dma("sp", xv[:, dc, base + o:base + o + n], xo.t[:, 0:n], [xo], [])

    def declare_mixer_inputs(self):
        L = DEPTH
        self.s5v = self.din("s5v", [L, 128, 192])
        self.s5A = self.din("s5A", [L, 128, 64, 16])
        self.s5B = self.din("s5B", [L, 128, 64, 16])
        self.s5CA = self.din("s5CA", [L, 128, 64, 16])
        self.s5CB = self.din("s5CB", [L, 128, 64, 16])
        self.s5w = self.din("s5w", [L, 128, 8])
        self.glu_w = self.din("glu_w", [L, 512, 512])
        self.nidx8 = self.din("nidx8", [2, 128, T // 8])
        self.selu = self.din("selu", [128, 64, 128], BF16)
        self.selt = self.din("selt", [128, 64, 128], BF16)
        self.bmask = self.din("bmask", [128, 256])
        self.lruv = self.din("lruv", [L, 128, 44])
        self.lru_wa = self.din("lru_wa", [L, 2, 4, 128, 128])
        self.lru_wx = self.din("lru_wx", [L, 2, 4, 128, 128])
        self.ygd = self.dscr("ygd", [2, 512, T])
        self.mlb = self.din("mlb", [L, 128, 16])
        self.onw = self.din("onw", [L, 128, 512])
        self.selc = self.din("selc", [16, 2048])
        self.mlav = self.din("mlav", [L, 128, 896])
        self.w_q_up = self.din("w_q_up", [L, 384, 768])
        self.w_kv_up = self.din("w_kv_up", [L, 128, 1024])
        self.ropec = self.din("ropec", [128, NT, 32])
        self.ropes = self.din("ropes", [128, NT, 32])

    def setup_mixer_consts(self):
        k = self.k
        self.sgn = self.C.t[:, 896:897]
        self.oneT = k.sb("oneT", [128, 1], F32)
        k.op("dve", lambda e: e.memset(self.oneT.t[:], 1.0), [], [self.oneT])
        self.hpiT = k.sb("hpiT", [128, 1], F32)
        k.op("dve", lambda e: e.memset(self.hpiT.t[:], float(np.pi / 2)), [], [self.hpiT])

    def mixers(self, l):
        which = self.dbg.get("mixers", ("s5", "lru", "mlstm", "mla"))
        if "s5" in which:
            self.mixer_s5(l)
        if "lru" in which:
            self.mixer_lru(l)
        if "mlstm" in which:
            self.mixer_mlstm(l)
        if "mla" in which:
            self.mixer_mla(l)

    def frac_centered(self, out, u, ki, tmp, n, bufs):
        k = self.k
        k.cp(ki, u, bufs, bufs)
        k.tt(tmp, u, ki, ALU.subtract, bufs, bufs)
        k.stt(out, tmp, 0.5, tmp, ALU.is_gt, ALU.subtract, bufs, bufs)
        k.stt(out, out, 0.5, out, ALU.is_gt, ALU.subtract, bufs, bufs)

    def sincos(self, sin_out, cos_out, r, tmp, bufs):
        k = self.k
        k.act(sin_out, r, AF.Sin, bufs, bufs, scale=TWO_PI)
        k.stt(tmp, r, 0.25, r, ALU.is_gt, ALU.subtract, bufs, bufs)
        k.act(cos_out, tmp, AF.Sin, bufs + [self.hpiT], bufs, scale=-TWO_PI, bias=self.hpiT.t[:, 0:1])

    def gelu_tanh(self, out, y, t1, s1, bufs):
        k = self.k
        k.tt(t1, y, y, ALU.mult, bufs, bufs)
        k.ts(t1, t1, 0.044715, 1.0, ALU.mult, ALU.add, bufs, bufs)
        k.tt(t1, t1, y, ALU.mult, bufs, bufs)
        k.act(s1, t1, AF.Sigmoid, bufs, bufs, scale=1.5957691216057308)
        k.tt(out, y, s1, ALU.mult, bufs, bufs)

    def mixer_s5(self, l):
        k = self.k
        NK = T // 8
        KC = CTX // 8
        with k.scope():
            pv = k.sb("s5pv", [128, 192], F32)
            k.dma("sp", pv.t[:], self.s5v[l], [], [pv])
            A = k.sb("s5A", [128, 64, 16], F32)
            Bm = k.sb("s5B", [128, 64, 16], F32)
            CA = k.sb("s5CA", [128, 64, 16], F32)
            CB = k.sb("s5CB", [128, 64, 16], F32)
            k.dma("sp", A.t[:], self.s5A[l], [], [A])
            k.dma("sp", Bm.t[:], self.s5B[l], [], [Bm])
            k.dma("sp", CA.t[:], self.s5CA[l], [], [CA])
            k.dma("sp", CB.t[:], self.s5CB[l], [], [CB])
            sw = k.sb("s5w", [128, 8], F32)
            k.dma("sp", sw.t[:], self.s5w[l], [], [sw])
            nid = k.sb("nidx8", [128, 2, NK], F32)
            for d in range(2):
                k.dma("sp", nid.t[:, d, :], self.nidx8[d], [], [nid])
            selu = k.sb("selu", [128, 64, 128], BF16)
            selt = k.sb("selt", [128, 64, 128], BF16)
            k.dma("sp", selu.t[:], self.selu[:, :, :], [], [selu])
            k.dma("sp", selt.t[:], self.selt[:, :, :], [], [selt])
            bmask = k.sb("bmask", [128, 256], F32)
            k.dma("sp", bmask.t[:], self.bmask[:, :], [], [bmask])
            W = k.sb("s5work", [128, 16, 64], F32)
            WI = k.sb("s5worki", [128, 64], I32)
            Wb = [W]
            lr, li, dt, mag, ang, fT, sn, cs_, t0_, t1_, fr, fi, den, lrdt, f8, ar1 = [W.t[:, i, :] for i in range(16)]
            k.ts(lr, pv.t[:, 0:64], -1e-4, None, ALU.min, None, [pv], Wb)
            k.cp(li, pv.t[:, 64:128], [pv], Wb)
            k.act(dt, pv.t[:, 128:192], AF.Exp, [pv], Wb)
            k.tt(lrdt, lr, dt, ALU.mult, Wb, Wb)
            k.act(mag, lrdt, AF.Exp, Wb, Wb)
            k.tt(ang, li, dt, ALU.mult, Wb, Wb)
            k.ts(t0_, ang, 1.0 / TWO_PI, None, ALU.mult, None, Wb, Wb)
            self.frac_centered(fT, t0_, WI.t[:], t1_, 64, Wb + [WI])
            self.sincos(sn, cs_, fT, t1_, Wb)
            k.tt(t0_, mag, cs_, ALU.mult, Wb, Wb)
            k.ts(ar1, t0_, -1.0, None, ALU.add, None, Wb, Wb)
            k.tt(t1_, mag, sn, ALU.mult, Wb, Wb)
            k.tt(den, lr, lr, ALU.mult, Wb, Wb)
            k.tt(t0_, li, li, ALU.mult, Wb, Wb)
            k.tt(den, den, t0_, ALU.add, Wb, Wb)
            k.op("dve", lambda e: e.reciprocal(out=den, in_=den), Wb, Wb)
            k.tt(fr, ar1, lr, ALU.mult, Wb, Wb)
            k.tt(t0_, t1_, li, ALU.mult, Wb, Wb)
            k.tt(fr, fr, t0_, ALU.add, Wb, Wb)
            k.tt(fr, fr, den, ALU.mult, Wb, Wb)
            k.tt(fi, t1_, lr, ALU.mult, Wb, Wb)
            k.tt(t0_, ar1, li, ALU.mult, Wb, Wb)
            k.tt(fi, fi, t0_, ALU.subtract, Wb, Wb)
            k.tt(fi, fi, den, ALU.mult, Wb, Wb)
            nsgn = k.sb("nsgn", [128, 1], F32)
            k.ts(nsgn.t[:], self.sgn, -1.0, None, ALU.mult, None, [self.C], [nsgn])
            M8 = k.sb("mag8", [128, 64], F32)
            k.act(M8.t[:], lrdt, AF.Exp, Wb, [M8], scale=8.0)
            k.ts(t0_, fT, 8.0, None, ALU.mult, None, Wb, Wb)
            self.frac_centered(f8, t0_, WI.t[:], t1_, 64, Wb + [WI])
            PR = k.sb("PR", [128, 16, 64], F32)
            PI = k.sb("PI", [128, 16, 64], F32)
            PRN = k.sb("PRN", [128, 16, 64], F32)
            PRM = k.sb("PRM", [128, 16, 64], F32)
            PIS = k.sb("PIS", [128, 16, 64], F32)
            PIM = k.sb("PIM", [128, 16, 64], F32)
            GR = k.sb("GR", [128, 16, 64], F32)
            GI = k.sb("GI", [128, 16, 64], F32)
            GRN = k.sb("GRN", [128, 16, 64], F32)
            GIS = k.sb("GIS", [128, 16, 64], F32)
            GIM = k.sb("GIM", [128, 16, 64], F32)
            PWs = [PR, PI, PRN, PRM, PIS, PIM, GR, GI, GRN, GIS, GIM]
            for m in range(-7, 9):
                mi = m + 7
                k.act(t0_, lrdt, AF.Exp, Wb, Wb, scale=float(m))
                k.ts(ang, fT, float(m), None, ALU.mult, None, Wb, Wb)
                self.frac_centered(den, ang, WI.t[:], t1_, 64, Wb + [WI])
                self.sincos(sn, cs_, den, t1_, Wb)
                k.tt(PR.t[:, mi, :], t0_, cs_, ALU.mult, Wb, [PR])
                k.tt(PI.t[:, mi, :], t0_, sn, ALU.mult, Wb, [PI])
                k.ts(PRN.t[:, mi, :], PR.t[:, mi, :], nsgn.t[:, 0:1], None, ALU.mult, None, [PR, nsgn], [PRN])
                k.ts(PRM.t[:, mi, :], PR.t[:, mi, :], -1.0, None, ALU.mult, None, [PR], [PRM])
                k.ts(PIS.t[:, mi, :], PI.t[:, mi, :], self.sgn, None, ALU.mult, None, [PI, self.C], [PIS])
                k.ts(PIM.t[:, mi, :], PI.t[:, mi, :], -1.0, None, ALU.mult, None, [PI], [PIM])
                k.tt(t0_, PR.t[:, mi, :], fr, ALU.mult, [PR] + Wb, Wb)
                k.tt(t1_, PI.t[:, mi, :], fi, ALU.mult, [PI] + Wb, Wb)
                k.tt(GR.t[:, mi, :], t0_, t1_, ALU.subtract, Wb, [GR])
                k.tt(t0_, PR.t[:, mi, :], fi, ALU.mult, [PR] + Wb, Wb)
                k.tt(t1_, PI.t[:, mi, :], fr, ALU.mult, [PI] + Wb, Wb)
                k.tt(GI.t[:, mi, :], t0_, t1_, ALU.add, Wb, [GI])
                k.ts(GRN.t[:, mi, :], GR.t[:, mi, :], nsgn.t[:, 0:1], None, ALU.mult, None, [GR, nsgn], [GRN])
                k.ts(GIS.t[:, mi, :], GI.t[:, mi, :], self.sgn, None, ALU.mult, None, [GI, self.C], [GIS])
                k.ts(GIM.t[:, mi, :], GI.t[:, mi, :], -1.0, None, ALU.mult, None, [GI], [GIM])

            LB = k.sb("LB", [128, 128], F32)
            RC = k.sb("RC", [128, 128], F32)
            LST = k.sb("LST", [128, 128], F32)
            LS2T = k.sb("LS2T", [128, 128], F32)
            W1f = k.sb("W1f", [128, 128], F32)
            W2f = k.sb("W2f", [128, 128], F32)
            tA = k.ring("tA", [128, 8, 16], F32, 6)
            Mi_r = k.ring("Mi", [128, 128], BF16, 2)
            LS_r = k.ring("LS", [128, 128], BF16, 2)
            LS2_r = k.ring("LS2", [128, 128], BF16, 2)
            W1_r = k.ring("W1", [128, 128], BF16, 2)
            W2_r = k.ring("W2", [128, 128], BF16, 2)
            C8 = k.sb("C8", [128, NK], F32)
            S8 = k.sb("S8", [128, NK], F32)
            U8 = k.sb("U8", [128, NK], F32)
            K8 = k.sb("K8", [128, NK], I32)
            R8 = k.sb("R8", [128, NK], F32)
            Ug = [k.sb("Ug", [128, NK], BF16) for _ in range(2)]
            t1r = k.ring("t1", [128, NK], F32, 2)
            btr = k.ring("bt", [128, NK], F32, 2)
            Gr_ = k.ring("G", [128, NK], F32, 2)
            V1r = k.ring("V1", [128, NK], BF16, 2)
            V2r = k.ring("V2", [128, NK], BF16, 2)
            Ysb = [[k.sb("Ysb", [128, NK], BF16) for _ in range(2)] for _ in range(8)]
            ub = [k.sb("ub", [128, T], BF16) for _ in range(2)]
            yacc = [k.sb("yacc", [128, T], F32) for _ in range(2)]
            fin = k.ring("fin", [128, 512], F32, 6)
            wc32 = k.ring("wc32", [128, 2048], F32, 2)
            wc16 = k.ring("wc16", [128, 2048], BF16, 2)
            wgen = self.wcast_gen(l, wc32, wc16)
            pY = [k.ps("pY0"), k.ps("pY1")]
            pP = k.psring("pP", 2)
            pW = k.psring("pW", 2)
            pUn = k.psring("pUn", 2)
            ei = 0

            def build(dst, coefA, coefB, srcA, srcB, mlist, dg):
                mi0 = mlist[0] + 7
                if mlist[1] - mlist[0] == 1:
                    msl = slice(mi0, mi0 + 8)
                else:
                    msl = slice(mi0, (mi0 - 8) if mi0 - 8 >= 0 else None, -1)
                ca = coefA.t[:, msl, dg:dg + 1].to_broadcast([128, 8, 16])
                cb = coefB.t[:, msl, dg:dg + 1].to_broadcast([128, 8, 16])
                sa = srcA.t[:, dg:dg + 1, :].to_broadcast([128, 8, 16])
                sb_ = srcB.t[:, dg:dg + 1, :].to_broadcast([128, 8, 16])
                ta = tA.next()
                tb = tA.next()
                k.tt(ta.t[:], sa, ca, ALU.mult, [srcA, coefA], [ta])
                k.tt(tb.t[:], sb_, cb, ALU.mult, [srcB, coefB], [tb], eng="pool")
                k.tt(dst.t[:].rearrange("p (m c) -> p m c", c=16), ta.t[:], tb.t[:], ALU.add, [ta, tb], [dst])

            for c in range(4):
                for b in range(2):
                    for (t0, n) in TB:
                        u_ = fin.next()
                        k.dma("sp", u_.t[:, 0:n], self.zf[b, c * 128:(c + 1) * 128, t0:t0 + n], [], [u_])
                        k.cp(ub[b].t[:, t0:t0 + n], u_.t[:, 0:n], [u_], [ub[b]], eng="act")
                for j in range(8):
                    g = 8 * c + j
                    for b in range(2):
                        p = pW.next()
                        for s_ in range(8):
                            k.mm(p.t[:, 0:NK], selu.t[:, j * 8 + s_, :], ub[b].t[:, s_:T:8], s_ == 0, s_ == 7, [selu, ub[b]], [p])
                        k.cp(Ug[b].t[:], p.t[:, 0:NK], [p], [Ug[b]], eng="act")
                    for d in range(2):
                        dg = d * 32 + g
                        if d == 0:
                            mLB = [-s_ for s_ in range(8)]
                            mLS = [7 - s_ for s_ in range(8)]
                            mRC = [t_ for t_ in range(8)]
                            mW = [t_ + 1 for t_ in range(8)]
                        else:
                            mLB = [s_ for s_ in range(8)]
                            mLS = [s_ for s_ in range(8)]
                            mRC = [-t_ for t_ in range(8)]
                            mW = [8 - t_ for t_ in range(8)]
                        build(LB, GRN, GIM, A, Bm, mLB, dg)
                        build(RC, PR, PIS, CA, CB, mRC, dg)
                        build(LST, GR, GIS, A, Bm, mLS, dg)
                        build(LS2T, GI, GRN, A, Bm, mLS, dg)
                        build(W1f, PRN, PIM, CA, CB, mW, dg)
                        build(W2f, PIS, PRM, CA, CB, mW, dg)
                        p = pW.next()
                        k.mm(p.t[:, 0:128], LB.t[:], RC.t[:], True, True, [LB, RC], [p])
                        Mi = Mi_r.next()
                        k.tt(Mi.t[:], p.t[:, 0:128], bmask.t[:, d * 128:(d + 1) * 128], ALU.mult, [p, bmask], [Mi])
                        p = pW.next()
                        k.tr(p.t[:, 0:128], LST.t[:], self.identF, [LST, self.C], [p])
                        k.tr(p.t[:, 128:256], LS2T.t[:], self.identF, [LS2T, self.C], [p])
                        LS = LS_r.next()
                        LS2 = LS2_r.next()
                        k.cp(LS.t[:], p.t[:, 0:128], [p], [LS], eng="act")
                        k.cp(LS2.t[:], p.t[:, 128:256], [p], [LS2], eng="act")
                        W1 = W1_r.next()
                        W2 = W2_r.next()
                        k.cp(W1.t[:], W1f.t[:], [W1f], [W1], eng="pool")
                        k.cp(W2.t[:], W2f.t[:], [W2f], [W2], eng="pool")
                        k.ts(U8.t[:], nid.t[:, d, :], f8[:, dg:dg + 1], None, ALU.mult, None, [nid] + Wb, [U8])
                        self.frac_centered(R8.t[:], U8.t[:], K8.t[:], U8.t[:], NK, [U8, K8, R8])
                        self.sincos(S8.t[:], C8.t[:], R8.t[:], U8.t[:], [R8, U8, S8, C8])
                        for b in range(2):
                            next(wgen, None)
                            p1 = pP.next()
                            p2 = pP.next()
                            k.mm(p1.t[:, 0:NK], LS.t[:], Ug[b].t[:], True, True, [LS, Ug[b]], [p1])
                            k.mm(p2.t[:, 0:NK], LS2.t[:], Ug[b].t[:], True, True, [LS2, Ug[b]], [p2])
                            t1 = t1r.next()
                            bt = btr.next()
                            k.tt(t1.t[:], p1.t[:, 0:NK], C8.t[:], ALU.mult, [p1, C8], [t1])
                            k.tt(bt.t[:], p2.t[:, 0:NK], S8.t[:], ALU.mult, [p2, S8], [bt])
                            k.tt(bt.t[:], bt.t[:], t1.t[:], ALU.add, [bt, t1], [bt])
                            G = Gr_.next()
                            rm = M8.t[:, dg:dg + 1]
                            if d == 0:
                                k.scan(G.t[:, 0:KC], rm.to_broadcast([128, KC]), bt.t[:, 0:KC], 0.0, [bt, M8], [G])
                                k.scan(G.t[:, KC:NK], rm.to_broadcast([128, NK - KC]), bt.t[:, KC:NK], G.t[:, KC - 1:KC], [bt, G, M8], [G])
                            else:
                                k.scan(G.t[:, 0:KC][:, ::-1], rm.to_broadcast([128, KC]), bt.t[:, 0:KC][:, ::-1], 0.0, [bt, M8], [G])
                                k.scan(G.t[:, KC:NK][:, ::-1], rm.to_broadcast([128, NK - KC]), bt.t[:, KC:NK][:, ::-1], G.t[:, 0:1], [bt, G, M8], [G])
                            V1 = V1r.next()
                            V2 = V2r.next()
                            k.tt(V1.t[:], G.t[:], C8.t[:], ALU.mult, [G, C8], [V1])
                            k.tt(V2.t[:], G.t[:], S8.t[:], ALU.mult, [G, S8], [V2])
                            py = pY[b]
                            k.mm(py.t[:, 0:NK], Mi.t[:], Ug[b].t[:], d == 0, False, [Mi, Ug[b]], [py])
                            if d == 0:
                                segs = [(1, NK, 0)]
                            else:
                                segs = [(0, KC - 1, 1), (KC, NK - 1, KC + 1), (NK - 1, NK, 0)]
                            for si, (o0, o1, s0) in enumerate(segs):
                                n_ = o1 - o0
                                lastmm = (d == 1 and si == len(segs) - 1)
                                k.mm(py.t[:, o0:o1], W1.t[:], V1.t[:, s0:s0 + n_], False, False, [W1, V1], [py])
                                k.mm(py.t[:, o0:o1], W2.t[:], V2.t[:, s0:s0 + n_], False, lastmm, [W2, V2], [py])
                    for b in range(2):
                        k.cp(Ysb[j][b].t[:], pY[b].t[:, 0:NK], [pY[b]], [Ysb[j][b]], eng=("act" if b else "dve"))
                for b in range(2):
                    for bb in range(5):
                        nk = 64 if bb < 4 else 32
                        p = pUn.next()
                        for t_ in range(8):
                            for j in range(8):
                                k.mm(p.t[:, t_:8 * nk:8], selt.t[:, j * 8 + t_, :], Ysb[j][b].t[:, 64 * bb:64 * bb + nk], j == 0, j == 7,
                                     [selt, Ysb[j][b]], [p])
                        k.cp(yacc[b].t[:, 512 * bb:512 * bb + 8 * nk], p.t[:, 0:8 * nk], [p], [yacc[b]], eng=("act" if bb % 2 else "dve"))
                for b in range(2):
                    for (t0, n) in TB:
                        u_ = fin.next()
                        k.dma("sp", u_.t[:, 0:n], self.zf[b, c * 128:(c + 1) * 128, t0:t0 + n], [], [u_])
                        y_ = fin.next()
                        k.stt(y_.t[:, 0:n], u_.t[:, 0:n], sw.t[:, c:c + 1], yacc[b].t[:, t0:t0 + n], ALU.mult, ALU.add,
                              [u_, sw, yacc[b]], [y_])
                        o_ = fin.next()
                        a_ = fin.next()
                        b_ = fin.next()
                        self.gelu_tanh(o_.t[:, 0:n], y_.t[:, 0:n], a_.t[:, 0:n], b_.t[:, 0:n], [o_, y_, a_, b_])
                        k.dma("sp", self.ygd[b, c * 128:(c + 1) * 128, t0:t0 + n], o_.t[:, 0:n], [o_], [])
            for _ in wgen:
                pass
            self.wcast_done = True
        with k.scope():
            sw = k.sb("s5w", [128, 8], F32)
            k.dma("sp", sw.t[:], self.s5w[l], [], [sw])
            gw32 = k.sb("gw32", [128, 4, 512], F32)
            gw = k.sb("gw", [128, 4, 512], BF16)
            k.dma("sp", gw32.t[:], self.glu_w[l].rearrange("(kc p) o -> p kc o", p=128), [], [gw32])
            k.cp(gw.t[:], gw32.t[:], [gw32], [gw], eng="pool")
            yg = k.sb("yg", [128, 4, T], F32)
            ygb = k.sb("ygb", [128, 4, T], BF16)
            sg = k.ring("sg", [128, 512], F32, 2)
            ob = k.ring("ob", [128, 512], BF16, 3)
            pp = k.psring("pg", 3)
            for b in range(2):
                for c in range(4):
                    k.dma("sp", yg.t[:, c, :], self.ygd[b, c * 128:(c + 1) * 128, :], [], [yg])
                    k.cp(ygb.t[:, c, :], yg.t[:, c, :], [yg], [ygb], eng="act")
                for co in range(4):
                    for (t0, n) in TB:
                        p = pp.next()
                        for kc in range(4):
                            k.mm(p.t[:, 0:n], gw.t[:, kc, co * 128:(co + 1) * 128], ygb.t[:, kc, t0:t0 + n], kc == 0, kc == 3, [gw, ygb], [p])
                        s_ = sg.next()
                        k.act(s_.t[:, 0:n], p.t[:, 0:n], AF.Sigmoid, [p, sw], [s_], bias=sw.t[:, 4 + co:5 + co])
                        o = ob.next()
                        k.tt(o.t[:, 0:n], yg.t[:, co, t0:t0 + n], s_.t[:, 0:n], ALU.mult, [yg, s_], [o])
                        k.dma("sp", self.cc[b, co * 128:(co + 1) * 128, t0:t0 + n], o.t[:, 0:n], [o], [])

    def mixer_lru(self, l):
        k = self.k
        with k.scope():
            lv = k.sb("lruv", [128, 44], F32)
            k.dma("sp", lv.t[:], self.lruv[l], [], [lv])
            cw = lv.t[:, 0:16].rearrange("p (c j) -> p c j", j=4)
            cb = lv.t[:, 16:20]
            ba = lv.t[:, 20:28].rearrange("p (d c) -> p d c", c=4)
            bx = lv.t[:, 28:36].rearrange("p (d c) -> p d c", c=4)
            lam = lv.t[:, 36:44]
            sp = k.sb("lrusp", [128, 16], F32)
            k.act(sp.t[:, 0:8], lam, AF.Exp, [lv], [sp], scale=-1.0)
            k.act(sp.t[:, 0:8], sp.t[:, 0:8], AF.Ln, [sp, self.oneT], [sp], bias=self.oneT.t[:, 0:1])
            k.ts(sp.t[:, 8:16], sp.t[:, 0:8], -16.0, None, ALU.mult, None, [sp], [sp])
            k.ts(sp.t[:, 0:8], sp.t[:, 0:8], -8.0, None, ALU.mult, None, [sp], [sp])
            wa32 = k.sb("wa32", [128, 8, 128], F32)
            wx32 = k.sb("wx32", [128, 8, 128], F32)
            wa = k.sb("wa", [128, 8, 128], BF16)
            wx = k.sb("wx", [128, 8, 128], BF16)
            k.dma("sp", wa32.t[:], self.lru_wa[l].rearrange("d n c o -> c (d n) o"), [], [wa32])
            k.dma("sp", wx32.t[:], self.lru_wx[l].rearrange("d n c o -> c (d n) o"), [], [wx32])
            k.cp(wa.t[:], wa32.t[:], [wa32], [wa])
            k.cp(wx.t[:], wx32.t[:], [wx32], [wx])
            x = k.sb("lx", [128, T], F32)
            gt = k.sb("lg", [128, T], F32)
            xs = k.sb("lxs", [128, T], F32)
            xsb = k.sb("lxsb", [128, T], BF16)
            r_ = k.sb("lr", [128, T], F32)
            i_ = k.sb("li", [128, T], F32)
            a_ = k.sb("la", [128, T], F32)
            q_ = k.sb("lq", [128, T], F32)
            h_ = k.sb("lh", [128, T], F32)
            ys = k.sb("lys", [128, T], F32)
            ob = k.sb("lob", [128, T], BF16)
            pp = k.psring("pl", 4)
            for b in range(2):
                for c in range(4):
                    k.dma("sp", x.t[:], self.zf[b, 1536 + c * 128:1536 + (c + 1) * 128, :], [], [x])
                    k.dma("sp", gt.t[:], self.zf[b, 2048 + c * 128:2048 + (c + 1) * 128, :], [], [gt])
                    k.ts(xs.t[:], x.t[:], cw[:, c, 2:3], cb[:, c:c + 1], ALU.mult, ALU.add, [x, lv], [xs])
                    for jtap in (0, 1, 3):
                        o = jtap - 2
                        for (r0, r1) in ((0, CTX), (CTX, T)):
                            a0 = r0 + max(0, -o)
                            a1 = r1 - max(0, o)
                            k.stt(xs.t[:, a0:a1], x.t[:, a0 + o:a1 + o], cw[:, c, jtap:jtap + 1], xs.t[:, a0:a1],
                                  ALU.mult, ALU.add, [x, lv, xs], [xs])
                    k.cp(xsb.t[:], xs.t[:], [xs], [xsb], eng="act")
                    for d in range(2):
                        for (t0, n) in TB:
                            p = pp.next()
                            k.mm(p.t[:, 0:n], wa.t[:, d * 4 + c, :], xsb.t[:, t0:t0 + n], True, True, [wa, xsb], [p])
                            k.act(r_.t[:, t0:t0 + n], p.t[:, 0:n], AF.Sigmoid, [p, lv], [r_], bias=ba[:, d, c:c + 1])
                            p = pp.next()
                            k.mm(p.t[:, 0:n], wx.t[:, d * 4 + c, :], xsb.t[:, t0:t0 + n], True, True, [wx, xsb], [p])
                            k.act(i_.t[:, t0:t0 + n], p.t[:, 0:n], AF.Sigmoid, [p, lv], [i_], bias=bx[:, d, c:c + 1])
                        dc = d * 4 + c
                        k.act(a_.t[:], r_.t[:], AF.Exp, [r_, sp], [a_], scale=sp.t[:, dc:dc + 1])
                        k.act(q_.t[:], r_.t[:], AF.Exp, [r_, sp], [q_], scale=sp.t[:, 8 + dc:9 + dc])
                        k.act(q_.t[:], q_.t[:], AF.Sqrt, [q_, self.oneT], [q_], scale=-1.0, bias=self.oneT.t[:, 0:1])
                        k.tt(q_.t[:], q_.t[:], i_.t[:], ALU.mult, [q_, i_], [q_])
                        k.tt(q_.t[:], q_.t[:], xs.t[:], ALU.mult, [q_, xs], [q_])
                        if d == 0:
                            k.scan(h_.t[:, 0:CTX], a_.t[:, 0:CTX], q_.t[:, 0:CTX], 0.0, [a_, q_], [h_])
                            k.scan(h_.t[:, CTX:T], a_.t[:, CTX:T], q_.t[:, CTX:T], h_.t[:, CTX - 1:CTX], [a_, q_, h_], [h_])
                            k.cp(ys.t[:], h_.t[:], [h_], [ys], eng="pool")
                        else:
                            k.scan(h_.t[:, 0:CTX][:, ::-1], a_.t[:, 0:CTX][:, ::-1], q_.t[:, 0:CTX][:, ::-1], 0.0, [a_, q_], [h_])
                            k.scan(h_.t[:, CTX:T][:, ::-1], a_.t[:, CTX:T][:, ::-1], q_.t[:, CTX:T][:, ::-1], h_.t[:, 0:1], [a_, q_, h_], [h_])
                            k.tt(ys.t[:], ys.t[:], h_.t[:], ALU.add, [ys, h_], [ys])
                    self.gelu_tanh(h_.t[:], gt.t[:], a_.t[:], q_.t[:], [h_, gt, a_, q_])
                    k.tt(ob.t[:], ys.t[:], h_.t[:], ALU.mult, [ys, h_], [ob])
                    k.dma("sp", self.cc[b, 1536 + c * 128:1536 + (c + 1) * 128, :], ob.t[:], [ob], [])

    def mixer_mlstm(self, l):
        k = self.k
        KS = 128 ** -0.5
        TRI3 = self.C.t[:, 256:640]
        MASK = [self.C.t[:, 640:768], self.C.t[:, 768:896]]
        for b in range(2):
            with k.scope():
                Hacc = k.sb("Hacc", [128, NT, 512], F32)
                k.op("pool", lambda e: e.memset(Hacc.t[:], 0.0), [], [Hacc])
                with k.scope():
                    mlb = k.sb("mlb", [128, 16], F32)
                    k.dma("sp", mlb.t[:], self.mlb[l], [], [mlb])
                    sel = k.sb("sel", [16, 2048], F32)
                    nsel = k.sb("nsel", [16, 2048], F32)
                    k.dma("sp", sel.t[:], self.selc[:, :], [], [sel])
                    k.ts(nsel.t[:], sel.t[:], -1.0, None, ALU.mult, None, [sel], [nsel])
                    QT = k.sb("QT", [128, 4, T], BF16)
                    KT = k.sb("KT", [128, 4, T], BF16)
                    Kt = k.sb("Kt", [128, NT, 512], BF16)
                    Va = k.sb("Va", [128, NT, 4, 129], BF16)
                    G16 = k.sb("G16", [128, NT, 16], F32)
                    R = k.sb("R", [16, NT, 384], F32)
                    with k.scope():
                        st = k.ring("st", [128, T], F32, 2)
                        zr = k.ring("zr", [128, 1552], F32, 2)
                        pr = k.psring("pr", 2)
                        for h in range(4):
                            s_ = st.next()
                            k.dma("sp", s_.t[:], self.zf[b, 512 + h * 128:512 + (h + 1) * 128, :], [], [s_])
                            k.cp(QT.t[:, h, :], s_.t[:], [s_], [QT], eng="act")
                            s_ = st.next()
                            k.dma("sp", s_.t[:], self.zf[b, 1024 + h * 128:1024 + (h + 1) * 128, :], [], [s_])
                            k.act(KT.t[:, h, :], s_.t[:], AF.Copy, [s_], [KT], scale=KS)
                        k.op("pool", lambda e: e.memset(Va.t[:], 1.0), [], [Va])
                        for tt in range(NT):
                            z = zr.next()
                            k.dma("sp", z.t[:], self.zt[b, tt * 128:(tt + 1) * 128, 0:1552], [], [z])
                            k.act(Kt.t[:, tt, :], z.t[:, 0:512], AF.Copy, [z], [Kt], scale=KS)
                            k.cp(Va.t[:, tt, :, 0:128], z.t[:, 512:1024].rearrange("p (h e) -> p h e", h=4), [z], [Va])
                            k.tt(G16.t[:, tt, :], z.t[:, 1536:1552], mlb.t[:], ALU.add, [z, mlb], [G16])
                        for d in range(2):
                            gv = G16.t[:, :, d * 8 + 4:d * 8 + 8]
                            k.act(gv, gv, AF.Exp, [G16], [G16], scale=-1.0)
                            k.act(gv, gv, AF.Ln, [G16, self.oneT], [G16], bias=self.oneT.t[:, 0:1])
                            k.ts(gv, gv, -1.0, None, ALU.mult, None, [G16], [G16])
                        for tt in range(NT):
                            p = pr.next()
                            k.mm(p.t[0:16, 0:384], G16.t[:, tt, :], TRI3, True, True, [G16, self.C], [p])
                            k.cp(R.t[:, tt, :], p.t[0:16, 0:384], [p], [R], eng="act")
                    CT32_ = {}
                    CTb_ = {}
                    for d in range(2):
                        for h in range(4):
                            CT32_[d, h] = k.sb("CT32", [128, 129], F32)
                            CTb_[d, h] = k.sb("CTb", [128, 129], BF16)
                            k.op("dve", lambda e: e.memset(CT32_[d, h].t[:], 0.0), [], [CT32_[d, h]])
                            k.op("dve", lambda e: e.memset(CTb_[d, h].t[:], 0.0), [], [CTb_[d, h]])
                    EDr = k.ring("ED", [128, 128], F32, 4)
                    EBr = k.ring("EB", [128, 128], F32, 4)
                    STr = k.ring("ST", [128, 128], BF16, 4)
                    QSr = k.ring("QS", [128, 128], BF16, 4)
                    VWr = k.ring("VW", [128, 129], BF16, 4)
                    dnr = k.ring("dn", [128, 2], F32, 4)
                    pD = k.psring("pD", 2)
                    pB = k.psring("pB", 1)
                    pS = k.psring("pS", 2)
                    pN = k.psring("pN", 2)
                    pC = k.psring("pC", 1)
                    orders = [list(range(NT)), [1, 0] + list(range(NT - 1, 1, -1))]
                    for step in range(NT):
                        for d in range(2):
                            tt = orders[d][step]
                            bsl = slice(0, 128) if d == 0 else slice(128, 256)
                            last = 127 if d == 0 else 0
                            for h in range(4):
                                CT32 = CT32_[d, h]
                                CTb = CTb_[d, h]
                                kli = d * 8 + h
                                klf = d * 8 + 4 + h
                                SLI = sel.t[0:16, kli * 128:(kli + 1) * 128]
                                SLF = sel.t[0:16, klf * 128:(klf + 1) * 128]
                                NLF = nsel.t[0:16, klf * 128:(klf + 1) * 128]
                                tsl = slice(tt * 128, (tt + 1) * 128)
                                Rb = R.t[0:16, tt, bsl]
                                Rg = R.t[0:16, tt, 256:384]
                                pd_ = pD.next()
                                k.mm(pd_.t[:, 0:128], Rg, SLI, True, False, [R, sel], [pd_])
                                k.mm(pd_.t[:, 0:128], Rb, NLF, False, False, [R, nsel], [pd_])
                                k.mm(pd_.t[:, 0:128], SLF, Rb, False, False, [R, sel], [pd_])
                                k.mm(pd_.t[:, 0:128], self.identF, MASK[d], False, True, [self.C], [pd_])
                                ED = EDr.next()
                                k.act(ED.t[:], pd_.t[:, 0:128], AF.Exp, [pd_], [ED])
                                pb_ = pB.next()
                                k.mm(pb_.t[:, 0:128], SLF, Rb, True, True, [R, sel], [pb_])
                                EB = EBr.next()
                                k.act(EB.t[:], pb_.t[:, 0:128], AF.Exp, [pb_], [EB])
                                ps_ = pS.next()
                                k.mm(ps_.t[:, 0:128], KT.t[:, h, tsl], QT.t[:, h, tsl], True, True, [KT, QT], [ps_])
                                ST = STr.next()
                                k.tt(ST.t[:], ps_.t[:, 0:128], ED.t[:], ALU.mult, [ps_, ED], [ST])
                                QS = QSr.next()
                                k.tt(QS.t[:], QT.t[:, h, tsl], EB.t[:], ALU.mult, [QT, EB], [QS])
                                pn_ = pN.next()
                                k.mm(pn_.t[:, 0:129], QS.t[:], CTb.t[:], True, False, [QS, CTb], [pn_])
                                k.mm(pn_.t[:, 0:129], ST.t[:], Va.t[:, tt, h, :], False, True, [ST, Va], [pn_])
                                dn = dnr.next()
                                k.act(dn.t[:, 0:1], pn_.t[:, 128:129], AF.Abs, [pn_], [dn])
                                k.ts(dn.t[:, 0:1], dn.t[:, 0:1], 1.0, None, ALU.max, None, [dn], [dn])
                                k.op("dve", lambda e: e.reciprocal(out=dn.t[:, 1:2], in_=dn.t[:, 0:1]), [dn], [dn])
                                hs = Hacc.t[:, tt, h * 128:(h + 1) * 128]
                                k.stt(hs, pn_.t[:, 0:128], dn.t[:, 1:2], hs, ALU.mult, ALU.add, [pn_, dn, Hacc], [Hacc])
                                VW = VWr.next()
                                k.act(VW.t[:], Va.t[:, tt, h, :], AF.Identity, [Va, ED], [VW], scale=ED.t[:, last:last + 1])
                                pc_ = pC.next()
                                k.mm(pc_.t[:, 0:129], Kt.t[:, tt, h * 128:(h + 1) * 128], VW.t[:], True, True, [Kt, VW], [pc_])
                                k.stt(CT32.t[:], CT32.t[:], EB.t[:, last:last + 1], pc_.t[:, 0:129], ALU.mult, ALU.add,
                                      [CT32, EB, pc_], [CT32])
                                k.cp(CTb.t[:], CT32.t[:], [CT32], [CTb], eng="act")
                with k.scope():
                    onw = k.sb("onw", [128, 512], F32)
                    k.dma("sp", onw.t[:], self.onw[l], [], [onw])
                    OT = k.sb("OT", [128, 4, T], BF16)
                    mor = k.ring("mo", [128, 512], F32, 2)
                    sqr = k.ring("sqh", [128, 512], F32, 2)
                    ssr = k.ring("ss4", [128, 4], F32, 2)
                    pT = k.psring("pT", 2)
                    for tt in range(NT):
                        H = Hacc.t[:, tt, :]
                        sq = sqr.next()
                        k.tt(sq.t[:], H, H, ALU.mult, [Hacc], [sq])
                        ss = ssr.next()
                        k.op("dve", lambda e: e.tensor_reduce(out=ss.t[:], in_=sq.t[:].rearrange("p (h e) -> p h e", h=4), axis=AX.X, op=ALU.add), [sq], [ss])
                        k.act(ss.t[:], ss.t[:], AF.Sqrt, [ss, self.epsT], [ss], scale=1.0 / 128, bias=self.epsT.t[:, 0:1])
                        k.op("dve", lambda e: e.reciprocal(out=ss.t[:], in_=ss.t[:]), [ss], [ss])
                        for h in range(4):
                            hsl = slice(h * 128, (h + 1) * 128)
                            k.stt(sq.t[:, hsl], H[:, hsl], ss.t[:, h:h + 1], onw.t[:, hsl], ALU.mult, ALU.mult, [Hacc, ss, onw], [sq])
                        mo = mor.next()
                        k.dma("sp", mo.t[:], self.zt[b, tt * 128:(tt + 1) * 128, 1024:1536], [], [mo])
                        k.act(mo.t[:], mo.t[:], AF.Sigmoid, [mo], [mo])
                        k.tt(sq.t[:], sq.t[:], mo.t[:], ALU.mult, [sq, mo], [sq])
                        p = pT.next()
                        for h in range(4):
                            hsl = slice(h * 128, (h + 1) * 128)
                            k.tr(p.t[:, hsl], sq.t[:, hsl], self.identF, [sq, self.C], [p])
                        k.cp(OT.t[:, :, tt * 128:(tt + 1) * 128], p.t[:, 0:512].rearrange("p (h e) -> p h e", h=4), [p], [OT], eng="act")
                    for h in range(4):
                        k.dma("sp", self.cc[b, 512 + h * 128:512 + (h + 1) * 128, :], OT.t[:, h, :], [OT], [])

    def mixer_mla(self, l):
        k = self.k
        SC = 192 ** -0.5
        for b in range(2):
            with k.scope():
                QT = k.sb("aQT", [128, 4, T], BF16)
                QT2 = k.sb("aQT2", [64, 4, T], BF16)
                KT = k.sb("aKT", [128, 4, T], BF16)
                KT2 = k.sb("aKT2", [64, 4, T], BF16)
                Va = k.sb("aVa", [128, NT, 4, 129], BF16)
                k.op("pool", lambda e: e.memset(Va.t[:], 1.0), [], [Va])
                with k.scope():
                    nv = k.sb("mlav", [128, 896], F32)
                    k.dma("sp", nv.t[:], self.mlav[l], [], [nv])
                    QAW = nv.t[:, 0:384]
                    KVAW = nv.t[:, 384:512]
                    NW = [nv.t[:, 512:704], nv.t[:, 704:896]]
                    rc = k.sb("ropec", [128, NT, 32], F32)
                    rs = k.sb("ropes", [128, NT, 32], F32)
                    k.dma("sp", rc.t[:], self.ropec[:, :, :], [], [rc])
                    k.dma("sp", rs.t[:], self.ropes[:, :, :], [], [rs])
                    wq32 = k.sb("wq32", [128, 3, 768], F32)
                    wq = k.sb("wq", [128, 3, 768], BF16)
                    wkv32 = k.sb("wkv32", [128, 1024], F32)
                    wkv = k.sb("wkv", [128, 1024], BF16)
                    k.dma("sp", wq32.t[:], self.w_q_up[l].rearrange("(kc p) o -> p kc o", p=128), [], [wq32])
                    k.dma("sp", wkv32.t[:], self.w_kv_up[l], [], [wkv32])
                    k.cp(wq.t[:], wq32.t[:], [wq32], [wq], eng="pool")
                    k.cp(wkv.t[:], wkv32.t[:], [wkv32], [wkv], eng="pool")
                    Zr = k.ring("Z", [128, 576], F32, 2)
                    jk = k.sb("junk", [128, 384], F32)
                    ssr = k.ring("ss2", [128, 2], F32, 2)
                    cnr = k.ring("cn", [128, 512], F32, 2)
                    cTr = k.ring("cT", [128, 4, 128], BF16, 2)
                    Xr = [k.ring("X0", [128, 4, 192], F32, 2), k.ring("X1", [128, 4, 192], F32, 2)]
                    sqx = k.sb("sqx", [128, 768], F32)
                    s4r = k.ring("s4", [128, 4], F32, 2)
                    tmp = [k.sb("rt", [128, 4, 2, 16], F32) for _ in range(4)]
                    pT = k.psring("apT", 1)
                    pq = k.psring("apq", 2)
                    pk = k.psring("apk", 2)
                    pX = k.psring("apX", 2)
                    for tt in range(NT):
                        tsl = slice(tt * 128, (tt + 1) * 128)
                        Z = Zr.next()
                        k.dma("sp", Z.t[:], self.zt[b, tsl, 1552:2128], [], [Z])
                        ss = ssr.next()
                        k.act(jk.t[:, 0:384], Z.t[:, 0:384], AF.Square, [Z], [jk, ss], accum_out=ss.t[:, 0:1])
                        k.act(jk.t[:, 0:128], Z.t[:, 384:512], AF.Square, [Z], [jk, ss], accum_out=ss.t[:, 1:2])
                        k.act(ss.t[:, 0:1], ss.t[:, 0:1], AF.Sqrt, [ss, self.epsT], [ss], scale=1.0 / 384, bias=self.epsT.t[:, 0:1])
                        k.act(ss.t[:, 1:2], ss.t[:, 1:2], AF.Sqrt, [ss, self.epsT], [ss], scale=1.0 / 128, bias=self.epsT.t[:, 0:1])
                        k.op("dve", lambda e: e.reciprocal(out=ss.t[:], in_=ss.t[:]), [ss], [ss])
                        cn = cnr.next()
                        k.stt(cn.t[:, 0:384], Z.t[:, 0:384], ss.t[:, 0:1], QAW, ALU.mult, ALU.mult, [Z, ss, nv], [cn])
                        k.stt(cn.t[:, 384:512], Z.t[:, 384:512], ss.t[:, 1:2], KVAW, ALU.mult, ALU.mult, [Z, ss, nv], [cn])
                        p = pT.next()
                        for c4 in range(4):
                            k.tr(p.t[:, c4 * 128:(c4 + 1) * 128], cn.t[:, c4 * 128:(c4 + 1) * 128], self.identF, [cn, self.C], [p])
                        cT = cTr.next()
                        k.cp(cT.t[:], p.t[:, 0:512].rearrange("p (c e) -> p c e", c=4), [p], [cT], eng="act")
                        Xq = Xr[0].next()
                        Xk = Xr[1].next()
                        for nb in range(2):
                            p = pq.next()
                            for kc in range(3):
                                k.mm(p.t[:, 0:384], cT.t[:, kc, :], wq.t[:, kc, nb * 384:(nb + 1) * 384], kc == 0, kc == 2, [cT, wq], [p])
                            k.cp(Xq.t[:, 2 * nb:2 * nb + 2, :], p.t[:, 0:384].rearrange("p (h e) -> p h e", h=2), [p], [Xq], eng="act")
                        for nb in range(2):
                            p = pk.next()
                            k.mm(p.t[:, 0:512], cT.t[:, 3, :], wkv.t[:, nb * 512:(nb + 1) * 512], True, True, [cT, wkv], [p])
                            pv4 = p.t[:, 0:512].rearrange("p (h e) -> p h e", h=2)
                            k.cp(Xk.t[:, 2 * nb:2 * nb + 2, 0:128], pv4[:, :, 0:128], [p], [Xk])
                            k.cp(Va.t[:, tt, 2 * nb:2 * nb + 2, 0:128], pv4[:, :, 128:256], [p], [Va], eng="act")
                        for h in range(4):
                            k.cp(Xk.t[:, h, 128:192], Z.t[:, 512:576], [Z], [Xk], eng="pool")
                        for qi, X in enumerate((Xq, Xk)):
                            Xf = X.t[:].rearrange("p h e -> p (h e)")
                            k.tt(sqx.t[:], Xf, Xf, ALU.mult, [X], [sqx])
                            s4 = s4r.next()
                            k.op("dve", lambda e: e.tensor_reduce(out=s4.t[:], in_=sqx.t[:].rearrange("p (h e) -> p h e", h=4), axis=AX.X, op=ALU.add), [sqx], [s4])
                            k.act(s4.t[:], s4.t[:], AF.Sqrt, [s4, self.epsT], [s4], scale=1.0 / 192, bias=self.epsT.t[:, 0:1])
                            k.op("dve", lambda e: e.reciprocal(out=s4.t[:], in_=s4.t[:]), [s4], [s4])
                            for h in range(4):
                                k.stt(X.t[:, h, :], X.t[:, h, :], s4.t[:, h:h + 1], NW[qi], ALU.mult, ALU.mult, [X, s4, nv], [X])
                            rp = X.t[:, :, 128:192].rearrange("p h (a b f) -> p h a b f", a=2, b=2)
                            x1 = rp[:, :, :, 0, :]
                            x2 = rp[:, :, :, 1, :]
                            cosb = rc.t[:, tt, :].rearrange("p (o a f) -> p o a f", o=1, a=2).to_broadcast([128, 4, 2, 16])
                            sinb = rs.t[:, tt, :].rearrange("p (o a f) -> p o a f", o=1, a=2).to_broadcast([128, 4, 2, 16])
                            TT = tmp
                            k.tt(TT[0].t[:], x1, cosb, ALU.mult, [X, rc], [TT[0]])
                            k.tt(TT[1].t[:], x2, sinb, ALU.mult, [X, rs], [TT[1]])
                            k.tt(TT[2].t[:], x2, cosb, ALU.mult, [X, rc], [TT[2]])
                            k.tt(TT[3].t[:], x1, sinb, ALU.mult, [X, rs], [TT[3]])
                            k.tt(x1, TT[0].t[:], TT[1].t[:], ALU.subtract, [TT[0], TT[1], X], [X])
                            k.tt(x2, TT[2].t[:], TT[3].t[:], ALU.add, [TT[2], TT[3], X], [X])
                            dst, dst2 = (QT, QT2) if qi == 0 else (KT, KT2)
                            for hp in range(2):
                                p = pX.next()
                                for hh in range(2):
                                    h = 2 * hp + hh
                                    k.tr(p.t[:, hh * 256:hh * 256 + 128], X.t[:, h, 0:128], self.identF, [X, self.C], [p])
                                    k.tr(p.t[0:64, hh * 256 + 128:hh * 256 + 256], X.t[:, h, 128:192], self.identF, [X, self.C], [p])
                                pv_ = p.t[:, 0:512].rearrange("p (h e) -> p h e", h=2)
                                k.cp(dst.t[:, 2 * hp:2 * hp + 2, tsl], pv_[:, :, 0:128], [p], [dst], eng="act")
                                k.cp(dst2.t[0:64, 2 * hp:2 * hp + 2, tsl], p.t[0:64, 0:512].rearrange("p (h e) -> p h e", h=2)[:, :, 128:256], [p], [dst2])
                if self.dbg.get("mla_noattn"):
                    continue
                with k.scope():
                    Pr = k.ring("P", [128, 512], BF16, 3)
                    MT = k.ring("MT", [128, T], BF16, 2)
                    o32 = k.ring("o32", [128, 128], F32, 2)
                    rdr = k.ring("rd", [128, 1], F32, 2)
                    pS = k.psring("aS", 2)
                    po = [k.ps("apo%d" % i) for i in range(4)]
                    pT = k.psring("aT", 1)
                    blocks = [(0, 256, [0, 1])] + [(CTX + 512 * i, 512, list(range(NT))) for i in range(4)]
                    for h in range(4):
                        mt = MT.next()
                        for (q0, nq, kts) in blocks:
                            nsub = nq // 128
                            for idx, kt in enumerate(kts):
                                ksl = slice(kt * 128, (kt + 1) * 128)
                                ps_ = pS.next()
                                k.mm(ps_.t[:, 0:nq], KT.t[:, h, ksl], QT.t[:, h, q0:q0 + nq], True, False, [KT, QT], [ps_])
                                k.mm(ps_.t[:, 0:nq], KT2.t[0:64, h, ksl], QT2.t[0:64, h, q0:q0 + nq], False, True, [KT2, QT2], [ps_])
                                P = Pr.next()
                                k.act(P.t[:, 0:nq], ps_.t[:, 0:nq], AF.Exp, [ps_], [P], scale=SC)
                                for qs in range(nsub):
                                    k.mm(po[qs].t[:, 0:129], P.t[:, qs * 128:(qs + 1) * 128], Va.t[:, kt, h, :],
                                         idx == 0, idx == len(kts) - 1, [P, Va], [po[qs]])
                            for qs in range(nsub):
                                rd = rdr.next()
                                k.op("dve", lambda e: e.reciprocal(out=rd.t[:], in_=po[qs].t[:, 128:129]), [po[qs]], [rd])
                                o = o32.next()
                                k.ts(o.t[:], po[qs].t[:, 0:128], rd.t[:, 0:1], None, ALU.mult, None, [po[qs], rd], [o])
                                p = pT.next()
                                k.tr(p.t[:, 0:128], o.t[:], self.identF, [o, self.C], [p])
                                k.cp(mt.t[:, q0 + qs * 128:q0 + (qs + 1) * 128], p.t[:, 0:128], [p], [mt], eng="act")
                        k.dma("sp", self.cc[b, 1024 + h * 128:1024 + (h + 1) * 128, :], mt.t[:], [mt], [])


def make_consts():
    c = np.zeros((128, 2048), np.float32)
    i = np.arange(128)
    c[:, 0:128] = np.eye(128)
    c[:, 128:256] = 1.0
    c[:, 256:384] = (i[:, None] <= i[None, :])
    c[:, 384:512] = (i[:, None] >= i[None, :])
    c[:, 512:640] = np.eye(128)
    c[:, 640:768] = np.where(i[:, None] <= i[None, :], 0.0, -30000.0)
    c[:, 768:896] = np.where(i[:, None] >= i[None, :], 0.0, -30000.0)
    return c


def prep_common(inp):
    m = {}
    f = np.float32
    m["ada_w"] = inp["ada_w"]
    m["ada_bT"] = np.ascontiguousarray(inp["ada_b"].reshape(4, 96, 128).transpose(0, 2, 1))
    m["n1w"] = np.ascontiguousarray(inp["norm1_w"].reshape(4, 16, 128).transpose(0, 2, 1))
    m["n2w"] = np.ascontiguousarray(inp["norm2_w"].reshape(4, 16, 128).transpose(0, 2, 1))
    m["w_in"] = inp["w_in"]
    m["w_out"] = inp["w_out"]
    m["mlp_w1"] = inp["mlp_w1"]
    m["mlp_w2"] = inp["mlp_w2"]
    m["consts"] = make_consts()
    prep_mixers(inp, m)
    return m


def prep_core(inp, common, core):
    b0 = 2 * core
    m = dict(common)
    xin = np.concatenate([inp["ctx"][b0:b0 + 2], inp["x"][b0:b0 + 2]], axis=1)
    m["xin"] = np.ascontiguousarray(xin.transpose(0, 2, 1))
    c3 = np.stack([inp["c"][b0], inp["c"][b0 + 1], inp["c_ctx"]], axis=1)
    m["cT"] = np.ascontiguousarray(c3.reshape(16, 128, 3).transpose(1, 0, 2))
    return m


def _dup(a):
    return np.concatenate([a, a], axis=0)


def prep_mixers(inp, m):
    L = DEPTH
    c = m["consts"]
    c[:64, 896] = -1.0
    c[64:, 896] = 1.0
    lre = inp["s5_lam_re"].transpose(0, 3, 1, 2).reshape(L, 64, 64)
    lim = inp["s5_lam_im"].transpose(0, 3, 1, 2).reshape(L, 64, 64)
    ldt = np.broadcast_to(inp["s5_log_dt"].reshape(L, 1, 64), (L, 64, 64))
    s5v = np.concatenate([lre, lim, ldt], axis=2)
    m["s5v"] = np.ascontiguousarray(np.concatenate([s5v, s5v], axis=1))
    bre = inp["s5_b_re"].transpose(0, 3, 1, 2, 4).reshape(L, 64, 64, 16)
    bim = inp["s5_b_im"].transpose(0, 3, 1, 2, 4).reshape(L, 64, 64, 16)
    m["s5A"] = np.ascontiguousarray(np.concatenate([bre, bim], axis=1))
    m["s5B"] = np.ascontiguousarray(np.concatenate([bim, bre], axis=1))
    cre = inp["s5_c_re"].transpose(0, 4, 1, 2, 3).reshape(L, 64, 64, 16)
    cim = inp["s5_c_im"].transpose(0, 4, 1, 2, 3).reshape(L, 64, 64, 16)
    m["s5CA"] = np.ascontiguousarray(np.concatenate([cre, cim], axis=1))
    m["s5CB"] = np.ascontiguousarray(np.concatenate([cim, cre], axis=1))
    dsk = inp["s5_d"].reshape(L, 4, 128).transpose(0, 2, 1)
    glb = inp["s5_glu_b"].reshape(L, 4, 128).transpose(0, 2, 1)
    m["s5w"] = np.ascontiguousarray(np.concatenate([dsk, glb], axis=2))
    m["glu_w"] = inp["s5_glu_w"]
    NK, KC = T // 8, CTX // 8
    n0 = np.arange(NK, dtype=np.float32)
    n1 = np.concatenate([KC - 1 - np.arange(KC), KC + (NK - KC - 1 - np.arange(NK - KC))]).astype(np.float32)
    m["nidx8"] = np.ascontiguousarray(np.broadcast_to(np.stack([n0, n1])[:, None, :], (2, 128, NK)))
    selu = np.zeros((128, 8, 8, 128), np.float32)
    selt = np.zeros((128, 8, 8, 128), np.float32)
    for jj in range(8):
        for ss in range(8):
            for ci in range(16):
                selu[16 * jj + ci, jj, ss, 16 * ss + ci] = 1.0
                selt[16 * ss + ci, jj, ss, 16 * jj + ci] = 1.0
    m["selu"] = selu.reshape(128, 64, 128).astype(ml_dtypes.bfloat16)
    m["selt"] = selt.reshape(128, 64, 128).astype(ml_dtypes.bfloat16)
    blk = np.arange(128) // 16
    m["bmask"] = np.concatenate([(blk[None, :] >= blk[:, None]), (blk[None, :] <= blk[:, None])], axis=1).astype(np.float32)
    cw = inp["lru_conv_w"].reshape(L, 4, 4, 128).transpose(0, 3, 2, 1).reshape(L, 128, 16)
    cb = inp["lru_conv_b"].reshape(L, 4, 128).transpose(0, 2, 1)
    def dc(a):
        return a.reshape(L, 2, 4, 128).transpose(0, 3, 1, 2).reshape(L, 128, 8)
    m["lruv"] = np.ascontiguousarray(np.concatenate([cw, cb, dc(inp["lru_ba"]), dc(inp["lru_bx"]), dc(inp["lru_lam"])], axis=2))
    m["lru_wa"] = inp["lru_wa"]
    m["lru_wx"] = inp["lru_wx"]
    gb = np.concatenate([inp["ml_ig_bias"], inp["ml_fg_bias"]], axis=2).reshape(L, 1, 16)
    m["mlb"] = np.ascontiguousarray(np.broadcast_to(gb, (L, 128, 16)))
    m["onw"] = np.ascontiguousarray(np.broadcast_to(inp["ml_out_norm"].reshape(L, 1, 512), (L, 128, 512)))
    selc = np.zeros((16, 16, 128), np.float32)
    for kk in range(16):
        selc[kk, kk, :] = 1.0
    m["selc"] = selc.reshape(16, 2048)
    nv = np.concatenate([inp["mla_q_a_norm"], inp["mla_kv_a_norm"], inp["mla_q_norm"], inp["mla_k_norm"]], axis=1)
    m["mlav"] = np.ascontiguousarray(np.broadcast_to(nv.reshape(L, 1, 896), (L, 128, 896)))
    m["w_q_up"] = inp["mla_w_q_up"]
    m["w_kv_up"] = inp["mla_w_kv_up"]
    q = np.arange(LAT)
    inv = (np.float32(10000.0) ** (-np.arange(16, dtype=np.float32) / np.float32(16))).astype(np.float32)
    ang = np.concatenate([(q // 64).astype(np.float32)[:, None] * inv, (q % 64).astype(np.float32)[:, None] * inv], axis=1)
    cosf = np.ones((T, 32), np.float32)
    sinf = np.zeros((T, 32), np.float32)
    cosf[CTX:] = np.cos(ang.astype(np.float32))
    sinf[CTX:] = np.sin(ang.astype(np.float32))
    m["ropec"] = np.ascontiguousarray(cosf.reshape(NT, 128, 32).transpose(1, 0, 2))
    m["ropes"] = np.ascontiguousarray(sinf.reshape(NT, 128, 32).transpose(1, 0, 2))


_PROG = None


def kernel(**inputs):
    global _PROG
    inp = {k_: np.asarray(v) for k_, v in inputs.items()}
    if _PROG is None:
        _PROG = Prog()
    prog = _PROG
    common = prep_common(inp)
    in_maps = []
    for core in range(8):
        m = prep_core(inp, common, core)
        in_maps.append({n: m[n] for n in prog.inp})
    res = run_bass_kernel_spmd(prog.nc, in_maps, core_ids=list(range(8)))
    outs = [r["yout"] for r in res.results]
    y = np.concatenate(outs, axis=0)
    return np.ascontiguousarray(y.transpose(0, 2, 1)).astype(np.float32)
```
